# Optimizing a Trainium2 kernel written in Bass

```python
import jax, jax.numpy as jnp
from jax import lax
import numpy as np

D_MODEL = 1024
BATCH = 16
SEQ = 256
DEPTH = 4
DEC_BATCH = 4
DEC_SEQ = 1024
PAST_LEN = 512

GRID_W = 64
WIN_H = 8
WIN_W = 16
HEAD_DIM = 64
A_HEADS = 8
A_WIDTH = A_HEADS * HEAD_DIM
R_HEADS = 4
R_KEY_DIM = 64
R_VAL_DIM = 64
R_WIDTH = R_HEADS * R_KEY_DIM
F_GROUPS = 4
F_GROUP_DIM = 64
F_WIDTH = F_GROUPS * F_GROUP_DIM
MIX_WIDTH = A_WIDTH + R_WIDTH + F_WIDTH
IN_COLS = 4 * A_WIDTH + 5 * R_WIDTH + 2 * F_WIDTH
CHUNK = 64
Q_BLOCK = 128
EPS = 1e-6

kernel_name = "hybrid_natten_hgrn2_fnet_diffusion_step"


def _rmsnorm(x, g):
    x32 = x.astype(jnp.float32)
    y = x32 * lax.rsqrt(jnp.mean(x32 * x32, axis=-1, keepdims=True) + EPS)
    return (y * g.astype(jnp.float32)).astype(x.dtype)


def _split_columns(p):
    sizes = [A_WIDTH] * 4 + [R_WIDTH] * 5 + [F_WIDTH] * 2
    out, start = [], 0
    for s in sizes:
        out.append(p[..., start:start + s])
        start += s
    return out


def _heads(t, n):
    return t.reshape(t.shape[:-1] + (n, t.shape[-1] // n))


def _context_attention(q, k, v):
    B, S, H, dh = q.shape
    nb = S // Q_BLOCK
    qb = q.reshape(B, nb, Q_BLOCK, H, dh).transpose(1, 0, 2, 3, 4)
    scale = dh ** -0.5

    def blk(qi):
        s = jnp.einsum('bqhd,bkhd->bhqk', qi, k).astype(jnp.float32) * scale
        p = jax.nn.softmax(s, axis=-1).astype(v.dtype)
        return jnp.einsum('bhqk,bkhd->bqhd', p, v)

    o = lax.map(blk, qb)
    return o.transpose(1, 0, 2, 3, 4).reshape(B, S, H, dh)


def _neighbourhood_attention(q, k, v, k_ctx, v_ctx, rpb):
    B, N, H, dh = q.shape
    rows = N // GRID_W
    kh = min(WIN_H, rows)
    qg = q.reshape(B, rows, GRID_W, H, dh)
    kg = k.reshape(B, rows, GRID_W, H, dh)
    vg = v.reshape(B, rows, GRID_W, H, dh)
    cols = jnp.arange(GRID_W)
    c0 = jnp.clip(cols - WIN_W // 2, 0, GRID_W - WIN_W)
    col_in = (cols[None, :] >= c0[:, None]) & (cols[None, :] < c0[:, None] + WIN_W)
    col_idx = jnp.clip(cols[None, :] - cols[:, None] + WIN_W - 1, 0, 2 * WIN_W - 2)
    scale = dh ** -0.5

    def one_row(r):
        r0 = jnp.clip(r - kh // 2, 0, rows - kh)
        q_r = lax.dynamic_index_in_dim(qg, r, axis=1, keepdims=False)
        k_r = lax.dynamic_slice_in_dim(kg, r0, kh, axis=1)
        v_r = lax.dynamic_slice_in_dim(vg, r0, kh, axis=1)
        row_idx = r0 + jnp.arange(kh) - r + WIN_H - 1
        bias = rpb[:, row_idx[:, None, None], col_idx[None]]
        s_lat = jnp.einsum('bqhd,bjkhd->bhqjk', q_r, k_r).astype(jnp.float32) * scale
        s_lat = s_lat + bias.transpose(0, 2, 1, 3)[None].astype(jnp.float32)
        s_lat = jnp.where(col_in[:, None, :], s_lat, -jnp.inf)
        s_ctx = jnp.einsum('bqhd,bphd->bhqp', q_r, k_ctx).astype(jnp.float32) * scale
        s = jnp.concatenate([s_lat.reshape(B, H, GRID_W, kh * GRID_W), s_ctx], axis=-1)
        p = jax.nn.softmax(s, axis=-1).astype(v.dtype)
        v_all = jnp.concatenate([v_r.reshape(B, kh * GRID_W, H, dh), v_ctx], axis=1)
        return jnp.einsum('bhqk,bkhd->bqhd', p, v_all)

    o = lax.map(one_row, jnp.arange(rows))
    return o.transpose(1, 0, 2, 3, 4).reshape(B, N, H, dh)


def _hgrn_gates(z, lb):
    log_f = jnp.logaddexp(jnp.log(lb), jnp.log1p(-lb) + jax.nn.log_sigmoid(z))
    k = (1.0 - lb) * jax.nn.sigmoid(-z)
    return log_f, k


def _hgrn_scan(q, k, v, log_f, s0):
    B, N, H, _ = q.shape
    nc = N // CHUNK

    def to_chunks(t):
        return t.reshape(B, nc, CHUNK, H, t.shape[-1]).transpose(1, 0, 3, 2, 4)

    causal = jnp.tril(jnp.ones((CHUNK, CHUNK), dtype=bool))

    def step(s, inp):
        qc, kc, vc, gc = inp
        G = jnp.cumsum(gc, axis=2)
        diff = G[:, :, :, None, :] - G[:, :, None, :, :]
        decay = jnp.exp(jnp.where(causal[:, :, None], diff, -jnp.inf))
        attn = jnp.einsum('bhtd,bhsd,bhtsd->bhts', qc, kc, decay)
        o = jnp.einsum('bhtd,bhde->bhte', qc * jnp.exp(G), s) + jnp.einsum('bhts,bhse->bhte', attn, vc)
        G_last = G[:, :, -1:, :]
        s_new = jnp.exp(G_last[:, :, 0, :])[..., None] * s + jnp.einsum(
            'bhsd,bhse->bhde', kc * jnp.exp(G_last - G), vc)
        return s_new, o

    s_fin, o = lax.scan(step, s0, (to_chunks(q), to_chunks(k), to_chunks(v), to_chunks(log_f)))
    o = o.transpose(1, 0, 3, 2, 4).reshape(B, N, H, v.shape[-1])
    return o, s_fin


def _hgrn_bidir(qr, zf, zb, ir, lb_f, lb_b, s0):
    f32 = jnp.float32
    q = jax.nn.silu(_heads(qr.astype(f32), R_HEADS))
    v = _heads(ir.astype(f32), R_HEADS)
    s0 = s0.astype(f32)
    lf_f, k_f = _hgrn_gates(_heads(zf.astype(f32), R_HEADS), lb_f.reshape(R_HEADS, R_KEY_DIM))
    lf_b, k_b = _hgrn_gates(_heads(zb.astype(f32), R_HEADS), lb_b.reshape(R_HEADS, R_KEY_DIM))
    flip = lambda t: t[:, ::-1]
    o_f, s_f = _hgrn_scan(q, k_f, v, lf_f, s0[:, 0])
    o_b, s_b = _hgrn_scan(flip(q), flip(k_b), flip(v), flip(lf_b), s0[:, 1])
    return o_f + flip(o_b), jnp.stack([s_f, s_b], axis=1)


def _fourier(u, w_f):
    B, N, _ = u.shape
    ug = u.astype(jnp.float32).reshape(B, N, F_GROUPS, F_GROUP_DIM)
    y = jnp.fft.fftn(ug, axes=(1, 3), norm='ortho').real.reshape(B, N, F_WIDTH).astype(u.dtype)
    return y @ w_f


def _trunk_layer(x, mod, g_pre, w_in, rpb, lb_f, lb_b, g_hgrn, w_fnet, w_out, g_post,
                 ctx_kv=None, ctx_state=None):
    shift, scale, gate = jnp.split(mod, 3, axis=-1)
    h = _rmsnorm(x, g_pre) * (1 + scale) + shift
    proj = h @ w_in
    qa, ka, va, ga, qr, zf, zb, ir, gr, uf, gf = _split_columns(proj)
    qa, ka, va = _heads(qa, A_HEADS), _heads(ka, A_HEADS), _heads(va, A_HEADS)
    B, N, _ = x.shape
    if ctx_kv is None:
        oa = _context_attention(qa, ka, va)
        s0 = jnp.zeros((B, 2, R_HEADS, R_KEY_DIM, R_VAL_DIM), jnp.float32)
    else:
        oa = _neighbourhood_attention(qa, ka, va, ctx_kv[0], ctx_kv[1], rpb)
        s0 = ctx_state
    o_r, s_fin = _hgrn_bidir(qr, zf, zb, ir, lb_f, lb_b, s0)
    o_r = _rmsnorm(o_r, g_hgrn.reshape(R_HEADS, R_VAL_DIM)).reshape(B, N, R_WIDTH).astype(x.dtype)
    o_f = _fourier(uf, w_fnet)
    mixed = jnp.concatenate([
        oa.reshape(B, N, A_WIDTH) * jax.nn.silu(ga),
        o_r * jax.nn.silu(gr),
        o_f * jax.nn.silu(gf)], axis=-1)
    out = mixed @ w_out
    x = x + gate * _rmsnorm(out, g_post)
    return x, ka, va, s_fin


def setup_inputs(seed: int = 0) -> dict:
    key = jax.random.key(seed)
    ks = jax.random.split(key, 17)
    nrm = jax.random.normal
    f32 = jnp.float32
    return {
        'x_prompt': nrm(ks[0], (BATCH, SEQ, D_MODEL), f32),
        'x_sample': nrm(ks[1], (DEC_BATCH, DEC_SEQ, D_MODEL), f32),
        'cache_attn_k': nrm(ks[2], (DEC_BATCH, DEPTH, PAST_LEN, A_HEADS, HEAD_DIM), f32),
        'cache_attn_v': nrm(ks[3], (DEC_BATCH, DEPTH, PAST_LEN, A_HEADS, HEAD_DIM), f32),
        'state_hgrn': 0.5 * nrm(ks[4], (DEC_BATCH, DEPTH, 2, R_HEADS, R_KEY_DIM, R_VAL_DIM), f32),
        'c': nrm(ks[5], (DEC_BATCH, D_MODEL), f32),
        'c_ctx': nrm(ks[6], (D_MODEL,), f32),
        'w_ada': 0.5 * D_MODEL ** -0.5 * nrm(ks[7], (DEPTH, D_MODEL, 3 * D_MODEL), f32),
        'b_ada': 0.02 * nrm(ks[8], (DEPTH, 3 * D_MODEL), f32),
        'g_pre': 1.0 + 0.02 * nrm(ks[9], (DEPTH, D_MODEL), f32),
        'w_in': D_MODEL ** -0.5 * nrm(ks[10], (DEPTH, D_MODEL, IN_COLS), f32),
        'rpb': 0.1 * nrm(ks[11], (DEPTH, A_HEADS, 2 * WIN_H - 1, 2 * WIN_W - 1), f32),
        'lb_logits': nrm(ks[12], (2, DEPTH, R_WIDTH), f32),
        'g_hgrn': 1.0 + 0.02 * nrm(ks[13], (DEPTH, R_WIDTH), f32),
        'w_fnet': F_WIDTH ** -0.5 * nrm(ks[14], (DEPTH, F_WIDTH, F_WIDTH), f32),
        'w_out': MIX_WIDTH ** -0.5 * nrm(ks[15], (DEPTH, MIX_WIDTH, D_MODEL), f32),
        'g_post': 1.0 + 0.02 * nrm(ks[16], (DEPTH, D_MODEL), f32),
    }


def reference(x_prompt, x_sample, cache_attn_k, cache_attn_v, state_hgrn, c, c_ctx,
              w_ada, b_ada, g_pre, w_in, rpb, lb_logits, g_hgrn, w_fnet, w_out, g_post):
    lb = jnp.cumsum(jax.nn.softmax(lb_logits.astype(jnp.float32), axis=1), axis=1)
    lb = jnp.maximum(lb - lb[:, :1], 0.0)
    yp, ys = x_prompt, x_sample
    new_k, new_v, new_s = [], [], []
    for l in range(DEPTH):
        params = (g_pre[l], w_in[l], rpb[l], lb[0, l], lb[1, l], g_hgrn[l], w_fnet[l], w_out[l], g_post[l])
        mod_ctx = jax.nn.silu(c_ctx) @ w_ada[l] + b_ada[l]
        yp, k_l, v_l, s_l = _trunk_layer(yp, mod_ctx, *params)
        new_k.append(k_l)
        new_v.append(v_l)
        new_s.append(s_l.astype(x_prompt.dtype))
        mod_lat = (jax.nn.silu(c) @ w_ada[l] + b_ada[l])[:, None, :]
        ys, _, _, _ = _trunk_layer(ys, mod_lat, *params,
                                   ctx_kv=(cache_attn_k[:, l], cache_attn_v[:, l]),
                                   ctx_state=state_hgrn[:, l])
    new_cache_k = jnp.stack(new_k, axis=1)
    new_cache_v = jnp.stack(new_v, axis=1)
    new_state_hgrn = jnp.stack(new_s, axis=1)
    return (yp, ys, new_cache_k, new_cache_v, new_state_hgrn)
```

```python
import numpy as np
import ml_dtypes
from contextlib import ExitStack
import concourse.bass as bass
import concourse.mybir as mybir
from concourse.bass_utils import run_bass_kernel_spmd

F32 = mybir.dt.float32
BF16 = mybir.dt.bfloat16
AF = mybir.ActivationFunctionType
ALU = mybir.AluOpType
AX = mybir.AxisListType

NL = 4
D = 1024
T = 1024
NT = 8
EPS = 1e-6
NEG = -30000.0
KT = {0: [0, 1, 2, 3], 1: [0, 1, 2, 3], 2: [0, 1, 2, 3, 4], 3: [1, 2, 3, 4, 5],
      4: [2, 3, 4, 5, 6], 5: [3, 4, 5, 6, 7], 6: [4, 5, 6, 7], 7: [4, 5, 6, 7]}
JK = [(j, kt) for j in range(8) for kt in KT[j]]
JKI = {p: i for i, p in enumerate(JK)}
NDS = 24
NSW = 72


class TK:
    def __init__(s, nc, st):
        s.nc = nc
        s.E = {'pe': nc.tensor, 'act': nc.scalar, 'dve': nc.vector, 'pool': nc.gpsimd, 'sp': nc.sync}
        s.sem = {k: st.enter_context(nc.semaphore('s_' + k)) for k in ('pe', 'act', 'dve', 'pool')}
        s.cnt = {k: 0 for k in s.E}
        s.seen = {k: {} for k in s.E}
        s.lw = {}
        s.rd = {}
        s.dsems = [st.enter_context(nc.semaphore('d%d' % i)) for i in range(NDS)]
        s.dcnt = [0] * NDS
        s.dnext = 0
        s.swsems = [st.enter_context(nc.semaphore('w%d' % i)) for i in range(NSW)]
        s.swnext = 0
        s.swlow = 0

    def _wait(s, eng, key, val):
        if eng == 'pe' and key == 'pe':
            return
        if s.seen[eng].get(key, 0) >= val:
            return
        if isinstance(key, str):
            semobj = s.sem[key]
        elif key >= 1000:
            semobj = s.swsems[key - 1000]
        else:
            semobj = s.dsems[key]
        s.E[eng].wait_ge(semobj, val)
        s.seen[eng][key] = val

    def _deps(s, eng, reads, writes):
        for k in reads:
            w = s.lw.get(k)
            if w:
                s._wait(eng, *w)
            if k.startswith('ps'):
                for rk, rv in s.rd.get(k, {}).items():
                    if rk != eng:
                        s._wait(eng, rk, rv)
        for k in writes:
            w = s.lw.get(k)
            if w:
                s._wait(eng, *w)
            for rk, rv in s.rd.get(k, {}).items():
                s._wait(eng, rk, rv)

    def _book(s, tag, reads, writes):
        for k in reads:
            d = s.rd.setdefault(k, {})
            d[tag[0]] = max(d.get(tag[0], 0), tag[1])
        for k in writes:
            s.lw[k] = tag
            s.rd[k] = {}

    def op(s, eng, fn, reads=(), writes=()):
        s._deps(eng, reads, writes)
        inst = fn(s.E[eng])
        s.cnt[eng] += 1
        inst.then_inc(s.sem[eng], 1)
        s._book((eng, s.cnt[eng]), reads, writes)

    def dma(s, q, out, in_, reads=(), writes=()):
        if q == 'pool':
            assert s.swnext < NSW, "out of one-shot semaphores"
            i = s.swnext
            s.swnext += 1
            s._deps(q, reads, writes)
            s.E[q].dma_start(out=out, in_=in_).then_inc(s.swsems[i], 16)
            s._book((1000 + i, 16), reads, writes)
            return
        i = s.dnext
        s.dnext = (s.dnext + 1) % NDS
        if s.dcnt[i] > 0:
            s._wait(q, i, s.dcnt[i])
        s._deps(q, reads, writes)
        s.dcnt[i] += 16
        s.E[q].dma_start(out=out, in_=in_).then_inc(s.dsems[i], 16)
        s._book((i, s.dcnt[i]), reads, writes)

    def barrier(s):
        engs = ('pe', 'act', 'dve', 'pool', 'sp')
        snap = dict(s.cnt)
        dsnap = list(s.dcnt)
        for e in engs:
            for o in ('pe', 'act', 'dve', 'pool'):
                if o != e and snap[o] > 0:
                    s._wait(e, o, snap[o])
            for i in range(NDS):
                if dsnap[i] > 0:
                    s._wait(e, i, dsnap[i])
            for i in range(s.swlow, s.swnext):
                s._wait(e, 1000 + i, 16)
        s.swlow = s.swnext

    def finish(s):
        for i in range(NDS):
            if s.dcnt[i] > 0:
                s._wait('sp', i, s.dcnt[i])
        for i in range(s.swnext):
            s._wait('sp', 1000 + i, 16)
        for k in ('pe', 'act', 'dve', 'pool'):
            if s.cnt[k] > 0:
                s._wait('sp', k, s.cnt[k])


def build_nc(nl=NL, dbg=False, upto=None):
    nc = bass.Bass("TRN2", target_bir_lowering=False)
    _order = ['M0', 'M1', 'M2', 'M3', 'M', 'A0', 'A1', 'A1a', 'A1b', 'A1c', 'A2', 'A', 'R0', 'R1', 'R2', 'R3', 'R', 'F0', 'F1', 'F']

    def stop(p):
        return upto is not None and _order.index(upto) <= _order.index(p)

    def din(name, shape, dt=F32):
        return nc.dram_tensor(name, list(shape), dt, kind="ExternalInput")

    def dout(name, shape, dt=F32):
        return nc.dram_tensor(name, list(shape), dt, kind="ExternalOutput")

    x_d = din("x", [T, D])
    cvec_d = din("cvec", [128, 8])
    ctxk_d = din("ctxk", [NL, 512, 512])
    ctxv_d = din("ctxv", [NL, 512, 512])
    s0_d = din("s0", [NL, 2, 128, 2, 64])
    flags_d = din("flags", [128, 2])
    wada_d = din("w_ada", [NL, D, 3 * D])
    bada_d = din("b_ada", [NL, 3 * D])
    gpre_d = din("g_pre", [NL, D])
    win_d = din("w_in", [NL, D, 3840])
    tpad_d = din("tpad", [NL, 8, 23, 127])
    lbl_d = din("lb_logits", [2, NL, 256])
    ghg_d = din("g_hgrn", [NL, 256])
    wfn_d = din("w_fnet", [NL, 256, 256])
    wout_d = din("w_out", [NL, D, D])
    gpost_d = din("g_post", [NL, D])
    cf32_d = din("cf32", [128, 7, 128])
    rowb_d = din("rowbias", [128, 74])
    cb16_d = din("cb16", [128, 9, 128], BF16)
    csn_d = din("csn", [8, 128, 2, 1024], BF16)

    y_d = dout("y", [T, D])
    nk_d = dout("newk", [NL, T, 512])
    nv_d = dout("newv", [NL, T, 512])
    ns_d = dout("news", [NL, 2, 4, 128, 2, 64])
    dbg_d = dout("dbgmixed", [T, D], BF16) if dbg else None

    with ExitStack() as st:
        def sb(name, shape, dt=F32):
            return st.enter_context(nc.sbuf_tensor(name, list(shape), dt))

        tk = TK(nc, st)
        x_sb = sb("x_sb", [128, NT, D])
        hT = sb("hT", [128, 8, T], BF16)
        mixed = sb("mixed", [128, NT, D], BF16)
        wst = [sb("wst0", [128, 8, 512], BF16)]
        wbf = [sb("wbf%d" % i, [128, 8, 512], BF16) for i in range(2)]
        gg = sb("gg", [128, D])
        modN = sb("modN", [128, 3 * D], BF16)
        screp = sb("screp", [128, 8, 128], BF16)
        brow = sb("brow", [1, 512])
        ones_row = sb("ones_row", [1, 128])
        csil = sb("csil", [128, 8])
        cf32 = sb("cf32s", [128, 7, 128])
        rowb = sb("rowbs", [128, 74])
        cb16 = sb("cb16s", [128, 9, 128], BF16)
        flags = sb("flagss", [128, 2])
        lbl = sb("lbl", [128, 2, 256])
        oml = sb("oml", [128, 2, 256])
        ghgB = sb("ghgB", [128, 256])
        ssq = sb("ssq", [128, 16])
        rstd = sb("rstd", [128, 16])
        ARW = 21120
        arena = sb("arena", [128, ARW])
        apos = [0]

        def areset():
            apos[0] = 0

        def aget(shape, dt=F32):
            n = 1
            for d_ in shape[1:]:
                n *= d_
            words = n if dt == F32 else (n + 1) // 2
            a0 = apos[0]
            apos[0] += words
            assert apos[0] <= ARW, ("arena overflow", apos[0])
            v = arena[:, a0:a0 + words]
            if dt != F32:
                v = v.bitcast(dt)
            if len(shape) == 3:
                v = v.rearrange("p (a b) -> p a b", a=shape[1])
            elif len(shape) == 4:
                v = v.rearrange("p (a b c) -> p a b c", a=shape[1], b=shape[2])
            return v

        psbig = [st.enter_context(nc.psum_tensor("psb%d" % i, [128, 1024], F32)) for i in range(4)]
        ps = [psbig[i // 2][:, (i % 2) * 512:(i % 2 + 1) * 512] for i in range(8)]
        psk = ["ps%d" % i for i in range(8)]
        bank_rr = [0]

        def nb(avoid=()):
            while True:
                b = bank_rr[0]
                bank_rr[0] = (b + 1) % 8
                if b not in avoid:
                    return b

        J2 = cf32[:, 0, :]
        IDF = cf32[:, 1, :]
        CMT = cf32[:, 2, :]
        TR = [cf32[:, 3, :], cf32[:, 4, :]]
        SEL = [cf32[:, 5, 0:8], cf32[:, 5, 8:16]]
        HM = cf32[:, 5, 16:18]
        QM = cf32[:, 5, 18:22]
        HME = cf32[:, 6, :].rearrange("p (a c) -> p a c", a=2)
        IDB = cb16[:, 0, :]
        TRI = [cb16[:, 1, :], cb16[:, 2, :]]
        C4S4 = cb16
        J2B = cb16[:, 7, :]
        CMTB = cb16[:, 8, :]

        tk.dma('sp', cf32[:], cf32_d.ap(), writes=['cf32'])
        tk.dma('sp', rowb[:], rowb_d.ap(), writes=['rowb'])
        tk.dma('sp', cb16[:], cb16_d.ap(), writes=['cb16'])
        tk.dma('sp', flags[:], flags_d.ap(), writes=['flags'])
        tk.dma('sp', csil[:], cvec_d.ap(), writes=['csil'])
        for t in range(NT):
            tk.dma('sp', x_sb[:, t, :], x_d.ap()[t * 128:(t + 1) * 128, :], writes=['x%d' % t])
        tk.op('pool', lambda e: e.memset(ones_row[:], 1.0), writes=['ones_row'])
        tk.op('act', lambda e: e.activation(out=csil[:], in_=csil[:], func=AF.Silu), reads=['csil'], writes=['csil'])

        wring = [0]

        wbring = [0]

        def load_w(src_ap, ncols):
            wi = wbring[0]
            wbring[0] ^= 1
            tk.dma('pool', wbf[wi][:, :, 0:ncols], src_ap.rearrange("(kc p) n -> p kc n", p=128), writes=['wbf%d' % wi])
            return wi

        wseq = []
        for l_ in range(nl):
            for (c0_, n_) in ((0, 512), (512, 512), (1024, 512), (1536, 512), (2048, 512), (2816, 512), (2560, 256), (3328, 512)):
                wseq.append((win_d, l_, c0_, n_))
            wseq.append((wout_d, l_, 0, 512))
            wseq.append((wout_d, l_, 512, 512))
        wstate = {'ptr': 0, 'loaded': {}}

        def _issue(i):
            if i < len(wseq) and i not in wstate['loaded']:
                d_, l_, c0_, n_ = wseq[i]
                wstate['loaded'][i] = load_w(d_.ap()[l_, :, c0_:c0_ + n_], n_)

        def next_w(prefetch=True):
            i = wstate['ptr']
            wstate['ptr'] += 1
            _issue(i)
            if prefetch:
                _issue(i + 1)
            return wstate['loaded'][i]

        def proj_tm(wi, c0, ncols, t, b):
            for kc in range(8):
                tk.op('pe', lambda e, kc=kc: e.matmul(ps[b][:, 0:ncols], lhsT=hT[:, kc, t * 128:(t + 1) * 128],
                                                     rhs=wbf[wi][:, kc, c0:c0 + ncols], start=(kc == 0), stop=(kc == 7)),
                      reads=['hT', 'wbf%d' % wi], writes=[psk[b]])

        def proj_fm(wi, ci, g, b):
            for kc in range(8):
                tk.op('pe', lambda e, kc=kc: e.matmul(ps[b][:, 0:512], lhsT=wbf[wi][:, kc, ci * 128:(ci + 1) * 128],
                                                     rhs=hT[:, kc, g * 512:(g + 1) * 512], start=(kc == 0), stop=(kc == 7)),
                      reads=['hT', 'wbf%d' % wi], writes=[psk[b]])

        def psb16(b):
            return ps[b].bitcast(BF16)

        def emit_mod_chunk(lm, ch, banks=None):
            tk.dma('pool', wst[0][:], wada_d.ap()[lm, :, ch * 512:(ch + 1) * 512].rearrange("(kc p) n -> p kc n", p=128), writes=['wst0'])
            tk.dma('sp', brow[:], bada_d.ap()[lm:lm + 1, ch * 512:(ch + 1) * 512], writes=['brow'])
            b = nb() if banks is None else banks[ch % len(banks)]
            for kc in range(8):
                tk.op('pe', lambda e, kc=kc: e.matmul(ps[b][:, :], lhsT=screp[:, kc, :], rhs=wst[0][:, kc, :], start=(kc == 0), stop=False),
                      reads=['screp', 'wst0'], writes=[psk[b]])
            tk.op('pe', lambda e: e.matmul(ps[b][:, :], lhsT=ones_row[0:1, :], rhs=brow[0:1, :], start=False, stop=True),
                  reads=['ones_row', 'brow'], writes=[psk[b]])
            tk.op('act', lambda e: e.copy(out=modN[:, ch * 512:(ch + 1) * 512], in_=ps[b][:, :]), reads=[psk[b]], writes=['modN'])

        tk.op('dve', lambda e: e.tensor_copy(out=screp[:], in_=csil[:].unsqueeze(2).broadcast_to([128, 8, 128])), reads=['csil'], writes=['screp'])
        for ch in range(6):
            emit_mod_chunk(0, ch)

        for l in range(nl):
            if l == 0:
                tk.barrier()
            areset()
            apos[0] = 11520
            gbc = aget([128, 2, D])
            lbt = aget([128, 2, NL, 256])
            junk = aget([128, D], BF16)
            tmpf = aget([128, D])
            hb = [aget([128, D], BF16) for _ in range(2)]
            modA = aget([128, D])
            tk.dma('sp', gbc[:, 0, :], bass.AP(gpre_d, l * D, [[0, 128], [1, D]]), writes=['gbc'])
            tk.dma('sp', gbc[:, 1, :], bass.AP(gpost_d, l * D, [[0, 128], [1, D]]), writes=['gbc'])
            tk.dma('sp', lbt[:].rearrange("p a l c -> p (a l c)"), bass.AP(lbl_d, 0, [[0, 128], [1, 2 * NL * 256]]), writes=['lbt'])
            if l == 0:
                tk.op('dve', lambda e: e.memset(lbl[:], 0.0), writes=['lbl'])
            else:
                mx = tmpf[:, 0:512].rearrange("p (a c) -> p a c", a=2)
                sm = tmpf[:, 512:1024].rearrange("p (a c) -> p a c", a=2)
                tk.op('dve', lambda e: e.tensor_tensor(out=mx, in0=lbt[:, :, 0, :], in1=lbt[:, :, 1, :], op=ALU.max),
                      reads=['lbt'], writes=['tmpf'])
                for l2 in range(2, NL):
                    tk.op('dve', lambda e, l2=l2: e.tensor_tensor(out=mx, in0=mx, in1=lbt[:, :, l2, :], op=ALU.max),
                          reads=['lbt', 'tmpf'], writes=['tmpf'])
                for l2 in range(NL):
                    tk.op('dve', lambda e, l2=l2: e.tensor_tensor(out=lbt[:, :, l2, :], in0=lbt[:, :, l2, :], in1=mx, op=ALU.subtract),
                          reads=['lbt', 'tmpf'], writes=['lbt'])
                tk.op('act', lambda e: e.activation(out=lbt[:].rearrange("p a l c -> p (a l c)"), in_=lbt[:].rearrange("p a l c -> p (a l c)"), func=AF.Exp),
                      reads=['lbt'], writes=['lbt'])
                tk.op('dve', lambda e: e.tensor_tensor(out=sm, in0=lbt[:, :, 0, :], in1=lbt[:, :, 1, :], op=ALU.add),
                      reads=['lbt'], writes=['tmpf'])
                for l2 in range(2, NL):
                    tk.op('dve', lambda e, l2=l2: e.tensor_tensor(out=sm, in0=sm, in1=lbt[:, :, l2, :], op=ALU.add),
                          reads=['lbt', 'tmpf'], writes=['tmpf'])
                tk.op('dve', lambda e: e.reciprocal(out=sm, in_=sm), reads=['tmpf'], writes=['tmpf'])
                tk.op('dve', lambda e: e.tensor_copy(out=lbl[:], in_=lbt[:, :, 1, :]), reads=['lbt'], writes=['lbl'])
                for l2 in range(2, l + 1):
                    tk.op('dve', lambda e, l2=l2: e.tensor_tensor(out=lbl[:], in0=lbl[:], in1=lbt[:, :, l2, :], op=ALU.add),
                          reads=['lbt', 'lbl'], writes=['lbl'])
                tk.op('dve', lambda e: e.tensor_tensor(out=lbl[:], in0=lbl[:], in1=sm, op=ALU.mult), reads=['lbl', 'tmpf'], writes=['lbl'])
            tk.op('dve', lambda e: e.tensor_scalar(out=oml[:], in0=lbl[:], scalar1=-0.5, scalar2=0.5, op0=ALU.mult, op1=ALU.add),
                  reads=['lbl'], writes=['oml'])
            tk.op('dve', lambda e: e.tensor_scalar(out=lbl[:], in0=lbl[:], scalar1=0.5, scalar2=0.5, op0=ALU.mult, op1=ALU.add),
                  reads=['lbl'], writes=['lbl'])
            if stop('M0'):
                break
            tk.op('dve', lambda e: e.scalar_tensor_tensor(out=modA[:], in0=modN[:, D:2 * D], scalar=1.0, in1=gbc[:, 0, :], op0=ALU.add, op1=ALU.mult),
                  reads=['modN', 'gbc'], writes=['modA'])
            tk.op('dve', lambda e: e.tensor_tensor(out=gg[:], in0=modN[:, 2 * D:3 * D], in1=gbc[:, 1, :], op=ALU.mult), reads=['modN', 'gbc'], writes=['gg'])
            if stop('M1'):
                break
            for t in range(NT):
                tk.op('act', lambda e, t=t: e.activation(out=junk[:], in_=x_sb[:, t, :], func=AF.Square, accum_out=ssq[:, t:t + 1]),
                      reads=['x%d' % t], writes=['junk', 'ssq'])
            tk.op('dve', lambda e: e.tensor_scalar(out=rstd[:, 0:8], in0=ssq[:, 0:8], scalar1=1.0 / D, scalar2=EPS, op0=ALU.mult, op1=ALU.add),
                  reads=['ssq'], writes=['rstd'])
            tk.op('act', lambda e: e.activation(out=rstd[:, 0:8], in_=rstd[:, 0:8], func=AF.Ln), reads=['rstd'], writes=['rstd'])
            tk.op('act', lambda e: e.activation(out=rstd[:, 0:8], in_=rstd[:, 0:8], func=AF.Exp, scale=-0.5), reads=['rstd'], writes=['rstd'])
            if stop('M2'):
                break
            for t in range(NT):
                hbi = t % 2
                tk.op('dve', lambda e, t=t: e.scalar_tensor_tensor(out=tmpf[:], in0=x_sb[:, t, :], scalar=rstd[:, t:t + 1], in1=modA[:],
                                                                   op0=ALU.mult, op1=ALU.mult),
                      reads=['x%d' % t, 'rstd', 'modA'], writes=['tmpf'])
                tk.op('dve', lambda e: e.tensor_tensor(out=hb[hbi][:], in0=tmpf[:], in1=modN[:, 0:D], op=ALU.add),
                      reads=['tmpf', 'modN'], writes=['hb%d' % hbi])
                if stop('M3'):
                    continue
                b = nb()
                for kc in range(8):
                    tk.op('pe', lambda e, kc=kc: e.transpose(out=psb16(b)[:, kc * 128:(kc + 1) * 128], in_=hb[hbi][:, kc * 128:(kc + 1) * 128], identity=IDB),
                          reads=['hb%d' % hbi, 'cb16'], writes=[psk[b]])
                tk.op('act', lambda e, t=t: e.copy(out=hT[:, :, t * 128:(t + 1) * 128], in_=psb16(b)[:, :].rearrange("p (k c) -> p k c", k=8)),
                      reads=[psk[b]], writes=['hT'])

            if stop('M'):
                break
            tk.barrier()
            areset()
            qT = aget([128, 4, T], BF16)
            kT = aget([128, 4, T], BF16)
            ckT = aget([128, 4, 512], BF16)
            vaug = aget([128, NT, 8, 66], BF16)
            cvaug = aget([128, 4, 8, 66], BF16)
            sga = aget([128, NT, 512], BF16)
            expT = aget([128, 7, 8, 128], BF16)
            Eb = [aget([128, 8, 128], BF16) for _ in range(3)]
            Pb = [aget([128, 8, 128], BF16) for _ in range(2)]
            hk = [Eb[0].bitcast(F32) if False else None, None]
            ost = [aget([128, 512]) for _ in range(2)]
            rden = aget([128, 8])
            otmp = aget([128, 8, 64])
            ckb = aget([128, 4, 512], BF16)
            hkA = aget([128, 8, 128])
            hkB = aget([128, 8, 128])
            hk = [hkA, hkB]
            tk.op('pool', lambda e: e.memset(vaug[:, :, :, 64:66], 1.0), writes=['vaug'])
            tk.op('dve', lambda e: e.tensor_copy(out=cvaug[:, :, :, 64:66].rearrange("p a b c -> p (a b) c"),
                                                 in_=flags[:, 0:1].unsqueeze(2).broadcast_to([128, 32, 2])),
                  reads=['flags'], writes=['cvaug'])
            def toep_dma(di):
                dl = di - 3
                hi = di % 2
                for qr in range(2):
                    for krl in range(2):
                        off = ((l * 8) * 23 + (2 * dl + krl - qr + 11)) * 127
                        src = bass.AP(tpad_d, off, [[1, 64], [23 * 127, 8], [1, 64]])
                        tk.dma('sp', hk[hi][qr * 64:(qr + 1) * 64, :, krl * 64:(krl + 1) * 64], src, writes=['hk%d_%d' % (hi, qr * 2 + krl)])

            def toep_mm(di):
                hi = di % 2
                bA = nb()
                bB = nb()
                for h in range(8):
                    b = bA if h % 2 == 0 else bB
                    o = ps[b][:, (h // 2) * 128:(h // 2 + 1) * 128]
                    tk.op('pe', lambda e, h=h, o=o: e.matmul(o, lhsT=hk[hi][:, h, :], rhs=J2, start=True, stop=False),
                          reads=['hk%d_%d' % (hi, x) for x in range(4)] + ['cf32'], writes=[psk[b]])
                    tk.op('pe', lambda e, o=o: e.matmul(o, lhsT=IDF, rhs=CMT, start=False, stop=True),
                          reads=['cf32'], writes=[psk[b]])
                for bi, b in enumerate((bA, bB)):
                    tk.op('act', lambda e, bi=bi, b=b: e.activation(out=expT[:, di, bi * 4:(bi + 1) * 4, :].rearrange("p a c -> p (a c)"),
                                                                    in_=ps[b][:, :], func=AF.Exp),
                          reads=[psk[b]], writes=['expT'])

            toep_dma(0)
            toep_dma(1)
            if stop('A0'):
                break
            ckk = 'ckb'
            tk.dma('pool', ckb[:], ctxk_d.ap()[l].rearrange("(c p) n -> p c n", p=128), writes=[ckk])
            for c in range(4):
                b = nb()
                for pr in range(4):
                    tk.op('pe', lambda e, pr=pr: e.transpose(out=psb16(b)[:, pr * 128:(pr + 1) * 128], in_=ckb[:, c, pr * 128:(pr + 1) * 128], identity=IDB),
                          reads=[ckk, 'cb16'], writes=[psk[b]])
                tk.op('act', lambda e, c=c: e.copy(out=ckT[:, :, c * 128:(c + 1) * 128], in_=psb16(b)[:, 0:512].rearrange("p (k c) -> p k c", k=4)),
                      reads=[psk[b]], writes=['ckT'])
            tk.dma('pool', ckb[:], ctxv_d.ap()[l].rearrange("(c p) n -> p c n", p=128), reads=[], writes=[ckk])
            tk.op('pool', lambda e: e.tensor_copy(out=cvaug[:, :, :, 0:64], in_=ckb[:].rearrange("p c (h d) -> p c h d", h=8)), reads=[ckk], writes=['cvaug'])
            if stop('A1'):
                break
            toep_mm(0)
            toep_dma(2)
            wi = next_w()
            for pr in range(4):
                for g in range(2):
                    b = nb()
                    proj_fm(wi, pr, g, b)
                    tk.op('act', lambda e, pr=pr, g=g: e.copy(out=qT[:, pr, g * 512:(g + 1) * 512], in_=ps[b][:, :]), reads=[psk[b]], writes=['qT'])
            if stop('A1a'):
                break
            toep_mm(1)
            toep_dma(3)
            wi = next_w()
            for pr in range(4):
                for g in range(2):
                    b = nb()
                    proj_fm(wi, pr, g, b)
                    tk.op('act', lambda e, pr=pr, g=g: e.copy(out=kT[:, pr, g * 512:(g + 1) * 512], in_=ps[b][:, :]), reads=[psk[b]], writes=['kT'])
            toep_mm(2)
            toep_dma(4)
            for t in range(NT):
                b = nb()
                proj_tm(wi, 0, 512, t, b)
                oi = t % 2
                tk.op('dve', lambda e: e.tensor_copy(out=ost[oi][:], in_=ps[b][:, :]), reads=[psk[b]], writes=['ost%d' % oi])
                tk.dma('sp', nk_d.ap()[l, t * 128:(t + 1) * 128, :], ost[oi][:], reads=['ost%d' % oi])
            toep_mm(3)
            toep_dma(5)
            if stop('A1b'):
                break
            wi = next_w()
            for t in range(NT):
                b = nb()
                proj_tm(wi, 0, 512, t, b)
                oi = t % 2
                tk.op('dve', lambda e: e.tensor_copy(out=ost[oi][:], in_=ps[b][:, :]), reads=[psk[b]], writes=['ost%d' % oi])
                tk.op('act', lambda e, t=t: e.copy(out=vaug[:, t, :, 0:64], in_=ost[oi][:].rearrange("p (h d) -> p h d", h=8)),
                      reads=['ost%d' % oi], writes=['vaug'])
                tk.dma('sp', nv_d.ap()[l, t * 128:(t + 1) * 128, :], ost[oi][:], reads=['ost%d' % oi])
            if stop('A1c'):
                break
            toep_mm(4)
            toep_dma(6)
            wi = next_w()
            for t in range(NT):
                b = nb()
                proj_tm(wi, 0, 512, t, b)
                tk.op('act', lambda e, t=t: e.activation(out=sga[:, t, :], in_=ps[b][:, :], func=AF.Silu), reads=[psk[b]], writes=['sga'])
            toep_mm(5)
            toep_mm(6)
            if stop('A2'):
                break
            OA, OB = 6, 7
            spairs = [(0, 1), (2, 3), (4, 5)]
            allsteps = []
            for j in range(8):
                st_ = [('l', kt) for kt in KT[j]] + [('c', c) for c in range(4)]
                for si_, (kind, idx) in enumerate(st_):
                    allsteps.append((j, si_, len(st_), kind, idx))

            def emit_S(k):
                j, si_, ns_, kind, idx = allsteps[k]
                sA, sB = spairs[k % 3]
                for h in range(8):
                    b = sA if h % 2 == 0 else sB
                    r0 = (h % 2) * 64
                    ksrc = kT[r0:r0 + 64, h // 2, idx * 128:(idx + 1) * 128] if kind == 'l' else ckT[r0:r0 + 64, h // 2, idx * 128:(idx + 1) * 128]
                    tk.op('pe', lambda e, h=h, b=b, ksrc=ksrc, r0=r0: e.matmul(ps[b][:, (h // 2) * 128:(h // 2 + 1) * 128], lhsT=ksrc,
                                                                              rhs=qT[r0:r0 + 64, h // 2, j * 128:(j + 1) * 128], start=True, stop=True),
                          reads=['qT', 'kT' if kind == 'l' else 'ckT'], writes=[psk[b]])

            def emit_rest(k):
                j, si_, ns_, kind, idx = allsteps[k]
                sA, sB = spairs[k % 3]
                sl = k % 3
                big = psbig[sA // 2]
                if kind == 'l':
                    jk = JKI[(j, idx)]
                    for hf in range(2):
                        tk.op('act', lambda e, hf=hf: e.activation(
                            out=Eb[sl][:, :, hf * 64:(hf + 1) * 64],
                            in_=big[:, :].rearrange("p (a c) -> p a c", a=8)[:, :, hf * 64:(hf + 1) * 64],
                            func=AF.Exp, scale=0.125, bias=rowb[:, jk * 2 + hf:jk * 2 + hf + 1]),
                            reads=[psk[sA], psk[sB], 'rowb'], writes=['Eb%d' % sl])
                    di = idx - j + 3
                    pl = k % 2
                    tk.op('dve', lambda e, di=di: e.tensor_tensor(out=Pb[pl][:], in0=Eb[sl][:], in1=expT[:, di, :, :], op=ALU.mult),
                          reads=['Eb%d' % sl, 'expT'], writes=['Pb%d' % pl])
                    lhs, lk = Pb[pl], 'Pb%d' % pl
                    vsrc, vk = vaug, 'vaug'
                else:
                    tk.op('act', lambda e: e.activation(out=Eb[sl][:].rearrange("p a c -> p (a c)"), in_=big[:, :], func=AF.Exp, scale=0.125),
                          reads=[psk[sA], psk[sB]], writes=['Eb%d' % sl])
                    lhs, lk = Eb[sl], 'Eb%d' % sl
                    vsrc, vk = cvaug, 'cvaug'
                for e_ in range(8):
                    h = 2 * (e_ % 4) + e_ // 4
                    ob = OA if e_ < 4 else OB
                    tk.op('pe', lambda e, e_=e_, h=h, ob=ob, lhs=lhs, vsrc=vsrc: e.matmul(
                        ps[ob][:, (e_ % 4) * 66:(e_ % 4) * 66 + 66], lhsT=lhs[:, e_, :], rhs=vsrc[:, idx, h, :],
                        start=(si_ == 0 and e_ % 4 == 0), stop=(si_ == ns_ - 1), skip_group_check=True),
                        reads=[lk, vk], writes=[psk[ob]])
                if si_ == ns_ - 1:
                    for bi, ob in enumerate((OA, OB)):
                        tk.op('dve', lambda e, bi=bi, ob=ob: e.reciprocal(out=rden[:, bi * 4:(bi + 1) * 4],
                                                                          in_=ps[ob][:, 0:264].rearrange("p (a c) -> p a c", a=4)[:, :, 64]),
                              reads=[psk[ob]], writes=['rden'])
                    for bi, ob in enumerate((OA, OB)):
                        tk.op('dve', lambda e, bi=bi, ob=ob: e.tensor_tensor(
                            out=otmp[:, bi:8:2, :], in0=ps[ob][:, 0:264].rearrange("p (a c) -> p a c", a=4)[:, :, 0:64],
                            in1=rden[:, bi * 4:(bi + 1) * 4].unsqueeze(2).broadcast_to([128, 4, 64]), op=ALU.mult),
                            reads=[psk[ob], 'rden'], writes=['otmp'])
                    tk.op('dve', lambda e: e.tensor_tensor(out=mixed[:, j, 0:512], in0=otmp[:].rearrange("p h d -> p (h d)"), in1=sga[:, j, :], op=ALU.mult),
                          reads=['otmp', 'sga'], writes=['mixed%d' % j])

            emit_S(0)
            emit_S(1)
            for k in range(len(allsteps)):
                if k + 2 < len(allsteps):
                    emit_S(k + 2)
                emit_rest(k)

            if stop('A'):
                break
            tk.barrier()
            areset()
            qh = aget([128, NT, 256], BF16)
            sgb = aget([128, NT, 256])
            qE = sgb.rearrange("p a b -> p (a b)")[:, 0:1024].bitcast(BF16).rearrange("p (a b) -> p a b", a=NT)
            kE = sgb.rearrange("p a b -> p (a b)")[:, 1024:2048].bitcast(BF16).rearrange("p (a b) -> p a b", a=NT)
            vh = aget([128, NT, 256], BF16)
            vhmF = aget([128, 4096])
            vhm = vhmF.bitcast(BF16).rearrange("p (q t c) -> p q t c", q=4, t=NT)
            A_ = vhmF[:, 0:2048].rearrange("p (e c) -> p e c", e=64)
            B_ = vhmF[:, 2048:4096].rearrange("p (e c) -> p e c", e=64)
            sgr = aget([128, NT, 256], BF16)
            fS = aget([128, 4096])
            fbuf = fS[:, 0:2048].rearrange("p (a b) -> p a b", a=NT)
            SinPm = fS.bitcast(BF16).rearrange("p (h c e) -> p h c e", h=4, c=32)
            lfbuf = aget([128, NT, 256])
            kETm = lfbuf.rearrange("p a b -> p (a b)").bitcast(BF16).rearrange("p (h t) -> p h t", h=4)
            osq = lfbuf
            tmpE = [aget([128, 512]) for _ in range(2)]
            qET = aget([128, 2, T], BF16)
            ATm = [aget([128, 4, 128], BF16) for _ in range(2)]
            osum = aget([128, NT, 256])
            gs3 = aget([128, 2, 32, 3])
            es3 = aget([128, 2, 32, 3])
            S0b = aget([128, 2, 64])
            nsb = aget([128, 4, 2, 64])
            hss = aget([128, 32])
            tk.dma('sp', ghgB[:], bass.AP(ghg_d, l * 256, [[0, 128], [1, 256]]), writes=['ghgB'])
            wi = next_w()
            for t in range(NT):
                b = nb()
                proj_tm(wi, 0, 512, t, b)
                tk.op('act', lambda e, t=t: e.activation(out=qh[:, t, :], in_=ps[b][:, 0:256], func=AF.Silu), reads=[psk[b]], writes=['qh'])
                tk.op('act', lambda e, t=t: e.activation(out=sgb[:, t, :], in_=ps[b][:, 256:512], func=AF.Tanh, scale=0.5), reads=[psk[b]], writes=['sQ'])
            wi = next_w()
            for t in range(NT):
                b = nb()
                proj_tm(wi, 0, 512, t, b)
                tk.op('act', lambda e, t=t: e.activation(out=sgr[:, t, :], in_=ps[b][:, 256:512], func=AF.Silu), reads=[psk[b]], writes=['sgr'])
                tk.op('dve', lambda e, t=t: e.tensor_copy(out=vh[:, t, :], in_=ps[b][:, 0:256]), reads=[psk[b]], writes=['vh'])

            if stop('R0'):
                break
            rstop = False
            for dr in range(2):
                lb_bc = lbl[:, dr, :].unsqueeze(1).broadcast_to([128, NT, 256])
                oml_bc = oml[:, dr, :].unsqueeze(1).broadcast_to([128, NT, 256])
                tk.op('dve', lambda e: e.tensor_tensor(out=fbuf[:], in0=sgb[:], in1=oml_bc, op=ALU.mult), reads=['sQ', 'oml'], writes=['fS'])
                tk.op('dve', lambda e: e.tensor_tensor(out=fbuf[:], in0=fbuf[:], in1=lb_bc, op=ALU.add), reads=['fS', 'lbl'], writes=['fS'])
                tk.op('act', lambda e: e.activation(out=lfbuf[:].rearrange("p a b -> p (a b)"), in_=fbuf[:].rearrange("p a b -> p (a b)"), func=AF.Ln),
                      reads=['fS'], writes=['lK'])
                tk.op('dve', lambda e: e.tensor_scalar(out=fbuf[:], in0=fbuf[:], scalar1=-1.0, scalar2=1.0, op0=ALU.mult, op1=ALU.add),
                      reads=['fS'], writes=['fS'])
                bS = nb()
                for t in range(NT):
                    for pr in range(2):
                        tt_ = t if dr == 0 else 7 - t
                        tk.op('pe', lambda e, t=t, pr=pr, tt_=tt_: e.matmul(ps[bS][:, (pr * 8 + tt_) * 8:(pr * 8 + tt_) * 8 + 8], lhsT=lfbuf[:, t, pr * 128:(pr + 1) * 128],
                                                                   rhs=SEL[dr], start=True, stop=True),
                              reads=['lK', 'cf32'], writes=[psk[bS]])
                psS = ps[bS][:, 0:128].rearrange("p (a c r) -> p a c r", a=2, c=32)
                tk.op('dve', lambda e: e.tensor_copy(out=gs3[:, :, :, 0:2], in_=psS), reads=[psk[bS]], writes=['gs3'])
                tk.op('dve', lambda e: e.tensor_tensor(out=gs3[:, :, :, 2], in0=gs3[:, :, :, 1], in1=gs3[:, :, :, 0], op=ALU.subtract),
                      reads=['gs3'], writes=['gs3'])
                tk.op('act', lambda e: e.activation(out=es3[:].rearrange("p a c r -> p (a c r)"), in_=gs3[:].rearrange("p a c r -> p (a c r)"), func=AF.Exp),
                      reads=['gs3'], writes=['es3'])
                tk.op('dve', lambda e: e.tensor_scalar(out=es3[:, :, 8:32:8, 0:2], in0=es3[:, :, 8:32:8, 0:2], scalar1=flags[:, 1:2], scalar2=None, op0=ALU.mult),
                      reads=['es3', 'flags'], writes=['es3'])
                for tp in range(4):
                    b = nb()
                    for i in range(2):
                        t = 2 * tp + i
                        tk.op('pe', lambda e, t=t, i=i: e.matmul(ps[b][:, i * 256:(i + 1) * 256], lhsT=TR[dr], rhs=lfbuf[:, t, :], start=True, stop=True),
                              reads=['cf32', 'lK'], writes=[psk[b]])
                    tk.op('act', lambda e: e.activation(out=tmpE[0][:], in_=ps[b][:, :], func=AF.Exp), reads=[psk[b]], writes=['tmpE0'])
                    tk.op('act', lambda e: e.activation(out=tmpE[1][:], in_=ps[b][:, :], func=AF.Exp, scale=-1.0), reads=[psk[b]], writes=['tmpE1'])
                    tk.op('dve', lambda e, tp=tp: e.tensor_tensor(out=qE[:, 2 * tp:2 * tp + 2, :].rearrange("p a c -> p (a c)"),
                                                                  in0=qh[:, 2 * tp:2 * tp + 2, :].rearrange("p a c -> p (a c)"), in1=tmpE[0][:], op=ALU.mult),
                          reads=['qh', 'tmpE0'], writes=['sQ'])
                    tk.op('dve', lambda e, tp=tp: e.tensor_tensor(out=kE[:, 2 * tp:2 * tp + 2, :].rearrange("p a c -> p (a c)"),
                                                                  in0=fbuf[:, 2 * tp:2 * tp + 2, :].rearrange("p a c -> p (a c)"), in1=tmpE[1][:], op=ALU.mult),
                          reads=['fS', 'tmpE1'], writes=['sQ'])
                for cp in range(4):
                    tk.op('dve', lambda e, cp=cp: e.tensor_scalar(out=vhm[:, cp, :, :], in0=vh[:], scalar1=QM[:, cp:cp + 1], scalar2=None, op0=ALU.mult),
                          reads=['vh', 'cf32'], writes=['vhm'])
                for t in range(NT):
                    b = nb()
                    for pr in range(2):
                        tk.op('pe', lambda e, t=t, pr=pr: e.transpose(out=psb16(b)[:, pr * 128:(pr + 1) * 128], in_=qE[:, t, pr * 128:(pr + 1) * 128], identity=IDB),
                              reads=['sQ', 'cb16'], writes=[psk[b]])
                        tk.op('pe', lambda e, t=t, pr=pr: e.transpose(out=psb16(b)[:, (2 + pr) * 128:(3 + pr) * 128], in_=kE[:, t, pr * 128:(pr + 1) * 128], identity=IDB),
                              reads=['sQ', 'cb16'], writes=[psk[b]])
                    tk.op('act', lambda e, t=t: e.copy(out=qET[:, :, t * 128:(t + 1) * 128], in_=psb16(b)[:, 0:256].rearrange("p (a c) -> p a c", a=2)),
                          reads=[psk[b]], writes=['qET'])
                    for h in range(4):
                        tk.op('act', lambda e, t=t, h=h: e.activation(out=kETm[:, h, t * 128:(t + 1) * 128], in_=psb16(b)[:, (2 + h // 2) * 128:(3 + h // 2) * 128],
                                                                      func=AF.Copy, scale=HM[:, h % 2:h % 2 + 1]),
                              reads=[psk[b], 'cf32'], writes=['lK'])
                if stop('R1'):
                    rstop = True
                    break
                tk.dma('sp', S0b[:], s0_d.ap()[l, dr], writes=['S0b'])
                kvb_all = [[nb() for _ in range(4)] for _ in range(2)]
                for pr in range(2):
                    kvb = kvb_all[pr]
                    for c in range(32):
                        cq = c if dr == 0 else 31 - c
                        b = kvb[cq // 8]
                        for h in (2 * pr, 2 * pr + 1):
                            o = ps[b][(h % 2) * 64:(h % 2) * 64 + 64, (cq % 8) * 64:(cq % 8) * 64 + 64]
                            tk.op('pe', lambda e, c=c, h=h, o=o: e.matmul(o, lhsT=kE[:, c // 4, h * 64:(h + 1) * 64], rhs=vhm[:, c % 4, c // 4, h * 64:(h + 1) * 64],
                                                                          start=True, stop=True),
                                  reads=['sQ', 'vhm'], writes=[psk[b]])
                for pr in range(2):
                    kvb = kvb_all[pr]
                    for g in range(4):
                        tk.op('dve', lambda e, g=g: e.tensor_tensor(out=B_[:, :, g * 8:(g + 1) * 8].rearrange("p e c -> p c e"),
                                                                    in0=ps[kvb[g]][:, :].rearrange("p (c e) -> p c e", c=8),
                                                                    in1=es3[:, pr, g * 8:(g + 1) * 8, 2].unsqueeze(2).broadcast_to([128, 8, 64]), op=ALU.mult),
                              reads=[psk[kvb[g]], 'es3'], writes=['vhm'])
                    if dr == 0 and pr == 0:
                        wi = next_w()
                        for t in range(NT):
                            b = kvb[t % 4]
                            proj_tm(wi, 0, 256, t, b)
                            tk.op('act', lambda e, t=t: e.activation(out=sgb[:, t, :], in_=ps[b][:, 0:256], func=AF.Tanh, scale=0.5), reads=[psk[b]], writes=['sQ'])
                    tk.op('dve', lambda e: e.scalar_tensor_tensor(out=B_[:, :, 0], in0=S0b[:, pr, :], scalar=es3[:, pr, 0, 1:2], in1=B_[:, :, 0], op0=ALU.mult, op1=ALU.add),
                          reads=['S0b', 'es3', 'vhm'], writes=['vhm'])
                    tk.op('dve', lambda e: e.tensor_copy(out=A_[:], in_=es3[:, pr, :, 1].unsqueeze(1).broadcast_to([128, 64, 32])), reads=['es3', 'vhm'], writes=['vhm'])
                    tk.op('dve', lambda e: e.memset(A_[:, :, 0:1], 0.0), reads=['vhm'], writes=['vhm'])
                    tk.op('dve', lambda e: e.tensor_tensor_scan(out=B_[:].rearrange("p e c -> p (e c)"), data0=A_[:].rearrange("p e c -> p (e c)"),
                                                                data1=B_[:].rearrange("p e c -> p (e c)"), initial=0.0, op0=ALU.mult, op1=ALU.add),
                          reads=['vhm'], writes=['vhm'])
                    tk.op('dve', lambda e: e.tensor_copy(out=nsb[:, :, pr, :], in_=B_[:, :, 7:32:8].rearrange("p e k -> p k e")), reads=['vhm'], writes=['nsb'])
                    for h2 in range(2):
                        tk.op('dve', lambda e, h2=h2: e.scalar_tensor_tensor(out=SinPm[:, 2 * pr + h2, 1:32, :], in0=B_[:, :, 0:31].rearrange("p e c -> p c e"),
                                                                             scalar=HM[:, h2:h2 + 1], in1=es3[:, pr, 1:32, 0].unsqueeze(2).broadcast_to([128, 31, 64]),
                                                                             op0=ALU.mult, op1=ALU.mult),
                              reads=['vhm', 'es3', 'cf32'], writes=['fS'])
                        tk.op('dve', lambda e, h2=h2: e.scalar_tensor_tensor(out=SinPm[:, 2 * pr + h2, 0, :], in0=S0b[:, pr, :], scalar=HM[:, h2:h2 + 1],
                                                                             in1=es3[:, pr, 0, 0:1].broadcast_to([128, 64]), op0=ALU.mult, op1=ALU.mult),
                              reads=['S0b', 'es3', 'cf32'], writes=['fS'])
                nsb3 = nsb[:].rearrange("p s a e -> p s (a e)")
                if dr == 0:
                    tk.dma('sp', ns_d.ap()[l, 0].rearrange("s p a e -> p s (a e)"), nsb3, reads=['nsb'])
                else:
                    for k in range(4):
                        tk.dma('sp', ns_d.ap()[l, 1, 3 - k].rearrange("p a e -> p (a e)"), nsb3[:, k, :], reads=['nsb'])
                if l + 1 < nl:
                    for ch in range(3 * dr, 3 * dr + 3):
                        emit_mod_chunk(l + 1, ch)
                if stop('R2'):
                    rstop = True
                    break
                for t in range(NT):
                    bA_ = nb()
                    ai = t % 2
                    for h in range(4):
                        tk.op('pe', lambda e, t=t, h=h: e.matmul(ps[bA_][:, h * 128:(h + 1) * 128], lhsT=kETm[:, h, t * 128:(t + 1) * 128],
                                                                 rhs=qET[:, h // 2, t * 128:(t + 1) * 128], start=True, stop=True),
                              reads=['lK', 'qET'], writes=[psk[bA_]])
                    tk.op('dve', lambda e: e.tensor_tensor(out=ATm[ai][:], in0=ps[bA_][:, :].rearrange("p (a c) -> p a c", a=4),
                                                           in1=TRI[dr].unsqueeze(1).broadcast_to([128, 4, 128]), op=ALU.mult),
                          reads=[psk[bA_], 'cb16'], writes=['ATm%d' % ai])
                    bO = nb()
                    for h in range(4):
                        tk.op('pe', lambda e, t=t, h=h: e.matmul(ps[bO][:, h * 64:(h + 1) * 64], lhsT=ATm[ai][:, h, :], rhs=vh[:, t, h * 64:(h + 1) * 64],
                                                                 start=True, stop=False, skip_group_check=True),
                              reads=['ATm%d' % ai, 'vh'], writes=[psk[bO]])
                        for cp in range(4):
                            tk.op('pe', lambda e, t=t, h=h, cp=cp: e.matmul(ps[bO][cp * 32:(cp + 1) * 32, h * 64:(h + 1) * 64],
                                                                            lhsT=qET[:, h // 2, t * 128 + cp * 32:t * 128 + cp * 32 + 32],
                                                                            rhs=SinPm[:, h, (4 * t + cp) if dr == 0 else 31 - (4 * t + cp), :], start=False, stop=(cp == 3), skip_group_check=True,
                                                                            tile_position=(0, cp * 32)),
                                  reads=['qET', 'fS'], writes=[psk[bO]])
                    if dr == 0:
                        tk.op('act', lambda e, t=t: e.copy(out=osum[:, t, :], in_=ps[bO][:, 0:256]), reads=[psk[bO]], writes=['osum'])
                    else:
                        tk.op('dve', lambda e, t=t: e.tensor_tensor(out=osum[:, t, :], in0=osum[:, t, :], in1=ps[bO][:, 0:256], op=ALU.add),
                              reads=[psk[bO], 'osum'], writes=['osum'])
                if stop('R3'):
                    rstop = True
                    break
            if rstop:
                break
            tk.op('dve', lambda e: e.tensor_tensor(out=osq[:], in0=osum[:], in1=osum[:], op=ALU.mult), reads=['osum'], writes=['lK'])
            tk.op('dve', lambda e: e.tensor_reduce(out=hss[:], in_=osq[:].rearrange("p t (h d) -> p (t h) d", h=4), axis=AX.X, op=ALU.add),
                  reads=['lK'], writes=['hss'])
            tk.op('dve', lambda e: e.tensor_scalar(out=hss[:], in0=hss[:], scalar1=1.0 / 64, scalar2=EPS, op0=ALU.mult, op1=ALU.add), reads=['hss'], writes=['hss'])
            tk.op('act', lambda e: e.activation(out=hss[:], in_=hss[:], func=AF.Ln), reads=['hss'], writes=['hss'])
            tk.op('act', lambda e: e.activation(out=hss[:], in_=hss[:], func=AF.Exp, scale=-0.5), reads=['hss'], writes=['hss'])
            tk.op('dve', lambda e: e.tensor_tensor(out=osum[:].rearrange("p t (h d) -> p (t h) d", h=4), in0=osum[:].rearrange("p t (h d) -> p (t h) d", h=4),
                                                   in1=hss[:].unsqueeze(2).broadcast_to([128, 32, 64]), op=ALU.mult),
                  reads=['osum', 'hss'], writes=['osum'])
            tk.op('dve', lambda e: e.tensor_tensor(out=osum[:], in0=osum[:], in1=ghgB[:].unsqueeze(1).broadcast_to([128, NT, 256]), op=ALU.mult),
                  reads=['osum', 'ghgB'], writes=['osum'])
            tk.op('dve', lambda e: e.tensor_tensor(out=mixed[:, :, 512:768], in0=osum[:], in1=sgr[:], op=ALU.mult),
                  reads=['osum', 'sgr'], writes=['mixed%d' % j for j in range(8)])

            if stop('R'):
                break
            tk.barrier()
            areset()
            uT = aget([128, 2, T], BF16)
            sgf = aget([128, NT, 256], BF16)
            ucs = aget([128, NT, 2, 256], BF16)
            yT = aget([128, 2, T], BF16)
            csnb = [aget([128, 2, 1024], BF16) for _ in range(4)]
            assert apos[0] + 2048 <= 11520
            wfs = aget([128, 2, 256])
            wfb = aget([128, 2, 256], BF16)
            junk = aget([128, 512], BF16)
            tmpf = aget([128, D])
            for kt_ in range(4):
                tk.dma('sp', csnb[kt_][:], csn_d.ap()[kt_], writes=['csnb%d' % kt_])
            wi = next_w()
            for ci in range(2):
                for g in range(2):
                    b = nb()
                    proj_fm(wi, ci, g, b)
                    tk.op('act', lambda e, ci=ci, g=g: e.copy(out=uT[:, ci, g * 512:(g + 1) * 512], in_=ps[b][:, :]), reads=[psk[b]], writes=['uT'])
            for t in range(NT):
                b = nb()
                proj_tm(wi, 256, 256, t, b)
                tk.op('act', lambda e, t=t: e.activation(out=sgf[:, t, :], in_=ps[b][:, 0:256], func=AF.Silu), reads=[psk[b]], writes=['sgf'])
            tk.dma('sp', wfs[:], wfn_d.ap()[l].rearrange("(c p) n -> p c n", p=128), writes=['wfs'])
            tk.op('pool', lambda e: e.tensor_copy(out=wfb[:], in_=wfs[:]), reads=['wfs'], writes=['wfb'])
            for t in range(NT):
                b = nb()
                for cs in range(2):
                    for ct in range(2):
                        tk.op('pe', lambda e, t=t, cs=cs, ct=ct: e.matmul(ps[b][:, cs * 256 + ct * 128:cs * 256 + ct * 128 + 128], lhsT=uT[:, ct, t * 128:(t + 1) * 128],
                                                                          rhs=C4S4[:, 3 + cs * 2 + ct, :], start=True, stop=True),
                              reads=['uT', 'cb16'], writes=[psk[b]])
                tk.op('act', lambda e, t=t: e.copy(out=ucs[:, t, :, :].rearrange("p a c -> p (a c)"), in_=ps[b][:, :]), reads=[psk[b]], writes=['ucs'])
            if stop('F0'):
                break
            yb = [nb() for _ in range(4)]
            for kt_ in range(8):
                ci = kt_ % 4
                if kt_ >= 4:
                    tk.dma('sp', csnb[ci][:], csn_d.ap()[kt_], writes=['csnb%d' % ci])
                for ct in range(2):
                    for g in range(2):
                        b = yb[ct * 2 + g]
                        for cs in range(2):
                            tk.op('pe', lambda e, kt_=kt_, ct=ct, g=g, cs=cs: e.matmul(ps[b][:, :], lhsT=ucs[:, kt_, cs, ct * 128:(ct + 1) * 128],
                                                                                       rhs=csnb[ci][:, cs, g * 512:(g + 1) * 512],
                                                                                       start=(kt_ == 0 and cs == 0), stop=(kt_ == 7 and cs == 1)),
                                  reads=['ucs', 'csnb%d' % ci], writes=[psk[b]])
            for ct in range(2):
                for g in range(2):
                    b = yb[ct * 2 + g]
                    tk.op('act', lambda e, ct=ct, g=g, b=b: e.copy(out=yT[:, ct, g * 512:(g + 1) * 512], in_=ps[b][:, :]), reads=[psk[b]], writes=['yT'])
            for t in range(NT):
                b = nb()
                for ct in range(2):
                    tk.op('pe', lambda e, t=t, ct=ct: e.matmul(ps[b][:, 0:256], lhsT=yT[:, ct, t * 128:(t + 1) * 128], rhs=wfb[:, ct, :], start=(ct == 0), stop=(ct == 1)),
                          reads=['yT', 'wfb'], writes=[psk[b]])
                tk.op('dve', lambda e, t=t: e.tensor_tensor(out=mixed[:, t, 768:1024], in0=ps[b][:, 0:256], in1=sgf[:, t, :], op=ALU.mult),
                      reads=[psk[b], 'sgf'], writes=['mixed%d' % t])

            if dbg and l == nl - 1:
                for t in range(NT):
                    tk.dma('sp', dbg_d.ap()[t * 128:(t + 1) * 128, :], mixed[:, t, :], reads=['mixed%d' % t])

            if stop('F1'):
                break
            w0 = next_w()
            w1 = next_w(prefetch=False)
            for t in range(NT):
                b = nb()
                for kc in range(8):
                    tk.op('pe', lambda e, t=t, kc=kc: e.transpose(out=psb16(b)[:, kc * 128:(kc + 1) * 128], in_=mixed[:, t, kc * 128:(kc + 1) * 128], identity=IDB),
                          reads=['mixed%d' % t, 'cb16'], writes=[psk[b]])
                tk.op('act', lambda e, t=t: e.copy(out=hT[:, :, t * 128:(t + 1) * 128], in_=psb16(b)[:, :].rearrange("p (k c) -> p k c", k=8)),
                      reads=[psk[b]], writes=['hT'])
            for t in range(NT):
                bb = [nb(), nb()]
                for hf, wi in enumerate((w0, w1)):
                    proj_tm(wi, 0, 512, t, bb[hf])
                    tk.op('act', lambda e, t=t, hf=hf: e.activation(out=junk[:], in_=ps[bb[hf]][:, :], func=AF.Square, accum_out=ssq[:, 8 + hf:9 + hf]),
                          reads=[psk[bb[hf]]], writes=['junk', 'ssq'])
                tk.op('dve', lambda e: e.tensor_tensor(out=rstd[:, 8:9], in0=ssq[:, 8:9], in1=ssq[:, 9:10], op=ALU.add), reads=['ssq'], writes=['rstd'])
                tk.op('dve', lambda e: e.tensor_scalar(out=rstd[:, 8:9], in0=rstd[:, 8:9], scalar1=1.0 / D, scalar2=EPS, op0=ALU.mult, op1=ALU.add),
                      reads=['rstd'], writes=['rstd'])
                tk.op('act', lambda e: e.activation(out=rstd[:, 8:9], in_=rstd[:, 8:9], func=AF.Ln), reads=['rstd'], writes=['rstd'])
                tk.op('act', lambda e: e.activation(out=rstd[:, 8:9], in_=rstd[:, 8:9], func=AF.Exp, scale=-0.5), reads=['rstd'], writes=['rstd'])
                for hf in range(2):
                    tk.op('dve', lambda e, hf=hf: e.scalar_tensor_tensor(out=tmpf[:, hf * 512:(hf + 1) * 512], in0=ps[bb[hf]][:, :], scalar=rstd[:, 8:9],
                                                                         in1=gg[:, hf * 512:(hf + 1) * 512], op0=ALU.mult, op1=ALU.mult),
                          reads=[psk[bb[hf]], 'rstd', 'gg'], writes=['tmpf'])
                tk.op('dve', lambda e, t=t: e.tensor_tensor(out=x_sb[:, t, :], in0=x_sb[:, t, :], in1=tmpf[:], op=ALU.add),
                      reads=['x%d' % t, 'tmpf'], writes=['x%d' % t])
            _issue(wstate['ptr'])

        for t in range(NT):
            tk.dma('sp', y_d.ap()[t * 128:(t + 1) * 128, :], x_sb[:, t, :], reads=['x%d' % t])
        tk.finish()
    return nc


def _consts(is_sample):
    cf32 = np.zeros((128, 7, 128), np.float32)
    p = np.arange(128)
    J2 = np.zeros((128, 128), np.float32)
    for a in range(2):
        for i in range(64):
            J2[a * 64 + i, a * 64 + 63 - i] = 1.0
    cf32[:, 0] = J2
    cf32[:, 1] = np.eye(128, dtype=np.float32)
    cm = np.zeros((128, 128), np.float32)
    if is_sample:
        qc = np.arange(64)
        c0 = np.clip(qc - 8, 0, 48)
        kc = np.arange(64)
        valid = (kc[:, None] >= c0[None, :]) & (kc[:, None] < c0[None, :] + 16)
        m = np.where(valid, 0.0, NEG).astype(np.float32)
        cm = np.tile(m, (2, 2))
    cf32[:, 2] = cm
    s = np.arange(32)[:, None]
    t = np.arange(32)[None, :]
    trf = (s <= t).astype(np.float32) - (s <= 15).astype(np.float32)
    trb = (s >= t).astype(np.float32) - (s >= 16).astype(np.float32)
    for a in range(4):
        cf32[a * 32:(a + 1) * 32, 3, a * 32:(a + 1) * 32] = trf
        cf32[a * 32:(a + 1) * 32, 4, a * 32:(a + 1) * 32] = trb
    sl = np.arange(128) % 32
    ch = np.arange(128) // 32
    selcols = np.zeros((128, 128), np.float32)
    for a in range(4):
        selcols[:, a * 2 + 0] = ((ch == a) & (sl <= 15))
        selcols[:, a * 2 + 1] = (ch == a)
        selcols[:, 8 + a * 2 + 0] = ((ch == 3 - a) & (sl >= 16))
        selcols[:, 8 + a * 2 + 1] = (ch == 3 - a)
        selcols[:, 18 + a] = (ch == a)
    selcols[:, 16] = (np.arange(128) < 64)
    selcols[:, 17] = (np.arange(128) >= 64)
    cf32[:, 5] = selcols
    cf32[0:64, 6, 0:64] = 1.0
    cf32[64:128, 6, 64:128] = 1.0
    rowb = np.zeros((128, 74), np.float32)
    for i, (j, kt) in enumerate(JK):
        for hf in range(2):
            for krl in range(2):
                if is_sample:
                    qr = 2 * j + hf
                    kr = 2 * kt + krl
                    r0 = int(np.clip(qr - 4, 0, 8))
                    ok = (r0 <= kr < r0 + 8)
                else:
                    ok = (kt // 2 == j // 2)
                rowb[krl * 64:(krl + 1) * 64, i * 2 + hf] = 0.0 if ok else NEG
    cb16 = np.zeros((128, 9, 128), np.float32)
    cb16[:, 7] = J2
    cb16[:, 8] = cm
    cb16[:, 0] = np.eye(128)
    mf = (s <= t).astype(np.float32)
    mb = (s >= t).astype(np.float32)
    z = np.zeros((64, 64), np.float32)
    for a in range(4):
        cb16[a * 32:(a + 1) * 32, 1, a * 32:(a + 1) * 32] = mf
        cb16[a * 32:(a + 1) * 32, 2, a * 32:(a + 1) * 32] = mb
    ang = 2 * np.pi * np.outer(np.arange(64), np.arange(64)) / 64
    c4 = np.cos(ang) / 8.0
    s4 = np.sin(ang) / 8.0
    for ct in range(2):
        cb16[:, 3 + ct] = np.block([[c4, z], [z, c4]])
        cb16[:, 5 + ct] = np.block([[s4, z], [z, s4]])
    n = 1024 if is_sample else 256
    idx = np.arange(n)
    a2 = 2 * np.pi * ((np.outer(idx, idx)) % n) / n
    cn = np.cos(a2) / np.sqrt(n)
    sn = -np.sin(a2) / np.sqrt(n)
    CN = np.zeros((1024, 1024), np.float64)
    SN = np.zeros((1024, 1024), np.float64)
    for i in range(1024 // n):
        CN[i * n:(i + 1) * n, i * n:(i + 1) * n] = cn
        SN[i * n:(i + 1) * n, i * n:(i + 1) * n] = sn
    csn = np.stack([CN.reshape(8, 128, 1024), SN.reshape(8, 128, 1024)], axis=2)
    return dict(cf32=cf32, rowbias=rowb, cb16=cb16.astype(ml_dtypes.bfloat16), csn=csn.astype(ml_dtypes.bfloat16))


def _in_maps(x_prompt, x_sample, cache_attn_k, cache_attn_v, state_hgrn, c, c_ctx,
             w_ada, b_ada, g_pre, w_in, rpb, lb_logits, g_hgrn, w_fnet, w_out, g_post):
    f = lambda a: np.ascontiguousarray(np.asarray(a, dtype=np.float32))
    shared = dict(w_ada=f(w_ada), b_ada=f(b_ada), g_pre=f(g_pre), w_in=f(w_in), lb_logits=f(lb_logits),
                  g_hgrn=f(g_hgrn), w_fnet=f(w_fnet), w_out=f(w_out), g_post=f(g_post))
    tp = np.zeros((NL, 8, 23, 127), np.float32)
    tp[:, :, 4:19, 48:79] = f(rpb)
    cs = _consts(True)
    cp = _consts(False)
    maps = []
    for i in range(8):
        m = dict(shared)
        if i < 4:
            m["x"] = f(x_sample[i])
            m["cvec"] = f(np.asarray(c[i]).reshape(8, 128).T)
            m["ctxk"] = f(np.asarray(cache_attn_k[i]).reshape(NL, 512, 512))
            m["ctxv"] = f(np.asarray(cache_attn_v[i]).reshape(NL, 512, 512))
            s = np.asarray(state_hgrn[i]).reshape(NL, 2, 2, 2, 64, 64)
            m["s0"] = f(s.transpose(0, 1, 3, 4, 2, 5).reshape(NL, 2, 128, 2, 64))
            m["flags"] = np.ones((128, 2), np.float32)
            m["tpad"] = tp
            m.update(cs)
        else:
            m["x"] = f(np.asarray(x_prompt[4 * (i - 4):4 * (i - 3)]).reshape(T, D))
            m["cvec"] = f(np.asarray(c_ctx).reshape(8, 128).T)
            m["ctxk"] = np.zeros((NL, 512, 512), np.float32)
            m["ctxv"] = np.zeros((NL, 512, 512), np.float32)
            m["s0"] = np.zeros((NL, 2, 128, 2, 64), np.float32)
            m["flags"] = np.zeros((128, 2), np.float32)
            m["tpad"] = np.zeros_like(tp)
            m.update(cp)
        maps.append(m)
    return maps


_NC_CACHE = {}


def kernel(**inputs):
    if 'nc' not in _NC_CACHE:
        _NC_CACHE['nc'] = build_nc()
    nc = _NC_CACHE['nc']
    maps = _in_maps(**inputs)
    res = run_bass_kernel_spmd(nc, maps, core_ids=list(range(8)))
    r = res.results
    y_sample = np.stack([r[i]["y"] for i in range(4)], axis=0).astype(np.float32)
    y_prompt = np.concatenate([r[i]["y"].reshape(4, 256, D) for i in range(4, 8)], axis=0).astype(np.float32)
    nk = np.concatenate([r[i]["newk"].reshape(NL, 4, 256, 8, 64).transpose(1, 0, 2, 3, 4) for i in range(4, 8)], axis=0)
    nv = np.concatenate([r[i]["newv"].reshape(NL, 4, 256, 8, 64).transpose(1, 0, 2, 3, 4) for i in range(4, 8)], axis=0)
    ns = np.concatenate([r[i]["news"].reshape(NL, 2, 4, 2, 64, 2, 64).transpose(2, 0, 1, 5, 3, 4, 6).reshape(4, NL, 2, 4, 64, 64)
                         for i in range(4, 8)], axis=0)
    return (y_prompt, y_sample, np.ascontiguousarray(nk, dtype=np.float32), np.ascontiguousarray(nv, dtype=np.float32),
            np.ascontiguousarray(ns, dtype=np.float32))
```

```python
import numpy as np
import ml_dtypes
from contextlib import ExitStack
import concourse.bass as bass
import concourse.mybir as mybir
from concourse.bass_utils import run_bass_kernel_spmd

F32 = mybir.dt.float32
BF16 = mybir.dt.bfloat16
AF = mybir.ActivationFunctionType
ALU = mybir.AluOpType
AX = mybir.AxisListType

NL = 4
D = 1024
T = 1024
NT = 8
EPS = 1e-6
NEG = -30000.0
KT = {0: [0, 1, 2, 3], 1: [0, 1, 2, 3], 2: [0, 1, 2, 3, 4], 3: [1, 2, 3, 4, 5],
      4: [2, 3, 4, 5, 6], 5: [3, 4, 5, 6, 7], 6: [4, 5, 6, 7], 7: [4, 5, 6, 7]}
JK = [(j, kt) for j in range(8) for kt in KT[j]]
JKI = {p: i for i, p in enumerate(JK)}
NDS = 24
NSW = 72


class TK:
    def __init__(s, nc, st):
        s.nc = nc
        s.E = {'pe': nc.tensor, 'act': nc.scalar, 'dve': nc.vector, 'pool': nc.gpsimd, 'sp': nc.sync}
        s.sem = {k: st.enter_context(nc.semaphore('s_' + k)) for k in ('pe', 'act', 'dve', 'pool')}
        s.cnt = {k: 0 for k in s.E}
        s.seen = {k: {} for k in s.E}
        s.lw = {}
        s.rd = {}
        s.dsems = [st.enter_context(nc.semaphore('d%d' % i)) for i in range(NDS)]
        s.dcnt = [0] * NDS
        s.dnext = 0
        s.swsems = [st.enter_context(nc.semaphore('w%d' % i)) for i in range(NSW)]
        s.swnext = 0
        s.swlow = 0

    def _wait(s, eng, key, val):
        if eng == 'pe' and key == 'pe':
            return
        if s.seen[eng].get(key, 0) >= val:
            return
        if isinstance(key, str):
            semobj = s.sem[key]
        elif key >= 1000:
            semobj = s.swsems[key - 1000]
        else:
            semobj = s.dsems[key]
        s.E[eng].wait_ge(semobj, val)
        s.seen[eng][key] = val

    def _deps(s, eng, reads, writes):
        for k in reads:
            w = s.lw.get(k)
            if w:
                s._wait(eng, *w)
            if k.startswith('ps'):
                for rk, rv in s.rd.get(k, {}).items():
                    if rk != eng:
                        s._wait(eng, rk, rv)
        for k in writes:
            w = s.lw.get(k)
            if w:
                s._wait(eng, *w)
            for rk, rv in s.rd.get(k, {}).items():
                s._wait(eng, rk, rv)

    def _book(s, tag, reads, writes):
        for k in reads:
            d = s.rd.setdefault(k, {})
            d[tag[0]] = max(d.get(tag[0], 0), tag[1])
        for k in writes:
            s.lw[k] = tag
            s.rd[k] = {}

    def op(s, eng, fn, reads=(), writes=()):
        s._deps(eng, reads, writes)
        inst = fn(s.E[eng])
        s.cnt[eng] += 1
        inst.then_inc(s.sem[eng], 1)
        s._book((eng, s.cnt[eng]), reads, writes)

    def dma(s, q, out, in_, reads=(), writes=()):
        if q == 'pool':
            assert s.swnext < NSW, "out of one-shot semaphores"
            i = s.swnext
            s.swnext += 1
            s._deps(q, reads, writes)
            s.E[q].dma_start(out=out, in_=in_).then_inc(s.swsems[i], 16)
            s._book((1000 + i, 16), reads, writes)
            return
        i = s.dnext
        s.dnext = (s.dnext + 1) % NDS
        if s.dcnt[i] > 0:
            s._wait(q, i, s.dcnt[i])
        s._deps(q, reads, writes)
        s.dcnt[i] += 16
        s.E[q].dma_start(out=out, in_=in_).then_inc(s.dsems[i], 16)
        s._book((i, s.dcnt[i]), reads, writes)

    def barrier(s):
        engs = ('pe', 'act', 'dve', 'pool', 'sp')
        snap = dict(s.cnt)
        dsnap = list(s.dcnt)
        for e in engs:
            for o in ('pe', 'act', 'dve', 'pool'):
                if o != e and snap[o] > 0:
                    s._wait(e, o, snap[o])
            for i in range(NDS):
                if dsnap[i] > 0:
                    s._wait(e, i, dsnap[i])
            for i in range(s.swlow, s.swnext):
                s._wait(e, 1000 + i, 16)
        s.swlow = s.swnext

    def finish(s):
        for i in range(NDS):
            if s.dcnt[i] > 0:
                s._wait('sp', i, s.dcnt[i])
        for i in range(s.swnext):
            s._wait('sp', 1000 + i, 16)
        for k in ('pe', 'act', 'dve', 'pool'):
            if s.cnt[k] > 0:
                s._wait('sp', k, s.cnt[k])


def build_nc(nl=NL, dbg=False, upto=None):
    nc = bass.Bass("TRN2", target_bir_lowering=False)
    _order = ['M0', 'M1', 'M2', 'M3', 'M', 'A0', 'A1', 'A1a', 'A1b', 'A1c', 'A2', 'A', 'R0', 'R1', 'R2', 'R3', 'R', 'F0', 'F1', 'F']

    def stop(p):
        return upto is not None and _order.index(upto) <= _order.index(p)

    def din(name, shape, dt=F32):
        return nc.dram_tensor(name, list(shape), dt, kind="ExternalInput")

    def dout(name, shape, dt=F32):
        return nc.dram_tensor(name, list(shape), dt, kind="ExternalOutput")

    x_d = din("x", [T, D])
    cvec_d = din("cvec", [128, 8])
    ctxk_d = din("ctxk", [NL, 512, 512])
    ctxv_d = din("ctxv", [NL, 512, 512])
    s0_d = din("s0", [NL, 2, 128, 2, 64])
    flags_d = din("flags", [128, 2])
    wada_d = din("w_ada", [NL, D, 3 * D])
    bada_d = din("b_ada", [NL, 3 * D])
    gpre_d = din("g_pre", [NL, D])
    win_d = din("w_in", [NL, D, 3840])
    tpad_d = din("tpad", [NL, 8, 23, 127])
    lbl_d = din("lb_logits", [2, NL, 256])
    ghg_d = din("g_hgrn", [NL, 256])
    wfn_d = din("w_fnet", [NL, 256, 256])
    wout_d = din("w_out", [NL, D, D])
    gpost_d = din("g_post", [NL, D])
    cf32_d = din("cf32", [128, 7, 128])
    rowb_d = din("rowbias", [128, 74])
    cb16_d = din("cb16", [128, 9, 128], BF16)
    csn_d = din("csn", [8, 128, 2, 1024], BF16)

    y_d = dout("y", [T, D])
    nk_d = dout("newk", [NL, T, 512])
    nv_d = dout("newv", [NL, T, 512])
    ns_d = dout("news", [NL, 2, 4, 128, 2, 64])
    dbg_d = dout("dbgmixed", [T, D], BF16) if dbg else None

    with ExitStack() as st:
        def sb(name, shape, dt=F32):
            return st.enter_context(nc.sbuf_tensor(name, list(shape), dt))

        tk = TK(nc, st)
        x_sb = sb("x_sb", [128, NT, D])
        hT = sb("hT", [128, 8, T], BF16)
        mixed = sb("mixed", [128, NT, D], BF16)
        wst = [sb("wst0", [128, 8, 512], BF16)]
        wbf = [sb("wbf%d" % i, [128, 8, 512], BF16) for i in range(2)]
        gg = sb("gg", [128, D])
        modN = sb("modN", [128, 3 * D], BF16)
        screp = sb("screp", [128, 8, 128], BF16)
        brow = sb("brow", [1, 512])
        ones_row = sb("ones_row", [1, 128])
        csil = sb("csil", [128, 8])
        cf32 = sb("cf32s", [128, 7, 128])
        rowb = sb("rowbs", [128, 74])
        cb16 = sb("cb16s", [128, 9, 128], BF16)
        flags = sb("flagss", [128, 2])
        lbl = sb("lbl", [128, 2, 256])
        oml = sb("oml", [128, 2, 256])
        ghgB = sb("ghgB", [128, 256])
        ssq = sb("ssq", [128, 16])
        rstd = sb("rstd", [128, 16])
        ARW = 21120
        arena = sb("arena", [128, ARW])
        apos = [0]

        def areset():
            apos[0] = 0

        def aget(shape, dt=F32):
            n = 1
            for d_ in shape[1:]:
                n *= d_
            words = n if dt == F32 else (n + 1) // 2
            a0 = apos[0]
            apos[0] += words
            assert apos[0] <= ARW, ("arena overflow", apos[0])
            v = arena[:, a0:a0 + words]
            if dt != F32:
                v = v.bitcast(dt)
            if len(shape) == 3:
                v = v.rearrange("p (a b) -> p a b", a=shape[1])
            elif len(shape) == 4:
                v = v.rearrange("p (a b c) -> p a b c", a=shape[1], b=shape[2])
            return v

        psbig = [st.enter_context(nc.psum_tensor("psb%d" % i, [128, 1024], F32)) for i in range(4)]
        ps = [psbig[i // 2][:, (i % 2) * 512:(i % 2 + 1) * 512] for i in range(8)]
        psk = ["ps%d" % i for i in range(8)]
        bank_rr = [0]

        def nb(avoid=()):
            while True:
                b = bank_rr[0]
                bank_rr[0] = (b + 1) % 8
                if b not in avoid:
                    return b

        J2 = cf32[:, 0, :]
        IDF = cf32[:, 1, :]
        CMT = cf32[:, 2, :]
        TR = [cf32[:, 3, :], cf32[:, 4, :]]
        SEL = [cf32[:, 5, 0:8], cf32[:, 5, 8:16]]
        HM = cf32[:, 5, 16:18]
        QM = cf32[:, 5, 18:22]
        HME = cf32[:, 6, :].rearrange("p (a c) -> p a c", a=2)
        IDB = cb16[:, 0, :]
        TRI = [cb16[:, 1, :], cb16[:, 2, :]]
        C4S4 = cb16
        J2B = cb16[:, 7, :]
        CMTB = cb16[:, 8, :]

        tk.dma('sp', cf32[:], cf32_d.ap(), writes=['cf32'])
        tk.dma('sp', rowb[:], rowb_d.ap(), writes=['rowb'])
        tk.dma('sp', cb16[:], cb16_d.ap(), writes=['cb16'])
        tk.dma('sp', flags[:], flags_d.ap(), writes=['flags'])
        tk.dma('sp', csil[:], cvec_d.ap(), writes=['csil'])
        for t in range(NT):
            tk.dma('sp', x_sb[:, t, :], x_d.ap()[t * 128:(t + 1) * 128, :], writes=['x%d' % t])
        tk.op('pool', lambda e: e.memset(ones_row[:], 1.0), writes=['ones_row'])
        tk.op('act', lambda e: e.activation(out=csil[:], in_=csil[:], func=AF.Silu), reads=['csil'], writes=['csil'])

        wring = [0]

        wbring = [0]

        def load_w(src_ap, ncols):
            wi = wbring[0]
            wbring[0] ^= 1
            tk.dma('pool', wbf[wi][:, :, 0:ncols], src_ap.rearrange("(kc p) n -> p kc n", p=128), writes=['wbf%d' % wi])
            return wi

        wseq = []
        for l_ in range(nl):
            for (c0_, n_) in ((0, 512), (512, 512), (1024, 512), (1536, 512), (2048, 512), (2816, 512), (2560, 256), (3328, 512)):
                wseq.append((win_d, l_, c0_, n_))
            wseq.append((wout_d, l_, 0, 512))
            wseq.append((wout_d, l_, 512, 512))
        wstate = {'ptr': 0, 'loaded': {}}

        def _issue(i):
            if i < len(wseq) and i not in wstate['loaded']:
                d_, l_, c0_, n_ = wseq[i]
                wstate['loaded'][i] = load_w(d_.ap()[l_, :, c0_:c0_ + n_], n_)

        def next_w(prefetch=True):
            i = wstate['ptr']
            wstate['ptr'] += 1
            _issue(i)
            if prefetch:
                _issue(i + 1)
            return wstate['loaded'][i]

        def proj_tm(wi, c0, ncols, t, b):
            for kc in range(8):
                tk.op('pe', lambda e, kc=kc: e.matmul(ps[b][:, 0:ncols], lhsT=hT[:, kc, t * 128:(t + 1) * 128],
                                                     rhs=wbf[wi][:, kc, c0:c0 + ncols], start=(kc == 0), stop=(kc == 7)),
                      reads=['hT', 'wbf%d' % wi], writes=[psk[b]])

        def proj_fm(wi, ci, g, b):
            for kc in range(8):
                tk.op('pe', lambda e, kc=kc: e.matmul(ps[b][:, 0:512], lhsT=wbf[wi][:, kc, ci * 128:(ci + 1) * 128],
                                                     rhs=hT[:, kc, g * 512:(g + 1) * 512], start=(kc == 0), stop=(kc == 7)),
                      reads=['hT', 'wbf%d' % wi], writes=[psk[b]])

        def psb16(b):
            return ps[b].bitcast(BF16)

        def emit_mod_chunk(lm, ch, banks=None):
            mod_dma(lm, ch)
            mod_mm(lm, ch, banks)

        def mod_dma(lm, ch):
            tk.dma('pool', wst[0][:], wada_d.ap()[lm, :, ch * 512:(ch + 1) * 512].rearrange("(kc p) n -> p kc n", p=128), writes=['wst0'])
            tk.dma('sp', brow[:], bada_d.ap()[lm:lm + 1, ch * 512:(ch + 1) * 512], writes=['brow'])

        def mod_mm(lm, ch, banks=None):
            b = nb() if banks is None else banks[ch % len(banks)]
            for kc in range(8):
                tk.op('pe', lambda e, kc=kc: e.matmul(ps[b][:, :], lhsT=screp[:, kc, :], rhs=wst[0][:, kc, :], start=(kc == 0), stop=False),
                      reads=['screp', 'wst0'], writes=[psk[b]])
            tk.op('pe', lambda e: e.matmul(ps[b][:, :], lhsT=ones_row[0:1, :], rhs=brow[0:1, :], start=False, stop=True),
                  reads=['ones_row', 'brow'], writes=[psk[b]])
            tk.op('act', lambda e: e.copy(out=modN[:, ch * 512:(ch + 1) * 512], in_=ps[b][:, :]), reads=[psk[b]], writes=['modN'])

        tk.op('dve', lambda e: e.tensor_copy(out=screp[:], in_=csil[:].unsqueeze(2).broadcast_to([128, 8, 128])), reads=['csil'], writes=['screp'])
        for ch in range(6):
            emit_mod_chunk(0, ch)

        for l in range(nl):
            if l == 0:
                tk.barrier()
            areset()
            apos[0] = 11520
            gbc = aget([128, 2, D])
            lbt = aget([128, 2, NL, 256])
            junk = aget([128, D], BF16)
            tmpf = aget([128, D])
            hb = [aget([128, D], BF16) for _ in range(2)]
            modA = aget([128, D])
            tk.dma('sp', gbc[:, 0, :], bass.AP(gpre_d, l * D, [[0, 128], [1, D]]), writes=['gbc'])
            tk.dma('sp', gbc[:, 1, :], bass.AP(gpost_d, l * D, [[0, 128], [1, D]]), writes=['gbc'])
            tk.dma('sp', lbt[:].rearrange("p a l c -> p (a l c)"), bass.AP(lbl_d, 0, [[0, 128], [1, 2 * NL * 256]]), writes=['lbt'])
            if l == 0:
                tk.op('dve', lambda e: e.memset(lbl[:], 0.0), writes=['lbl'])
            else:
                mx = tmpf[:, 0:512].rearrange("p (a c) -> p a c", a=2)
                sm = tmpf[:, 512:1024].rearrange("p (a c) -> p a c", a=2)
                tk.op('dve', lambda e: e.tensor_tensor(out=mx, in0=lbt[:, :, 0, :], in1=lbt[:, :, 1, :], op=ALU.max),
                      reads=['lbt'], writes=['tmpf'])
                for l2 in range(2, NL):
                    tk.op('dve', lambda e, l2=l2: e.tensor_tensor(out=mx, in0=mx, in1=lbt[:, :, l2, :], op=ALU.max),
                          reads=['lbt', 'tmpf'], writes=['tmpf'])
                for l2 in range(NL):
                    tk.op('dve', lambda e, l2=l2: e.tensor_tensor(out=lbt[:, :, l2, :], in0=lbt[:, :, l2, :], in1=mx, op=ALU.subtract),
                          reads=['lbt', 'tmpf'], writes=['lbt'])
                tk.op('act', lambda e: e.activation(out=lbt[:].rearrange("p a l c -> p (a l c)"), in_=lbt[:].rearrange("p a l c -> p (a l c)"), func=AF.Exp),
                      reads=['lbt'], writes=['lbt'])
                tk.op('dve', lambda e: e.tensor_tensor(out=sm, in0=lbt[:, :, 0, :], in1=lbt[:, :, 1, :], op=ALU.add),
                      reads=['lbt'], writes=['tmpf'])
                for l2 in range(2, NL):
                    tk.op('dve', lambda e, l2=l2: e.tensor_tensor(out=sm, in0=sm, in1=lbt[:, :, l2, :], op=ALU.add),
                          reads=['lbt', 'tmpf'], writes=['tmpf'])
                tk.op('dve', lambda e: e.reciprocal(out=sm, in_=sm), reads=['tmpf'], writes=['tmpf'])
                tk.op('dve', lambda e: e.tensor_copy(out=lbl[:], in_=lbt[:, :, 1, :]), reads=['lbt'], writes=['lbl'])
                for l2 in range(2, l + 1):
                    tk.op('dve', lambda e, l2=l2: e.tensor_tensor(out=lbl[:], in0=lbl[:], in1=lbt[:, :, l2, :], op=ALU.add),
                          reads=['lbt', 'lbl'], writes=['lbl'])
                tk.op('dve', lambda e: e.tensor_tensor(out=lbl[:], in0=lbl[:], in1=sm, op=ALU.mult), reads=['lbl', 'tmpf'], writes=['lbl'])
            tk.op('dve', lambda e: e.tensor_scalar(out=oml[:], in0=lbl[:], scalar1=-0.5, scalar2=0.5, op0=ALU.mult, op1=ALU.add),
                  reads=['lbl'], writes=['oml'])
            tk.op('dve', lambda e: e.tensor_scalar(out=lbl[:], in0=lbl[:], scalar1=0.5, scalar2=0.5, op0=ALU.mult, op1=ALU.add),
                  reads=['lbl'], writes=['lbl'])
            if stop('M0'):
                break
            tk.op('dve', lambda e: e.scalar_tensor_tensor(out=modA[:], in0=modN[:, D:2 * D], scalar=1.0, in1=gbc[:, 0, :], op0=ALU.add, op1=ALU.mult),
                  reads=['modN', 'gbc'], writes=['modA'])
            tk.op('dve', lambda e: e.tensor_tensor(out=gg[:], in0=modN[:, 2 * D:3 * D], in1=gbc[:, 1, :], op=ALU.mult), reads=['modN', 'gbc'], writes=['gg'])
            if stop('M1'):
                break
            for t in range(NT):
                tk.op('act', lambda e, t=t: e.activation(out=junk[:], in_=x_sb[:, t, :], func=AF.Square, accum_out=ssq[:, t:t + 1]),
                      reads=['x%d' % t], writes=['junk', 'ssq'])
            tk.op('dve', lambda e: e.tensor_scalar(out=rstd[:, 0:8], in0=ssq[:, 0:8], scalar1=1.0 / D, scalar2=EPS, op0=ALU.mult, op1=ALU.add),
                  reads=['ssq'], writes=['rstd'])
            tk.op('act', lambda e: e.activation(out=rstd[:, 0:8], in_=rstd[:, 0:8], func=AF.Ln), reads=['rstd'], writes=['rstd'])
            tk.op('act', lambda e: e.activation(out=rstd[:, 0:8], in_=rstd[:, 0:8], func=AF.Exp, scale=-0.5), reads=['rstd'], writes=['rstd'])
            if stop('M2'):
                break
            for t in range(NT):
                hbi = t % 2
                tk.op('dve', lambda e, t=t: e.scalar_tensor_tensor(out=tmpf[:], in0=x_sb[:, t, :], scalar=rstd[:, t:t + 1], in1=modA[:],
                                                                   op0=ALU.mult, op1=ALU.mult),
                      reads=['x%d' % t, 'rstd', 'modA'], writes=['tmpf'])
                tk.op('dve', lambda e: e.tensor_tensor(out=hb[hbi][:], in0=tmpf[:], in1=modN[:, 0:D], op=ALU.add),
                      reads=['tmpf', 'modN'], writes=['hb%d' % hbi])
                if stop('M3'):
                    continue
                b = nb()
                for kc in range(8):
                    tk.op('pe', lambda e, kc=kc: e.transpose(out=psb16(b)[:, kc * 128:(kc + 1) * 128], in_=hb[hbi][:, kc * 128:(kc + 1) * 128], identity=IDB),
                          reads=['hb%d' % hbi, 'cb16'], writes=[psk[b]])
                tk.op('act', lambda e, t=t: e.copy(out=hT[:, :, t * 128:(t + 1) * 128], in_=psb16(b)[:, :].rearrange("p (k c) -> p k c", k=8)),
                      reads=[psk[b]], writes=['hT'])

            if stop('M'):
                break
            tk.barrier()
            areset()
            qT = aget([128, 4, T], BF16)
            kT = aget([128, 4, T], BF16)
            ckT = aget([128, 4, 512], BF16)
            vaug = aget([128, NT, 8, 66], BF16)
            cvaug = aget([128, 4, 8, 66], BF16)
            sga = aget([128, NT, 512], BF16)
            expT = aget([128, 7, 8, 128], BF16)
            Eb = [aget([128, 8, 128], BF16) for _ in range(3)]
            Pb = [aget([128, 8, 128], BF16) for _ in range(2)]
            hk = [Eb[0].bitcast(F32) if False else None, None]
            ost = [aget([128, 512]) for _ in range(2)]
            rden = aget([128, 8])
            otmp = aget([128, 8, 64])
            ckb = aget([128, 4, 512], BF16)
            hkA = aget([128, 8, 128])
            hkB = aget([128, 8, 128])
            hk = [hkA, hkB]
            tk.op('pool', lambda e: e.memset(vaug[:, :, :, 64:66], 1.0), writes=['vaug'])
            tk.op('dve', lambda e: e.tensor_copy(out=cvaug[:, :, :, 64:66].rearrange("p a b c -> p (a b) c"),
                                                 in_=flags[:, 0:1].unsqueeze(2).broadcast_to([128, 32, 2])),
                  reads=['flags'], writes=['cvaug'])
            def toep_dma(di):
                dl = di - 3
                hi = di % 2
                for qr in range(2):
                    for krl in range(2):
                        off = ((l * 8) * 23 + (2 * dl + krl - qr + 11)) * 127
                        src = bass.AP(tpad_d, off, [[1, 64], [23 * 127, 8], [1, 64]])
                        tk.dma('sp', hk[hi][qr * 64:(qr + 1) * 64, :, krl * 64:(krl + 1) * 64], src, writes=['hk%d_%d' % (hi, qr * 2 + krl)])

            def toep_mm(di):
                hi = di % 2
                bA = nb()
                bB = nb()
                for h in range(8):
                    b = bA if h % 2 == 0 else bB
                    o = ps[b][:, (h // 2) * 128:(h // 2 + 1) * 128]
                    tk.op('pe', lambda e, h=h, o=o: e.matmul(o, lhsT=hk[hi][:, h, :], rhs=J2, start=True, stop=False),
                          reads=['hk%d_%d' % (hi, x) for x in range(4)] + ['cf32'], writes=[psk[b]])
                    tk.op('pe', lambda e, o=o: e.matmul(o, lhsT=IDF, rhs=CMT, start=False, stop=True),
                          reads=['cf32'], writes=[psk[b]])
                for bi, b in enumerate((bA, bB)):
                    tk.op('act', lambda e, bi=bi, b=b: e.activation(out=expT[:, di, bi * 4:(bi + 1) * 4, :].rearrange("p a c -> p (a c)"),
                                                                    in_=ps[b][:, :], func=AF.Exp),
                          reads=[psk[b]], writes=['expT'])

            toep_dma(0)
            toep_dma(1)
            if stop('A0'):
                break
            ckk = 'ckb'
            tk.dma('pool', ckb[:], ctxk_d.ap()[l].rearrange("(c p) n -> p c n", p=128), writes=[ckk])
            for c in range(4):
                b = nb()
                for pr in range(4):
                    tk.op('pe', lambda e, pr=pr: e.transpose(out=psb16(b)[:, pr * 128:(pr + 1) * 128], in_=ckb[:, c, pr * 128:(pr + 1) * 128], identity=IDB),
                          reads=[ckk, 'cb16'], writes=[psk[b]])
                tk.op('act', lambda e, c=c: e.copy(out=ckT[:, :, c * 128:(c + 1) * 128], in_=psb16(b)[:, 0:512].rearrange("p (k c) -> p k c", k=4)),
                      reads=[psk[b]], writes=['ckT'])
            tk.dma('pool', ckb[:], ctxv_d.ap()[l].rearrange("(c p) n -> p c n", p=128), reads=[], writes=[ckk])
            tk.op('pool', lambda e: e.tensor_copy(out=cvaug[:, :, :, 0:64], in_=ckb[:].rearrange("p c (h d) -> p c h d", h=8)), reads=[ckk], writes=['cvaug'])
            if stop('A1'):
                break
            toep_mm(0)
            toep_dma(2)
            wi = next_w()
            for pr in range(4):
                for g in range(2):
                    b = nb()
                    proj_fm(wi, pr, g, b)
                    tk.op('act', lambda e, pr=pr, g=g: e.copy(out=qT[:, pr, g * 512:(g + 1) * 512], in_=ps[b][:, :]), reads=[psk[b]], writes=['qT'])
            if stop('A1a'):
                break
            toep_mm(1)
            toep_dma(3)
            wi = next_w()
            for pr in range(4):
                for g in range(2):
                    b = nb()
                    proj_fm(wi, pr, g, b)
                    tk.op('act', lambda e, pr=pr, g=g: e.copy(out=kT[:, pr, g * 512:(g + 1) * 512], in_=ps[b][:, :]), reads=[psk[b]], writes=['kT'])
            toep_mm(2)
            toep_dma(4)
            for t in range(NT):
                b = nb()
                proj_tm(wi, 0, 512, t, b)
                oi = t % 2
                tk.op('dve', lambda e: e.tensor_copy(out=ost[oi][:], in_=ps[b][:, :]), reads=[psk[b]], writes=['ost%d' % oi])
                tk.dma('sp', nk_d.ap()[l, t * 128:(t + 1) * 128, :], ost[oi][:], reads=['ost%d' % oi])
            toep_mm(3)
            toep_dma(5)
            if stop('A1b'):
                break
            wi = next_w()
            for t in range(NT):
                b = nb()
                proj_tm(wi, 0, 512, t, b)
                oi = t % 2
                tk.op('dve', lambda e: e.tensor_copy(out=ost[oi][:], in_=ps[b][:, :]), reads=[psk[b]], writes=['ost%d' % oi])
                tk.op('act', lambda e, t=t: e.copy(out=vaug[:, t, :, 0:64], in_=ost[oi][:].rearrange("p (h d) -> p h d", h=8)),
                      reads=['ost%d' % oi], writes=['vaug'])
                tk.dma('sp', nv_d.ap()[l, t * 128:(t + 1) * 128, :], ost[oi][:], reads=['ost%d' % oi])
            if stop('A1c'):
                break
            toep_mm(4)
            toep_dma(6)
            wi = next_w()
            for t in range(NT):
                b = nb()
                proj_tm(wi, 0, 512, t, b)
                tk.op('act', lambda e, t=t: e.activation(out=sga[:, t, :], in_=ps[b][:, :], func=AF.Silu), reads=[psk[b]], writes=['sga'])
            toep_mm(5)
            toep_mm(6)
            if stop('A2'):
                break
            OA, OB = 6, 7
            spairs = [(0, 1), (2, 3), (4, 5)]
            allsteps = []
            for j in range(8):
                st_ = [('l', kt) for kt in KT[j]] + [('c', c) for c in range(4)]
                for si_, (kind, idx) in enumerate(st_):
                    allsteps.append((j, si_, len(st_), kind, idx))

            def emit_S(k):
                j, si_, ns_, kind, idx = allsteps[k]
                sA, sB = spairs[k % 3]
                for h in range(8):
                    b = sA if h % 2 == 0 else sB
                    r0 = (h % 2) * 64
                    ksrc = kT[r0:r0 + 64, h // 2, idx * 128:(idx + 1) * 128] if kind == 'l' else ckT[r0:r0 + 64, h // 2, idx * 128:(idx + 1) * 128]
                    tk.op('pe', lambda e, h=h, b=b, ksrc=ksrc, r0=r0: e.matmul(ps[b][:, (h // 2) * 128:(h // 2 + 1) * 128], lhsT=ksrc,
                                                                              rhs=qT[r0:r0 + 64, h // 2, j * 128:(j + 1) * 128], start=True, stop=True),
                          reads=['qT', 'kT' if kind == 'l' else 'ckT'], writes=[psk[b]])

            def emit_rest(k):
                j, si_, ns_, kind, idx = allsteps[k]
                sA, sB = spairs[k % 3]
                sl = k % 3
                big = psbig[sA // 2]
                if kind == 'l':
                    jk = JKI[(j, idx)]
                    for hf in range(2):
                        tk.op('act', lambda e, hf=hf: e.activation(
                            out=Eb[sl][:, :, hf * 64:(hf + 1) * 64],
                            in_=big[:, :].rearrange("p (a c) -> p a c", a=8)[:, :, hf * 64:(hf + 1) * 64],
                            func=AF.Exp, scale=0.125, bias=rowb[:, jk * 2 + hf:jk * 2 + hf + 1]),
                            reads=[psk[sA], psk[sB], 'rowb'], writes=['Eb%d' % sl])
                    di = idx - j + 3
                    pl = k % 2
                    tk.op('dve', lambda e, di=di: e.tensor_tensor(out=Pb[pl][:], in0=Eb[sl][:], in1=expT[:, di, :, :], op=ALU.mult),
                          reads=['Eb%d' % sl, 'expT'], writes=['Pb%d' % pl])
                    lhs, lk = Pb[pl], 'Pb%d' % pl
                    vsrc, vk = vaug, 'vaug'
                else:
                    tk.op('act', lambda e: e.activation(out=Eb[sl][:].rearrange("p a c -> p (a c)"), in_=big[:, :], func=AF.Exp, scale=0.125),
                          reads=[psk[sA], psk[sB]], writes=['Eb%d' % sl])
                    lhs, lk = Eb[sl], 'Eb%d' % sl
                    vsrc, vk = cvaug, 'cvaug'
                for e_ in range(8):
                    h = 2 * (e_ % 4) + e_ // 4
                    ob = OA if e_ < 4 else OB
                    tk.op('pe', lambda e, e_=e_, h=h, ob=ob, lhs=lhs, vsrc=vsrc: e.matmul(
                        ps[ob][:, (e_ % 4) * 66:(e_ % 4) * 66 + 66], lhsT=lhs[:, e_, :], rhs=vsrc[:, idx, h, :],
                        start=(si_ == 0 and e_ % 4 == 0), stop=(si_ == ns_ - 1), skip_group_check=True),
                        reads=[lk, vk], writes=[psk[ob]])
                if si_ == ns_ - 1:
                    for bi, ob in enumerate((OA, OB)):
                        tk.op('dve', lambda e, bi=bi, ob=ob: e.reciprocal(out=rden[:, bi * 4:(bi + 1) * 4],
                                                                          in_=ps[ob][:, 0:264].rearrange("p (a c) -> p a c", a=4)[:, :, 64]),
                              reads=[psk[ob]], writes=['rden'])
                    for bi, ob in enumerate((OA, OB)):
                        tk.op('dve', lambda e, bi=bi, ob=ob: e.tensor_tensor(
                            out=otmp[:, bi:8:2, :], in0=ps[ob][:, 0:264].rearrange("p (a c) -> p a c", a=4)[:, :, 0:64],
                            in1=rden[:, bi * 4:(bi + 1) * 4].unsqueeze(2).broadcast_to([128, 4, 64]), op=ALU.mult),
                            reads=[psk[ob], 'rden'], writes=['otmp'])
                    tk.op('dve', lambda e: e.tensor_tensor(out=mixed[:, j, 0:512], in0=otmp[:].rearrange("p h d -> p (h d)"), in1=sga[:, j, :], op=ALU.mult),
                          reads=['otmp', 'sga'], writes=['mixed%d' % j])

            emit_S(0)
            emit_S(1)
            for k in range(len(allsteps)):
                if k + 2 < len(allsteps):
                    emit_S(k + 2)
                emit_rest(k)

            if stop('A'):
                break
            tk.barrier()
            areset()
            qh = aget([128, NT, 256], BF16)
            sgb = aget([128, NT, 256])
            qE = sgb.rearrange("p a b -> p (a b)")[:, 0:1024].bitcast(BF16).rearrange("p (a b) -> p a b", a=NT)
            kE = sgb.rearrange("p a b -> p (a b)")[:, 1024:2048].bitcast(BF16).rearrange("p (a b) -> p a b", a=NT)
            vh = aget([128, NT, 256], BF16)
            vhmF = aget([128, 4096])
            vhm = vhmF.bitcast(BF16).rearrange("p (q t c) -> p q t c", q=4, t=NT)
            A_ = vhmF[:, 0:2048].rearrange("p (e c) -> p e c", e=64)
            B_ = vhmF[:, 2048:4096].rearrange("p (e c) -> p e c", e=64)
            sgr = aget([128, NT, 256], BF16)
            fS = aget([128, 4096])
            fbuf = fS[:, 0:2048].rearrange("p (a b) -> p a b", a=NT)
            SinPm = fS.bitcast(BF16).rearrange("p (h c e) -> p h c e", h=4, c=32)
            lfbuf = aget([128, NT, 256])
            kETm = lfbuf.rearrange("p a b -> p (a b)").bitcast(BF16).rearrange("p (h t) -> p h t", h=4)
            osq = lfbuf
            tmpE = [aget([128, 512]) for _ in range(2)]
            qET = aget([128, 2, T], BF16)
            ATm = [aget([128, 4, 128], BF16) for _ in range(2)]
            osum = aget([128, NT, 256])
            gs3 = aget([128, 2, 32, 3])
            es3 = aget([128, 2, 32, 3])
            S0b = aget([128, 2, 64])
            nsb = aget([128, 4, 2, 64])
            hss = aget([128, 32])
            tk.dma('sp', ghgB[:], bass.AP(ghg_d, l * 256, [[0, 128], [1, 256]]), writes=['ghgB'])
            wi = next_w()
            for t in range(NT):
                b = nb()
                proj_tm(wi, 0, 512, t, b)
                tk.op('act', lambda e, t=t: e.activation(out=qh[:, t, :], in_=ps[b][:, 0:256], func=AF.Silu), reads=[psk[b]], writes=['qh'])
                tk.op('act', lambda e, t=t: e.activation(out=sgb[:, t, :], in_=ps[b][:, 256:512], func=AF.Tanh, scale=0.5), reads=[psk[b]], writes=['sQ'])
            wi = next_w()
            for t in range(NT):
                b = nb()
                proj_tm(wi, 0, 512, t, b)
                tk.op('act', lambda e, t=t: e.activation(out=sgr[:, t, :], in_=ps[b][:, 256:512], func=AF.Silu), reads=[psk[b]], writes=['sgr'])
                tk.op('dve', lambda e, t=t: e.tensor_copy(out=vh[:, t, :], in_=ps[b][:, 0:256]), reads=[psk[b]], writes=['vh'])

            if stop('R0'):
                break
            rstop = False
            for dr in range(2):
                if l + 1 < nl:
                    mod_dma(l + 1, 3 * dr)
                lb_bc = lbl[:, dr, :].unsqueeze(1).broadcast_to([128, NT, 256])
                oml_bc = oml[:, dr, :].unsqueeze(1).broadcast_to([128, NT, 256])
                tk.op('dve', lambda e: e.tensor_tensor(out=fbuf[:], in0=sgb[:], in1=oml_bc, op=ALU.mult), reads=['sQ', 'oml'], writes=['fS'])
                tk.op('dve', lambda e: e.tensor_tensor(out=fbuf[:], in0=fbuf[:], in1=lb_bc, op=ALU.add), reads=['fS', 'lbl'], writes=['fS'])
                tk.op('act', lambda e: e.activation(out=lfbuf[:].rearrange("p a b -> p (a b)"), in_=fbuf[:].rearrange("p a b -> p (a b)"), func=AF.Ln),
                      reads=['fS'], writes=['lK'])
                tk.op('dve', lambda e: e.tensor_scalar(out=fbuf[:], in0=fbuf[:], scalar1=-1.0, scalar2=1.0, op0=ALU.mult, op1=ALU.add),
                      reads=['fS'], writes=['fS'])
                bS = nb()
                for t in range(NT):
                    for pr in range(2):
                        tt_ = t if dr == 0 else 7 - t
                        tk.op('pe', lambda e, t=t, pr=pr, tt_=tt_: e.matmul(ps[bS][:, (pr * 8 + tt_) * 8:(pr * 8 + tt_) * 8 + 8], lhsT=lfbuf[:, t, pr * 128:(pr + 1) * 128],
                                                                   rhs=SEL[dr], start=True, stop=True),
                              reads=['lK', 'cf32'], writes=[psk[bS]])
                psS = ps[bS][:, 0:128].rearrange("p (a c r) -> p a c r", a=2, c=32)
                tk.op('dve', lambda e: e.tensor_copy(out=gs3[:, :, :, 0:2], in_=psS), reads=[psk[bS]], writes=['gs3'])
                tk.op('dve', lambda e: e.tensor_tensor(out=gs3[:, :, :, 2], in0=gs3[:, :, :, 1], in1=gs3[:, :, :, 0], op=ALU.subtract),
                      reads=['gs3'], writes=['gs3'])
                tk.op('act', lambda e: e.activation(out=es3[:].rearrange("p a c r -> p (a c r)"), in_=gs3[:].rearrange("p a c r -> p (a c r)"), func=AF.Exp),
                      reads=['gs3'], writes=['es3'])
                tk.op('dve', lambda e: e.tensor_scalar(out=es3[:, :, 8:32:8, 0:2], in0=es3[:, :, 8:32:8, 0:2], scalar1=flags[:, 1:2], scalar2=None, op0=ALU.mult),
                      reads=['es3', 'flags'], writes=['es3'])
                for tp in range(4):
                    b = nb()
                    for i in range(2):
                        t = 2 * tp + i
                        tk.op('pe', lambda e, t=t, i=i: e.matmul(ps[b][:, i * 256:(i + 1) * 256], lhsT=TR[dr], rhs=lfbuf[:, t, :], start=True, stop=True),
                              reads=['cf32', 'lK'], writes=[psk[b]])
                    tk.op('act', lambda e: e.activation(out=tmpE[0][:], in_=ps[b][:, :], func=AF.Exp), reads=[psk[b]], writes=['tmpE0'])
                    tk.op('act', lambda e: e.activation(out=tmpE[1][:], in_=ps[b][:, :], func=AF.Exp, scale=-1.0), reads=[psk[b]], writes=['tmpE1'])
                    tk.op('dve', lambda e, tp=tp: e.tensor_tensor(out=qE[:, 2 * tp:2 * tp + 2, :].rearrange("p a c -> p (a c)"),
                                                                  in0=qh[:, 2 * tp:2 * tp + 2, :].rearrange("p a c -> p (a c)"), in1=tmpE[0][:], op=ALU.mult),
                          reads=['qh', 'tmpE0'], writes=['sQ'])
                    tk.op('dve', lambda e, tp=tp: e.tensor_tensor(out=kE[:, 2 * tp:2 * tp + 2, :].rearrange("p a c -> p (a c)"),
                                                                  in0=fbuf[:, 2 * tp:2 * tp + 2, :].rearrange("p a c -> p (a c)"), in1=tmpE[1][:], op=ALU.mult),
                          reads=['fS', 'tmpE1'], writes=['sQ'])
                for cp in range(4):
                    tk.op('dve', lambda e, cp=cp: e.tensor_scalar(out=vhm[:, cp, :, :], in0=vh[:], scalar1=QM[:, cp:cp + 1], scalar2=None, op0=ALU.mult),
                          reads=['vh', 'cf32'], writes=['vhm'])
                for t in range(NT):
                    b = nb()
                    for pr in range(2):
                        tk.op('pe', lambda e, t=t, pr=pr: e.transpose(out=psb16(b)[:, pr * 128:(pr + 1) * 128], in_=qE[:, t, pr * 128:(pr + 1) * 128], identity=IDB),
                              reads=['sQ', 'cb16'], writes=[psk[b]])
                        tk.op('pe', lambda e, t=t, pr=pr: e.transpose(out=psb16(b)[:, (2 + pr) * 128:(3 + pr) * 128], in_=kE[:, t, pr * 128:(pr + 1) * 128], identity=IDB),
                              reads=['sQ', 'cb16'], writes=[psk[b]])
                    tk.op('act', lambda e, t=t: e.copy(out=qET[:, :, t * 128:(t + 1) * 128], in_=psb16(b)[:, 0:256].rearrange("p (a c) -> p a c", a=2)),
                          reads=[psk[b]], writes=['qET'])
                    for h in range(4):
                        tk.op('act', lambda e, t=t, h=h: e.activation(out=kETm[:, h, t * 128:(t + 1) * 128], in_=psb16(b)[:, (2 + h // 2) * 128:(3 + h // 2) * 128],
                                                                      func=AF.Copy, scale=HM[:, h % 2:h % 2 + 1]),
                              reads=[psk[b], 'cf32'], writes=['lK'])
                if stop('R1'):
                    rstop = True
                    break
                tk.dma('sp', S0b[:], s0_d.ap()[l, dr], writes=['S0b'])
                kvb_all = [[nb() for _ in range(4)] for _ in range(2)]
                for pr in range(2):
                    kvb = kvb_all[pr]
                    for c in range(32):
                        cq = c if dr == 0 else 31 - c
                        b = kvb[cq // 8]
                        for h in (2 * pr, 2 * pr + 1):
                            o = ps[b][(h % 2) * 64:(h % 2) * 64 + 64, (cq % 8) * 64:(cq % 8) * 64 + 64]
                            tk.op('pe', lambda e, c=c, h=h, o=o: e.matmul(o, lhsT=kE[:, c // 4, h * 64:(h + 1) * 64], rhs=vhm[:, c % 4, c // 4, h * 64:(h + 1) * 64],
                                                                          start=True, stop=True),
                                  reads=['sQ', 'vhm'], writes=[psk[b]])
                for pr in range(2):
                    kvb = kvb_all[pr]
                    for g in range(4):
                        tk.op('dve', lambda e, g=g: e.tensor_tensor(out=B_[:, :, g * 8:(g + 1) * 8].rearrange("p e c -> p c e"),
                                                                    in0=ps[kvb[g]][:, :].rearrange("p (c e) -> p c e", c=8),
                                                                    in1=es3[:, pr, g * 8:(g + 1) * 8, 2].unsqueeze(2).broadcast_to([128, 8, 64]), op=ALU.mult),
                              reads=[psk[kvb[g]], 'es3'], writes=['vhm'])
                    if dr == 0 and pr == 0:
                        wi = next_w()
                        for t in range(NT):
                            b = kvb[t % 4]
                            proj_tm(wi, 0, 256, t, b)
                            tk.op('act', lambda e, t=t: e.activation(out=sgb[:, t, :], in_=ps[b][:, 0:256], func=AF.Tanh, scale=0.5), reads=[psk[b]], writes=['sQ'])
                    tk.op('dve', lambda e: e.scalar_tensor_tensor(out=B_[:, :, 0], in0=S0b[:, pr, :], scalar=es3[:, pr, 0, 1:2], in1=B_[:, :, 0], op0=ALU.mult, op1=ALU.add),
                          reads=['S0b', 'es3', 'vhm'], writes=['vhm'])
                    tk.op('dve', lambda e: e.tensor_copy(out=A_[:], in_=es3[:, pr, :, 1].unsqueeze(1).broadcast_to([128, 64, 32])), reads=['es3', 'vhm'], writes=['vhm'])
                    tk.op('dve', lambda e: e.memset(A_[:, :, 0:1], 0.0), reads=['vhm'], writes=['vhm'])
                    tk.op('dve', lambda e: e.tensor_tensor_scan(out=B_[:].rearrange("p e c -> p (e c)"), data0=A_[:].rearrange("p e c -> p (e c)"),
                                                                data1=B_[:].rearrange("p e c -> p (e c)"), initial=0.0, op0=ALU.mult, op1=ALU.add),
                          reads=['vhm'], writes=['vhm'])
                    tk.op('dve', lambda e: e.tensor_copy(out=nsb[:, :, pr, :], in_=B_[:, :, 7:32:8].rearrange("p e k -> p k e")), reads=['vhm'], writes=['nsb'])
                    for h2 in range(2):
                        tk.op('dve', lambda e, h2=h2: e.scalar_tensor_tensor(out=SinPm[:, 2 * pr + h2, 1:32, :], in0=B_[:, :, 0:31].rearrange("p e c -> p c e"),
                                                                             scalar=HM[:, h2:h2 + 1], in1=es3[:, pr, 1:32, 0].unsqueeze(2).broadcast_to([128, 31, 64]),
                                                                             op0=ALU.mult, op1=ALU.mult),
                              reads=['vhm', 'es3', 'cf32'], writes=['fS'])
                        tk.op('dve', lambda e, h2=h2: e.scalar_tensor_tensor(out=SinPm[:, 2 * pr + h2, 0, :], in0=S0b[:, pr, :], scalar=HM[:, h2:h2 + 1],
                                                                             in1=es3[:, pr, 0, 0:1].broadcast_to([128, 64]), op0=ALU.mult, op1=ALU.mult),
                              reads=['S0b', 'es3', 'cf32'], writes=['fS'])
                nsb3 = nsb[:].rearrange("p s a e -> p s (a e)")
                if dr == 0:
                    tk.dma('sp', ns_d.ap()[l, 0].rearrange("s p a e -> p s (a e)"), nsb3, reads=['nsb'])
                else:
                    for k in range(4):
                        tk.dma('sp', ns_d.ap()[l, 1, 3 - k].rearrange("p a e -> p (a e)"), nsb3[:, k, :], reads=['nsb'])
                if l + 1 < nl:
                    mod_mm(l + 1, 3 * dr)
                    mod_dma(l + 1, 3 * dr + 1)
                if stop('R2'):
                    rstop = True
                    break
                for t in range(NT):
                    bA_ = nb()
                    ai = t % 2
                    for h in range(4):
                        tk.op('pe', lambda e, t=t, h=h: e.matmul(ps[bA_][:, h * 128:(h + 1) * 128], lhsT=kETm[:, h, t * 128:(t + 1) * 128],
                                                                 rhs=qET[:, h // 2, t * 128:(t + 1) * 128], start=True, stop=True),
                              reads=['lK', 'qET'], writes=[psk[bA_]])
                    tk.op('dve', lambda e: e.tensor_tensor(out=ATm[ai][:], in0=ps[bA_][:, :].rearrange("p (a c) -> p a c", a=4),
                                                           in1=TRI[dr].unsqueeze(1).broadcast_to([128, 4, 128]), op=ALU.mult),
                          reads=[psk[bA_], 'cb16'], writes=['ATm%d' % ai])
                    bO = nb()
                    for h in range(4):
                        tk.op('pe', lambda e, t=t, h=h: e.matmul(ps[bO][:, h * 64:(h + 1) * 64], lhsT=ATm[ai][:, h, :], rhs=vh[:, t, h * 64:(h + 1) * 64],
                                                                 start=True, stop=False, skip_group_check=True),
                              reads=['ATm%d' % ai, 'vh'], writes=[psk[bO]])
                        for cp in range(4):
                            tk.op('pe', lambda e, t=t, h=h, cp=cp: e.matmul(ps[bO][cp * 32:(cp + 1) * 32, h * 64:(h + 1) * 64],
                                                                            lhsT=qET[:, h // 2, t * 128 + cp * 32:t * 128 + cp * 32 + 32],
                                                                            rhs=SinPm[:, h, (4 * t + cp) if dr == 0 else 31 - (4 * t + cp), :], start=False, stop=(cp == 3), skip_group_check=True,
                                                                            tile_position=(0, cp * 32)),
                                  reads=['qET', 'fS'], writes=[psk[bO]])
                    if l + 1 < nl and t == 3:
                        mod_mm(l + 1, 3 * dr + 1)
                        mod_dma(l + 1, 3 * dr + 2)
                    if l + 1 < nl and t == 7:
                        mod_mm(l + 1, 3 * dr + 2)
                    if dr == 0:
                        tk.op('act', lambda e, t=t: e.copy(out=osum[:, t, :], in_=ps[bO][:, 0:256]), reads=[psk[bO]], writes=['osum'])
                    else:
                        tk.op('dve', lambda e, t=t: e.tensor_tensor(out=osum[:, t, :], in0=osum[:, t, :], in1=ps[bO][:, 0:256], op=ALU.add),
                              reads=[psk[bO], 'osum'], writes=['osum'])
                if stop('R3'):
                    rstop = True
                    break
            if rstop:
                break
            tk.op('dve', lambda e: e.tensor_tensor(out=osq[:], in0=osum[:], in1=osum[:], op=ALU.mult), reads=['osum'], writes=['lK'])
            tk.op('dve', lambda e: e.tensor_reduce(out=hss[:], in_=osq[:].rearrange("p t (h d) -> p (t h) d", h=4), axis=AX.X, op=ALU.add),
                  reads=['lK'], writes=['hss'])
            tk.op('dve', lambda e: e.tensor_scalar(out=hss[:], in0=hss[:], scalar1=1.0 / 64, scalar2=EPS, op0=ALU.mult, op1=ALU.add), reads=['hss'], writes=['hss'])
            tk.op('act', lambda e: e.activation(out=hss[:], in_=hss[:], func=AF.Ln), reads=['hss'], writes=['hss'])
            tk.op('act', lambda e: e.activation(out=hss[:], in_=hss[:], func=AF.Exp, scale=-0.5), reads=['hss'], writes=['hss'])
            tk.op('dve', lambda e: e.tensor_tensor(out=osum[:].rearrange("p t (h d) -> p (t h) d", h=4), in0=osum[:].rearrange("p t (h d) -> p (t h) d", h=4),
                                                   in1=hss[:].unsqueeze(2).broadcast_to([128, 32, 64]), op=ALU.mult),
                  reads=['osum', 'hss'], writes=['osum'])
            tk.op('dve', lambda e: e.tensor_tensor(out=osum[:], in0=osum[:], in1=ghgB[:].unsqueeze(1).broadcast_to([128, NT, 256]), op=ALU.mult),
                  reads=['osum', 'ghgB'], writes=['osum'])
            tk.op('dve', lambda e: e.tensor_tensor(out=mixed[:, :, 512:768], in0=osum[:], in1=sgr[:], op=ALU.mult),
                  reads=['osum', 'sgr'], writes=['mixed%d' % j for j in range(8)])

            if stop('R'):
                break
            tk.barrier()
            areset()
            uT = aget([128, 2, T], BF16)
            sgf = aget([128, NT, 256], BF16)
            ucs = aget([128, NT, 2, 256], BF16)
            yT = aget([128, 2, T], BF16)
            csnb = [aget([128, 2, 1024], BF16) for _ in range(4)]
            assert apos[0] + 2048 <= 11520
            wfs = aget([128, 2, 256])
            wfb = aget([128, 2, 256], BF16)
            junk = aget([128, 512], BF16)
            tmpf = aget([128, D])
            for kt_ in range(4):
                tk.dma('sp', csnb[kt_][:], csn_d.ap()[kt_], writes=['csnb%d' % kt_])
            wi = next_w()
            for ci in range(2):
                for g in range(2):
                    b = nb()
                    proj_fm(wi, ci, g, b)
                    tk.op('act', lambda e, ci=ci, g=g: e.copy(out=uT[:, ci, g * 512:(g + 1) * 512], in_=ps[b][:, :]), reads=[psk[b]], writes=['uT'])
            for t in range(NT):
                b = nb()
                proj_tm(wi, 256, 256, t, b)
                tk.op('act', lambda e, t=t: e.activation(out=sgf[:, t, :], in_=ps[b][:, 0:256], func=AF.Silu), reads=[psk[b]], writes=['sgf'])
            tk.dma('sp', wfs[:], wfn_d.ap()[l].rearrange("(c p) n -> p c n", p=128), writes=['wfs'])
            tk.op('pool', lambda e: e.tensor_copy(out=wfb[:], in_=wfs[:]), reads=['wfs'], writes=['wfb'])
            for t in range(NT):
                b = nb()
                for cs in range(2):
                    for ct in range(2):
                        tk.op('pe', lambda e, t=t, cs=cs, ct=ct: e.matmul(ps[b][:, cs * 256 + ct * 128:cs * 256 + ct * 128 + 128], lhsT=uT[:, ct, t * 128:(t + 1) * 128],
                                                                          rhs=C4S4[:, 3 + cs * 2 + ct, :], start=True, stop=True),
                              reads=['uT', 'cb16'], writes=[psk[b]])
                tk.op('act', lambda e, t=t: e.copy(out=ucs[:, t, :, :].rearrange("p a c -> p (a c)"), in_=ps[b][:, :]), reads=[psk[b]], writes=['ucs'])
            if stop('F0'):
                break
            yb = [nb() for _ in range(4)]
            for kt_ in range(8):
                ci = kt_ % 4
                if kt_ >= 4:
                    tk.dma('sp', csnb[ci][:], csn_d.ap()[kt_], writes=['csnb%d' % ci])
                for ct in range(2):
                    for g in range(2):
                        b = yb[ct * 2 + g]
                        for cs in range(2):
                            tk.op('pe', lambda e, kt_=kt_, ct=ct, g=g, cs=cs: e.matmul(ps[b][:, :], lhsT=ucs[:, kt_, cs, ct * 128:(ct + 1) * 128],
                                                                                       rhs=csnb[ci][:, cs, g * 512:(g + 1) * 512],
                                                                                       start=(kt_ == 0 and cs == 0), stop=(kt_ == 7 and cs == 1)),
                                  reads=['ucs', 'csnb%d' % ci], writes=[psk[b]])
            for ct in range(2):
                for g in range(2):
                    b = yb[ct * 2 + g]
                    tk.op('act', lambda e, ct=ct, g=g, b=b: e.copy(out=yT[:, ct, g * 512:(g + 1) * 512], in_=ps[b][:, :]), reads=[psk[b]], writes=['yT'])
            for t in range(NT):
                b = nb()
                for ct in range(2):
                    tk.op('pe', lambda e, t=t, ct=ct: e.matmul(ps[b][:, 0:256], lhsT=yT[:, ct, t * 128:(t + 1) * 128], rhs=wfb[:, ct, :], start=(ct == 0), stop=(ct == 1)),
                          reads=['yT', 'wfb'], writes=[psk[b]])
                tk.op('dve', lambda e, t=t: e.tensor_tensor(out=mixed[:, t, 768:1024], in0=ps[b][:, 0:256], in1=sgf[:, t, :], op=ALU.mult),
                      reads=[psk[b], 'sgf'], writes=['mixed%d' % t])

            if dbg and l == nl - 1:
                for t in range(NT):
                    tk.dma('sp', dbg_d.ap()[t * 128:(t + 1) * 128, :], mixed[:, t, :], reads=['mixed%d' % t])

            if stop('F1'):
                break
            w0 = next_w()
            w1 = next_w(prefetch=False)
            for t in range(NT):
                b = nb()
                for kc in range(8):
                    tk.op('pe', lambda e, t=t, kc=kc: e.transpose(out=psb16(b)[:, kc * 128:(kc + 1) * 128], in_=mixed[:, t, kc * 128:(kc + 1) * 128], identity=IDB),
                          reads=['mixed%d' % t, 'cb16'], writes=[psk[b]])
                tk.op('act', lambda e, t=t: e.copy(out=hT[:, :, t * 128:(t + 1) * 128], in_=psb16(b)[:, :].rearrange("p (k c) -> p k c", k=8)),
                      reads=[psk[b]], writes=['hT'])
            for t in range(NT):
                bb = [nb(), nb()]
                for hf, wi in enumerate((w0, w1)):
                    proj_tm(wi, 0, 512, t, bb[hf])
                    tk.op('act', lambda e, t=t, hf=hf: e.activation(out=junk[:], in_=ps[bb[hf]][:, :], func=AF.Square, accum_out=ssq[:, 8 + hf:9 + hf]),
                          reads=[psk[bb[hf]]], writes=['junk', 'ssq'])
                tk.op('dve', lambda e: e.tensor_tensor(out=rstd[:, 8:9], in0=ssq[:, 8:9], in1=ssq[:, 9:10], op=ALU.add), reads=['ssq'], writes=['rstd'])
                tk.op('dve', lambda e: e.tensor_scalar(out=rstd[:, 8:9], in0=rstd[:, 8:9], scalar1=1.0 / D, scalar2=EPS, op0=ALU.mult, op1=ALU.add),
                      reads=['rstd'], writes=['rstd'])
                tk.op('act', lambda e: e.activation(out=rstd[:, 8:9], in_=rstd[:, 8:9], func=AF.Ln), reads=['rstd'], writes=['rstd'])
                tk.op('act', lambda e: e.activation(out=rstd[:, 8:9], in_=rstd[:, 8:9], func=AF.Exp, scale=-0.5), reads=['rstd'], writes=['rstd'])
                for hf in range(2):
                    tk.op('dve', lambda e, hf=hf: e.scalar_tensor_tensor(out=tmpf[:, hf * 512:(hf + 1) * 512], in0=ps[bb[hf]][:, :], scalar=rstd[:, 8:9],
                                                                         in1=gg[:, hf * 512:(hf + 1) * 512], op0=ALU.mult, op1=ALU.mult),
                          reads=[psk[bb[hf]], 'rstd', 'gg'], writes=['tmpf'])
                tk.op('dve', lambda e, t=t: e.tensor_tensor(out=x_sb[:, t, :], in0=x_sb[:, t, :], in1=tmpf[:], op=ALU.add),
                      reads=['x%d' % t, 'tmpf'], writes=['x%d' % t])
            _issue(wstate['ptr'])

        for t in range(NT):
            tk.dma('sp', y_d.ap()[t * 128:(t + 1) * 128, :], x_sb[:, t, :], reads=['x%d' % t])
        tk.finish()
    return nc


def _consts(is_sample):
    cf32 = np.zeros((128, 7, 128), np.float32)
    p = np.arange(128)
    J2 = np.zeros((128, 128), np.float32)
    for a in range(2):
        for i in range(64):
            J2[a * 64 + i, a * 64 + 63 - i] = 1.0
    cf32[:, 0] = J2
    cf32[:, 1] = np.eye(128, dtype=np.float32)
    cm = np.zeros((128, 128), np.float32)
    if is_sample:
        qc = np.arange(64)
        c0 = np.clip(qc - 8, 0, 48)
        kc = np.arange(64)
        valid = (kc[:, None] >= c0[None, :]) & (kc[:, None] < c0[None, :] + 16)
        m = np.where(valid, 0.0, NEG).astype(np.float32)
        cm = np.tile(m, (2, 2))
    cf32[:, 2] = cm
    s = np.arange(32)[:, None]
    t = np.arange(32)[None, :]
    trf = (s <= t).astype(np.float32) - (s <= 15).astype(np.float32)
    trb = (s >= t).astype(np.float32) - (s >= 16).astype(np.float32)
    for a in range(4):
        cf32[a * 32:(a + 1) * 32, 3, a * 32:(a + 1) * 32] = trf
        cf32[a * 32:(a + 1) * 32, 4, a * 32:(a + 1) * 32] = trb
    sl = np.arange(128) % 32
    ch = np.arange(128) // 32
    selcols = np.zeros((128, 128), np.float32)
    for a in range(4):
        selcols[:, a * 2 + 0] = ((ch == a) & (sl <= 15))
        selcols[:, a * 2 + 1] = (ch == a)
        selcols[:, 8 + a * 2 + 0] = ((ch == 3 - a) & (sl >= 16))
        selcols[:, 8 + a * 2 + 1] = (ch == 3 - a)
        selcols[:, 18 + a] = (ch == a)
    selcols[:, 16] = (np.arange(128) < 64)
    selcols[:, 17] = (np.arange(128) >= 64)
    cf32[:, 5] = selcols
    cf32[0:64, 6, 0:64] = 1.0
    cf32[64:128, 6, 64:128] = 1.0
    rowb = np.zeros((128, 74), np.float32)
    for i, (j, kt) in enumerate(JK):
        for hf in range(2):
            for krl in range(2):
                if is_sample:
                    qr = 2 * j + hf
                    kr = 2 * kt + krl
                    r0 = int(np.clip(qr - 4, 0, 8))
                    ok = (r0 <= kr < r0 + 8)
                else:
                    ok = (kt // 2 == j // 2)
                rowb[krl * 64:(krl + 1) * 64, i * 2 + hf] = 0.0 if ok else NEG
    cb16 = np.zeros((128, 9, 128), np.float32)
    cb16[:, 7] = J2
    cb16[:, 8] = cm
    cb16[:, 0] = np.eye(128)
    mf = (s <= t).astype(np.float32)
    mb = (s >= t).astype(np.float32)
    z = np.zeros((64, 64), np.float32)
    for a in range(4):
        cb16[a * 32:(a + 1) * 32, 1, a * 32:(a + 1) * 32] = mf
        cb16[a * 32:(a + 1) * 32, 2, a * 32:(a + 1) * 32] = mb
    ang = 2 * np.pi * np.outer(np.arange(64), np.arange(64)) / 64
    c4 = np.cos(ang) / 8.0
    s4 = np.sin(ang) / 8.0
    for ct in range(2):
        cb16[:, 3 + ct] = np.block([[c4, z], [z, c4]])
        cb16[:, 5 + ct] = np.block([[s4, z], [z, s4]])
    n = 1024 if is_sample else 256
    idx = np.arange(n)
    a2 = 2 * np.pi * ((np.outer(idx, idx)) % n) / n
    cn = np.cos(a2) / np.sqrt(n)
    sn = -np.sin(a2) / np.sqrt(n)
    CN = np.zeros((1024, 1024), np.float64)
    SN = np.zeros((1024, 1024), np.float64)
    for i in range(1024 // n):
        CN[i * n:(i + 1) * n, i * n:(i + 1) * n] = cn
        SN[i * n:(i + 1) * n, i * n:(i + 1) * n] = sn
    csn = np.stack([CN.reshape(8, 128, 1024), SN.reshape(8, 128, 1024)], axis=2)
    return dict(cf32=cf32, rowbias=rowb, cb16=cb16.astype(ml_dtypes.bfloat16), csn=csn.astype(ml_dtypes.bfloat16))


def _in_maps(x_prompt, x_sample, cache_attn_k, cache_attn_v, state_hgrn, c, c_ctx,
             w_ada, b_ada, g_pre, w_in, rpb, lb_logits, g_hgrn, w_fnet, w_out, g_post):
    f = lambda a: np.ascontiguousarray(np.asarray(a, dtype=np.float32))
    shared = dict(w_ada=f(w_ada), b_ada=f(b_ada), g_pre=f(g_pre), w_in=f(w_in), lb_logits=f(lb_logits),
                  g_hgrn=f(g_hgrn), w_fnet=f(w_fnet), w_out=f(w_out), g_post=f(g_post))
    tp = np.zeros((NL, 8, 23, 127), np.float32)
    tp[:, :, 4:19, 48:79] = f(rpb)
    cs = _consts(True)
    cp = _consts(False)
    maps = []
    for i in range(8):
        m = dict(shared)
        if i < 4:
            m["x"] = f(x_sample[i])
            m["cvec"] = f(np.asarray(c[i]).reshape(8, 128).T)
            m["ctxk"] = f(np.asarray(cache_attn_k[i]).reshape(NL, 512, 512))
            m["ctxv"] = f(np.asarray(cache_attn_v[i]).reshape(NL, 512, 512))
            s = np.asarray(state_hgrn[i]).reshape(NL, 2, 2, 2, 64, 64)
            m["s0"] = f(s.transpose(0, 1, 3, 4, 2, 5).reshape(NL, 2, 128, 2, 64))
            m["flags"] = np.ones((128, 2), np.float32)
            m["tpad"] = tp
            m.update(cs)
        else:
            m["x"] = f(np.asarray(x_prompt[4 * (i - 4):4 * (i - 3)]).reshape(T, D))
            m["cvec"] = f(np.asarray(c_ctx).reshape(8, 128).T)
            m["ctxk"] = np.zeros((NL, 512, 512), np.float32)
            m["ctxv"] = np.zeros((NL, 512, 512), np.float32)
            m["s0"] = np.zeros((NL, 2, 128, 2, 64), np.float32)
            m["flags"] = np.zeros((128, 2), np.float32)
            m["tpad"] = np.zeros_like(tp)
            m.update(cp)
        maps.append(m)
    return maps


_NC_CACHE = {}


def kernel(**inputs):
    if 'nc' not in _NC_CACHE:
        _NC_CACHE['nc'] = build_nc()
    nc = _NC_CACHE['nc']
    maps = _in_maps(**inputs)
    res = run_bass_kernel_spmd(nc, maps, core_ids=list(range(8)))
    r = res.results
    y_sample = np.stack([r[i]["y"] for i in range(4)], axis=0).astype(np.float32)
    y_prompt = np.concatenate([r[i]["y"].reshape(4, 256, D) for i in range(4, 8)], axis=0).astype(np.float32)
    nk = np.concatenate([r[i]["newk"].reshape(NL, 4, 256, 8, 64).transpose(1, 0, 2, 3, 4) for i in range(4, 8)], axis=0)
    nv = np.concatenate([r[i]["newv"].reshape(NL, 4, 256, 8, 64).transpose(1, 0, 2, 3, 4) for i in range(4, 8)], axis=0)
    ns = np.concatenate([r[i]["news"].reshape(NL, 2, 4, 2, 64, 2, 64).transpose(2, 0, 1, 5, 3, 4, 6).reshape(4, NL, 2, 4, 64, 64)
                         for i in range(4, 8)], axis=0)
    return (y_prompt, y_sample, np.ascontiguousarray(nk, dtype=np.float32), np.ascontiguousarray(nv, dtype=np.float32),
            np.ascontiguousarray(ns, dtype=np.float32))
```

```python
import numpy as np
import ml_dtypes
from contextlib import ExitStack
import concourse.bass as bass
import concourse.mybir as mybir
from concourse.bass_utils import run_bass_kernel_spmd

F32 = mybir.dt.float32
BF16 = mybir.dt.bfloat16
AF = mybir.ActivationFunctionType
ALU = mybir.AluOpType
AX = mybir.AxisListType

NL = 4
D = 1024
T = 1024
NT = 8
EPS = 1e-6
NEG = -30000.0
KT = {0: [0, 1, 2, 3], 1: [0, 1, 2, 3], 2: [0, 1, 2, 3, 4], 3: [1, 2, 3, 4, 5],
      4: [2, 3, 4, 5, 6], 5: [3, 4, 5, 6, 7], 6: [4, 5, 6, 7], 7: [4, 5, 6, 7]}
JK = [(j, kt) for j in range(8) for kt in KT[j]]
JKI = {p: i for i, p in enumerate(JK)}
NDS = 24
NSW = 72


class TK:
    def __init__(s, nc, st):
        s.nc = nc
        s.E = {'pe': nc.tensor, 'act': nc.scalar, 'dve': nc.vector, 'pool': nc.gpsimd, 'sp': nc.sync}
        s.sem = {k: st.enter_context(nc.semaphore('s_' + k)) for k in ('pe', 'act', 'dve', 'pool')}
        s.cnt = {k: 0 for k in s.E}
        s.seen = {k: {} for k in s.E}
        s.lw = {}
        s.rd = {}
        s.dsems = [st.enter_context(nc.semaphore('d%d' % i)) for i in range(NDS)]
        s.dcnt = [0] * NDS
        s.dnext = 0
        s.swsems = [st.enter_context(nc.semaphore('w%d' % i)) for i in range(NSW)]
        s.swnext = 0
        s.swlow = 0

    def _wait(s, eng, key, val):
        if eng == 'pe' and key == 'pe':
            return
        if s.seen[eng].get(key, 0) >= val:
            return
        if isinstance(key, str):
            semobj = s.sem[key]
        elif key >= 1000:
            semobj = s.swsems[key - 1000]
        else:
            semobj = s.dsems[key]
        s.E[eng].wait_ge(semobj, val)
        s.seen[eng][key] = val

    def _deps(s, eng, reads, writes):
        for k in reads:
            w = s.lw.get(k)
            if w:
                s._wait(eng, *w)
            if k.startswith('ps'):
                for rk, rv in s.rd.get(k, {}).items():
                    if rk != eng:
                        s._wait(eng, rk, rv)
        for k in writes:
            w = s.lw.get(k)
            if w:
                s._wait(eng, *w)
            for rk, rv in s.rd.get(k, {}).items():
                s._wait(eng, rk, rv)

    def _book(s, tag, reads, writes):
        for k in reads:
            d = s.rd.setdefault(k, {})
            d[tag[0]] = max(d.get(tag[0], 0), tag[1])
        for k in writes:
            s.lw[k] = tag
            s.rd[k] = {}

    def op(s, eng, fn, reads=(), writes=()):
        s._deps(eng, reads, writes)
        inst = fn(s.E[eng])
        s.cnt[eng] += 1
        inst.then_inc(s.sem[eng], 1)
        s._book((eng, s.cnt[eng]), reads, writes)

    def dma(s, q, out, in_, reads=(), writes=()):
        if q == 'pool':
            assert s.swnext < NSW, "out of one-shot semaphores"
            i = s.swnext
            s.swnext += 1
            s._deps(q, reads, writes)
            s.E[q].dma_start(out=out, in_=in_).then_inc(s.swsems[i], 16)
            s._book((1000 + i, 16), reads, writes)
            return
        i = s.dnext
        s.dnext = (s.dnext + 1) % NDS
        if s.dcnt[i] > 0:
            s._wait(q, i, s.dcnt[i])
        s._deps(q, reads, writes)
        s.dcnt[i] += 16
        s.E[q].dma_start(out=out, in_=in_).then_inc(s.dsems[i], 16)
        s._book((i, s.dcnt[i]), reads, writes)

    def barrier(s):
        engs = ('pe', 'act', 'dve', 'pool', 'sp')
        snap = dict(s.cnt)
        dsnap = list(s.dcnt)
        for e in engs:
            for o in ('pe', 'act', 'dve', 'pool'):
                if o != e and snap[o] > 0:
                    s._wait(e, o, snap[o])
            for i in range(NDS):
                if dsnap[i] > 0:
                    s._wait(e, i, dsnap[i])
            for i in range(s.swlow, s.swnext):
                s._wait(e, 1000 + i, 16)
        s.swlow = s.swnext

    def finish(s):
        for i in range(NDS):
            if s.dcnt[i] > 0:
                s._wait('sp', i, s.dcnt[i])
        for i in range(s.swnext):
            s._wait('sp', 1000 + i, 16)
        for k in ('pe', 'act', 'dve', 'pool'):
            if s.cnt[k] > 0:
                s._wait('sp', k, s.cnt[k])


def build_nc(nl=NL, dbg=False, upto=None):
    nc = bass.Bass("TRN2", target_bir_lowering=False)
    _order = ['M0', 'M1', 'M2', 'M3', 'M', 'A0', 'A1', 'A1a', 'A1b', 'A1c', 'A2', 'A', 'R0', 'R1', 'R2', 'R3', 'R', 'F0', 'F1', 'F']

    def stop(p):
        return upto is not None and _order.index(upto) <= _order.index(p)

    def din(name, shape, dt=F32):
        return nc.dram_tensor(name, list(shape), dt, kind="ExternalInput")

    def dout(name, shape, dt=F32):
        return nc.dram_tensor(name, list(shape), dt, kind="ExternalOutput")

    x_d = din("x", [T, D])
    cvec_d = din("cvec", [128, 8])
    ctxk_d = din("ctxk", [NL, 512, 512])
    ctxv_d = din("ctxv", [NL, 512, 512])
    s0_d = din("s0", [NL, 2, 128, 2, 64])
    flags_d = din("flags", [128, 2])
    wada_d = din("w_ada", [NL, D, 3 * D])
    bada_d = din("b_ada", [NL, 3 * D])
    gpre_d = din("g_pre", [NL, D])
    win_d = din("w_in", [NL, D, 3840])
    tpad_d = din("tpad", [NL, 8, 23, 127])
    lbl_d = din("lb_logits", [2, NL, 256])
    ghg_d = din("g_hgrn", [NL, 256])
    wfn_d = din("w_fnet", [NL, 256, 256])
    wout_d = din("w_out", [NL, D, D])
    gpost_d = din("g_post", [NL, D])
    cf32_d = din("cf32", [128, 7, 128])
    rowb_d = din("rowbias", [128, 74])
    cb16_d = din("cb16", [128, 9, 128], BF16)
    csn_d = din("csn", [8, 128, 2, 1024], BF16)

    y_d = dout("y", [T, D])
    nk_d = dout("newk", [NL, T, 512])
    nv_d = dout("newv", [NL, T, 512])
    ns_d = dout("news", [NL, 2, 4, 128, 2, 64])
    dbg_d = dout("dbgmixed", [T, D], BF16) if dbg else None

    with ExitStack() as st:
        def sb(name, shape, dt=F32):
            return st.enter_context(nc.sbuf_tensor(name, list(shape), dt))

        tk = TK(nc, st)
        x_sb = sb("x_sb", [128, NT, D])
        hT = sb("hT", [128, 8, T], BF16)
        mixed = sb("mixed", [128, NT, D], BF16)
        wst = [sb("wst0", [128, 8, 512], BF16)]
        wbf = [sb("wbf%d" % i, [128, 8, 512], BF16) for i in range(2)]
        gg = sb("gg", [128, D])
        modN = sb("modN", [128, 3 * D], BF16)
        screp = sb("screp", [128, 8, 128], BF16)
        brow = sb("brow", [1, 512])
        ones_row = sb("ones_row", [1, 128])
        csil = sb("csil", [128, 8])
        cf32 = sb("cf32s", [128, 7, 128])
        rowb = sb("rowbs", [128, 74])
        cb16 = sb("cb16s", [128, 9, 128], BF16)
        flags = sb("flagss", [128, 2])
        lbl = sb("lbl", [128, 2, 256])
        oml = sb("oml", [128, 2, 256])
        ghgB = sb("ghgB", [128, 256])
        ssq = sb("ssq", [128, 16])
        rstd = sb("rstd", [128, 16])
        ARW = 21120
        arena = sb("arena", [128, ARW])
        apos = [0]

        def areset():
            apos[0] = 0

        def aget(shape, dt=F32):
            n = 1
            for d_ in shape[1:]:
                n *= d_
            words = n if dt == F32 else (n + 1) // 2
            a0 = apos[0]
            apos[0] += words
            assert apos[0] <= ARW, ("arena overflow", apos[0])
            v = arena[:, a0:a0 + words]
            if dt != F32:
                v = v.bitcast(dt)
            if len(shape) == 3:
                v = v.rearrange("p (a b) -> p a b", a=shape[1])
            elif len(shape) == 4:
                v = v.rearrange("p (a b c) -> p a b c", a=shape[1], b=shape[2])
            return v

        psbig = [st.enter_context(nc.psum_tensor("psb%d" % i, [128, 1024], F32)) for i in range(4)]
        ps = [psbig[i // 2][:, (i % 2) * 512:(i % 2 + 1) * 512] for i in range(8)]
        psk = ["ps%d" % i for i in range(8)]
        bank_rr = [0]

        def nb(avoid=()):
            while True:
                b = bank_rr[0]
                bank_rr[0] = (b + 1) % 8
                if b not in avoid:
                    return b

        J2 = cf32[:, 0, :]
        IDF = cf32[:, 1, :]
        CMT = cf32[:, 2, :]
        TR = [cf32[:, 3, :], cf32[:, 4, :]]
        SEL = [cf32[:, 5, 0:8], cf32[:, 5, 8:16]]
        HM = cf32[:, 5, 16:18]
        QM = cf32[:, 5, 18:22]
        HME = cf32[:, 6, :].rearrange("p (a c) -> p a c", a=2)
        IDB = cb16[:, 0, :]
        TRI = [cb16[:, 1, :], cb16[:, 2, :]]
        C4S4 = cb16
        J2B = cb16[:, 7, :]
        CMTB = cb16[:, 8, :]

        tk.dma('sp', cf32[:], cf32_d.ap(), writes=['cf32'])
        tk.dma('sp', rowb[:], rowb_d.ap(), writes=['rowb'])
        tk.dma('sp', cb16[:], cb16_d.ap(), writes=['cb16'])
        tk.dma('sp', flags[:], flags_d.ap(), writes=['flags'])
        tk.dma('sp', csil[:], cvec_d.ap(), writes=['csil'])
        for t in range(NT):
            tk.dma('sp', x_sb[:, t, :], x_d.ap()[t * 128:(t + 1) * 128, :], writes=['x%d' % t])
        tk.op('pool', lambda e: e.memset(ones_row[:], 1.0), writes=['ones_row'])
        tk.op('act', lambda e: e.activation(out=csil[:], in_=csil[:], func=AF.Silu), reads=['csil'], writes=['csil'])

        wring = [0]

        wbring = [0]

        def load_w(src_ap, ncols):
            wi = wbring[0]
            wbring[0] ^= 1
            tk.dma('pool', wbf[wi][:, :, 0:ncols], src_ap.rearrange("(kc p) n -> p kc n", p=128), writes=['wbf%d' % wi])
            return wi

        wseq = []
        for l_ in range(nl):
            for (c0_, n_) in ((0, 512), (512, 512), (1024, 512), (1536, 512), (2048, 512), (2816, 512), (2560, 256), (3328, 512)):
                wseq.append((win_d, l_, c0_, n_))
            wseq.append((wout_d, l_, 0, 512))
            wseq.append((wout_d, l_, 512, 512))
        wstate = {'ptr': 0, 'loaded': {}}

        def _issue(i):
            if i < len(wseq) and i not in wstate['loaded']:
                d_, l_, c0_, n_ = wseq[i]
                wstate['loaded'][i] = load_w(d_.ap()[l_, :, c0_:c0_ + n_], n_)

        def next_w(prefetch=True):
            i = wstate['ptr']
            wstate['ptr'] += 1
            _issue(i)
            if prefetch:
                _issue(i + 1)
            return wstate['loaded'][i]

        def proj_tm(wi, c0, ncols, t, b):
            for kc in range(8):
                tk.op('pe', lambda e, kc=kc: e.matmul(ps[b][:, 0:ncols], lhsT=hT[:, kc, t * 128:(t + 1) * 128],
                                                     rhs=wbf[wi][:, kc, c0:c0 + ncols], start=(kc == 0), stop=(kc == 7)),
                      reads=['hT', 'wbf%d' % wi], writes=[psk[b]])

        def proj_fm(wi, ci, g, b):
            for kc in range(8):
                tk.op('pe', lambda e, kc=kc: e.matmul(ps[b][:, 0:512], lhsT=wbf[wi][:, kc, ci * 128:(ci + 1) * 128],
                                                     rhs=hT[:, kc, g * 512:(g + 1) * 512], start=(kc == 0), stop=(kc == 7)),
                      reads=['hT', 'wbf%d' % wi], writes=[psk[b]])

        def psb16(b):
            return ps[b].bitcast(BF16)

        def emit_mod_chunk(lm, ch, banks=None):
            mod_dma(lm, ch)
            mod_mm(lm, ch, banks)

        def mod_dma(lm, ch):
            tk.dma('pool', wst[0][:], wada_d.ap()[lm, :, ch * 512:(ch + 1) * 512].rearrange("(kc p) n -> p kc n", p=128), writes=['wst0'])
            tk.dma('sp', brow[:], bada_d.ap()[lm:lm + 1, ch * 512:(ch + 1) * 512], writes=['brow'])

        def mod_mm(lm, ch, banks=None):
            b = nb() if banks is None else banks[ch % len(banks)]
            for kc in range(8):
                tk.op('pe', lambda e, kc=kc: e.matmul(ps[b][:, :], lhsT=screp[:, kc, :], rhs=wst[0][:, kc, :], start=(kc == 0), stop=False),
                      reads=['screp', 'wst0'], writes=[psk[b]])
            tk.op('pe', lambda e: e.matmul(ps[b][:, :], lhsT=ones_row[0:1, :], rhs=brow[0:1, :], start=False, stop=True),
                  reads=['ones_row', 'brow'], writes=[psk[b]])
            tk.op('act', lambda e: e.copy(out=modN[:, ch * 512:(ch + 1) * 512], in_=ps[b][:, :]), reads=[psk[b]], writes=['modN'])

        tk.op('dve', lambda e: e.tensor_copy(out=screp[:], in_=csil[:].unsqueeze(2).broadcast_to([128, 8, 128])), reads=['csil'], writes=['screp'])
        for ch in range(6):
            emit_mod_chunk(0, ch)

        for l in range(nl):
            if l == 0:
                tk.barrier()
            areset()
            apos[0] = 11520
            gbc = aget([128, 2, D])
            lbt = aget([128, 2, NL, 256])
            junk = aget([128, D], BF16)
            tmpf = aget([128, D])
            hb = [aget([128, D], BF16) for _ in range(2)]
            modA = aget([128, D])
            tk.dma('sp', gbc[:, 0, :], bass.AP(gpre_d, l * D, [[0, 128], [1, D]]), writes=['gbc'])
            tk.dma('sp', gbc[:, 1, :], bass.AP(gpost_d, l * D, [[0, 128], [1, D]]), writes=['gbc'])
            tk.dma('sp', lbt[:].rearrange("p a l c -> p (a l c)"), bass.AP(lbl_d, 0, [[0, 128], [1, 2 * NL * 256]]), writes=['lbt'])
            if l == 0:
                tk.op('dve', lambda e: e.memset(lbl[:], 0.0), writes=['lbl'])
            else:
                mx = tmpf[:, 0:512].rearrange("p (a c) -> p a c", a=2)
                sm = tmpf[:, 512:1024].rearrange("p (a c) -> p a c", a=2)
                tk.op('dve', lambda e: e.tensor_tensor(out=mx, in0=lbt[:, :, 0, :], in1=lbt[:, :, 1, :], op=ALU.max),
                      reads=['lbt'], writes=['tmpf'])
                for l2 in range(2, NL):
                    tk.op('dve', lambda e, l2=l2: e.tensor_tensor(out=mx, in0=mx, in1=lbt[:, :, l2, :], op=ALU.max),
                          reads=['lbt', 'tmpf'], writes=['tmpf'])
                for l2 in range(NL):
                    tk.op('dve', lambda e, l2=l2: e.tensor_tensor(out=lbt[:, :, l2, :], in0=lbt[:, :, l2, :], in1=mx, op=ALU.subtract),
                          reads=['lbt', 'tmpf'], writes=['lbt'])
                tk.op('act', lambda e: e.activation(out=lbt[:].rearrange("p a l c -> p (a l c)"), in_=lbt[:].rearrange("p a l c -> p (a l c)"), func=AF.Exp),
                      reads=['lbt'], writes=['lbt'])
                tk.op('dve', lambda e: e.tensor_tensor(out=sm, in0=lbt[:, :, 0, :], in1=lbt[:, :, 1, :], op=ALU.add),
                      reads=['lbt'], writes=['tmpf'])
                for l2 in range(2, NL):
                    tk.op('dve', lambda e, l2=l2: e.tensor_tensor(out=sm, in0=sm, in1=lbt[:, :, l2, :], op=ALU.add),
                          reads=['lbt', 'tmpf'], writes=['tmpf'])
                tk.op('dve', lambda e: e.reciprocal(out=sm, in_=sm), reads=['tmpf'], writes=['tmpf'])
                tk.op('dve', lambda e: e.tensor_copy(out=lbl[:], in_=lbt[:, :, 1, :]), reads=['lbt'], writes=['lbl'])
                for l2 in range(2, l + 1):
                    tk.op('dve', lambda e, l2=l2: e.tensor_tensor(out=lbl[:], in0=lbl[:], in1=lbt[:, :, l2, :], op=ALU.add),
                          reads=['lbt', 'lbl'], writes=['lbl'])
                tk.op('dve', lambda e: e.tensor_tensor(out=lbl[:], in0=lbl[:], in1=sm, op=ALU.mult), reads=['lbl', 'tmpf'], writes=['lbl'])
            tk.op('dve', lambda e: e.tensor_scalar(out=oml[:], in0=lbl[:], scalar1=-0.5, scalar2=0.5, op0=ALU.mult, op1=ALU.add),
                  reads=['lbl'], writes=['oml'])
            tk.op('dve', lambda e: e.tensor_scalar(out=lbl[:], in0=lbl[:], scalar1=0.5, scalar2=0.5, op0=ALU.mult, op1=ALU.add),
                  reads=['lbl'], writes=['lbl'])
            if stop('M0'):
                break
            tk.op('dve', lambda e: e.scalar_tensor_tensor(out=modA[:], in0=modN[:, D:2 * D], scalar=1.0, in1=gbc[:, 0, :], op0=ALU.add, op1=ALU.mult),
                  reads=['modN', 'gbc'], writes=['modA'])
            tk.op('dve', lambda e: e.tensor_tensor(out=gg[:], in0=modN[:, 2 * D:3 * D], in1=gbc[:, 1, :], op=ALU.mult), reads=['modN', 'gbc'], writes=['gg'])
            if stop('M1'):
                break
            for t in range(NT):
                tk.op('act', lambda e, t=t: e.activation(out=junk[:], in_=x_sb[:, t, :], func=AF.Square, accum_out=ssq[:, t:t + 1]),
                      reads=['x%d' % t], writes=['junk', 'ssq'])
            tk.op('dve', lambda e: e.tensor_scalar(out=rstd[:, 0:8], in0=ssq[:, 0:8], scalar1=1.0 / D, scalar2=EPS, op0=ALU.mult, op1=ALU.add),
                  reads=['ssq'], writes=['rstd'])
            tk.op('act', lambda e: e.activation(out=rstd[:, 0:8], in_=rstd[:, 0:8], func=AF.Ln), reads=['rstd'], writes=['rstd'])
            tk.op('act', lambda e: e.activation(out=rstd[:, 0:8], in_=rstd[:, 0:8], func=AF.Exp, scale=-0.5), reads=['rstd'], writes=['rstd'])
            if stop('M2'):
                break
            for t in range(NT):
                hbi = t % 2
                tk.op('dve', lambda e, t=t: e.scalar_tensor_tensor(out=tmpf[:], in0=x_sb[:, t, :], scalar=rstd[:, t:t + 1], in1=modA[:],
                                                                   op0=ALU.mult, op1=ALU.mult),
                      reads=['x%d' % t, 'rstd', 'modA'], writes=['tmpf'])
                tk.op('dve', lambda e: e.tensor_tensor(out=hb[hbi][:], in0=tmpf[:], in1=modN[:, 0:D], op=ALU.add),
                      reads=['tmpf', 'modN'], writes=['hb%d' % hbi])
                if stop('M3'):
                    continue
                b = nb()
                for kc in range(8):
                    tk.op('pe', lambda e, kc=kc: e.transpose(out=psb16(b)[:, kc * 128:(kc + 1) * 128], in_=hb[hbi][:, kc * 128:(kc + 1) * 128], identity=IDB),
                          reads=['hb%d' % hbi, 'cb16'], writes=[psk[b]])
                tk.op('act', lambda e, t=t: e.copy(out=hT[:, :, t * 128:(t + 1) * 128], in_=psb16(b)[:, :].rearrange("p (k c) -> p k c", k=8)),
                      reads=[psk[b]], writes=['hT'])

            if stop('M'):
                break
            tk.barrier()
            areset()
            qT = aget([128, 4, T], BF16)
            kT = aget([128, 4, T], BF16)
            ckT = aget([128, 4, 512], BF16)
            vaug = aget([128, NT, 8, 66], BF16)
            cvaug = aget([128, 4, 8, 66], BF16)
            sga = aget([128, NT, 512], BF16)
            expT = aget([128, 7, 8, 128], BF16)
            Eb = [aget([128, 8, 128], BF16) for _ in range(3)]
            Pb = [aget([128, 8, 128], BF16) for _ in range(2)]
            hk = [Eb[0].bitcast(F32) if False else None, None]
            ost = [aget([128, 512]) for _ in range(2)]
            rden = aget([128, 8])
            otmp = aget([128, 8, 64])
            ckb = aget([128, 4, 512], BF16)
            hkA = aget([128, 8, 128])
            hkB = aget([128, 8, 128])
            hk = [hkA, hkB]
            tk.op('pool', lambda e: e.memset(vaug[:, :, :, 64:66], 1.0), writes=['vaug'])
            tk.op('dve', lambda e: e.tensor_copy(out=cvaug[:, :, :, 64:66].rearrange("p a b c -> p (a b) c"),
                                                 in_=flags[:, 0:1].unsqueeze(2).broadcast_to([128, 32, 2])),
                  reads=['flags'], writes=['cvaug'])
            def toep_dma(di):
                dl = di - 3
                hi = di % 2
                for qr in range(2):
                    for krl in range(2):
                        off = ((l * 8) * 23 + (2 * dl + krl - qr + 11)) * 127
                        src = bass.AP(tpad_d, off, [[1, 64], [23 * 127, 8], [1, 64]])
                        tk.dma('sp', hk[hi][qr * 64:(qr + 1) * 64, :, krl * 64:(krl + 1) * 64], src, writes=['hk%d_%d' % (hi, qr * 2 + krl)])

            def toep_mm(di):
                hi = di % 2
                bA = nb()
                bB = nb()
                for h in range(8):
                    b = bA if h % 2 == 0 else bB
                    o = ps[b][:, (h // 2) * 128:(h // 2 + 1) * 128]
                    tk.op('pe', lambda e, h=h, o=o: e.matmul(o, lhsT=hk[hi][:, h, :], rhs=J2, start=True, stop=False),
                          reads=['hk%d_%d' % (hi, x) for x in range(4)] + ['cf32'], writes=[psk[b]])
                    tk.op('pe', lambda e, o=o: e.matmul(o, lhsT=IDF, rhs=CMT, start=False, stop=True),
                          reads=['cf32'], writes=[psk[b]])
                for bi, b in enumerate((bA, bB)):
                    tk.op('act', lambda e, bi=bi, b=b: e.activation(out=expT[:, di, bi * 4:(bi + 1) * 4, :].rearrange("p a c -> p (a c)"),
                                                                    in_=ps[b][:, :], func=AF.Exp),
                          reads=[psk[b]], writes=['expT'])

            toep_dma(0)
            toep_dma(1)
            if stop('A0'):
                break
            ckk = 'ckb'
            tk.dma('pool', ckb[:], ctxk_d.ap()[l].rearrange("(c p) n -> p c n", p=128), writes=[ckk])
            for c in range(4):
                b = nb()
                for pr in range(4):
                    tk.op('pe', lambda e, pr=pr: e.transpose(out=psb16(b)[:, pr * 128:(pr + 1) * 128], in_=ckb[:, c, pr * 128:(pr + 1) * 128], identity=IDB),
                          reads=[ckk, 'cb16'], writes=[psk[b]])
                tk.op('act', lambda e, c=c: e.copy(out=ckT[:, :, c * 128:(c + 1) * 128], in_=psb16(b)[:, 0:512].rearrange("p (k c) -> p k c", k=4)),
                      reads=[psk[b]], writes=['ckT'])
            tk.dma('pool', ckb[:], ctxv_d.ap()[l].rearrange("(c p) n -> p c n", p=128), reads=[], writes=[ckk])
            tk.op('pool', lambda e: e.tensor_copy(out=cvaug[:, :, :, 0:64], in_=ckb[:].rearrange("p c (h d) -> p c h d", h=8)), reads=[ckk], writes=['cvaug'])
            if stop('A1'):
                break
            toep_mm(0)
            toep_dma(2)
            wi = next_w()
            for pr in range(4):
                for g in range(2):
                    b = nb()
                    proj_fm(wi, pr, g, b)
                    tk.op('act', lambda e, pr=pr, g=g: e.copy(out=qT[:, pr, g * 512:(g + 1) * 512], in_=ps[b][:, :]), reads=[psk[b]], writes=['qT'])
            if stop('A1a'):
                break
            toep_mm(1)
            toep_dma(3)
            wi = next_w()
            for pr in range(4):
                for g in range(2):
                    b = nb()
                    proj_fm(wi, pr, g, b)
                    tk.op('act', lambda e, pr=pr, g=g: e.copy(out=kT[:, pr, g * 512:(g + 1) * 512], in_=ps[b][:, :]), reads=[psk[b]], writes=['kT'])
            toep_mm(2)
            toep_dma(4)
            for t in range(NT):
                b = nb()
                proj_tm(wi, 0, 512, t, b)
                oi = t % 2
                tk.op('dve', lambda e: e.tensor_copy(out=ost[oi][:], in_=ps[b][:, :]), reads=[psk[b]], writes=['ost%d' % oi])
                tk.dma('sp', nk_d.ap()[l, t * 128:(t + 1) * 128, :], ost[oi][:], reads=['ost%d' % oi])
            toep_mm(3)
            toep_dma(5)
            if stop('A1b'):
                break
            wi = next_w()
            for t in range(NT):
                b = nb()
                proj_tm(wi, 0, 512, t, b)
                oi = t % 2
                tk.op('dve', lambda e: e.tensor_copy(out=ost[oi][:], in_=ps[b][:, :]), reads=[psk[b]], writes=['ost%d' % oi])
                tk.op('act', lambda e, t=t: e.copy(out=vaug[:, t, :, 0:64], in_=ost[oi][:].rearrange("p (h d) -> p h d", h=8)),
                      reads=['ost%d' % oi], writes=['vaug'])
                tk.dma('sp', nv_d.ap()[l, t * 128:(t + 1) * 128, :], ost[oi][:], reads=['ost%d' % oi])
            if stop('A1c'):
                break
            toep_mm(4)
            toep_dma(6)
            wi = next_w()
            for t in range(NT):
                b = nb()
                proj_tm(wi, 0, 512, t, b)
                tk.op('act', lambda e, t=t: e.activation(out=sga[:, t, :], in_=ps[b][:, :], func=AF.Silu), reads=[psk[b]], writes=['sga'])
            toep_mm(5)
            toep_mm(6)
            if stop('A2'):
                break
            OA, OB = 6, 7
            spairs = [(0, 1), (2, 3), (4, 5)]
            allsteps = []
            for j in range(8):
                st_ = [('l', kt) for kt in KT[j]] + [('c', c) for c in range(4)]
                for si_, (kind, idx) in enumerate(st_):
                    allsteps.append((j, si_, len(st_), kind, idx))

            def emit_S(k):
                j, si_, ns_, kind, idx = allsteps[k]
                sA, sB = spairs[k % 3]
                for h in range(8):
                    b = sA if h % 2 == 0 else sB
                    r0 = (h % 2) * 64
                    ksrc = kT[r0:r0 + 64, h // 2, idx * 128:(idx + 1) * 128] if kind == 'l' else ckT[r0:r0 + 64, h // 2, idx * 128:(idx + 1) * 128]
                    tk.op('pe', lambda e, h=h, b=b, ksrc=ksrc, r0=r0: e.matmul(ps[b][:, (h // 2) * 128:(h // 2 + 1) * 128], lhsT=ksrc,
                                                                              rhs=qT[r0:r0 + 64, h // 2, j * 128:(j + 1) * 128], start=True, stop=True),
                          reads=['qT', 'kT' if kind == 'l' else 'ckT'], writes=[psk[b]])

            def emit_rest(k):
                j, si_, ns_, kind, idx = allsteps[k]
                sA, sB = spairs[k % 3]
                sl = k % 3
                big = psbig[sA // 2]
                if kind == 'l':
                    jk = JKI[(j, idx)]
                    for hf in range(2):
                        tk.op('act', lambda e, hf=hf: e.activation(
                            out=Eb[sl][:, :, hf * 64:(hf + 1) * 64],
                            in_=big[:, :].rearrange("p (a c) -> p a c", a=8)[:, :, hf * 64:(hf + 1) * 64],
                            func=AF.Exp, scale=0.125, bias=rowb[:, jk * 2 + hf:jk * 2 + hf + 1]),
                            reads=[psk[sA], psk[sB], 'rowb'], writes=['Eb%d' % sl])
                    di = idx - j + 3
                    pl = k % 2
                    tk.op('dve', lambda e, di=di: e.tensor_tensor(out=Pb[pl][:], in0=Eb[sl][:], in1=expT[:, di, :, :], op=ALU.mult),
                          reads=['Eb%d' % sl, 'expT'], writes=['Pb%d' % pl])
                    lhs, lk = Pb[pl], 'Pb%d' % pl
                    vsrc, vk = vaug, 'vaug'
                else:
                    tk.op('act', lambda e: e.activation(out=Eb[sl][:].rearrange("p a c -> p (a c)"), in_=big[:, :], func=AF.Exp, scale=0.125),
                          reads=[psk[sA], psk[sB]], writes=['Eb%d' % sl])
                    lhs, lk = Eb[sl], 'Eb%d' % sl
                    vsrc, vk = cvaug, 'cvaug'
                for e_ in range(8):
                    h = 2 * (e_ % 4) + e_ // 4
                    ob = OA if e_ < 4 else OB
                    tk.op('pe', lambda e, e_=e_, h=h, ob=ob, lhs=lhs, vsrc=vsrc: e.matmul(
                        ps[ob][:, (e_ % 4) * 66:(e_ % 4) * 66 + 66], lhsT=lhs[:, e_, :], rhs=vsrc[:, idx, h, :],
                        start=(si_ == 0 and e_ % 4 == 0), stop=(si_ == ns_ - 1), skip_group_check=True),
                        reads=[lk, vk], writes=[psk[ob]])
                if si_ == ns_ - 1:
                    for bi, ob in enumerate((OA, OB)):
                        tk.op('dve', lambda e, bi=bi, ob=ob: e.reciprocal(out=rden[:, bi * 4:(bi + 1) * 4],
                                                                          in_=ps[ob][:, 0:264].rearrange("p (a c) -> p a c", a=4)[:, :, 64]),
                              reads=[psk[ob]], writes=['rden'])
                    for bi, ob in enumerate((OA, OB)):
                        tk.op('dve', lambda e, bi=bi, ob=ob: e.tensor_tensor(
                            out=otmp[:, bi:8:2, :], in0=ps[ob][:, 0:264].rearrange("p (a c) -> p a c", a=4)[:, :, 0:64],
                            in1=rden[:, bi * 4:(bi + 1) * 4].unsqueeze(2).broadcast_to([128, 4, 64]), op=ALU.mult),
                            reads=[psk[ob], 'rden'], writes=['otmp'])
                    tk.op('dve', lambda e: e.tensor_tensor(out=mixed[:, j, 0:512], in0=otmp[:].rearrange("p h d -> p (h d)"), in1=sga[:, j, :], op=ALU.mult),
                          reads=['otmp', 'sga'], writes=['mixed%d' % j])

            emit_S(0)
            emit_S(1)
            for k in range(len(allsteps)):
                if k + 2 < len(allsteps):
                    emit_S(k + 2)
                emit_rest(k)

            if stop('A'):
                break
            tk.barrier()
            areset()
            qh = aget([128, NT, 256], BF16)
            sgb = aget([128, NT, 256])
            qE = sgb.rearrange("p a b -> p (a b)")[:, 0:1024].bitcast(BF16).rearrange("p (a b) -> p a b", a=NT)
            kE = sgb.rearrange("p a b -> p (a b)")[:, 1024:2048].bitcast(BF16).rearrange("p (a b) -> p a b", a=NT)
            vh = aget([128, NT, 256], BF16)
            vhmF = aget([128, 4096])
            vhm = vhmF.bitcast(BF16).rearrange("p (q t c) -> p q t c", q=4, t=NT)
            A_ = vhmF[:, 0:2048].rearrange("p (e c) -> p e c", e=64)
            B_ = vhmF[:, 2048:4096].rearrange("p (e c) -> p e c", e=64)
            sgr = aget([128, NT, 256], BF16)
            fS = aget([128, 4096])
            fbuf = fS[:, 0:2048].rearrange("p (a b) -> p a b", a=NT)
            SinPm = fS.bitcast(BF16).rearrange("p (h c e) -> p h c e", h=4, c=32)
            lfbuf = aget([128, NT, 256])
            kETm = lfbuf.rearrange("p a b -> p (a b)").bitcast(BF16).rearrange("p (h t) -> p h t", h=4)
            osq = lfbuf
            tmpE = [aget([128, 512]) for _ in range(2)]
            qET = aget([128, 2, T], BF16)
            ATm = [aget([128, 4, 128], BF16) for _ in range(2)]
            osum = aget([128, NT, 256])
            gs3 = aget([128, 2, 32, 3])
            es3 = aget([128, 2, 32, 3])
            S0b = aget([128, 2, 64])
            nsb = aget([128, 4, 2, 64])
            hss = aget([128, 32])
            tk.dma('sp', ghgB[:], bass.AP(ghg_d, l * 256, [[0, 128], [1, 256]]), writes=['ghgB'])
            wi = next_w()
            for t in range(NT):
                b = nb()
                proj_tm(wi, 0, 512, t, b)
                tk.op('act', lambda e, t=t: e.activation(out=qh[:, t, :], in_=ps[b][:, 0:256], func=AF.Silu), reads=[psk[b]], writes=['qh'])
                tk.op('act', lambda e, t=t: e.activation(out=sgb[:, t, :], in_=ps[b][:, 256:512], func=AF.Tanh, scale=0.5), reads=[psk[b]], writes=['sQ'])
            wi = next_w()
            for t in range(NT):
                b = nb()
                proj_tm(wi, 0, 512, t, b)
                tk.op('act', lambda e, t=t: e.activation(out=sgr[:, t, :], in_=ps[b][:, 256:512], func=AF.Silu), reads=[psk[b]], writes=['sgr'])
                tk.op('dve', lambda e, t=t: e.tensor_copy(out=vh[:, t, :], in_=ps[b][:, 0:256]), reads=[psk[b]], writes=['vh'])

            if stop('R0'):
                break
            rstop = False
            for dr in range(2):
                if l + 1 < nl:
                    mod_dma(l + 1, 3 * dr)
                lb_bc = lbl[:, dr, :].unsqueeze(1).broadcast_to([128, NT, 256])
                oml_bc = oml[:, dr, :].unsqueeze(1).broadcast_to([128, NT, 256])
                tk.op('dve', lambda e: e.tensor_tensor(out=fbuf[:], in0=sgb[:], in1=oml_bc, op=ALU.mult), reads=['sQ', 'oml'], writes=['fS'])
                tk.op('dve', lambda e: e.tensor_tensor(out=fbuf[:], in0=fbuf[:], in1=lb_bc, op=ALU.add), reads=['fS', 'lbl'], writes=['fS'])
                tk.op('act', lambda e: e.activation(out=lfbuf[:].rearrange("p a b -> p (a b)"), in_=fbuf[:].rearrange("p a b -> p (a b)"), func=AF.Ln),
                      reads=['fS'], writes=['lK'])
                tk.op('dve', lambda e: e.tensor_scalar(out=fbuf[:], in0=fbuf[:], scalar1=-1.0, scalar2=1.0, op0=ALU.mult, op1=ALU.add),
                      reads=['fS'], writes=['fS'])
                bS = nb()
                for t in range(NT):
                    for pr in range(2):
                        tt_ = t if dr == 0 else 7 - t
                        tk.op('pe', lambda e, t=t, pr=pr, tt_=tt_: e.matmul(ps[bS][:, (pr * 8 + tt_) * 8:(pr * 8 + tt_) * 8 + 8], lhsT=lfbuf[:, t, pr * 128:(pr + 1) * 128],
                                                                   rhs=SEL[dr], start=True, stop=True),
                              reads=['lK', 'cf32'], writes=[psk[bS]])
                psS = ps[bS][:, 0:128].rearrange("p (a c r) -> p a c r", a=2, c=32)
                tk.op('dve', lambda e: e.tensor_copy(out=gs3[:, :, :, 0:2], in_=psS), reads=[psk[bS]], writes=['gs3'])
                tk.op('dve', lambda e: e.tensor_tensor(out=gs3[:, :, :, 2], in0=gs3[:, :, :, 1], in1=gs3[:, :, :, 0], op=ALU.subtract),
                      reads=['gs3'], writes=['gs3'])
                tk.op('act', lambda e: e.activation(out=es3[:].rearrange("p a c r -> p (a c r)"), in_=gs3[:].rearrange("p a c r -> p (a c r)"), func=AF.Exp),
                      reads=['gs3'], writes=['es3'])
                tk.op('dve', lambda e: e.tensor_scalar(out=es3[:, :, 8:32:8, 0:2], in0=es3[:, :, 8:32:8, 0:2], scalar1=flags[:, 1:2], scalar2=None, op0=ALU.mult),
                      reads=['es3', 'flags'], writes=['es3'])
                for tp in range(4):
                    b = nb()
                    for i in range(2):
                        t = 2 * tp + i
                        tk.op('pe', lambda e, t=t, i=i: e.matmul(ps[b][:, i * 256:(i + 1) * 256], lhsT=TR[dr], rhs=lfbuf[:, t, :], start=True, stop=True),
                              reads=['cf32', 'lK'], writes=[psk[b]])
                    tk.op('act', lambda e: e.activation(out=tmpE[0][:], in_=ps[b][:, :], func=AF.Exp), reads=[psk[b]], writes=['tmpE0'])
                    tk.op('act', lambda e: e.activation(out=tmpE[1][:], in_=ps[b][:, :], func=AF.Exp, scale=-1.0), reads=[psk[b]], writes=['tmpE1'])
                    tk.op('dve', lambda e, tp=tp: e.tensor_tensor(out=qE[:, 2 * tp:2 * tp + 2, :].rearrange("p a c -> p (a c)"),
                                                                  in0=qh[:, 2 * tp:2 * tp + 2, :].rearrange("p a c -> p (a c)"), in1=tmpE[0][:], op=ALU.mult),
                          reads=['qh', 'tmpE0'], writes=['sQ'])
                    tk.op('dve', lambda e, tp=tp: e.tensor_tensor(out=kE[:, 2 * tp:2 * tp + 2, :].rearrange("p a c -> p (a c)"),
                                                                  in0=fbuf[:, 2 * tp:2 * tp + 2, :].rearrange("p a c -> p (a c)"), in1=tmpE[1][:], op=ALU.mult),
                          reads=['fS', 'tmpE1'], writes=['sQ'])
                for cp in range(4):
                    tk.op('dve', lambda e, cp=cp: e.tensor_scalar(out=vhm[:, cp, :, :], in0=vh[:], scalar1=QM[:, cp:cp + 1], scalar2=None, op0=ALU.mult),
                          reads=['vh', 'cf32'], writes=['vhm'])
                for t in range(NT):
                    b = nb()
                    for pr in range(2):
                        tk.op('pe', lambda e, t=t, pr=pr: e.transpose(out=psb16(b)[:, pr * 128:(pr + 1) * 128], in_=qE[:, t, pr * 128:(pr + 1) * 128], identity=IDB),
                              reads=['sQ', 'cb16'], writes=[psk[b]])
                        tk.op('pe', lambda e, t=t, pr=pr: e.transpose(out=psb16(b)[:, (2 + pr) * 128:(3 + pr) * 128], in_=kE[:, t, pr * 128:(pr + 1) * 128], identity=IDB),
                              reads=['sQ', 'cb16'], writes=[psk[b]])
                    tk.op('act', lambda e, t=t: e.copy(out=qET[:, :, t * 128:(t + 1) * 128], in_=psb16(b)[:, 0:256].rearrange("p (a c) -> p a c", a=2)),
                          reads=[psk[b]], writes=['qET'])
                    for h in range(4):
                        tk.op('act', lambda e, t=t, h=h: e.activation(out=kETm[:, h, t * 128:(t + 1) * 128], in_=psb16(b)[:, (2 + h // 2) * 128:(3 + h // 2) * 128],
                                                                      func=AF.Copy, scale=HM[:, h % 2:h % 2 + 1]),
                              reads=[psk[b], 'cf32'], writes=['lK'])
                if stop('R1'):
                    rstop = True
                    break
                tk.dma('sp', S0b[:], s0_d.ap()[l, dr], writes=['S0b'])
                kvb_all = [[nb() for _ in range(4)] for _ in range(2)]
                for pr in range(2):
                    kvb = kvb_all[pr]
                    for c in range(32):
                        cq = c if dr == 0 else 31 - c
                        b = kvb[cq // 8]
                        for h in (2 * pr, 2 * pr + 1):
                            o = ps[b][(h % 2) * 64:(h % 2) * 64 + 64, (cq % 8) * 64:(cq % 8) * 64 + 64]
                            tk.op('pe', lambda e, c=c, h=h, o=o: e.matmul(o, lhsT=kE[:, c // 4, h * 64:(h + 1) * 64], rhs=vhm[:, c % 4, c // 4, h * 64:(h + 1) * 64],
                                                                          start=True, stop=True),
                                  reads=['sQ', 'vhm'], writes=[psk[b]])
                for pr in range(2):
                    kvb = kvb_all[pr]
                    for g in range(4):
                        tk.op('dve', lambda e, g=g: e.tensor_tensor(out=B_[:, :, g * 8:(g + 1) * 8].rearrange("p e c -> p c e"),
                                                                    in0=ps[kvb[g]][:, :].rearrange("p (c e) -> p c e", c=8),
                                                                    in1=es3[:, pr, g * 8:(g + 1) * 8, 2].unsqueeze(2).broadcast_to([128, 8, 64]), op=ALU.mult),
                              reads=[psk[kvb[g]], 'es3'], writes=['vhm'])
                    if dr == 0 and pr == 0:
                        wi = next_w()
                        for t in range(NT):
                            b = kvb[t % 4]
                            proj_tm(wi, 0, 256, t, b)
                            tk.op('act', lambda e, t=t: e.activation(out=sgb[:, t, :], in_=ps[b][:, 0:256], func=AF.Tanh, scale=0.5), reads=[psk[b]], writes=['sQ'])
                    if dr == 1 and pr == 0:
                        wi = next_w()
                        uT_h = qh[:].rearrange("p a b -> p (a b)").rearrange("p (c t) -> p c t", c=2)
                        sgf_h = qE
                        hb_ = 0
                        for ci in range(2):
                            for g in range(2):
                                b = kvb[hb_ % 4]
                                hb_ += 1
                                proj_fm(wi, ci, g, b)
                                tk.op('act', lambda e, ci=ci, g=g: e.copy(out=uT_h[:, ci, g * 512:(g + 1) * 512], in_=ps[b][:, :]), reads=[psk[b]], writes=['qh'])
                        for t in range(NT):
                            b = kvb[hb_ % 4]
                            hb_ += 1
                            proj_tm(wi, 256, 256, t, b)
                            tk.op('act', lambda e, t=t: e.activation(out=sgf_h[:, t, :], in_=ps[b][:, 0:256], func=AF.Silu), reads=[psk[b]], writes=['sQ'])
                    tk.op('dve', lambda e: e.scalar_tensor_tensor(out=B_[:, :, 0], in0=S0b[:, pr, :], scalar=es3[:, pr, 0, 1:2], in1=B_[:, :, 0], op0=ALU.mult, op1=ALU.add),
                          reads=['S0b', 'es3', 'vhm'], writes=['vhm'])
                    tk.op('dve', lambda e: e.tensor_copy(out=A_[:], in_=es3[:, pr, :, 1].unsqueeze(1).broadcast_to([128, 64, 32])), reads=['es3', 'vhm'], writes=['vhm'])
                    tk.op('dve', lambda e: e.memset(A_[:, :, 0:1], 0.0), reads=['vhm'], writes=['vhm'])
                    tk.op('dve', lambda e: e.tensor_tensor_scan(out=B_[:].rearrange("p e c -> p (e c)"), data0=A_[:].rearrange("p e c -> p (e c)"),
                                                                data1=B_[:].rearrange("p e c -> p (e c)"), initial=0.0, op0=ALU.mult, op1=ALU.add),
                          reads=['vhm'], writes=['vhm'])
                    tk.op('dve', lambda e: e.tensor_copy(out=nsb[:, :, pr, :], in_=B_[:, :, 7:32:8].rearrange("p e k -> p k e")), reads=['vhm'], writes=['nsb'])
                    for h2 in range(2):
                        tk.op('dve', lambda e, h2=h2: e.scalar_tensor_tensor(out=SinPm[:, 2 * pr + h2, 1:32, :], in0=B_[:, :, 0:31].rearrange("p e c -> p c e"),
                                                                             scalar=HM[:, h2:h2 + 1], in1=es3[:, pr, 1:32, 0].unsqueeze(2).broadcast_to([128, 31, 64]),
                                                                             op0=ALU.mult, op1=ALU.mult),
                              reads=['vhm', 'es3', 'cf32'], writes=['fS'])
                        tk.op('dve', lambda e, h2=h2: e.scalar_tensor_tensor(out=SinPm[:, 2 * pr + h2, 0, :], in0=S0b[:, pr, :], scalar=HM[:, h2:h2 + 1],
                                                                             in1=es3[:, pr, 0, 0:1].broadcast_to([128, 64]), op0=ALU.mult, op1=ALU.mult),
                              reads=['S0b', 'es3', 'cf32'], writes=['fS'])
                nsb3 = nsb[:].rearrange("p s a e -> p s (a e)")
                if dr == 0:
                    tk.dma('sp', ns_d.ap()[l, 0].rearrange("s p a e -> p s (a e)"), nsb3, reads=['nsb'])
                else:
                    for k in range(4):
                        tk.dma('sp', ns_d.ap()[l, 1, 3 - k].rearrange("p a e -> p (a e)"), nsb3[:, k, :], reads=['nsb'])
                if l + 1 < nl:
                    mod_mm(l + 1, 3 * dr)
                    mod_dma(l + 1, 3 * dr + 1)
                if stop('R2'):
                    rstop = True
                    break
                for t in range(NT):
                    bA_ = nb()
                    ai = t % 2
                    for h in range(4):
                        tk.op('pe', lambda e, t=t, h=h: e.matmul(ps[bA_][:, h * 128:(h + 1) * 128], lhsT=kETm[:, h, t * 128:(t + 1) * 128],
                                                                 rhs=qET[:, h // 2, t * 128:(t + 1) * 128], start=True, stop=True),
                              reads=['lK', 'qET'], writes=[psk[bA_]])
                    tk.op('dve', lambda e: e.tensor_tensor(out=ATm[ai][:], in0=ps[bA_][:, :].rearrange("p (a c) -> p a c", a=4),
                                                           in1=TRI[dr].unsqueeze(1).broadcast_to([128, 4, 128]), op=ALU.mult),
                          reads=[psk[bA_], 'cb16'], writes=['ATm%d' % ai])
                    bO = nb()
                    for h in range(4):
                        tk.op('pe', lambda e, t=t, h=h: e.matmul(ps[bO][:, h * 64:(h + 1) * 64], lhsT=ATm[ai][:, h, :], rhs=vh[:, t, h * 64:(h + 1) * 64],
                                                                 start=True, stop=False, skip_group_check=True),
                              reads=['ATm%d' % ai, 'vh'], writes=[psk[bO]])
                        for cp in range(4):
                            tk.op('pe', lambda e, t=t, h=h, cp=cp: e.matmul(ps[bO][cp * 32:(cp + 1) * 32, h * 64:(h + 1) * 64],
                                                                            lhsT=qET[:, h // 2, t * 128 + cp * 32:t * 128 + cp * 32 + 32],
                                                                            rhs=SinPm[:, h, (4 * t + cp) if dr == 0 else 31 - (4 * t + cp), :], start=False, stop=(cp == 3), skip_group_check=True,
                                                                            tile_position=(0, cp * 32)),
                                  reads=['qET', 'fS'], writes=[psk[bO]])
                    if l + 1 < nl and t == 3:
                        mod_mm(l + 1, 3 * dr + 1)
                        mod_dma(l + 1, 3 * dr + 2)
                    if l + 1 < nl and t == 7:
                        mod_mm(l + 1, 3 * dr + 2)
                    if dr == 0:
                        tk.op('act', lambda e, t=t: e.copy(out=osum[:, t, :], in_=ps[bO][:, 0:256]), reads=[psk[bO]], writes=['osum'])
                    else:
                        tk.op('dve', lambda e, t=t: e.tensor_tensor(out=osum[:, t, :], in0=osum[:, t, :], in1=ps[bO][:, 0:256], op=ALU.add),
                              reads=[psk[bO], 'osum'], writes=['osum'])
                if stop('R3'):
                    rstop = True
                    break
            if rstop:
                break
            tk.op('dve', lambda e: e.tensor_tensor(out=osq[:], in0=osum[:], in1=osum[:], op=ALU.mult), reads=['osum'], writes=['lK'])
            tk.op('dve', lambda e: e.tensor_reduce(out=hss[:], in_=osq[:].rearrange("p t (h d) -> p (t h) d", h=4), axis=AX.X, op=ALU.add),
                  reads=['lK'], writes=['hss'])
            tk.op('dve', lambda e: e.tensor_scalar(out=hss[:], in0=hss[:], scalar1=1.0 / 64, scalar2=EPS, op0=ALU.mult, op1=ALU.add), reads=['hss'], writes=['hss'])
            tk.op('act', lambda e: e.activation(out=hss[:], in_=hss[:], func=AF.Ln), reads=['hss'], writes=['hss'])
            tk.op('act', lambda e: e.activation(out=hss[:], in_=hss[:], func=AF.Exp, scale=-0.5), reads=['hss'], writes=['hss'])
            tk.op('dve', lambda e: e.tensor_tensor(out=osum[:].rearrange("p t (h d) -> p (t h) d", h=4), in0=osum[:].rearrange("p t (h d) -> p (t h) d", h=4),
                                                   in1=hss[:].unsqueeze(2).broadcast_to([128, 32, 64]), op=ALU.mult),
                  reads=['osum', 'hss'], writes=['osum'])
            tk.op('dve', lambda e: e.tensor_tensor(out=osum[:], in0=osum[:], in1=ghgB[:].unsqueeze(1).broadcast_to([128, NT, 256]), op=ALU.mult),
                  reads=['osum', 'ghgB'], writes=['osum'])
            tk.op('dve', lambda e: e.tensor_tensor(out=mixed[:, :, 512:768], in0=osum[:], in1=sgr[:], op=ALU.mult),
                  reads=['osum', 'sgr'], writes=['mixed%d' % j for j in range(8)])

            if stop('R'):
                break
            tk.barrier()
            areset()
            uT = aget([128, 2, T], BF16)
            sgf = aget([128, NT, 256], BF16)
            ucs = aget([128, NT, 2, 256], BF16)
            yT = aget([128, 2, T], BF16)
            csnb = [aget([128, 2, 1024], BF16) for _ in range(4)]
            assert apos[0] + 2048 <= 11520
            wfs = aget([128, 2, 256])
            wfb = aget([128, 2, 256], BF16)
            junk = aget([128, 512], BF16)
            tmpf = aget([128, D])
            for kt_ in range(4):
                tk.dma('sp', csnb[kt_][:], csn_d.ap()[kt_], writes=['csnb%d' % kt_])
            assert True
            tk.dma('sp', wfs[:], wfn_d.ap()[l].rearrange("(c p) n -> p c n", p=128), writes=['wfs'])
            tk.op('pool', lambda e: e.tensor_copy(out=wfb[:], in_=wfs[:]), reads=['wfs'], writes=['wfb'])
            for t in range(NT):
                b = nb()
                for cs in range(2):
                    for ct in range(2):
                        tk.op('pe', lambda e, t=t, cs=cs, ct=ct: e.matmul(ps[b][:, cs * 256 + ct * 128:cs * 256 + ct * 128 + 128], lhsT=uT[:, ct, t * 128:(t + 1) * 128],
                                                                          rhs=C4S4[:, 3 + cs * 2 + ct, :], start=True, stop=True),
                              reads=['uT', 'cb16'], writes=[psk[b]])
                tk.op('act', lambda e, t=t: e.copy(out=ucs[:, t, :, :].rearrange("p a c -> p (a c)"), in_=ps[b][:, :]), reads=[psk[b]], writes=['ucs'])
            if stop('F0'):
                break
            yb = [nb() for _ in range(4)]
            for kt_ in range(8):
                ci = kt_ % 4
                if kt_ >= 4:
                    tk.dma('sp', csnb[ci][:], csn_d.ap()[kt_], writes=['csnb%d' % ci])
                for ct in range(2):
                    for g in range(2):
                        b = yb[ct * 2 + g]
                        for cs in range(2):
                            tk.op('pe', lambda e, kt_=kt_, ct=ct, g=g, cs=cs: e.matmul(ps[b][:, :], lhsT=ucs[:, kt_, cs, ct * 128:(ct + 1) * 128],
                                                                                       rhs=csnb[ci][:, cs, g * 512:(g + 1) * 512],
                                                                                       start=(kt_ == 0 and cs == 0), stop=(kt_ == 7 and cs == 1)),
                                  reads=['ucs', 'csnb%d' % ci], writes=[psk[b]])
            for ct in range(2):
                for g in range(2):
                    b = yb[ct * 2 + g]
                    tk.op('act', lambda e, ct=ct, g=g, b=b: e.copy(out=yT[:, ct, g * 512:(g + 1) * 512], in_=ps[b][:, :]), reads=[psk[b]], writes=['yT'])
            for t in range(NT):
                b = nb()
                for ct in range(2):
                    tk.op('pe', lambda e, t=t, ct=ct: e.matmul(ps[b][:, 0:256], lhsT=yT[:, ct, t * 128:(t + 1) * 128], rhs=wfb[:, ct, :], start=(ct == 0), stop=(ct == 1)),
                          reads=['yT', 'wfb'], writes=[psk[b]])
                tk.op('dve', lambda e, t=t: e.tensor_tensor(out=mixed[:, t, 768:1024], in0=ps[b][:, 0:256], in1=sgf[:, t, :], op=ALU.mult),
                      reads=[psk[b], 'sgf'], writes=['mixed%d' % t])

            if dbg and l == nl - 1:
                for t in range(NT):
                    tk.dma('sp', dbg_d.ap()[t * 128:(t + 1) * 128, :], mixed[:, t, :], reads=['mixed%d' % t])

            if stop('F1'):
                break
            w0 = next_w()
            w1 = next_w(prefetch=False)
            for t in range(NT):
                b = nb()
                for kc in range(8):
                    tk.op('pe', lambda e, t=t, kc=kc: e.transpose(out=psb16(b)[:, kc * 128:(kc + 1) * 128], in_=mixed[:, t, kc * 128:(kc + 1) * 128], identity=IDB),
                          reads=['mixed%d' % t, 'cb16'], writes=[psk[b]])
                tk.op('act', lambda e, t=t: e.copy(out=hT[:, :, t * 128:(t + 1) * 128], in_=psb16(b)[:, :].rearrange("p (k c) -> p k c", k=8)),
                      reads=[psk[b]], writes=['hT'])
            for t in range(NT):
                bb = [nb(), nb()]
                for hf, wi in enumerate((w0, w1)):
                    proj_tm(wi, 0, 512, t, bb[hf])
                    tk.op('act', lambda e, t=t, hf=hf: e.activation(out=junk[:], in_=ps[bb[hf]][:, :], func=AF.Square, accum_out=ssq[:, 8 + hf:9 + hf]),
                          reads=[psk[bb[hf]]], writes=['junk', 'ssq'])
                tk.op('dve', lambda e: e.tensor_tensor(out=rstd[:, 8:9], in0=ssq[:, 8:9], in1=ssq[:, 9:10], op=ALU.add), reads=['ssq'], writes=['rstd'])
                tk.op('dve', lambda e: e.tensor_scalar(out=rstd[:, 8:9], in0=rstd[:, 8:9], scalar1=1.0 / D, scalar2=EPS, op0=ALU.mult, op1=ALU.add),
                      reads=['rstd'], writes=['rstd'])
                tk.op('act', lambda e: e.activation(out=rstd[:, 8:9], in_=rstd[:, 8:9], func=AF.Ln), reads=['rstd'], writes=['rstd'])
                tk.op('act', lambda e: e.activation(out=rstd[:, 8:9], in_=rstd[:, 8:9], func=AF.Exp, scale=-0.5), reads=['rstd'], writes=['rstd'])
                for hf in range(2):
                    tk.op('dve', lambda e, hf=hf: e.scalar_tensor_tensor(out=tmpf[:, hf * 512:(hf + 1) * 512], in0=ps[bb[hf]][:, :], scalar=rstd[:, 8:9],
                                                                         in1=gg[:, hf * 512:(hf + 1) * 512], op0=ALU.mult, op1=ALU.mult),
                          reads=[psk[bb[hf]], 'rstd', 'gg'], writes=['tmpf'])
                tk.op('dve', lambda e, t=t: e.tensor_tensor(out=x_sb[:, t, :], in0=x_sb[:, t, :], in1=tmpf[:], op=ALU.add),
                      reads=['x%d' % t, 'tmpf'], writes=['x%d' % t])
            _issue(wstate['ptr'])

        for t in range(NT):
            tk.dma('sp', y_d.ap()[t * 128:(t + 1) * 128, :], x_sb[:, t, :], reads=['x%d' % t])
        tk.finish()
    return nc


def _consts(is_sample):
    cf32 = np.zeros((128, 7, 128), np.float32)
    p = np.arange(128)
    J2 = np.zeros((128, 128), np.float32)
    for a in range(2):
        for i in range(64):
            J2[a * 64 + i, a * 64 + 63 - i] = 1.0
    cf32[:, 0] = J2
    cf32[:, 1] = np.eye(128, dtype=np.float32)
    cm = np.zeros((128, 128), np.float32)
    if is_sample:
        qc = np.arange(64)
        c0 = np.clip(qc - 8, 0, 48)
        kc = np.arange(64)
        valid = (kc[:, None] >= c0[None, :]) & (kc[:, None] < c0[None, :] + 16)
        m = np.where(valid, 0.0, NEG).astype(np.float32)
        cm = np.tile(m, (2, 2))
    cf32[:, 2] = cm
    s = np.arange(32)[:, None]
    t = np.arange(32)[None, :]
    trf = (s <= t).astype(np.float32) - (s <= 15).astype(np.float32)
    trb = (s >= t).astype(np.float32) - (s >= 16).astype(np.float32)
    for a in range(4):
        cf32[a * 32:(a + 1) * 32, 3, a * 32:(a + 1) * 32] = trf
        cf32[a * 32:(a + 1) * 32, 4, a * 32:(a + 1) * 32] = trb
    sl = np.arange(128) % 32
    ch = np.arange(128) // 32
    selcols = np.zeros((128, 128), np.float32)
    for a in range(4):
        selcols[:, a * 2 + 0] = ((ch == a) & (sl <= 15))
        selcols[:, a * 2 + 1] = (ch == a)
        selcols[:, 8 + a * 2 + 0] = ((ch == 3 - a) & (sl >= 16))
        selcols[:, 8 + a * 2 + 1] = (ch == 3 - a)
        selcols[:, 18 + a] = (ch == a)
    selcols[:, 16] = (np.arange(128) < 64)
    selcols[:, 17] = (np.arange(128) >= 64)
    cf32[:, 5] = selcols
    cf32[0:64, 6, 0:64] = 1.0
    cf32[64:128, 6, 64:128] = 1.0
    rowb = np.zeros((128, 74), np.float32)
    for i, (j, kt) in enumerate(JK):
        for hf in range(2):
            for krl in range(2):
                if is_sample:
                    qr = 2 * j + hf
                    kr = 2 * kt + krl
                    r0 = int(np.clip(qr - 4, 0, 8))
                    ok = (r0 <= kr < r0 + 8)
                else:
                    ok = (kt // 2 == j // 2)
                rowb[krl * 64:(krl + 1) * 64, i * 2 + hf] = 0.0 if ok else NEG
    cb16 = np.zeros((128, 9, 128), np.float32)
    cb16[:, 7] = J2
    cb16[:, 8] = cm
    cb16[:, 0] = np.eye(128)
    mf = (s <= t).astype(np.float32)
    mb = (s >= t).astype(np.float32)
    z = np.zeros((64, 64), np.float32)
    for a in range(4):
        cb16[a * 32:(a + 1) * 32, 1, a * 32:(a + 1) * 32] = mf
        cb16[a * 32:(a + 1) * 32, 2, a * 32:(a + 1) * 32] = mb
    ang = 2 * np.pi * np.outer(np.arange(64), np.arange(64)) / 64
    c4 = np.cos(ang) / 8.0
    s4 = np.sin(ang) / 8.0
    for ct in range(2):
        cb16[:, 3 + ct] = np.block([[c4, z], [z, c4]])
        cb16[:, 5 + ct] = np.block([[s4, z], [z, s4]])
    n = 1024 if is_sample else 256
    idx = np.arange(n)
    a2 = 2 * np.pi * ((np.outer(idx, idx)) % n) / n
    cn = np.cos(a2) / np.sqrt(n)
    sn = -np.sin(a2) / np.sqrt(n)
    CN = np.zeros((1024, 1024), np.float64)
    SN = np.zeros((1024, 1024), np.float64)
    for i in range(1024 // n):
        CN[i * n:(i + 1) * n, i * n:(i + 1) * n] = cn
        SN[i * n:(i + 1) * n, i * n:(i + 1) * n] = sn
    csn = np.stack([CN.reshape(8, 128, 1024), SN.reshape(8, 128, 1024)], axis=2)
    return dict(cf32=cf32, rowbias=rowb, cb16=cb16.astype(ml_dtypes.bfloat16), csn=csn.astype(ml_dtypes.bfloat16))


def _in_maps(x_prompt, x_sample, cache_attn_k, cache_attn_v, state_hgrn, c, c_ctx,
             w_ada, b_ada, g_pre, w_in, rpb, lb_logits, g_hgrn, w_fnet, w_out, g_post):
    f = lambda a: np.ascontiguousarray(np.asarray(a, dtype=np.float32))
    shared = dict(w_ada=f(w_ada), b_ada=f(b_ada), g_pre=f(g_pre), w_in=f(w_in), lb_logits=f(lb_logits),
                  g_hgrn=f(g_hgrn), w_fnet=f(w_fnet), w_out=f(w_out), g_post=f(g_post))
    tp = np.zeros((NL, 8, 23, 127), np.float32)
    tp[:, :, 4:19, 48:79] = f(rpb)
    cs = _consts(True)
    cp = _consts(False)
    maps = []
    for i in range(8):
        m = dict(shared)
        if i < 4:
            m["x"] = f(x_sample[i])
            m["cvec"] = f(np.asarray(c[i]).reshape(8, 128).T)
            m["ctxk"] = f(np.asarray(cache_attn_k[i]).reshape(NL, 512, 512))
            m["ctxv"] = f(np.asarray(cache_attn_v[i]).reshape(NL, 512, 512))
            s = np.asarray(state_hgrn[i]).reshape(NL, 2, 2, 2, 64, 64)
            m["s0"] = f(s.transpose(0, 1, 3, 4, 2, 5).reshape(NL, 2, 128, 2, 64))
            m["flags"] = np.ones((128, 2), np.float32)
            m["tpad"] = tp
            m.update(cs)
        else:
            m["x"] = f(np.asarray(x_prompt[4 * (i - 4):4 * (i - 3)]).reshape(T, D))
            m["cvec"] = f(np.asarray(c_ctx).reshape(8, 128).T)
            m["ctxk"] = np.zeros((NL, 512, 512), np.float32)
            m["ctxv"] = np.zeros((NL, 512, 512), np.float32)
            m["s0"] = np.zeros((NL, 2, 128, 2, 64), np.float32)
            m["flags"] = np.zeros((128, 2), np.float32)
            m["tpad"] = np.zeros_like(tp)
            m.update(cp)
        maps.append(m)
    return maps


_NC_CACHE = {}


def kernel(**inputs):
    if 'nc' not in _NC_CACHE:
        _NC_CACHE['nc'] = build_nc()
    nc = _NC_CACHE['nc']
    maps = _in_maps(**inputs)
    res = run_bass_kernel_spmd(nc, maps, core_ids=list(range(8)))
    r = res.results
    y_sample = np.stack([r[i]["y"] for i in range(4)], axis=0).astype(np.float32)
    y_prompt = np.concatenate([r[i]["y"].reshape(4, 256, D) for i in range(4, 8)], axis=0).astype(np.float32)
    nk = np.concatenate([r[i]["newk"].reshape(NL, 4, 256, 8, 64).transpose(1, 0, 2, 3, 4) for i in range(4, 8)], axis=0)
    nv = np.concatenate([r[i]["newv"].reshape(NL, 4, 256, 8, 64).transpose(1, 0, 2, 3, 4) for i in range(4, 8)], axis=0)
    ns = np.concatenate([r[i]["news"].reshape(NL, 2, 4, 2, 64, 2, 64).transpose(2, 0, 1, 5, 3, 4, 6).reshape(4, NL, 2, 4, 64, 64)
                         for i in range(4, 8)], axis=0)
    return (y_prompt, y_sample, np.ascontiguousarray(nk, dtype=np.float32), np.ascontiguousarray(nv, dtype=np.float32),
            np.ascontiguousarray(ns, dtype=np.float32))
```

```python
import numpy as np
import ml_dtypes
from contextlib import ExitStack
import concourse.bass as bass
import concourse.mybir as mybir
from concourse.bass_utils import run_bass_kernel_spmd

F32 = mybir.dt.float32
BF16 = mybir.dt.bfloat16
AF = mybir.ActivationFunctionType
ALU = mybir.AluOpType
AX = mybir.AxisListType

NL = 4
D = 1024
T = 1024
NT = 8
EPS = 1e-6
NEG = -30000.0
KT = {0: [0, 1, 2, 3], 1: [0, 1, 2, 3], 2: [0, 1, 2, 3, 4], 3: [1, 2, 3, 4, 5],
      4: [2, 3, 4, 5, 6], 5: [3, 4, 5, 6, 7], 6: [4, 5, 6, 7], 7: [4, 5, 6, 7]}
JK = [(j, kt) for j in range(8) for kt in KT[j]]
JKI = {p: i for i, p in enumerate(JK)}
NDS = 24
NSW = 72


class TK:
    def __init__(s, nc, st):
        s.nc = nc
        s.E = {'pe': nc.tensor, 'act': nc.scalar, 'dve': nc.vector, 'pool': nc.gpsimd, 'sp': nc.sync}
        s.sem = {k: st.enter_context(nc.semaphore('s_' + k)) for k in ('pe', 'act', 'dve', 'pool')}
        s.cnt = {k: 0 for k in s.E}
        s.seen = {k: {} for k in s.E}
        s.lw = {}
        s.rd = {}
        s.dsems = [st.enter_context(nc.semaphore('d%d' % i)) for i in range(NDS)]
        s.dcnt = [0] * NDS
        s.dnext = 0
        s.swsems = [st.enter_context(nc.semaphore('w%d' % i)) for i in range(NSW)]
        s.swnext = 0
        s.swlow = 0

    def _wait(s, eng, key, val):
        if eng == 'pe' and key == 'pe':
            return
        if s.seen[eng].get(key, 0) >= val:
            return
        if isinstance(key, str):
            semobj = s.sem[key]
        elif key >= 1000:
            semobj = s.swsems[key - 1000]
        else:
            semobj = s.dsems[key]
        s.E[eng].wait_ge(semobj, val)
        s.seen[eng][key] = val

    def _deps(s, eng, reads, writes):
        for k in reads:
            w = s.lw.get(k)
            if w:
                s._wait(eng, *w)
            if k.startswith('ps'):
                for rk, rv in s.rd.get(k, {}).items():
                    if rk != eng:
                        s._wait(eng, rk, rv)
        for k in writes:
            w = s.lw.get(k)
            if w:
                s._wait(eng, *w)
            for rk, rv in s.rd.get(k, {}).items():
                s._wait(eng, rk, rv)

    def _book(s, tag, reads, writes):
        for k in reads:
            d = s.rd.setdefault(k, {})
            d[tag[0]] = max(d.get(tag[0], 0), tag[1])
        for k in writes:
            s.lw[k] = tag
            s.rd[k] = {}

    def op(s, eng, fn, reads=(), writes=(), war_only=()):
        s._deps(eng, reads, writes)
        inst = fn(s.E[eng])
        s.cnt[eng] += 1
        inst.then_inc(s.sem[eng], 1)
        s._book((eng, s.cnt[eng]), tuple(reads) + tuple(war_only), writes)

    def dma(s, q, out, in_, reads=(), writes=()):
        if q == 'pool':
            assert s.swnext < NSW, "out of one-shot semaphores"
            i = s.swnext
            s.swnext += 1
            s._deps(q, reads, writes)
            s.E[q].dma_start(out=out, in_=in_).then_inc(s.swsems[i], 16)
            s._book((1000 + i, 16), reads, writes)
            return
        i = s.dnext
        s.dnext = (s.dnext + 1) % NDS
        if s.dcnt[i] > 0:
            s._wait(q, i, s.dcnt[i])
        s._deps(q, reads, writes)
        s.dcnt[i] += 16
        s.E[q].dma_start(out=out, in_=in_).then_inc(s.dsems[i], 16)
        s._book((i, s.dcnt[i]), reads, writes)

    def barrier(s):
        engs = ('pe', 'act', 'dve', 'pool', 'sp')
        snap = dict(s.cnt)
        dsnap = list(s.dcnt)
        for e in engs:
            for o in ('pe', 'act', 'dve', 'pool'):
                if o != e and snap[o] > 0:
                    s._wait(e, o, snap[o])
            for i in range(NDS):
                if dsnap[i] > 0:
                    s._wait(e, i, dsnap[i])
            for i in range(s.swlow, s.swnext):
                s._wait(e, 1000 + i, 16)
        s.swlow = s.swnext

    def finish(s):
        for i in range(NDS):
            if s.dcnt[i] > 0:
                s._wait('sp', i, s.dcnt[i])
        for i in range(s.swnext):
            s._wait('sp', 1000 + i, 16)
        for k in ('pe', 'act', 'dve', 'pool'):
            if s.cnt[k] > 0:
                s._wait('sp', k, s.cnt[k])


def build_nc(nl=NL, dbg=False, upto=None):
    nc = bass.Bass("TRN2", target_bir_lowering=False)
    _order = ['M0', 'M1', 'M2', 'M3', 'M', 'A0', 'A1', 'A1a', 'A1b', 'A1c', 'A2', 'A', 'R0', 'R1', 'R2', 'R3', 'R', 'F0', 'F1', 'F']

    def stop(p):
        return upto is not None and _order.index(upto) <= _order.index(p)

    def din(name, shape, dt=F32):
        return nc.dram_tensor(name, list(shape), dt, kind="ExternalInput")

    def dout(name, shape, dt=F32):
        return nc.dram_tensor(name, list(shape), dt, kind="ExternalOutput")

    x_d = din("x", [T, D])
    cvec_d = din("cvec", [128, 8])
    ctxk_d = din("ctxk", [NL, 512, 512])
    ctxv_d = din("ctxv", [NL, 512, 512])
    s0_d = din("s0", [NL, 2, 128, 2, 64])
    flags_d = din("flags", [128, 2])
    wada_d = din("w_ada", [NL, D, 3 * D])
    bada_d = din("b_ada", [NL, 3 * D])
    gpre_d = din("g_pre", [NL, D])
    win_d = din("w_in", [NL, D, 3840])
    tpad_d = din("tpad", [NL, 8, 23, 127])
    lbl_d = din("lb_logits", [2, NL, 256])
    ghg_d = din("g_hgrn", [NL, 256])
    wfn_d = din("w_fnet", [NL, 256, 256])
    wout_d = din("w_out", [NL, D, D])
    gpost_d = din("g_post", [NL, D])
    cf32_d = din("cf32", [128, 7, 128])
    rowb_d = din("rowbias", [128, 74])
    cb16_d = din("cb16", [128, 9, 128], BF16)
    csn_d = din("csn", [8, 128, 2, 1024], BF16)

    y_d = dout("y", [T, D])
    nk_d = dout("newk", [NL, T, 512])
    nv_d = dout("newv", [NL, T, 512])
    ns_d = dout("news", [NL, 2, 4, 128, 2, 64])
    dbg_d = dout("dbgmixed", [T, D], BF16) if dbg else None

    with ExitStack() as st:
        def sb(name, shape, dt=F32):
            return st.enter_context(nc.sbuf_tensor(name, list(shape), dt))

        tk = TK(nc, st)
        x_sb = sb("x_sb", [128, NT, D])
        hT = sb("hT", [128, 8, T], BF16)
        mixed = sb("mixed", [128, NT, D], BF16)
        wst = [sb("wst0", [128, 8, 512], BF16)]
        wbf = [sb("wbf%d" % i, [128, 8, 512], BF16) for i in range(2)]
        gg = sb("gg", [128, D])
        modN = sb("modN", [128, 3 * D], BF16)
        screp = sb("screp", [128, 8, 128], BF16)
        brow = sb("brow", [1, 512])
        ones_row = sb("ones_row", [1, 128])
        csil = sb("csil", [128, 8])
        cf32 = sb("cf32s", [128, 7, 128])
        rowb = sb("rowbs", [128, 74])
        cb16 = sb("cb16s", [128, 9, 128], BF16)
        flags = sb("flagss", [128, 2])
        lbl = sb("lbl", [128, 2, 256])
        oml = sb("oml", [128, 2, 256])
        ghgB = sb("ghgB", [128, 256])
        ssq = sb("ssq", [128, 16])
        rstd = sb("rstd", [128, 16])
        ARW = 21120
        arena = sb("arena", [128, ARW])
        apos = [0]

        def areset():
            apos[0] = 0

        def aget(shape, dt=F32):
            n = 1
            for d_ in shape[1:]:
                n *= d_
            words = n if dt == F32 else (n + 1) // 2
            a0 = apos[0]
            apos[0] += words
            assert apos[0] <= ARW, ("arena overflow", apos[0])
            v = arena[:, a0:a0 + words]
            if dt != F32:
                v = v.bitcast(dt)
            if len(shape) == 3:
                v = v.rearrange("p (a b) -> p a b", a=shape[1])
            elif len(shape) == 4:
                v = v.rearrange("p (a b c) -> p a b c", a=shape[1], b=shape[2])
            return v

        psbig = [st.enter_context(nc.psum_tensor("psb%d" % i, [128, 1024], F32)) for i in range(4)]
        ps = [psbig[i // 2][:, (i % 2) * 512:(i % 2 + 1) * 512] for i in range(8)]
        psk = ["ps%d" % i for i in range(8)]
        bank_rr = [0]

        def nb(avoid=()):
            while True:
                b = bank_rr[0]
                bank_rr[0] = (b + 1) % 8
                if b not in avoid:
                    return b

        J2 = cf32[:, 0, :]
        IDF = cf32[:, 1, :]
        CMT = cf32[:, 2, :]
        TR = [cf32[:, 3, :], cf32[:, 4, :]]
        SEL = [cf32[:, 5, 0:8], cf32[:, 5, 8:16]]
        HM = cf32[:, 5, 16:18]
        QM = cf32[:, 5, 18:22]
        HME = cf32[:, 6, :].rearrange("p (a c) -> p a c", a=2)
        IDB = cb16[:, 0, :]
        TRI = [cb16[:, 1, :], cb16[:, 2, :]]
        C4S4 = cb16
        J2B = cb16[:, 7, :]
        CMTB = cb16[:, 8, :]

        tk.dma('sp', cf32[:], cf32_d.ap(), writes=['cf32'])
        tk.dma('sp', rowb[:], rowb_d.ap(), writes=['rowb'])
        tk.dma('sp', cb16[:], cb16_d.ap(), writes=['cb16'])
        tk.dma('sp', flags[:], flags_d.ap(), writes=['flags'])
        tk.dma('sp', csil[:], cvec_d.ap(), writes=['csil'])
        for t in range(NT):
            tk.dma('sp', x_sb[:, t, :], x_d.ap()[t * 128:(t + 1) * 128, :], writes=['x%d' % t])
        tk.op('pool', lambda e: e.memset(ones_row[:], 1.0), writes=['ones_row'])
        tk.op('act', lambda e: e.activation(out=csil[:], in_=csil[:], func=AF.Silu), reads=['csil'], writes=['csil'])

        wring = [0]

        wbring = [0]

        def load_w(src_ap, ncols):
            wi = wbring[0]
            wbring[0] ^= 1
            tk.dma('pool', wbf[wi][:, :, 0:ncols], src_ap.rearrange("(kc p) n -> p kc n", p=128), writes=['wbf%d' % wi])
            return wi

        wseq = []
        for l_ in range(nl):
            for (c0_, n_) in ((0, 512), (512, 512), (1024, 512), (1536, 512), (2048, 512), (2816, 512), (2560, 256), (3328, 512)):
                wseq.append((win_d, l_, c0_, n_))
            wseq.append((wout_d, l_, 0, 512))
            wseq.append((wout_d, l_, 512, 512))
        wstate = {'ptr': 0, 'loaded': {}}

        def _issue(i):
            if i < len(wseq) and i not in wstate['loaded']:
                d_, l_, c0_, n_ = wseq[i]
                wstate['loaded'][i] = load_w(d_.ap()[l_, :, c0_:c0_ + n_], n_)

        def next_w(prefetch=True):
            i = wstate['ptr']
            wstate['ptr'] += 1
            _issue(i)
            if prefetch:
                _issue(i + 1)
            return wstate['loaded'][i]

        def proj_tm(wi, c0, ncols, t, b):
            for kc in range(8):
                tk.op('pe', lambda e, kc=kc: e.matmul(ps[b][:, 0:ncols], lhsT=hT[:, kc, t * 128:(t + 1) * 128],
                                                     rhs=wbf[wi][:, kc, c0:c0 + ncols], start=(kc == 0), stop=(kc == 7)),
                      reads=['hT', 'wbf%d' % wi], writes=[psk[b]])

        def proj_fm(wi, ci, g, b):
            for kc in range(8):
                tk.op('pe', lambda e, kc=kc: e.matmul(ps[b][:, 0:512], lhsT=wbf[wi][:, kc, ci * 128:(ci + 1) * 128],
                                                     rhs=hT[:, kc, g * 512:(g + 1) * 512], start=(kc == 0), stop=(kc == 7)),
                      reads=['hT', 'wbf%d' % wi], writes=[psk[b]])

        def psb16(b):
            return ps[b].bitcast(BF16)

        def emit_mod_chunk(lm, ch, banks=None):
            mod_dma(lm, ch)
            mod_mm(lm, ch, banks)

        def mod_dma(lm, ch):
            tk.dma('pool', wst[0][:], wada_d.ap()[lm, :, ch * 512:(ch + 1) * 512].rearrange("(kc p) n -> p kc n", p=128), writes=['wst0'])
            tk.dma('sp', brow[:], bada_d.ap()[lm:lm + 1, ch * 512:(ch + 1) * 512], writes=['brow'])

        def mod_mm(lm, ch, banks=None):
            b = nb() if banks is None else banks[ch % len(banks)]
            for kc in range(8):
                tk.op('pe', lambda e, kc=kc: e.matmul(ps[b][:, :], lhsT=screp[:, kc, :], rhs=wst[0][:, kc, :], start=(kc == 0), stop=False),
                      reads=['screp', 'wst0'], writes=[psk[b]])
            tk.op('pe', lambda e: e.matmul(ps[b][:, :], lhsT=ones_row[0:1, :], rhs=brow[0:1, :], start=False, stop=True),
                  reads=['ones_row', 'brow'], writes=[psk[b]])
            tk.op('act', lambda e: e.copy(out=modN[:, ch * 512:(ch + 1) * 512], in_=ps[b][:, :]), reads=[psk[b]], writes=['modN'])

        tk.op('dve', lambda e: e.tensor_copy(out=screp[:], in_=csil[:].unsqueeze(2).broadcast_to([128, 8, 128])), reads=['csil'], writes=['screp'])
        for ch in range(6):
            emit_mod_chunk(0, ch)

        for l in range(nl):
            if l == 0:
                tk.barrier()
            areset()
            apos[0] = 11520
            gbc = aget([128, 2, D])
            lbt = aget([128, 2, NL, 256])
            junk = aget([128, D], BF16)
            tmpf = aget([128, D])
            hb = [aget([128, D], BF16) for _ in range(2)]
            modA = aget([128, D])
            tk.dma('sp', gbc[:, 0, :], bass.AP(gpre_d, l * D, [[0, 128], [1, D]]), writes=['gbc'])
            tk.dma('sp', gbc[:, 1, :], bass.AP(gpost_d, l * D, [[0, 128], [1, D]]), writes=['gbc'])
            tk.dma('sp', lbt[:].rearrange("p a l c -> p (a l c)"), bass.AP(lbl_d, 0, [[0, 128], [1, 2 * NL * 256]]), writes=['lbt'])
            if l == 0:
                tk.op('dve', lambda e: e.memset(lbl[:], 0.0), writes=['lbl'])
            else:
                mx = tmpf[:, 0:512].rearrange("p (a c) -> p a c", a=2)
                sm = tmpf[:, 512:1024].rearrange("p (a c) -> p a c", a=2)
                tk.op('dve', lambda e: e.tensor_tensor(out=mx, in0=lbt[:, :, 0, :], in1=lbt[:, :, 1, :], op=ALU.max),
                      reads=['lbt'], writes=['tmpf'])
                for l2 in range(2, NL):
                    tk.op('dve', lambda e, l2=l2: e.tensor_tensor(out=mx, in0=mx, in1=lbt[:, :, l2, :], op=ALU.max),
                          reads=['lbt', 'tmpf'], writes=['tmpf'])
                for l2 in range(NL):
                    tk.op('dve', lambda e, l2=l2: e.tensor_tensor(out=lbt[:, :, l2, :], in0=lbt[:, :, l2, :], in1=mx, op=ALU.subtract),
                          reads=['lbt', 'tmpf'], writes=['lbt'])
                tk.op('act', lambda e: e.activation(out=lbt[:].rearrange("p a l c -> p (a l c)"), in_=lbt[:].rearrange("p a l c -> p (a l c)"), func=AF.Exp),
                      reads=['lbt'], writes=['lbt'])
                tk.op('dve', lambda e: e.tensor_tensor(out=sm, in0=lbt[:, :, 0, :], in1=lbt[:, :, 1, :], op=ALU.add),
                      reads=['lbt'], writes=['tmpf'])
                for l2 in range(2, NL):
                    tk.op('dve', lambda e, l2=l2: e.tensor_tensor(out=sm, in0=sm, in1=lbt[:, :, l2, :], op=ALU.add),
                          reads=['lbt', 'tmpf'], writes=['tmpf'])
                tk.op('dve', lambda e: e.reciprocal(out=sm, in_=sm), reads=['tmpf'], writes=['tmpf'])
                tk.op('dve', lambda e: e.tensor_copy(out=lbl[:], in_=lbt[:, :, 1, :]), reads=['lbt'], writes=['lbl'])
                for l2 in range(2, l + 1):
                    tk.op('dve', lambda e, l2=l2: e.tensor_tensor(out=lbl[:], in0=lbl[:], in1=lbt[:, :, l2, :], op=ALU.add),
                          reads=['lbt', 'lbl'], writes=['lbl'])
                tk.op('dve', lambda e: e.tensor_tensor(out=lbl[:], in0=lbl[:], in1=sm, op=ALU.mult), reads=['lbl', 'tmpf'], writes=['lbl'])
            tk.op('dve', lambda e: e.tensor_scalar(out=oml[:], in0=lbl[:], scalar1=-0.5, scalar2=0.5, op0=ALU.mult, op1=ALU.add),
                  reads=['lbl'], writes=['oml'])
            tk.op('dve', lambda e: e.tensor_scalar(out=lbl[:], in0=lbl[:], scalar1=0.5, scalar2=0.5, op0=ALU.mult, op1=ALU.add),
                  reads=['lbl'], writes=['lbl'])
            if stop('M0'):
                break
            tk.op('dve', lambda e: e.scalar_tensor_tensor(out=modA[:], in0=modN[:, D:2 * D], scalar=1.0, in1=gbc[:, 0, :], op0=ALU.add, op1=ALU.mult),
                  reads=['modN', 'gbc'], writes=['modA'])
            tk.op('dve', lambda e: e.tensor_tensor(out=gg[:], in0=modN[:, 2 * D:3 * D], in1=gbc[:, 1, :], op=ALU.mult), reads=['modN', 'gbc'], writes=['gg'])
            if stop('M1'):
                break
            for t in range(NT):
                tk.op('act', lambda e, t=t: e.activation(out=junk[:], in_=x_sb[:, t, :], func=AF.Square, accum_out=ssq[:, t:t + 1]),
                      reads=['x%d' % t], writes=['junk', 'ssq'])
            tk.op('dve', lambda e: e.tensor_scalar(out=rstd[:, 0:8], in0=ssq[:, 0:8], scalar1=1.0 / D, scalar2=EPS, op0=ALU.mult, op1=ALU.add),
                  reads=['ssq'], writes=['rstd'])
            tk.op('act', lambda e: e.activation(out=rstd[:, 0:8], in_=rstd[:, 0:8], func=AF.Ln), reads=['rstd'], writes=['rstd'])
            tk.op('act', lambda e: e.activation(out=rstd[:, 0:8], in_=rstd[:, 0:8], func=AF.Exp, scale=-0.5), reads=['rstd'], writes=['rstd'])
            if stop('M2'):
                break
            for t in range(NT):
                hbi = t % 2
                tk.op('dve', lambda e, t=t: e.scalar_tensor_tensor(out=tmpf[:], in0=x_sb[:, t, :], scalar=rstd[:, t:t + 1], in1=modA[:],
                                                                   op0=ALU.mult, op1=ALU.mult),
                      reads=['x%d' % t, 'rstd', 'modA'], writes=['tmpf'])
                tk.op('dve', lambda e: e.tensor_tensor(out=hb[hbi][:], in0=tmpf[:], in1=modN[:, 0:D], op=ALU.add),
                      reads=['tmpf', 'modN'], writes=['hb%d' % hbi])
                if stop('M3'):
                    continue
                b = nb()
                for kc in range(8):
                    tk.op('pe', lambda e, kc=kc: e.transpose(out=psb16(b)[:, kc * 128:(kc + 1) * 128], in_=hb[hbi][:, kc * 128:(kc + 1) * 128], identity=IDB),
                          reads=['hb%d' % hbi, 'cb16'], writes=[psk[b]])
                tk.op('act', lambda e, t=t: e.copy(out=hT[:, :, t * 128:(t + 1) * 128], in_=psb16(b)[:, :].rearrange("p (k c) -> p k c", k=8)),
                      reads=[psk[b]], writes=['hT'])

            if stop('M'):
                break
            tk.barrier()
            areset()
            qT = aget([128, 4, T], BF16)
            kT = aget([128, 4, T], BF16)
            ckT = aget([128, 4, 512], BF16)
            vaug = aget([128, NT, 8, 66], BF16)
            cvaug = aget([128, 4, 8, 66], BF16)
            sga = aget([128, NT, 512], BF16)
            expT = aget([128, 7, 8, 128], BF16)
            Eb = [aget([128, 8, 128], BF16) for _ in range(3)]
            Pb = [aget([128, 8, 128], BF16) for _ in range(2)]
            hk = [Eb[0].bitcast(F32) if False else None, None]
            ost = [aget([128, 512]) for _ in range(2)]
            rden = aget([128, 8])
            otmp = aget([128, 8, 64])
            ckb = aget([128, 4, 512], BF16)
            hkA = aget([128, 8, 128])
            hkB = aget([128, 8, 128])
            hk = [hkA, hkB]
            tk.op('pool', lambda e: e.memset(vaug[:, :, :, 64:66], 1.0), writes=['vaug'])
            tk.op('dve', lambda e: e.tensor_copy(out=cvaug[:, :, :, 64:66].rearrange("p a b c -> p (a b) c"),
                                                 in_=flags[:, 0:1].unsqueeze(2).broadcast_to([128, 32, 2])),
                  reads=['flags'], writes=['cvaug'])
            def toep_dma(di):
                dl = di - 3
                hi = di % 2
                for qr in range(2):
                    for krl in range(2):
                        off = ((l * 8) * 23 + (2 * dl + krl - qr + 11)) * 127
                        src = bass.AP(tpad_d, off, [[1, 64], [23 * 127, 8], [1, 64]])
                        tk.dma('sp', hk[hi][qr * 64:(qr + 1) * 64, :, krl * 64:(krl + 1) * 64], src, writes=['hk%d_%d' % (hi, qr * 2 + krl)])

            def toep_mm(di):
                hi = di % 2
                bA = nb()
                bB = nb()
                for h in range(8):
                    b = bA if h % 2 == 0 else bB
                    o = ps[b][:, (h // 2) * 128:(h // 2 + 1) * 128]
                    tk.op('pe', lambda e, h=h, o=o: e.matmul(o, lhsT=hk[hi][:, h, :], rhs=J2, start=True, stop=False),
                          reads=['hk%d_%d' % (hi, x) for x in range(4)] + ['cf32'], writes=[psk[b]])
                    tk.op('pe', lambda e, o=o: e.matmul(o, lhsT=IDF, rhs=CMT, start=False, stop=True),
                          reads=['cf32'], writes=[psk[b]])
                for bi, b in enumerate((bA, bB)):
                    tk.op('act', lambda e, bi=bi, b=b: e.activation(out=expT[:, di, bi * 4:(bi + 1) * 4, :].rearrange("p a c -> p (a c)"),
                                                                    in_=ps[b][:, :], func=AF.Exp),
                          reads=[psk[b]], writes=['expT'])

            toep_dma(0)
            toep_dma(1)
            if stop('A0'):
                break
            ckk = 'ckb'
            tk.dma('pool', ckb[:], ctxk_d.ap()[l].rearrange("(c p) n -> p c n", p=128), writes=[ckk])
            for c in range(4):
                b = nb()
                for pr in range(4):
                    tk.op('pe', lambda e, pr=pr: e.transpose(out=psb16(b)[:, pr * 128:(pr + 1) * 128], in_=ckb[:, c, pr * 128:(pr + 1) * 128], identity=IDB),
                          reads=[ckk, 'cb16'], writes=[psk[b]])
                tk.op('act', lambda e, c=c: e.copy(out=ckT[:, :, c * 128:(c + 1) * 128], in_=psb16(b)[:, 0:512].rearrange("p (k c) -> p k c", k=4)),
                      reads=[psk[b]], writes=['ckT'])
            tk.dma('pool', ckb[:], ctxv_d.ap()[l].rearrange("(c p) n -> p c n", p=128), reads=[], writes=[ckk])
            tk.op('pool', lambda e: e.tensor_copy(out=cvaug[:, :, :, 0:64], in_=ckb[:].rearrange("p c (h d) -> p c h d", h=8)), reads=[ckk], writes=['cvaug'])
            if stop('A1'):
                break
            toep_mm(0)
            toep_dma(2)
            wi = next_w()
            for pr in range(4):
                for g in range(2):
                    b = nb()
                    proj_fm(wi, pr, g, b)
                    tk.op('act', lambda e, pr=pr, g=g: e.copy(out=qT[:, pr, g * 512:(g + 1) * 512], in_=ps[b][:, :]), reads=[psk[b]], writes=['qT'])
            if stop('A1a'):
                break
            toep_mm(1)
            toep_dma(3)
            wi = next_w()
            for pr in range(4):
                for g in range(2):
                    b = nb()
                    proj_fm(wi, pr, g, b)
                    tk.op('act', lambda e, pr=pr, g=g: e.copy(out=kT[:, pr, g * 512:(g + 1) * 512], in_=ps[b][:, :]), reads=[psk[b]], writes=['kT'])
            toep_mm(2)
            toep_dma(4)
            for t in range(NT):
                b = nb()
                proj_tm(wi, 0, 512, t, b)
                oi = t % 2
                tk.op('dve', lambda e: e.tensor_copy(out=ost[oi][:], in_=ps[b][:, :]), reads=[psk[b]], writes=['ost%d' % oi])
                tk.dma('sp', nk_d.ap()[l, t * 128:(t + 1) * 128, :], ost[oi][:], reads=['ost%d' % oi])
            toep_mm(3)
            toep_dma(5)
            if stop('A1b'):
                break
            wi = next_w()
            for t in range(NT):
                b = nb()
                proj_tm(wi, 0, 512, t, b)
                oi = t % 2
                tk.op('dve', lambda e: e.tensor_copy(out=ost[oi][:], in_=ps[b][:, :]), reads=[psk[b]], writes=['ost%d' % oi])
                tk.op('act', lambda e, t=t: e.copy(out=vaug[:, t, :, 0:64], in_=ost[oi][:].rearrange("p (h d) -> p h d", h=8)),
                      reads=['ost%d' % oi], writes=['vaug'])
                tk.dma('sp', nv_d.ap()[l, t * 128:(t + 1) * 128, :], ost[oi][:], reads=['ost%d' % oi])
            if stop('A1c'):
                break
            toep_mm(4)
            toep_dma(6)
            wi = next_w()
            for t in range(NT):
                b = nb()
                proj_tm(wi, 0, 512, t, b)
                tk.op('act', lambda e, t=t: e.activation(out=sga[:, t, :], in_=ps[b][:, :], func=AF.Silu), reads=[psk[b]], writes=['sga'])
            toep_mm(5)
            toep_mm(6)
            if stop('A2'):
                break
            OA, OB = 6, 7
            spairs = [(0, 1), (2, 3), (4, 5)]
            allsteps = []
            for j in range(8):
                st_ = [('l', kt) for kt in KT[j]] + [('c', c) for c in range(4)]
                for si_, (kind, idx) in enumerate(st_):
                    allsteps.append((j, si_, len(st_), kind, idx))

            def emit_S(k):
                j, si_, ns_, kind, idx = allsteps[k]
                sA, sB = spairs[k % 3]
                for h in range(8):
                    b = sA if h % 2 == 0 else sB
                    r0 = (h % 2) * 64
                    ksrc = kT[r0:r0 + 64, h // 2, idx * 128:(idx + 1) * 128] if kind == 'l' else ckT[r0:r0 + 64, h // 2, idx * 128:(idx + 1) * 128]
                    tk.op('pe', lambda e, h=h, b=b, ksrc=ksrc, r0=r0: e.matmul(ps[b][:, (h // 2) * 128:(h // 2 + 1) * 128], lhsT=ksrc,
                                                                              rhs=qT[r0:r0 + 64, h // 2, j * 128:(j + 1) * 128], start=True, stop=True),
                          reads=['qT', 'kT' if kind == 'l' else 'ckT'], writes=[psk[b]])

            def emit_rest(k):
                j, si_, ns_, kind, idx = allsteps[k]
                sA, sB = spairs[k % 3]
                sl = k % 3
                big = psbig[sA // 2]
                if kind == 'l':
                    jk = JKI[(j, idx)]
                    for hf in range(2):
                        tk.op('act', lambda e, hf=hf: e.activation(
                            out=Eb[sl][:, :, hf * 64:(hf + 1) * 64],
                            in_=big[:, :].rearrange("p (a c) -> p a c", a=8)[:, :, hf * 64:(hf + 1) * 64],
                            func=AF.Exp, scale=0.125, bias=rowb[:, jk * 2 + hf:jk * 2 + hf + 1]),
                            reads=[psk[sA], psk[sB], 'rowb'], writes=['Eb%d' % sl])
                    di = idx - j + 3
                    pl = k % 2
                    tk.op('dve', lambda e, di=di: e.tensor_tensor(out=Pb[pl][:], in0=Eb[sl][:], in1=expT[:, di, :, :], op=ALU.mult),
                          reads=['Eb%d' % sl, 'expT'], writes=['Pb%d' % pl])
                    lhs, lk = Pb[pl], 'Pb%d' % pl
                    vsrc, vk = vaug, 'vaug'
                else:
                    tk.op('act', lambda e: e.activation(out=Eb[sl][:].rearrange("p a c -> p (a c)"), in_=big[:, :], func=AF.Exp, scale=0.125),
                          reads=[psk[sA], psk[sB]], writes=['Eb%d' % sl])
                    lhs, lk = Eb[sl], 'Eb%d' % sl
                    vsrc, vk = cvaug, 'cvaug'
                for e_ in range(8):
                    h = 2 * (e_ % 4) + e_ // 4
                    ob = OA if e_ < 4 else OB
                    tk.op('pe', lambda e, e_=e_, h=h, ob=ob, lhs=lhs, vsrc=vsrc: e.matmul(
                        ps[ob][:, (e_ % 4) * 66:(e_ % 4) * 66 + 66], lhsT=lhs[:, e_, :], rhs=vsrc[:, idx, h, :],
                        start=(si_ == 0 and e_ % 4 == 0), stop=(si_ == ns_ - 1), skip_group_check=True),
                        reads=[lk, vk], writes=[psk[ob]])
                if si_ == ns_ - 1:
                    for bi, ob in enumerate((OA, OB)):
                        tk.op('dve', lambda e, bi=bi, ob=ob: e.reciprocal(out=rden[:, bi * 4:(bi + 1) * 4],
                                                                          in_=ps[ob][:, 0:264].rearrange("p (a c) -> p a c", a=4)[:, :, 64]),
                              reads=[psk[ob]], writes=['rden'])
                    for bi, ob in enumerate((OA, OB)):
                        tk.op('dve', lambda e, bi=bi, ob=ob: e.tensor_tensor(
                            out=otmp[:, bi:8:2, :], in0=ps[ob][:, 0:264].rearrange("p (a c) -> p a c", a=4)[:, :, 0:64],
                            in1=rden[:, bi * 4:(bi + 1) * 4].unsqueeze(2).broadcast_to([128, 4, 64]), op=ALU.mult),
                            reads=[psk[ob], 'rden'], writes=['otmp'])
                    tk.op('dve', lambda e: e.tensor_tensor(out=mixed[:, j, 0:512], in0=otmp[:].rearrange("p h d -> p (h d)"), in1=sga[:, j, :], op=ALU.mult),
                          reads=['otmp', 'sga'], writes=['mixed%d' % j])

            emit_S(0)
            emit_S(1)
            for k in range(len(allsteps)):
                if k + 2 < len(allsteps):
                    emit_S(k + 2)
                emit_rest(k)

            if stop('A'):
                break
            tk.barrier()
            areset()
            qh = aget([128, NT, 256], BF16)
            sgb = aget([128, NT, 256])
            qE = sgb.rearrange("p a b -> p (a b)")[:, 0:1024].bitcast(BF16).rearrange("p (a b) -> p a b", a=NT)
            kE = sgb.rearrange("p a b -> p (a b)")[:, 1024:2048].bitcast(BF16).rearrange("p (a b) -> p a b", a=NT)
            vh = aget([128, NT, 256], BF16)
            vhmF = aget([128, 4096])
            vhm = vhmF.bitcast(BF16).rearrange("p (q t c) -> p q t c", q=4, t=NT)
            A_ = vhmF[:, 0:2048].rearrange("p (e c) -> p e c", e=64)
            B_ = vhmF[:, 2048:4096].rearrange("p (e c) -> p e c", e=64)
            sgr = aget([128, NT, 256], BF16)
            fS = aget([128, 4096])
            fbuf = fS[:, 0:2048].rearrange("p (a b) -> p a b", a=NT)
            SinPm = fS.bitcast(BF16).rearrange("p (h c e) -> p h c e", h=4, c=32)
            lfbuf = aget([128, NT, 256])
            kETm = lfbuf.rearrange("p a b -> p (a b)").bitcast(BF16).rearrange("p (h t) -> p h t", h=4)
            osq = lfbuf
            tmpE = [aget([128, 512]) for _ in range(2)]
            qET = aget([128, 2, T], BF16)
            ATm = [aget([128, 4, 128], BF16) for _ in range(2)]
            osum = aget([128, NT, 256])
            gs3 = aget([128, 2, 32, 3])
            es3 = aget([128, 2, 32, 3])
            S0b = aget([128, 2, 64])
            nsb = aget([128, 4, 2, 64])
            hss = aget([128, 32])
            tk.dma('sp', ghgB[:], bass.AP(ghg_d, l * 256, [[0, 128], [1, 256]]), writes=['ghgB'])
            wi = next_w()
            for t in range(NT):
                b = nb()
                proj_tm(wi, 0, 512, t, b)
                tk.op('act', lambda e, t=t: e.activation(out=qh[:, t, :], in_=ps[b][:, 0:256], func=AF.Silu), reads=[psk[b]], writes=['qh'])
                tk.op('act', lambda e, t=t: e.activation(out=sgb[:, t, :], in_=ps[b][:, 256:512], func=AF.Tanh, scale=0.5), reads=[psk[b]], writes=['sQ'])
            wi = next_w()
            for t in range(NT):
                b = nb()
                proj_tm(wi, 0, 512, t, b)
                tk.op('act', lambda e, t=t: e.activation(out=sgr[:, t, :], in_=ps[b][:, 256:512], func=AF.Silu), reads=[psk[b]], writes=['sgr'])
                tk.op('dve', lambda e, t=t: e.tensor_copy(out=vh[:, t, :], in_=ps[b][:, 0:256]), reads=[psk[b]], writes=['vh'])

            if stop('R0'):
                break
            rstop = False
            for dr in range(2):
                if l + 1 < nl:
                    mod_dma(l + 1, 3 * dr)
                lb_bc = lbl[:, dr, :].unsqueeze(1).broadcast_to([128, NT, 256])
                oml_bc = oml[:, dr, :].unsqueeze(1).broadcast_to([128, NT, 256])
                tk.op('dve', lambda e: e.tensor_tensor(out=fbuf[:], in0=sgb[:], in1=oml_bc, op=ALU.mult), reads=['sQ', 'oml'], writes=['fS'])
                tk.op('dve', lambda e: e.tensor_tensor(out=fbuf[:], in0=fbuf[:], in1=lb_bc, op=ALU.add), reads=['fS', 'lbl'], writes=['fS'])
                tk.op('act', lambda e: e.activation(out=lfbuf[:].rearrange("p a b -> p (a b)"), in_=fbuf[:].rearrange("p a b -> p (a b)"), func=AF.Ln),
                      reads=['fS'], writes=['lK'])
                tk.op('dve', lambda e: e.tensor_scalar(out=fbuf[:], in0=fbuf[:], scalar1=-1.0, scalar2=1.0, op0=ALU.mult, op1=ALU.add),
                      reads=['fS'], writes=['fS'])
                bS = nb()
                for t in range(NT):
                    for pr in range(2):
                        tt_ = t if dr == 0 else 7 - t
                        tk.op('pe', lambda e, t=t, pr=pr, tt_=tt_: e.matmul(ps[bS][:, (pr * 8 + tt_) * 8:(pr * 8 + tt_) * 8 + 8], lhsT=lfbuf[:, t, pr * 128:(pr + 1) * 128],
                                                                   rhs=SEL[dr], start=True, stop=True),
                              reads=['lK', 'cf32'], writes=[psk[bS]])
                psS = ps[bS][:, 0:128].rearrange("p (a c r) -> p a c r", a=2, c=32)
                tk.op('dve', lambda e: e.tensor_copy(out=gs3[:, :, :, 0:2], in_=psS), reads=[psk[bS]], writes=['gs3'])
                tk.op('dve', lambda e: e.tensor_tensor(out=gs3[:, :, :, 2], in0=gs3[:, :, :, 1], in1=gs3[:, :, :, 0], op=ALU.subtract),
                      reads=['gs3'], writes=['gs3'])
                tk.op('act', lambda e: e.activation(out=es3[:].rearrange("p a c r -> p (a c r)"), in_=gs3[:].rearrange("p a c r -> p (a c r)"), func=AF.Exp),
                      reads=['gs3'], writes=['es3'])
                tk.op('dve', lambda e: e.tensor_scalar(out=es3[:, :, 8:32:8, 0:2], in0=es3[:, :, 8:32:8, 0:2], scalar1=flags[:, 1:2], scalar2=None, op0=ALU.mult),
                      reads=['es3', 'flags'], writes=['es3'])
                for tp in range(4):
                    b = nb()
                    for i in range(2):
                        t = 2 * tp + i
                        tk.op('pe', lambda e, t=t, i=i: e.matmul(ps[b][:, i * 256:(i + 1) * 256], lhsT=TR[dr], rhs=lfbuf[:, t, :], start=True, stop=True),
                              reads=['cf32', 'lK'], writes=[psk[b]])
                    tk.op('act', lambda e: e.activation(out=tmpE[0][:], in_=ps[b][:, :], func=AF.Exp), reads=[psk[b]], writes=['tmpE0'])
                    tk.op('act', lambda e: e.activation(out=tmpE[1][:], in_=ps[b][:, :], func=AF.Exp, scale=-1.0), reads=[psk[b]], writes=['tmpE1'])
                    tk.op('dve', lambda e, tp=tp: e.tensor_tensor(out=qE[:, 2 * tp:2 * tp + 2, :].rearrange("p a c -> p (a c)"),
                                                                  in0=qh[:, 2 * tp:2 * tp + 2, :].rearrange("p a c -> p (a c)"), in1=tmpE[0][:], op=ALU.mult),
                          reads=['qh', 'tmpE0'], writes=['sQ', 'qk%d' % tp])
                    tk.op('dve', lambda e, tp=tp: e.tensor_tensor(out=kE[:, 2 * tp:2 * tp + 2, :].rearrange("p a c -> p (a c)"),
                                                                  in0=fbuf[:, 2 * tp:2 * tp + 2, :].rearrange("p a c -> p (a c)"), in1=tmpE[1][:], op=ALU.mult),
                          reads=['fS', 'tmpE1'], writes=['sQ', 'qk%d' % tp])
                for cp in range(4):
                    tk.op('dve', lambda e, cp=cp: e.tensor_scalar(out=vhm[:, cp, :, :], in0=vh[:], scalar1=QM[:, cp:cp + 1], scalar2=None, op0=ALU.mult),
                          reads=['vh', 'cf32'], writes=['vhm'])
                for t in range(NT):
                    b = nb()
                    for pr in range(2):
                        tk.op('pe', lambda e, t=t, pr=pr: e.transpose(out=psb16(b)[:, pr * 128:(pr + 1) * 128], in_=qE[:, t, pr * 128:(pr + 1) * 128], identity=IDB),
                              reads=['qk%d' % (t // 2), 'cb16'], writes=[psk[b]], war_only=['sQ'])
                        tk.op('pe', lambda e, t=t, pr=pr: e.transpose(out=psb16(b)[:, (2 + pr) * 128:(3 + pr) * 128], in_=kE[:, t, pr * 128:(pr + 1) * 128], identity=IDB),
                              reads=['qk%d' % (t // 2), 'cb16'], writes=[psk[b]], war_only=['sQ'])
                    tk.op('act', lambda e, t=t: e.copy(out=qET[:, :, t * 128:(t + 1) * 128], in_=psb16(b)[:, 0:256].rearrange("p (a c) -> p a c", a=2)),
                          reads=[psk[b]], writes=['qET'])
                    for h in range(4):
                        tk.op('act', lambda e, t=t, h=h: e.activation(out=kETm[:, h, t * 128:(t + 1) * 128], in_=psb16(b)[:, (2 + h // 2) * 128:(3 + h // 2) * 128],
                                                                      func=AF.Copy, scale=HM[:, h % 2:h % 2 + 1]),
                              reads=[psk[b], 'cf32'], writes=['lK'])
                if stop('R1'):
                    rstop = True
                    break
                tk.dma('sp', S0b[:], s0_d.ap()[l, dr], writes=['S0b'])
                kvb_all = [[nb() for _ in range(4)] for _ in range(2)]
                for pr in range(2):
                    kvb = kvb_all[pr]
                    for c in range(32):
                        cq = c if dr == 0 else 31 - c
                        b = kvb[cq // 8]
                        for h in (2 * pr, 2 * pr + 1):
                            o = ps[b][(h % 2) * 64:(h % 2) * 64 + 64, (cq % 8) * 64:(cq % 8) * 64 + 64]
                            tk.op('pe', lambda e, c=c, h=h, o=o: e.matmul(o, lhsT=kE[:, c // 4, h * 64:(h + 1) * 64], rhs=vhm[:, c % 4, c // 4, h * 64:(h + 1) * 64],
                                                                          start=True, stop=True),
                                  reads=['sQ', 'vhm'], writes=[psk[b]])
                for pr in range(2):
                    kvb = kvb_all[pr]
                    for g in range(4):
                        tk.op('dve', lambda e, g=g: e.tensor_tensor(out=B_[:, :, g * 8:(g + 1) * 8].rearrange("p e c -> p c e"),
                                                                    in0=ps[kvb[g]][:, :].rearrange("p (c e) -> p c e", c=8),
                                                                    in1=es3[:, pr, g * 8:(g + 1) * 8, 2].unsqueeze(2).broadcast_to([128, 8, 64]), op=ALU.mult),
                              reads=[psk[kvb[g]], 'es3'], writes=['vhm'])
                    if dr == 0 and pr == 0:
                        wi = next_w()
                        for t in range(NT):
                            b = kvb[t % 4]
                            proj_tm(wi, 0, 256, t, b)
                            tk.op('act', lambda e, t=t: e.activation(out=sgb[:, t, :], in_=ps[b][:, 0:256], func=AF.Tanh, scale=0.5), reads=[psk[b]], writes=['sQ'])
                    if dr == 1 and pr == 0:
                        wi = next_w()
                        uT_h = qh[:].rearrange("p a b -> p (a b)").rearrange("p (c t) -> p c t", c=2)
                        sgf_h = qE
                        hb_ = 0
                        for ci in range(2):
                            for g in range(2):
                                b = kvb[hb_ % 4]
                                hb_ += 1
                                proj_fm(wi, ci, g, b)
                                tk.op('act', lambda e, ci=ci, g=g: e.copy(out=uT_h[:, ci, g * 512:(g + 1) * 512], in_=ps[b][:, :]), reads=[psk[b]], writes=['qh'])
                        for t in range(NT):
                            b = kvb[hb_ % 4]
                            hb_ += 1
                            proj_tm(wi, 256, 256, t, b)
                            tk.op('act', lambda e, t=t: e.activation(out=sgf_h[:, t, :], in_=ps[b][:, 0:256], func=AF.Silu), reads=[psk[b]], writes=['sQ'])
                    tk.op('dve', lambda e: e.scalar_tensor_tensor(out=B_[:, :, 0], in0=S0b[:, pr, :], scalar=es3[:, pr, 0, 1:2], in1=B_[:, :, 0], op0=ALU.mult, op1=ALU.add),
                          reads=['S0b', 'es3', 'vhm'], writes=['vhm'])
                    tk.op('dve', lambda e: e.tensor_copy(out=A_[:], in_=es3[:, pr, :, 1].unsqueeze(1).broadcast_to([128, 64, 32])), reads=['es3', 'vhm'], writes=['vhm'])
                    tk.op('dve', lambda e: e.memset(A_[:, :, 0:1], 0.0), reads=['vhm'], writes=['vhm'])
                    tk.op('dve', lambda e: e.tensor_tensor_scan(out=B_[:].rearrange("p e c -> p (e c)"), data0=A_[:].rearrange("p e c -> p (e c)"),
                                                                data1=B_[:].rearrange("p e c -> p (e c)"), initial=0.0, op0=ALU.mult, op1=ALU.add),
                          reads=['vhm'], writes=['vhm'])
                    tk.op('dve', lambda e: e.tensor_copy(out=nsb[:, :, pr, :], in_=B_[:, :, 7:32:8].rearrange("p e k -> p k e")), reads=['vhm'], writes=['nsb'])
                    for h2 in range(2):
                        tk.op('dve', lambda e, h2=h2: e.scalar_tensor_tensor(out=SinPm[:, 2 * pr + h2, 1:32, :], in0=B_[:, :, 0:31].rearrange("p e c -> p c e"),
                                                                             scalar=HM[:, h2:h2 + 1], in1=es3[:, pr, 1:32, 0].unsqueeze(2).broadcast_to([128, 31, 64]),
                                                                             op0=ALU.mult, op1=ALU.mult),
                              reads=['vhm', 'es3', 'cf32'], writes=['fS'])
                        tk.op('dve', lambda e, h2=h2: e.scalar_tensor_tensor(out=SinPm[:, 2 * pr + h2, 0, :], in0=S0b[:, pr, :], scalar=HM[:, h2:h2 + 1],
                                                                             in1=es3[:, pr, 0, 0:1].broadcast_to([128, 64]), op0=ALU.mult, op1=ALU.mult),
                              reads=['S0b', 'es3', 'cf32'], writes=['fS'])
                nsb3 = nsb[:].rearrange("p s a e -> p s (a e)")
                if dr == 0:
                    tk.dma('sp', ns_d.ap()[l, 0].rearrange("s p a e -> p s (a e)"), nsb3, reads=['nsb'])
                else:
                    for k in range(4):
                        tk.dma('sp', ns_d.ap()[l, 1, 3 - k].rearrange("p a e -> p (a e)"), nsb3[:, k, :], reads=['nsb'])
                if l + 1 < nl:
                    mod_mm(l + 1, 3 * dr)
                    mod_dma(l + 1, 3 * dr + 1)
                if stop('R2'):
                    rstop = True
                    break
                for t in range(NT):
                    bA_ = nb()
                    ai = t % 2
                    for h in range(4):
                        tk.op('pe', lambda e, t=t, h=h: e.matmul(ps[bA_][:, h * 128:(h + 1) * 128], lhsT=kETm[:, h, t * 128:(t + 1) * 128],
                                                                 rhs=qET[:, h // 2, t * 128:(t + 1) * 128], start=True, stop=True),
                              reads=['lK', 'qET'], writes=[psk[bA_]])
                    tk.op('dve', lambda e: e.tensor_tensor(out=ATm[ai][:], in0=ps[bA_][:, :].rearrange("p (a c) -> p a c", a=4),
                                                           in1=TRI[dr].unsqueeze(1).broadcast_to([128, 4, 128]), op=ALU.mult),
                          reads=[psk[bA_], 'cb16'], writes=['ATm%d' % ai])
                    bO = nb()
                    for h in range(4):
                        tk.op('pe', lambda e, t=t, h=h: e.matmul(ps[bO][:, h * 64:(h + 1) * 64], lhsT=ATm[ai][:, h, :], rhs=vh[:, t, h * 64:(h + 1) * 64],
                                                                 start=True, stop=False, skip_group_check=True),
                              reads=['ATm%d' % ai, 'vh'], writes=[psk[bO]])
                        for cp in range(4):
                            tk.op('pe', lambda e, t=t, h=h, cp=cp: e.matmul(ps[bO][cp * 32:(cp + 1) * 32, h * 64:(h + 1) * 64],
                                                                            lhsT=qET[:, h // 2, t * 128 + cp * 32:t * 128 + cp * 32 + 32],
                                                                            rhs=SinPm[:, h, (4 * t + cp) if dr == 0 else 31 - (4 * t + cp), :], start=False, stop=(cp == 3), skip_group_check=True,
                                                                            tile_position=(0, cp * 32)),
                                  reads=['qET', 'fS'], writes=[psk[bO]])
                    if l + 1 < nl and t == 3:
                        mod_mm(l + 1, 3 * dr + 1)
                        mod_dma(l + 1, 3 * dr + 2)
                    if l + 1 < nl and t == 7:
                        mod_mm(l + 1, 3 * dr + 2)
                    if dr == 0:
                        tk.op('act', lambda e, t=t: e.copy(out=osum[:, t, :], in_=ps[bO][:, 0:256]), reads=[psk[bO]], writes=['osum%d' % t])
                    else:
                        tk.op('dve', lambda e, t=t: e.tensor_tensor(out=osum[:, t, :], in0=osum[:, t, :], in1=ps[bO][:, 0:256], op=ALU.add),
                              reads=[psk[bO], 'osum%d' % t], writes=['osum%d' % t])
                if stop('R3'):
                    rstop = True
                    break
            if rstop:
                break
            tk.op('dve', lambda e: e.tensor_tensor(out=osq[:], in0=osum[:], in1=osum[:], op=ALU.mult), reads=['osum%d' % t_ for t_ in range(NT)], writes=['lK'])
            tk.op('dve', lambda e: e.tensor_reduce(out=hss[:], in_=osq[:].rearrange("p t (h d) -> p (t h) d", h=4), axis=AX.X, op=ALU.add),
                  reads=['lK'], writes=['hss'])
            tk.op('dve', lambda e: e.tensor_scalar(out=hss[:], in0=hss[:], scalar1=1.0 / 64, scalar2=EPS, op0=ALU.mult, op1=ALU.add), reads=['hss'], writes=['hss'])
            tk.op('act', lambda e: e.activation(out=hss[:], in_=hss[:], func=AF.Ln), reads=['hss'], writes=['hss'])
            tk.op('act', lambda e: e.activation(out=hss[:], in_=hss[:], func=AF.Exp, scale=-0.5), reads=['hss'], writes=['hss'])
            tk.op('dve', lambda e: e.tensor_tensor(out=osum[:].rearrange("p t (h d) -> p (t h) d", h=4), in0=osum[:].rearrange("p t (h d) -> p (t h) d", h=4),
                                                   in1=hss[:].unsqueeze(2).broadcast_to([128, 32, 64]), op=ALU.mult),
                  reads=['osum%d' % t_ for t_ in range(NT)] + ['hss'], writes=['osum%d' % t_ for t_ in range(NT)])
            tk.op('dve', lambda e: e.tensor_tensor(out=osum[:], in0=osum[:], in1=ghgB[:].unsqueeze(1).broadcast_to([128, NT, 256]), op=ALU.mult),
                  reads=['osum%d' % t_ for t_ in range(NT)] + ['ghgB'], writes=['osum%d' % t_ for t_ in range(NT)])
            tk.op('dve', lambda e: e.tensor_tensor(out=mixed[:, :, 512:768], in0=osum[:], in1=sgr[:], op=ALU.mult),
                  reads=['osum%d' % t_ for t_ in range(NT)] + ['sgr'], writes=['mixed%d' % j for j in range(8)])

            if stop('R'):
                break
            tk.barrier()
            areset()
            uT = aget([128, 2, T], BF16)
            sgf = aget([128, NT, 256], BF16)
            ucs = aget([128, NT, 2, 256], BF16)
            yT = aget([128, 2, T], BF16)
            csnb = [aget([128, 2, 1024], BF16) for _ in range(4)]
            assert apos[0] + 2048 <= 11520
            wfs = aget([128, 2, 256])
            wfb = aget([128, 2, 256], BF16)
            junk = aget([128, 512], BF16)
            tmpf = aget([128, D])
            for kt_ in range(4):
                tk.dma('sp', csnb[kt_][:], csn_d.ap()[kt_], writes=['csnb%d' % kt_])
            assert True
            tk.dma('sp', wfs[:], wfn_d.ap()[l].rearrange("(c p) n -> p c n", p=128), writes=['wfs'])
            tk.op('pool', lambda e: e.tensor_copy(out=wfb[:], in_=wfs[:]), reads=['wfs'], writes=['wfb'])
            for t in range(NT):
                b = nb()
                for cs in range(2):
                    for ct in range(2):
                        tk.op('pe', lambda e, t=t, cs=cs, ct=ct: e.matmul(ps[b][:, cs * 256 + ct * 128:cs * 256 + ct * 128 + 128], lhsT=uT[:, ct, t * 128:(t + 1) * 128],
                                                                          rhs=C4S4[:, 3 + cs * 2 + ct, :], start=True, stop=True),
                              reads=['uT', 'cb16'], writes=[psk[b]])
                tk.op('act', lambda e, t=t: e.copy(out=ucs[:, t, :, :].rearrange("p a c -> p (a c)"), in_=ps[b][:, :]), reads=[psk[b]], writes=['ucs'])
            if stop('F0'):
                break
            yb = [nb() for _ in range(4)]
            for kt_ in range(8):
                ci = kt_ % 4
                if kt_ >= 4:
                    tk.dma('sp', csnb[ci][:], csn_d.ap()[kt_], writes=['csnb%d' % ci])
                for ct in range(2):
                    for g in range(2):
                        b = yb[ct * 2 + g]
                        for cs in range(2):
                            tk.op('pe', lambda e, kt_=kt_, ct=ct, g=g, cs=cs: e.matmul(ps[b][:, :], lhsT=ucs[:, kt_, cs, ct * 128:(ct + 1) * 128],
                                                                                       rhs=csnb[ci][:, cs, g * 512:(g + 1) * 512],
                                                                                       start=(kt_ == 0 and cs == 0), stop=(kt_ == 7 and cs == 1)),
                                  reads=['ucs', 'csnb%d' % ci], writes=[psk[b]])
            for ct in range(2):
                for g in range(2):
                    b = yb[ct * 2 + g]
                    tk.op('act', lambda e, ct=ct, g=g, b=b: e.copy(out=yT[:, ct, g * 512:(g + 1) * 512], in_=ps[b][:, :]), reads=[psk[b]], writes=['yT'])
            for t in range(NT):
                b = nb()
                for ct in range(2):
                    tk.op('pe', lambda e, t=t, ct=ct: e.matmul(ps[b][:, 0:256], lhsT=yT[:, ct, t * 128:(t + 1) * 128], rhs=wfb[:, ct, :], start=(ct == 0), stop=(ct == 1)),
                          reads=['yT', 'wfb'], writes=[psk[b]])
                tk.op('dve', lambda e, t=t: e.tensor_tensor(out=mixed[:, t, 768:1024], in0=ps[b][:, 0:256], in1=sgf[:, t, :], op=ALU.mult),
                      reads=[psk[b], 'sgf'], writes=['mixed%d' % t])

            if dbg and l == nl - 1:
                for t in range(NT):
                    tk.dma('sp', dbg_d.ap()[t * 128:(t + 1) * 128, :], mixed[:, t, :], reads=['mixed%d' % t])

            if stop('F1'):
                break
            w0 = next_w()
            w1 = next_w(prefetch=False)
            for t in range(NT):
                b = nb()
                for kc in range(8):
                    tk.op('pe', lambda e, t=t, kc=kc: e.transpose(out=psb16(b)[:, kc * 128:(kc + 1) * 128], in_=mixed[:, t, kc * 128:(kc + 1) * 128], identity=IDB),
                          reads=['mixed%d' % t, 'cb16'], writes=[psk[b]])
                tk.op('act', lambda e, t=t: e.copy(out=hT[:, :, t * 128:(t + 1) * 128], in_=psb16(b)[:, :].rearrange("p (k c) -> p k c", k=8)),
                      reads=[psk[b]], writes=['hT'])
            for t in range(NT):
                bb = [nb(), nb()]
                for hf, wi in enumerate((w0, w1)):
                    proj_tm(wi, 0, 512, t, bb[hf])
                    tk.op('act', lambda e, t=t, hf=hf: e.activation(out=junk[:], in_=ps[bb[hf]][:, :], func=AF.Square, accum_out=ssq[:, 8 + hf:9 + hf]),
                          reads=[psk[bb[hf]]], writes=['junk', 'ssq'])
                tk.op('dve', lambda e: e.tensor_tensor(out=rstd[:, 8:9], in0=ssq[:, 8:9], in1=ssq[:, 9:10], op=ALU.add), reads=['ssq'], writes=['rstd'])
                tk.op('dve', lambda e: e.tensor_scalar(out=rstd[:, 8:9], in0=rstd[:, 8:9], scalar1=1.0 / D, scalar2=EPS, op0=ALU.mult, op1=ALU.add),
                      reads=['rstd'], writes=['rstd'])
                tk.op('act', lambda e: e.activation(out=rstd[:, 8:9], in_=rstd[:, 8:9], func=AF.Ln), reads=['rstd'], writes=['rstd'])
                tk.op('act', lambda e: e.activation(out=rstd[:, 8:9], in_=rstd[:, 8:9], func=AF.Exp, scale=-0.5), reads=['rstd'], writes=['rstd'])
                for hf in range(2):
                    tk.op('dve', lambda e, hf=hf: e.scalar_tensor_tensor(out=tmpf[:, hf * 512:(hf + 1) * 512], in0=ps[bb[hf]][:, :], scalar=rstd[:, 8:9],
                                                                         in1=gg[:, hf * 512:(hf + 1) * 512], op0=ALU.mult, op1=ALU.mult),
                          reads=[psk[bb[hf]], 'rstd', 'gg'], writes=['tmpf'])
                tk.op('dve', lambda e, t=t: e.tensor_tensor(out=x_sb[:, t, :], in0=x_sb[:, t, :], in1=tmpf[:], op=ALU.add),
                      reads=['x%d' % t, 'tmpf'], writes=['x%d' % t])
            _issue(wstate['ptr'])

        for t in range(NT):
            tk.dma('sp', y_d.ap()[t * 128:(t + 1) * 128, :], x_sb[:, t, :], reads=['x%d' % t])
        tk.finish()
    return nc


def _consts(is_sample):
    cf32 = np.zeros((128, 7, 128), np.float32)
    p = np.arange(128)
    J2 = np.zeros((128, 128), np.float32)
    for a in range(2):
        for i in range(64):
            J2[a * 64 + i, a * 64 + 63 - i] = 1.0
    cf32[:, 0] = J2
    cf32[:, 1] = np.eye(128, dtype=np.float32)
    cm = np.zeros((128, 128), np.float32)
    if is_sample:
        qc = np.arange(64)
        c0 = np.clip(qc - 8, 0, 48)
        kc = np.arange(64)
        valid = (kc[:, None] >= c0[None, :]) & (kc[:, None] < c0[None, :] + 16)
        m = np.where(valid, 0.0, NEG).astype(np.float32)
        cm = np.tile(m, (2, 2))
    cf32[:, 2] = cm
    s = np.arange(32)[:, None]
    t = np.arange(32)[None, :]
    trf = (s <= t).astype(np.float32) - (s <= 15).astype(np.float32)
    trb = (s >= t).astype(np.float32) - (s >= 16).astype(np.float32)
    for a in range(4):
        cf32[a * 32:(a + 1) * 32, 3, a * 32:(a + 1) * 32] = trf
        cf32[a * 32:(a + 1) * 32, 4, a * 32:(a + 1) * 32] = trb
    sl = np.arange(128) % 32
    ch = np.arange(128) // 32
    selcols = np.zeros((128, 128), np.float32)
    for a in range(4):
        selcols[:, a * 2 + 0] = ((ch == a) & (sl <= 15))
        selcols[:, a * 2 + 1] = (ch == a)
        selcols[:, 8 + a * 2 + 0] = ((ch == 3 - a) & (sl >= 16))
        selcols[:, 8 + a * 2 + 1] = (ch == 3 - a)
        selcols[:, 18 + a] = (ch == a)
    selcols[:, 16] = (np.arange(128) < 64)
    selcols[:, 17] = (np.arange(128) >= 64)
    cf32[:, 5] = selcols
    cf32[0:64, 6, 0:64] = 1.0
    cf32[64:128, 6, 64:128] = 1.0
    rowb = np.zeros((128, 74), np.float32)
    for i, (j, kt) in enumerate(JK):
        for hf in range(2):
            for krl in range(2):
                if is_sample:
                    qr = 2 * j + hf
                    kr = 2 * kt + krl
                    r0 = int(np.clip(qr - 4, 0, 8))
                    ok = (r0 <= kr < r0 + 8)
                else:
                    ok = (kt // 2 == j // 2)
                rowb[krl * 64:(krl + 1) * 64, i * 2 + hf] = 0.0 if ok else NEG
    cb16 = np.zeros((128, 9, 128), np.float32)
    cb16[:, 7] = J2
    cb16[:, 8] = cm
    cb16[:, 0] = np.eye(128)
    mf = (s <= t).astype(np.float32)
    mb = (s >= t).astype(np.float32)
    z = np.zeros((64, 64), np.float32)
    for a in range(4):
        cb16[a * 32:(a + 1) * 32, 1, a * 32:(a + 1) * 32] = mf
        cb16[a * 32:(a + 1) * 32, 2, a * 32:(a + 1) * 32] = mb
    ang = 2 * np.pi * np.outer(np.arange(64), np.arange(64)) / 64
    c4 = np.cos(ang) / 8.0
    s4 = np.sin(ang) / 8.0
    for ct in range(2):
        cb16[:, 3 + ct] = np.block([[c4, z], [z, c4]])
        cb16[:, 5 + ct] = np.block([[s4, z], [z, s4]])
    n = 1024 if is_sample else 256
    idx = np.arange(n)
    a2 = 2 * np.pi * ((np.outer(idx, idx)) % n) / n
    cn = np.cos(a2) / np.sqrt(n)
    sn = -np.sin(a2) / np.sqrt(n)
    CN = np.zeros((1024, 1024), np.float64)
    SN = np.zeros((1024, 1024), np.float64)
    for i in range(1024 // n):
        CN[i * n:(i + 1) * n, i * n:(i + 1) * n] = cn
        SN[i * n:(i + 1) * n, i * n:(i + 1) * n] = sn
    csn = np.stack([CN.reshape(8, 128, 1024), SN.reshape(8, 128, 1024)], axis=2)
    return dict(cf32=cf32, rowbias=rowb, cb16=cb16.astype(ml_dtypes.bfloat16), csn=csn.astype(ml_dtypes.bfloat16))


def _in_maps(x_prompt, x_sample, cache_attn_k, cache_attn_v, state_hgrn, c, c_ctx,
             w_ada, b_ada, g_pre, w_in, rpb, lb_logits, g_hgrn, w_fnet, w_out, g_post):
    f = lambda a: np.ascontiguousarray(np.asarray(a, dtype=np.float32))
    shared = dict(w_ada=f(w_ada), b_ada=f(b_ada), g_pre=f(g_pre), w_in=f(w_in), lb_logits=f(lb_logits),
                  g_hgrn=f(g_hgrn), w_fnet=f(w_fnet), w_out=f(w_out), g_post=f(g_post))
    tp = np.zeros((NL, 8, 23, 127), np.float32)
    tp[:, :, 4:19, 48:79] = f(rpb)
    cs = _consts(True)
    cp = _consts(False)
    maps = []
    for i in range(8):
        m = dict(shared)
        if i < 4:
            m["x"] = f(x_sample[i])
            m["cvec"] = f(np.asarray(c[i]).reshape(8, 128).T)
            m["ctxk"] = f(np.asarray(cache_attn_k[i]).reshape(NL, 512, 512))
            m["ctxv"] = f(np.asarray(cache_attn_v[i]).reshape(NL, 512, 512))
            s = np.asarray(state_hgrn[i]).reshape(NL, 2, 2, 2, 64, 64)
            m["s0"] = f(s.transpose(0, 1, 3, 4, 2, 5).reshape(NL, 2, 128, 2, 64))
            m["flags"] = np.ones((128, 2), np.float32)
            m["tpad"] = tp
            m.update(cs)
        else:
            m["x"] = f(np.asarray(x_prompt[4 * (i - 4):4 * (i - 3)]).reshape(T, D))
            m["cvec"] = f(np.asarray(c_ctx).reshape(8, 128).T)
            m["ctxk"] = np.zeros((NL, 512, 512), np.float32)
            m["ctxv"] = np.zeros((NL, 512, 512), np.float32)
            m["s0"] = np.zeros((NL, 2, 128, 2, 64), np.float32)
            m["flags"] = np.zeros((128, 2), np.float32)
            m["tpad"] = np.zeros_like(tp)
            m.update(cp)
        maps.append(m)
    return maps


_NC_CACHE = {}


def kernel(**inputs):
    if 'nc' not in _NC_CACHE:
        _NC_CACHE['nc'] = build_nc()
    nc = _NC_CACHE['nc']
    maps = _in_maps(**inputs)
    res = run_bass_kernel_spmd(nc, maps, core_ids=list(range(8)))
    r = res.results
    y_sample = np.stack([r[i]["y"] for i in range(4)], axis=0).astype(np.float32)
    y_prompt = np.concatenate([r[i]["y"].reshape(4, 256, D) for i in range(4, 8)], axis=0).astype(np.float32)
    nk = np.concatenate([r[i]["newk"].reshape(NL, 4, 256, 8, 64).transpose(1, 0, 2, 3, 4) for i in range(4, 8)], axis=0)
    nv = np.concatenate([r[i]["newv"].reshape(NL, 4, 256, 8, 64).transpose(1, 0, 2, 3, 4) for i in range(4, 8)], axis=0)
    ns = np.concatenate([r[i]["news"].reshape(NL, 2, 4, 2, 64, 2, 64).transpose(2, 0, 1, 5, 3, 4, 6).reshape(4, NL, 2, 4, 64, 64)
                         for i in range(4, 8)], axis=0)
    return (y_prompt, y_sample, np.ascontiguousarray(nk, dtype=np.float32), np.ascontiguousarray(nv, dtype=np.float32),
            np.ascontiguousarray(ns, dtype=np.float32))
```

```python
import numpy as np
import ml_dtypes
from contextlib import ExitStack
import concourse.bass as bass
import concourse.mybir as mybir
from concourse.bass_utils import run_bass_kernel_spmd

F32 = mybir.dt.float32
BF16 = mybir.dt.bfloat16
AF = mybir.ActivationFunctionType
ALU = mybir.AluOpType
AX = mybir.AxisListType

NL = 4
D = 1024
T = 1024
NT = 8
EPS = 1e-6
NEG = -30000.0
KT = {0: [0, 1, 2, 3], 1: [0, 1, 2, 3], 2: [0, 1, 2, 3, 4], 3: [1, 2, 3, 4, 5],
      4: [2, 3, 4, 5, 6], 5: [3, 4, 5, 6, 7], 6: [4, 5, 6, 7], 7: [4, 5, 6, 7]}
JK = [(j, kt) for j in range(8) for kt in KT[j]]
JKI = {p: i for i, p in enumerate(JK)}
NDS = 24
NSW = 72


class TK:
    def __init__(s, nc, st):
        s.nc = nc
        s.E = {'pe': nc.tensor, 'act': nc.scalar, 'dve': nc.vector, 'pool': nc.gpsimd, 'sp': nc.sync}
        s.sem = {k: st.enter_context(nc.semaphore('s_' + k)) for k in ('pe', 'act', 'dve', 'pool')}
        s.cnt = {k: 0 for k in s.E}
        s.seen = {k: {} for k in s.E}
        s.lw = {}
        s.rd = {}
        s.dsems = [st.enter_context(nc.semaphore('d%d' % i)) for i in range(NDS)]
        s.dcnt = [0] * NDS
        s.dnext = 0
        s.swsems = [st.enter_context(nc.semaphore('w%d' % i)) for i in range(NSW)]
        s.swnext = 0
        s.swlow = 0

    def _wait(s, eng, key, val):
        if eng == 'pe' and key == 'pe':
            return
        if s.seen[eng].get(key, 0) >= val:
            return
        if isinstance(key, str):
            semobj = s.sem[key]
        elif key >= 1000:
            semobj = s.swsems[key - 1000]
        else:
            semobj = s.dsems[key]
        s.E[eng].wait_ge(semobj, val)
        s.seen[eng][key] = val

    def _deps(s, eng, reads, writes):
        for k in reads:
            w = s.lw.get(k)
            if w:
                s._wait(eng, *w)
            if k.startswith('ps'):
                for rk, rv in s.rd.get(k, {}).items():
                    if rk != eng:
                        s._wait(eng, rk, rv)
        for k in writes:
            w = s.lw.get(k)
            if w:
                s._wait(eng, *w)
            for rk, rv in s.rd.get(k, {}).items():
                s._wait(eng, rk, rv)

    def _book(s, tag, reads, writes):
        for k in reads:
            d = s.rd.setdefault(k, {})
            d[tag[0]] = max(d.get(tag[0], 0), tag[1])
        for k in writes:
            s.lw[k] = tag
            s.rd[k] = {}

    def op(s, eng, fn, reads=(), writes=(), war_only=()):
        s._deps(eng, reads, writes)
        inst = fn(s.E[eng])
        s.cnt[eng] += 1
        inst.then_inc(s.sem[eng], 1)
        s._book((eng, s.cnt[eng]), tuple(reads) + tuple(war_only), writes)

    def dma(s, q, out, in_, reads=(), writes=()):
        if q == 'pool':
            assert s.swnext < NSW, "out of one-shot semaphores"
            i = s.swnext
            s.swnext += 1
            s._deps(q, reads, writes)
            s.E[q].dma_start(out=out, in_=in_).then_inc(s.swsems[i], 16)
            s._book((1000 + i, 16), reads, writes)
            return
        i = s.dnext
        s.dnext = (s.dnext + 1) % NDS
        if s.dcnt[i] > 0:
            s._wait(q, i, s.dcnt[i])
        s._deps(q, reads, writes)
        s.dcnt[i] += 16
        s.E[q].dma_start(out=out, in_=in_).then_inc(s.dsems[i], 16)
        s._book((i, s.dcnt[i]), reads, writes)

    def barrier(s):
        engs = ('pe', 'act', 'dve', 'pool', 'sp')
        snap = dict(s.cnt)
        dsnap = list(s.dcnt)
        for e in engs:
            for o in ('pe', 'act', 'dve', 'pool'):
                if o != e and snap[o] > 0:
                    s._wait(e, o, snap[o])
            for i in range(NDS):
                if dsnap[i] > 0:
                    s._wait(e, i, dsnap[i])
            for i in range(s.swlow, s.swnext):
                s._wait(e, 1000 + i, 16)
        s.swlow = s.swnext

    def finish(s):
        for i in range(NDS):
            if s.dcnt[i] > 0:
                s._wait('sp', i, s.dcnt[i])
        for i in range(s.swnext):
            s._wait('sp', 1000 + i, 16)
        for k in ('pe', 'act', 'dve', 'pool'):
            if s.cnt[k] > 0:
                s._wait('sp', k, s.cnt[k])


def build_nc(nl=NL, dbg=False, upto=None):
    nc = bass.Bass("TRN2", target_bir_lowering=False)
    _order = ['M0', 'M1', 'M2', 'M3', 'M', 'A0', 'A1', 'A1a', 'A1b', 'A1c', 'A2', 'A', 'R0', 'R1', 'R2', 'R3', 'R', 'F0', 'F1', 'F']

    def stop(p):
        return upto is not None and _order.index(upto) <= _order.index(p)

    def din(name, shape, dt=F32):
        return nc.dram_tensor(name, list(shape), dt, kind="ExternalInput")

    def dout(name, shape, dt=F32):
        return nc.dram_tensor(name, list(shape), dt, kind="ExternalOutput")

    x_d = din("x", [T, D])
    cvec_d = din("cvec", [128, 8])
    ctxk_d = din("ctxk", [NL, 512, 512])
    ctxv_d = din("ctxv", [NL, 512, 512])
    s0_d = din("s0", [NL, 2, 128, 2, 64])
    flags_d = din("flags", [128, 2])
    wada_d = din("w_ada", [NL, D, 3 * D])
    bada_d = din("b_ada", [NL, 3 * D])
    gpre_d = din("g_pre", [NL, D])
    win_d = din("w_in", [NL, D, 3840])
    tpad_d = din("tpad", [NL, 8, 23, 127])
    lbl_d = din("lb_logits", [2, NL, 256])
    ghg_d = din("g_hgrn", [NL, 256])
    wfn_d = din("w_fnet", [NL, 256, 256])
    wout_d = din("w_out", [NL, D, D])
    gpost_d = din("g_post", [NL, D])
    cf32_d = din("cf32", [128, 7, 128])
    rowb_d = din("rowbias", [128, 74])
    cb16_d = din("cb16", [128, 9, 128], BF16)
    csn_d = din("csn", [8, 128, 2, 1024], BF16)

    y_d = dout("y", [T, D])
    nk_d = dout("newk", [NL, T, 512])
    nv_d = dout("newv", [NL, T, 512])
    ns_d = dout("news", [NL, 2, 4, 128, 2, 64])
    dbg_d = dout("dbgmixed", [T, D], BF16) if dbg else None

    with ExitStack() as st:
        def sb(name, shape, dt=F32):
            return st.enter_context(nc.sbuf_tensor(name, list(shape), dt))

        tk = TK(nc, st)
        x_sb = sb("x_sb", [128, NT, D])
        hT = sb("hT", [128, 8, T], BF16)
        mixed = sb("mixed", [128, NT, D], BF16)
        wst = [sb("wst0", [128, 8, 512], BF16)]
        wbf = [sb("wbf%d" % i, [128, 8, 512], BF16) for i in range(2)]
        gg = sb("gg", [128, D])
        modN = sb("modN", [128, 3 * D], BF16)
        screp = sb("screp", [128, 8, 128], BF16)
        brow = sb("brow", [1, 512])
        ones_row = sb("ones_row", [1, 128])
        csil = sb("csil", [128, 8])
        cf32 = sb("cf32s", [128, 7, 128])
        rowb = sb("rowbs", [128, 74])
        cb16 = sb("cb16s", [128, 9, 128], BF16)
        flags = sb("flagss", [128, 2])
        lbl = sb("lbl", [128, 2, 256])
        oml = sb("oml", [128, 2, 256])
        ghgB = sb("ghgB", [128, 256])
        ssq = sb("ssq", [128, 16])
        rstd = sb("rstd", [128, 16])
        ARW = 21120
        arena = sb("arena", [128, ARW])
        apos = [0]

        def areset():
            apos[0] = 0

        def aget(shape, dt=F32):
            n = 1
            for d_ in shape[1:]:
                n *= d_
            words = n if dt == F32 else (n + 1) // 2
            a0 = apos[0]
            apos[0] += words
            assert apos[0] <= ARW, ("arena overflow", apos[0])
            v = arena[:, a0:a0 + words]
            if dt != F32:
                v = v.bitcast(dt)
            if len(shape) == 3:
                v = v.rearrange("p (a b) -> p a b", a=shape[1])
            elif len(shape) == 4:
                v = v.rearrange("p (a b c) -> p a b c", a=shape[1], b=shape[2])
            return v

        psbig = [st.enter_context(nc.psum_tensor("psb%d" % i, [128, 1024], F32)) for i in range(4)]
        ps = [psbig[i // 2][:, (i % 2) * 512:(i % 2 + 1) * 512] for i in range(8)]
        psk = ["ps%d" % i for i in range(8)]
        bank_rr = [0]

        def nb(avoid=()):
            while True:
                b = bank_rr[0]
                bank_rr[0] = (b + 1) % 8
                if b not in avoid:
                    return b

        J2 = cf32[:, 0, :]
        IDF = cf32[:, 1, :]
        CMT = cf32[:, 2, :]
        TR = [cf32[:, 3, :], cf32[:, 4, :]]
        SEL = [cf32[:, 5, 0:8], cf32[:, 5, 8:16]]
        HM = cf32[:, 5, 16:18]
        QM = cf32[:, 5, 18:22]
        HME = cf32[:, 6, :].rearrange("p (a c) -> p a c", a=2)
        IDB = cb16[:, 0, :]
        TRI = [cb16[:, 1, :], cb16[:, 2, :]]
        C4S4 = cb16
        J2B = cb16[:, 7, :]
        CMTB = cb16[:, 8, :]

        tk.dma('sp', cf32[:], cf32_d.ap(), writes=['cf32'])
        tk.dma('sp', rowb[:], rowb_d.ap(), writes=['rowb'])
        tk.dma('sp', cb16[:], cb16_d.ap(), writes=['cb16'])
        tk.dma('sp', flags[:], flags_d.ap(), writes=['flags'])
        tk.dma('sp', csil[:], cvec_d.ap(), writes=['csil'])
        for t in range(NT):
            tk.dma('sp', x_sb[:, t, :], x_d.ap()[t * 128:(t + 1) * 128, :], writes=['x%d' % t])
        tk.op('pool', lambda e: e.memset(ones_row[:], 1.0), writes=['ones_row'])
        tk.op('act', lambda e: e.activation(out=csil[:], in_=csil[:], func=AF.Silu), reads=['csil'], writes=['csil'])

        wring = [0]

        wbring = [0]

        def load_w(src_ap, ncols):
            wi = wbring[0]
            wbring[0] ^= 1
            tk.dma('pool', wbf[wi][:, :, 0:ncols], src_ap.rearrange("(kc p) n -> p kc n", p=128), writes=['wbf%d' % wi])
            return wi

        wseq = []
        for l_ in range(nl):
            for (c0_, n_) in ((0, 512), (512, 512), (1024, 512), (1536, 512), (2048, 512), (2816, 512), (2560, 256), (3328, 512)):
                wseq.append((win_d, l_, c0_, n_))
            wseq.append((wout_d, l_, 0, 512))
            wseq.append((wout_d, l_, 512, 512))
        wstate = {'ptr': 0, 'loaded': {}}

        def _issue(i):
            if i < len(wseq) and i not in wstate['loaded']:
                d_, l_, c0_, n_ = wseq[i]
                wstate['loaded'][i] = load_w(d_.ap()[l_, :, c0_:c0_ + n_], n_)

        def next_w(prefetch=True):
            i = wstate['ptr']
            wstate['ptr'] += 1
            _issue(i)
            if prefetch:
                _issue(i + 1)
            return wstate['loaded'][i]

        def proj_tm(wi, c0, ncols, t, b):
            for kc in range(8):
                tk.op('pe', lambda e, kc=kc: e.matmul(ps[b][:, 0:ncols], lhsT=hT[:, kc, t * 128:(t + 1) * 128],
                                                     rhs=wbf[wi][:, kc, c0:c0 + ncols], start=(kc == 0), stop=(kc == 7)),
                      reads=['hT', 'wbf%d' % wi], writes=[psk[b]])

        def proj_fm(wi, ci, g, b):
            for kc in range(8):
                tk.op('pe', lambda e, kc=kc: e.matmul(ps[b][:, 0:512], lhsT=wbf[wi][:, kc, ci * 128:(ci + 1) * 128],
                                                     rhs=hT[:, kc, g * 512:(g + 1) * 512], start=(kc == 0), stop=(kc == 7)),
                      reads=['hT', 'wbf%d' % wi], writes=[psk[b]])

        def psb16(b):
            return ps[b].bitcast(BF16)

        def emit_mod_chunk(lm, ch, banks=None):
            mod_dma(lm, ch)
            mod_mm(lm, ch, banks)

        def mod_dma(lm, ch):
            tk.dma('pool', wst[0][:], wada_d.ap()[lm, :, ch * 512:(ch + 1) * 512].rearrange("(kc p) n -> p kc n", p=128), writes=['wst0'])
            tk.dma('sp', brow[:], bada_d.ap()[lm:lm + 1, ch * 512:(ch + 1) * 512], writes=['brow'])

        def mod_mm(lm, ch, banks=None):
            b = nb() if banks is None else banks[ch % len(banks)]
            for kc in range(8):
                tk.op('pe', lambda e, kc=kc: e.matmul(ps[b][:, :], lhsT=screp[:, kc, :], rhs=wst[0][:, kc, :], start=(kc == 0), stop=False),
                      reads=['screp', 'wst0'], writes=[psk[b]])
            tk.op('pe', lambda e: e.matmul(ps[b][:, :], lhsT=ones_row[0:1, :], rhs=brow[0:1, :], start=False, stop=True),
                  reads=['ones_row', 'brow'], writes=[psk[b]])
            tk.op('act', lambda e: e.copy(out=modN[:, ch * 512:(ch + 1) * 512], in_=ps[b][:, :]), reads=[psk[b]], writes=['modN'])

        tk.op('dve', lambda e: e.tensor_copy(out=screp[:], in_=csil[:].unsqueeze(2).broadcast_to([128, 8, 128])), reads=['csil'], writes=['screp'])
        for ch in range(6):
            emit_mod_chunk(0, ch)

        for l in range(nl):
            if l == 0:
                tk.barrier()
            areset()
            apos[0] = 11520
            gbc = aget([128, 2, D])
            lbt = aget([128, 2, NL, 256])
            junk = aget([128, D], BF16)
            tmpf = aget([128, D])
            hb = [aget([128, D], BF16) for _ in range(2)]
            modA = aget([128, D])
            tk.dma('sp', gbc[:, 0, :], bass.AP(gpre_d, l * D, [[0, 128], [1, D]]), writes=['gbc'])
            tk.dma('sp', gbc[:, 1, :], bass.AP(gpost_d, l * D, [[0, 128], [1, D]]), writes=['gbc'])
            tk.dma('sp', lbt[:].rearrange("p a l c -> p (a l c)"), bass.AP(lbl_d, 0, [[0, 128], [1, 2 * NL * 256]]), writes=['lbt'])
            if l == 0:
                tk.op('dve', lambda e: e.memset(lbl[:], 0.0), writes=['lbl'])
            else:
                mx = tmpf[:, 0:512].rearrange("p (a c) -> p a c", a=2)
                sm = tmpf[:, 512:1024].rearrange("p (a c) -> p a c", a=2)
                tk.op('dve', lambda e: e.tensor_tensor(out=mx, in0=lbt[:, :, 0, :], in1=lbt[:, :, 1, :], op=ALU.max),
                      reads=['lbt'], writes=['tmpf'])
                for l2 in range(2, NL):
                    tk.op('dve', lambda e, l2=l2: e.tensor_tensor(out=mx, in0=mx, in1=lbt[:, :, l2, :], op=ALU.max),
                          reads=['lbt', 'tmpf'], writes=['tmpf'])
                for l2 in range(NL):
                    tk.op('dve', lambda e, l2=l2: e.tensor_tensor(out=lbt[:, :, l2, :], in0=lbt[:, :, l2, :], in1=mx, op=ALU.subtract),
                          reads=['lbt', 'tmpf'], writes=['lbt'])
                tk.op('act', lambda e: e.activation(out=lbt[:].rearrange("p a l c -> p (a l c)"), in_=lbt[:].rearrange("p a l c -> p (a l c)"), func=AF.Exp),
                      reads=['lbt'], writes=['lbt'])
                tk.op('dve', lambda e: e.tensor_tensor(out=sm, in0=lbt[:, :, 0, :], in1=lbt[:, :, 1, :], op=ALU.add),
                      reads=['lbt'], writes=['tmpf'])
                for l2 in range(2, NL):
                    tk.op('dve', lambda e, l2=l2: e.tensor_tensor(out=sm, in0=sm, in1=lbt[:, :, l2, :], op=ALU.add),
                          reads=['lbt', 'tmpf'], writes=['tmpf'])
                tk.op('dve', lambda e: e.reciprocal(out=sm, in_=sm), reads=['tmpf'], writes=['tmpf'])
                tk.op('dve', lambda e: e.tensor_copy(out=lbl[:], in_=lbt[:, :, 1, :]), reads=['lbt'], writes=['lbl'])
                for l2 in range(2, l + 1):
                    tk.op('dve', lambda e, l2=l2: e.tensor_tensor(out=lbl[:], in0=lbl[:], in1=lbt[:, :, l2, :], op=ALU.add),
                          reads=['lbt', 'lbl'], writes=['lbl'])
                tk.op('dve', lambda e: e.tensor_tensor(out=lbl[:], in0=lbl[:], in1=sm, op=ALU.mult), reads=['lbl', 'tmpf'], writes=['lbl'])
            tk.op('dve', lambda e: e.tensor_scalar(out=oml[:], in0=lbl[:], scalar1=-0.5, scalar2=0.5, op0=ALU.mult, op1=ALU.add),
                  reads=['lbl'], writes=['oml'])
            tk.op('dve', lambda e: e.tensor_scalar(out=lbl[:], in0=lbl[:], scalar1=0.5, scalar2=0.5, op0=ALU.mult, op1=ALU.add),
                  reads=['lbl'], writes=['lbl'])
            if stop('M0'):
                break
            tk.op('dve', lambda e: e.scalar_tensor_tensor(out=modA[:], in0=modN[:, D:2 * D], scalar=1.0, in1=gbc[:, 0, :], op0=ALU.add, op1=ALU.mult),
                  reads=['modN', 'gbc'], writes=['modA'])
            tk.op('dve', lambda e: e.tensor_tensor(out=gg[:], in0=modN[:, 2 * D:3 * D], in1=gbc[:, 1, :], op=ALU.mult), reads=['modN', 'gbc'], writes=['gg'])
            if stop('M1'):
                break
            for t in range(NT):
                tk.op('act', lambda e, t=t: e.activation(out=junk[:], in_=x_sb[:, t, :], func=AF.Square, accum_out=ssq[:, t:t + 1]),
                      reads=['x%d' % t], writes=['junk', 'ssq'])
            tk.op('dve', lambda e: e.tensor_scalar(out=rstd[:, 0:8], in0=ssq[:, 0:8], scalar1=1.0 / D, scalar2=EPS, op0=ALU.mult, op1=ALU.add),
                  reads=['ssq'], writes=['rstd'])
            tk.op('act', lambda e: e.activation(out=rstd[:, 0:8], in_=rstd[:, 0:8], func=AF.Ln), reads=['rstd'], writes=['rstd'])
            tk.op('act', lambda e: e.activation(out=rstd[:, 0:8], in_=rstd[:, 0:8], func=AF.Exp, scale=-0.5), reads=['rstd'], writes=['rstd'])
            if stop('M2'):
                break
            for t in range(NT):
                hbi = t % 2
                tk.op('dve', lambda e, t=t: e.scalar_tensor_tensor(out=tmpf[:], in0=x_sb[:, t, :], scalar=rstd[:, t:t + 1], in1=modA[:],
                                                                   op0=ALU.mult, op1=ALU.mult),
                      reads=['x%d' % t, 'rstd', 'modA'], writes=['tmpf'])
                tk.op('dve', lambda e: e.tensor_tensor(out=hb[hbi][:], in0=tmpf[:], in1=modN[:, 0:D], op=ALU.add),
                      reads=['tmpf', 'modN'], writes=['hb%d' % hbi])
                if stop('M3'):
                    continue
                b = nb()
                for kc in range(8):
                    tk.op('pe', lambda e, kc=kc: e.transpose(out=psb16(b)[:, kc * 128:(kc + 1) * 128], in_=hb[hbi][:, kc * 128:(kc + 1) * 128], identity=IDB),
                          reads=['hb%d' % hbi, 'cb16'], writes=[psk[b]])
                tk.op('act', lambda e, t=t: e.copy(out=hT[:, :, t * 128:(t + 1) * 128], in_=psb16(b)[:, :].rearrange("p (k c) -> p k c", k=8)),
                      reads=[psk[b]], writes=['hT'])

            if stop('M'):
                break
            tk.barrier()
            areset()
            qT = aget([128, 4, T], BF16)
            kT = aget([128, 4, T], BF16)
            ckT = aget([128, 4, 512], BF16)
            vaug = aget([128, NT, 8, 66], BF16)
            cvaug = aget([128, 4, 8, 66], BF16)
            sga = aget([128, NT, 512], BF16)
            expT = aget([128, 7, 8, 128], BF16)
            Eb = [aget([128, 8, 128], BF16) for _ in range(3)]
            Pb = [aget([128, 8, 128], BF16) for _ in range(2)]
            hk = [Eb[0].bitcast(F32) if False else None, None]
            ost = [aget([128, 512]) for _ in range(2)]
            rden = aget([128, 8])
            otmp = aget([128, 8, 64])
            ckb = aget([128, 4, 512], BF16)
            hkA = aget([128, 8, 128])
            hkB = aget([128, 8, 128])
            hk = [hkA, hkB]
            tk.op('pool', lambda e: e.memset(vaug[:, :, :, 64:66], 1.0), writes=['vaug'])
            tk.op('dve', lambda e: e.tensor_copy(out=cvaug[:, :, :, 64:66].rearrange("p a b c -> p (a b) c"),
                                                 in_=flags[:, 0:1].unsqueeze(2).broadcast_to([128, 32, 2])),
                  reads=['flags'], writes=['cvaug'])
            def toep_dma(di):
                dl = di - 3
                hi = di % 2
                for qr in range(2):
                    for krl in range(2):
                        off = ((l * 8) * 23 + (2 * dl + krl - qr + 11)) * 127
                        src = bass.AP(tpad_d, off, [[1, 64], [23 * 127, 8], [1, 64]])
                        tk.dma('sp', hk[hi][qr * 64:(qr + 1) * 64, :, krl * 64:(krl + 1) * 64], src, writes=['hk%d_%d' % (hi, qr * 2 + krl)])

            def toep_mm(di):
                hi = di % 2
                bA = nb()
                bB = nb()
                for h in range(8):
                    b = bA if h % 2 == 0 else bB
                    o = ps[b][:, (h // 2) * 128:(h // 2 + 1) * 128]
                    tk.op('pe', lambda e, h=h, o=o: e.matmul(o, lhsT=hk[hi][:, h, :], rhs=J2, start=True, stop=False),
                          reads=['hk%d_%d' % (hi, x) for x in range(4)] + ['cf32'], writes=[psk[b]])
                    tk.op('pe', lambda e, o=o: e.matmul(o, lhsT=IDF, rhs=CMT, start=False, stop=True),
                          reads=['cf32'], writes=[psk[b]])
                for bi, b in enumerate((bA, bB)):
                    tk.op('act', lambda e, bi=bi, b=b: e.activation(out=expT[:, di, bi * 4:(bi + 1) * 4, :].rearrange("p a c -> p (a c)"),
                                                                    in_=ps[b][:, :], func=AF.Exp),
                          reads=[psk[b]], writes=['expT'])

            toep_dma(0)
            toep_dma(1)
            if stop('A0'):
                break
            ckk = 'ckb'
            tk.dma('pool', ckb[:], ctxk_d.ap()[l].rearrange("(c p) n -> p c n", p=128), writes=[ckk])
            for c in range(4):
                b = nb()
                for pr in range(4):
                    tk.op('pe', lambda e, pr=pr: e.transpose(out=psb16(b)[:, pr * 128:(pr + 1) * 128], in_=ckb[:, c, pr * 128:(pr + 1) * 128], identity=IDB),
                          reads=[ckk, 'cb16'], writes=[psk[b]])
                tk.op('act', lambda e, c=c: e.copy(out=ckT[:, :, c * 128:(c + 1) * 128], in_=psb16(b)[:, 0:512].rearrange("p (k c) -> p k c", k=4)),
                      reads=[psk[b]], writes=['ckT'])
            tk.dma('pool', ckb[:], ctxv_d.ap()[l].rearrange("(c p) n -> p c n", p=128), reads=[], writes=[ckk])
            tk.op('pool', lambda e: e.tensor_copy(out=cvaug[:, :, :, 0:64], in_=ckb[:].rearrange("p c (h d) -> p c h d", h=8)), reads=[ckk], writes=['cvaug'])
            if stop('A1'):
                break
            toep_mm(0)
            toep_dma(2)
            wi = next_w()
            for pr in range(4):
                for g in range(2):
                    b = nb()
                    proj_fm(wi, pr, g, b)
                    tk.op('act', lambda e, pr=pr, g=g: e.copy(out=qT[:, pr, g * 512:(g + 1) * 512], in_=ps[b][:, :]), reads=[psk[b]], writes=['qT'])
            if stop('A1a'):
                break
            toep_mm(1)
            toep_dma(3)
            wi = next_w()
            for pr in range(4):
                for g in range(2):
                    b = nb()
                    proj_fm(wi, pr, g, b)
                    tk.op('act', lambda e, pr=pr, g=g: e.copy(out=kT[:, pr, g * 512:(g + 1) * 512], in_=ps[b][:, :]), reads=[psk[b]], writes=['kT'])
            toep_mm(2)
            toep_dma(4)
            for t in range(NT):
                b = nb()
                proj_tm(wi, 0, 512, t, b)
                oi = t % 2
                tk.op('dve', lambda e: e.tensor_copy(out=ost[oi][:], in_=ps[b][:, :]), reads=[psk[b]], writes=['ost%d' % oi])
                tk.dma('sp', nk_d.ap()[l, t * 128:(t + 1) * 128, :], ost[oi][:], reads=['ost%d' % oi])
            toep_mm(3)
            toep_dma(5)
            if stop('A1b'):
                break
            wi = next_w()
            for t in range(NT):
                b = nb()
                proj_tm(wi, 0, 512, t, b)
                oi = t % 2
                tk.op('dve', lambda e: e.tensor_copy(out=ost[oi][:], in_=ps[b][:, :]), reads=[psk[b]], writes=['ost%d' % oi])
                tk.op('act', lambda e, t=t: e.copy(out=vaug[:, t, :, 0:64], in_=ost[oi][:].rearrange("p (h d) -> p h d", h=8)),
                      reads=['ost%d' % oi], writes=['vaug'])
                tk.dma('sp', nv_d.ap()[l, t * 128:(t + 1) * 128, :], ost[oi][:], reads=['ost%d' % oi])
            if stop('A1c'):
                break
            toep_mm(4)
            toep_dma(6)
            wi = next_w()
            for t in range(NT):
                b = nb()
                proj_tm(wi, 0, 512, t, b)
                tk.op('act', lambda e, t=t: e.activation(out=sga[:, t, :], in_=ps[b][:, :], func=AF.Silu), reads=[psk[b]], writes=['sga%d' % t])
            toep_mm(5)
            toep_mm(6)
            if stop('A2'):
                break
            OA, OB = 6, 7
            spairs = [(0, 1), (2, 3), (4, 5)]
            allsteps = []
            for j in range(8):
                st_ = [('l', kt) for kt in KT[j]] + [('c', c) for c in range(4)]
                for si_, (kind, idx) in enumerate(st_):
                    allsteps.append((j, si_, len(st_), kind, idx))

            def emit_S(k):
                j, si_, ns_, kind, idx = allsteps[k]
                sA, sB = spairs[k % 3]
                for h in range(8):
                    b = sA if h % 2 == 0 else sB
                    r0 = (h % 2) * 64
                    ksrc = kT[r0:r0 + 64, h // 2, idx * 128:(idx + 1) * 128] if kind == 'l' else ckT[r0:r0 + 64, h // 2, idx * 128:(idx + 1) * 128]
                    tk.op('pe', lambda e, h=h, b=b, ksrc=ksrc, r0=r0: e.matmul(ps[b][:, (h // 2) * 128:(h // 2 + 1) * 128], lhsT=ksrc,
                                                                              rhs=qT[r0:r0 + 64, h // 2, j * 128:(j + 1) * 128], start=True, stop=True),
                          reads=['qT', 'kT' if kind == 'l' else 'ckT'], writes=[psk[b]])

            def emit_rest(k):
                j, si_, ns_, kind, idx = allsteps[k]
                sA, sB = spairs[k % 3]
                sl = k % 3
                big = psbig[sA // 2]
                if kind == 'l':
                    jk = JKI[(j, idx)]
                    for hf in range(2):
                        tk.op('act', lambda e, hf=hf: e.activation(
                            out=Eb[sl][:, :, hf * 64:(hf + 1) * 64],
                            in_=big[:, :].rearrange("p (a c) -> p a c", a=8)[:, :, hf * 64:(hf + 1) * 64],
                            func=AF.Exp, scale=0.125, bias=rowb[:, jk * 2 + hf:jk * 2 + hf + 1]),
                            reads=[psk[sA], psk[sB], 'rowb'], writes=['Eb%d' % sl])
                    di = idx - j + 3
                    pl = k % 2
                    tk.op('dve', lambda e, di=di: e.tensor_tensor(out=Pb[pl][:], in0=Eb[sl][:], in1=expT[:, di, :, :], op=ALU.mult),
                          reads=['Eb%d' % sl, 'expT'], writes=['Pb%d' % pl])
                    lhs, lk = Pb[pl], 'Pb%d' % pl
                    vsrc, vk = vaug, 'vaug'
                else:
                    tk.op('act', lambda e: e.activation(out=Eb[sl][:].rearrange("p a c -> p (a c)"), in_=big[:, :], func=AF.Exp, scale=0.125),
                          reads=[psk[sA], psk[sB]], writes=['Eb%d' % sl])
                    lhs, lk = Eb[sl], 'Eb%d' % sl
                    vsrc, vk = cvaug, 'cvaug'
                for e_ in range(8):
                    h = 2 * (e_ % 4) + e_ // 4
                    ob = OA if e_ < 4 else OB
                    tk.op('pe', lambda e, e_=e_, h=h, ob=ob, lhs=lhs, vsrc=vsrc: e.matmul(
                        ps[ob][:, (e_ % 4) * 66:(e_ % 4) * 66 + 66], lhsT=lhs[:, e_, :], rhs=vsrc[:, idx, h, :],
                        start=(si_ == 0 and e_ % 4 == 0), stop=(si_ == ns_ - 1), skip_group_check=True),
                        reads=[lk, vk], writes=[psk[ob]])
                if si_ == ns_ - 1:
                    for bi, ob in enumerate((OA, OB)):
                        tk.op('dve', lambda e, bi=bi, ob=ob: e.reciprocal(out=rden[:, bi * 4:(bi + 1) * 4],
                                                                          in_=ps[ob][:, 0:264].rearrange("p (a c) -> p a c", a=4)[:, :, 64]),
                              reads=[psk[ob]], writes=['rden'])
                    for bi, ob in enumerate((OA, OB)):
                        tk.op('dve', lambda e, bi=bi, ob=ob: e.tensor_tensor(
                            out=otmp[:, bi:8:2, :], in0=ps[ob][:, 0:264].rearrange("p (a c) -> p a c", a=4)[:, :, 0:64],
                            in1=rden[:, bi * 4:(bi + 1) * 4].unsqueeze(2).broadcast_to([128, 4, 64]), op=ALU.mult),
                            reads=[psk[ob], 'rden'], writes=['otmp'])
                    tk.op('dve', lambda e: e.tensor_tensor(out=mixed[:, j, 0:512], in0=otmp[:].rearrange("p h d -> p (h d)"), in1=sga[:, j, :], op=ALU.mult),
                          reads=['otmp', 'sga%d' % j], writes=['mixed%d' % j])

            emit_S(0)
            emit_S(1)
            for k in range(len(allsteps)):
                if k + 2 < len(allsteps):
                    emit_S(k + 2)
                emit_rest(k)

            if stop('A'):
                break
            tk.barrier()
            areset()
            qh = aget([128, NT, 256], BF16)
            sgb = aget([128, NT, 256])
            qE = sgb.rearrange("p a b -> p (a b)")[:, 0:1024].bitcast(BF16).rearrange("p (a b) -> p a b", a=NT)
            kE = sgb.rearrange("p a b -> p (a b)")[:, 1024:2048].bitcast(BF16).rearrange("p (a b) -> p a b", a=NT)
            vh = aget([128, NT, 256], BF16)
            vhmF = aget([128, 4096])
            vhm = vhmF.bitcast(BF16).rearrange("p (q t c) -> p q t c", q=4, t=NT)
            A_ = vhmF[:, 0:2048].rearrange("p (e c) -> p e c", e=64)
            B_ = vhmF[:, 2048:4096].rearrange("p (e c) -> p e c", e=64)
            sgr = aget([128, NT, 256], BF16)
            fS = aget([128, 4096])
            fbuf = fS[:, 0:2048].rearrange("p (a b) -> p a b", a=NT)
            SinPm = fS.bitcast(BF16).rearrange("p (h c e) -> p h c e", h=4, c=32)
            lfbuf = aget([128, NT, 256])
            kETm = lfbuf.rearrange("p a b -> p (a b)").bitcast(BF16).rearrange("p (h t) -> p h t", h=4)
            osq = lfbuf
            tmpE = [aget([128, 512]) for _ in range(2)]
            qET = aget([128, 2, T], BF16)
            ATm = [aget([128, 4, 128], BF16) for _ in range(2)]
            osum = aget([128, NT, 256])
            gs3 = aget([128, 2, 32, 3])
            es3 = aget([128, 2, 32, 3])
            S0b = aget([128, 2, 64])
            nsb = aget([128, 4, 2, 64])
            hss = aget([128, 32])
            tk.dma('sp', ghgB[:], bass.AP(ghg_d, l * 256, [[0, 128], [1, 256]]), writes=['ghgB'])
            wi = next_w()
            for t in range(NT):
                b = nb()
                proj_tm(wi, 0, 512, t, b)
                tk.op('act', lambda e, t=t: e.activation(out=qh[:, t, :], in_=ps[b][:, 0:256], func=AF.Silu), reads=[psk[b]], writes=['qh'])
                tk.op('act', lambda e, t=t: e.activation(out=sgb[:, t, :], in_=ps[b][:, 256:512], func=AF.Tanh, scale=0.5), reads=[psk[b]], writes=['sQ'])
            wi = next_w()
            for t in range(NT):
                b = nb()
                proj_tm(wi, 0, 512, t, b)
                tk.op('act', lambda e, t=t: e.activation(out=sgr[:, t, :], in_=ps[b][:, 256:512], func=AF.Silu), reads=[psk[b]], writes=['sgr'])
                tk.op('dve', lambda e, t=t: e.tensor_copy(out=vh[:, t, :], in_=ps[b][:, 0:256]), reads=[psk[b]], writes=['vh'])

            if stop('R0'):
                break
            rstop = False
            for dr in range(2):
                if l + 1 < nl:
                    mod_dma(l + 1, 3 * dr)
                lb_bc = lbl[:, dr, :].unsqueeze(1).broadcast_to([128, NT, 256])
                oml_bc = oml[:, dr, :].unsqueeze(1).broadcast_to([128, NT, 256])
                tk.op('dve', lambda e: e.tensor_tensor(out=fbuf[:], in0=sgb[:], in1=oml_bc, op=ALU.mult), reads=['sQ', 'oml'], writes=['fS'])
                tk.op('dve', lambda e: e.tensor_tensor(out=fbuf[:], in0=fbuf[:], in1=lb_bc, op=ALU.add), reads=['fS', 'lbl'], writes=['fS'])
                tk.op('act', lambda e: e.activation(out=lfbuf[:].rearrange("p a b -> p (a b)"), in_=fbuf[:].rearrange("p a b -> p (a b)"), func=AF.Ln),
                      reads=['fS'], writes=['lK'])
                tk.op('dve', lambda e: e.tensor_scalar(out=fbuf[:], in0=fbuf[:], scalar1=-1.0, scalar2=1.0, op0=ALU.mult, op1=ALU.add),
                      reads=['fS'], writes=['fS'])
                bS = nb()
                for t in range(NT):
                    for pr in range(2):
                        tt_ = t if dr == 0 else 7 - t
                        tk.op('pe', lambda e, t=t, pr=pr, tt_=tt_: e.matmul(ps[bS][:, (pr * 8 + tt_) * 8:(pr * 8 + tt_) * 8 + 8], lhsT=lfbuf[:, t, pr * 128:(pr + 1) * 128],
                                                                   rhs=SEL[dr], start=True, stop=True),
                              reads=['lK', 'cf32'], writes=[psk[bS]])
                psS = ps[bS][:, 0:128].rearrange("p (a c r) -> p a c r", a=2, c=32)
                tk.op('dve', lambda e: e.tensor_copy(out=gs3[:, :, :, 0:2], in_=psS), reads=[psk[bS]], writes=['gs3'])
                tk.op('dve', lambda e: e.tensor_tensor(out=gs3[:, :, :, 2], in0=gs3[:, :, :, 1], in1=gs3[:, :, :, 0], op=ALU.subtract),
                      reads=['gs3'], writes=['gs3'])
                tk.op('act', lambda e: e.activation(out=es3[:].rearrange("p a c r -> p (a c r)"), in_=gs3[:].rearrange("p a c r -> p (a c r)"), func=AF.Exp),
                      reads=['gs3'], writes=['es3'])
                tk.op('dve', lambda e: e.tensor_scalar(out=es3[:, :, 8:32:8, 0:2], in0=es3[:, :, 8:32:8, 0:2], scalar1=flags[:, 1:2], scalar2=None, op0=ALU.mult),
                      reads=['es3', 'flags'], writes=['es3'])
                for tp in range(4):
                    b = nb()
                    for i in range(2):
                        t = 2 * tp + i
                        tk.op('pe', lambda e, t=t, i=i: e.matmul(ps[b][:, i * 256:(i + 1) * 256], lhsT=TR[dr], rhs=lfbuf[:, t, :], start=True, stop=True),
                              reads=['cf32', 'lK'], writes=[psk[b]])
                    tk.op('act', lambda e: e.activation(out=tmpE[0][:], in_=ps[b][:, :], func=AF.Exp), reads=[psk[b]], writes=['tmpE0'])
                    tk.op('act', lambda e: e.activation(out=tmpE[1][:], in_=ps[b][:, :], func=AF.Exp, scale=-1.0), reads=[psk[b]], writes=['tmpE1'])
                    tk.op('dve', lambda e, tp=tp: e.tensor_tensor(out=qE[:, 2 * tp:2 * tp + 2, :].rearrange("p a c -> p (a c)"),
                                                                  in0=qh[:, 2 * tp:2 * tp + 2, :].rearrange("p a c -> p (a c)"), in1=tmpE[0][:], op=ALU.mult),
                          reads=['qh', 'tmpE0'], writes=['sQ', 'qk%d' % tp])
                    tk.op('dve', lambda e, tp=tp: e.tensor_tensor(out=kE[:, 2 * tp:2 * tp + 2, :].rearrange("p a c -> p (a c)"),
                                                                  in0=fbuf[:, 2 * tp:2 * tp + 2, :].rearrange("p a c -> p (a c)"), in1=tmpE[1][:], op=ALU.mult),
                          reads=['fS', 'tmpE1'], writes=['sQ', 'qk%d' % tp])
                for cp in range(4):
                    tk.op('dve', lambda e, cp=cp: e.tensor_scalar(out=vhm[:, cp, :, :], in0=vh[:], scalar1=QM[:, cp:cp + 1], scalar2=None, op0=ALU.mult),
                          reads=['vh', 'cf32'], writes=['vhm'])
                tk._deps('act', [], ['lK'])
                for t in range(NT):
                    b = nb()
                    for pr in range(2):
                        tk.op('pe', lambda e, t=t, pr=pr: e.transpose(out=psb16(b)[:, pr * 128:(pr + 1) * 128], in_=qE[:, t, pr * 128:(pr + 1) * 128], identity=IDB),
                              reads=['qk%d' % (t // 2), 'cb16'], writes=[psk[b]], war_only=['sQ'])
                        tk.op('pe', lambda e, t=t, pr=pr: e.transpose(out=psb16(b)[:, (2 + pr) * 128:(3 + pr) * 128], in_=kE[:, t, pr * 128:(pr + 1) * 128], identity=IDB),
                              reads=['qk%d' % (t // 2), 'cb16'], writes=[psk[b]], war_only=['sQ'])
                    tk.op('act', lambda e, t=t: e.copy(out=qET[:, :, t * 128:(t + 1) * 128], in_=psb16(b)[:, 0:256].rearrange("p (a c) -> p a c", a=2)),
                          reads=[psk[b]], writes=['qET%d' % t])
                    for h in range(4):
                        tk.op('act', lambda e, t=t, h=h: e.activation(out=kETm[:, h, t * 128:(t + 1) * 128], in_=psb16(b)[:, (2 + h // 2) * 128:(3 + h // 2) * 128],
                                                                      func=AF.Copy, scale=HM[:, h % 2:h % 2 + 1]),
                              reads=[psk[b], 'cf32'], writes=['kT%d_%d' % (t, h)])
                if stop('R1'):
                    rstop = True
                    break
                tk.dma('sp', S0b[:], s0_d.ap()[l, dr], writes=['S0b'])
                kvb_all = [[nb() for _ in range(4)] for _ in range(2)]
                for pr in range(2):
                    kvb = kvb_all[pr]
                    for c in range(32):
                        cq = c if dr == 0 else 31 - c
                        b = kvb[cq // 8]
                        for h in (2 * pr, 2 * pr + 1):
                            o = ps[b][(h % 2) * 64:(h % 2) * 64 + 64, (cq % 8) * 64:(cq % 8) * 64 + 64]
                            tk.op('pe', lambda e, c=c, h=h, o=o: e.matmul(o, lhsT=kE[:, c // 4, h * 64:(h + 1) * 64], rhs=vhm[:, c % 4, c // 4, h * 64:(h + 1) * 64],
                                                                          start=True, stop=True),
                                  reads=['sQ', 'vhm'], writes=[psk[b]])
                for pr in range(2):
                    kvb = kvb_all[pr]
                    for g in range(4):
                        tk.op('dve', lambda e, g=g: e.tensor_tensor(out=B_[:, :, g * 8:(g + 1) * 8].rearrange("p e c -> p c e"),
                                                                    in0=ps[kvb[g]][:, :].rearrange("p (c e) -> p c e", c=8),
                                                                    in1=es3[:, pr, g * 8:(g + 1) * 8, 2].unsqueeze(2).broadcast_to([128, 8, 64]), op=ALU.mult),
                              reads=[psk[kvb[g]], 'es3'], writes=['vhm'])
                    if dr == 0 and pr == 0:
                        wi = next_w()
                        for t in range(NT):
                            b = kvb[t % 4]
                            proj_tm(wi, 0, 256, t, b)
                            tk.op('act', lambda e, t=t: e.activation(out=sgb[:, t, :], in_=ps[b][:, 0:256], func=AF.Tanh, scale=0.5), reads=[psk[b]], writes=['sQ'])
                    if dr == 1 and pr == 0:
                        wi = next_w()
                        uT_h = qh[:].rearrange("p a b -> p (a b)").rearrange("p (c t) -> p c t", c=2)
                        sgf_h = qE
                        hb_ = 0
                        for ci in range(2):
                            for g in range(2):
                                b = kvb[hb_ % 4]
                                hb_ += 1
                                proj_fm(wi, ci, g, b)
                                tk.op('act', lambda e, ci=ci, g=g: e.copy(out=uT_h[:, ci, g * 512:(g + 1) * 512], in_=ps[b][:, :]), reads=[psk[b]], writes=['qh'])
                        for t in range(NT):
                            b = kvb[hb_ % 4]
                            hb_ += 1
                            proj_tm(wi, 256, 256, t, b)
                            tk.op('act', lambda e, t=t: e.activation(out=sgf_h[:, t, :], in_=ps[b][:, 0:256], func=AF.Silu), reads=[psk[b]], writes=['sQ'])
                    tk.op('dve', lambda e: e.scalar_tensor_tensor(out=B_[:, :, 0], in0=S0b[:, pr, :], scalar=es3[:, pr, 0, 1:2], in1=B_[:, :, 0], op0=ALU.mult, op1=ALU.add),
                          reads=['S0b', 'es3', 'vhm'], writes=['vhm'])
                    tk.op('dve', lambda e: e.tensor_copy(out=A_[:], in_=es3[:, pr, :, 1].unsqueeze(1).broadcast_to([128, 64, 32])), reads=['es3', 'vhm'], writes=['vhm'])
                    tk.op('dve', lambda e: e.memset(A_[:, :, 0:1], 0.0), reads=['vhm'], writes=['vhm'])
                    tk.op('dve', lambda e: e.tensor_tensor_scan(out=B_[:].rearrange("p e c -> p (e c)"), data0=A_[:].rearrange("p e c -> p (e c)"),
                                                                data1=B_[:].rearrange("p e c -> p (e c)"), initial=0.0, op0=ALU.mult, op1=ALU.add),
                          reads=['vhm'], writes=['vhm'])
                    tk.op('dve', lambda e: e.tensor_copy(out=nsb[:, :, pr, :], in_=B_[:, :, 7:32:8].rearrange("p e k -> p k e")), reads=['vhm'], writes=['nsb'])
                    for h2 in range(2):
                        tk.op('dve', lambda e, h2=h2: e.scalar_tensor_tensor(out=SinPm[:, 2 * pr + h2, 1:32, :], in0=B_[:, :, 0:31].rearrange("p e c -> p c e"),
                                                                             scalar=HM[:, h2:h2 + 1], in1=es3[:, pr, 1:32, 0].unsqueeze(2).broadcast_to([128, 31, 64]),
                                                                             op0=ALU.mult, op1=ALU.mult),
                              reads=['vhm', 'es3', 'cf32'], writes=['fS'])
                        tk.op('dve', lambda e, h2=h2: e.scalar_tensor_tensor(out=SinPm[:, 2 * pr + h2, 0, :], in0=S0b[:, pr, :], scalar=HM[:, h2:h2 + 1],
                                                                             in1=es3[:, pr, 0, 0:1].broadcast_to([128, 64]), op0=ALU.mult, op1=ALU.mult),
                              reads=['S0b', 'es3', 'cf32'], writes=['fS'])
                nsb3 = nsb[:].rearrange("p s a e -> p s (a e)")
                if dr == 0:
                    tk.dma('sp', ns_d.ap()[l, 0].rearrange("s p a e -> p s (a e)"), nsb3, reads=['nsb'])
                else:
                    for k in range(4):
                        tk.dma('sp', ns_d.ap()[l, 1, 3 - k].rearrange("p a e -> p (a e)"), nsb3[:, k, :], reads=['nsb'])
                if l + 1 < nl:
                    mod_mm(l + 1, 3 * dr)
                    mod_dma(l + 1, 3 * dr + 1)
                if stop('R2'):
                    rstop = True
                    break
                for t in range(NT):
                    bA_ = nb()
                    ai = t % 2
                    for h in range(4):
                        tk.op('pe', lambda e, t=t, h=h: e.matmul(ps[bA_][:, h * 128:(h + 1) * 128], lhsT=kETm[:, h, t * 128:(t + 1) * 128],
                                                                 rhs=qET[:, h // 2, t * 128:(t + 1) * 128], start=True, stop=True),
                              reads=['kT%d_%d' % (t, h), 'qET%d' % t], writes=[psk[bA_]], war_only=['lK'])
                    tk.op('dve', lambda e: e.tensor_tensor(out=ATm[ai][:], in0=ps[bA_][:, :].rearrange("p (a c) -> p a c", a=4),
                                                           in1=TRI[dr].unsqueeze(1).broadcast_to([128, 4, 128]), op=ALU.mult),
                          reads=[psk[bA_], 'cb16'], writes=['ATm%d' % ai])
                    bO = nb()
                    for h in range(4):
                        tk.op('pe', lambda e, t=t, h=h: e.matmul(ps[bO][:, h * 64:(h + 1) * 64], lhsT=ATm[ai][:, h, :], rhs=vh[:, t, h * 64:(h + 1) * 64],
                                                                 start=True, stop=False, skip_group_check=True),
                              reads=['ATm%d' % ai, 'vh'], writes=[psk[bO]])
                        for cp in range(4):
                            tk.op('pe', lambda e, t=t, h=h, cp=cp: e.matmul(ps[bO][cp * 32:(cp + 1) * 32, h * 64:(h + 1) * 64],
                                                                            lhsT=qET[:, h // 2, t * 128 + cp * 32:t * 128 + cp * 32 + 32],
                                                                            rhs=SinPm[:, h, (4 * t + cp) if dr == 0 else 31 - (4 * t + cp), :], start=False, stop=(cp == 3), skip_group_check=True,
                                                                            tile_position=(0, cp * 32)),
                                  reads=['qET%d' % t, 'fS'], writes=[psk[bO]])
                    if l + 1 < nl and t == 3:
                        mod_mm(l + 1, 3 * dr + 1)
                        mod_dma(l + 1, 3 * dr + 2)
                    if l + 1 < nl and t == 7:
                        mod_mm(l + 1, 3 * dr + 2)
                    if dr == 0:
                        tk.op('act', lambda e, t=t: e.copy(out=osum[:, t, :], in_=ps[bO][:, 0:256]), reads=[psk[bO]], writes=['osum%d' % t])
                    else:
                        tk.op('dve', lambda e, t=t: e.tensor_tensor(out=osum[:, t, :], in0=osum[:, t, :], in1=ps[bO][:, 0:256], op=ALU.add),
                              reads=[psk[bO], 'osum%d' % t], writes=['osum%d' % t])
                if stop('R3'):
                    rstop = True
                    break
            if rstop:
                break
            tk.op('dve', lambda e: e.tensor_tensor(out=osq[:], in0=osum[:], in1=osum[:], op=ALU.mult), reads=['osum%d' % t_ for t_ in range(NT)], writes=['lK'])
            tk.op('dve', lambda e: e.tensor_reduce(out=hss[:], in_=osq[:].rearrange("p t (h d) -> p (t h) d", h=4), axis=AX.X, op=ALU.add),
                  reads=['lK'], writes=['hss'])
            tk.op('dve', lambda e: e.tensor_scalar(out=hss[:], in0=hss[:], scalar1=1.0 / 64, scalar2=EPS, op0=ALU.mult, op1=ALU.add), reads=['hss'], writes=['hss'])
            tk.op('act', lambda e: e.activation(out=hss[:], in_=hss[:], func=AF.Ln), reads=['hss'], writes=['hss'])
            tk.op('act', lambda e: e.activation(out=hss[:], in_=hss[:], func=AF.Exp, scale=-0.5), reads=['hss'], writes=['hss'])
            tk.op('dve', lambda e: e.tensor_tensor(out=osum[:].rearrange("p t (h d) -> p (t h) d", h=4), in0=osum[:].rearrange("p t (h d) -> p (t h) d", h=4),
                                                   in1=hss[:].unsqueeze(2).broadcast_to([128, 32, 64]), op=ALU.mult),
                  reads=['osum%d' % t_ for t_ in range(NT)] + ['hss'], writes=['osum%d' % t_ for t_ in range(NT)])
            tk.op('dve', lambda e: e.tensor_tensor(out=osum[:], in0=osum[:], in1=ghgB[:].unsqueeze(1).broadcast_to([128, NT, 256]), op=ALU.mult),
                  reads=['osum%d' % t_ for t_ in range(NT)] + ['ghgB'], writes=['osum%d' % t_ for t_ in range(NT)])
            tk.op('dve', lambda e: e.tensor_tensor(out=mixed[:, :, 512:768], in0=osum[:], in1=sgr[:], op=ALU.mult),
                  reads=['osum%d' % t_ for t_ in range(NT)] + ['sgr'], writes=['mixed%d' % j for j in range(8)])

            if stop('R'):
                break
            tk.barrier()
            areset()
            uT = aget([128, 2, T], BF16)
            sgf = aget([128, NT, 256], BF16)
            ucs = aget([128, NT, 2, 256], BF16)
            yT = aget([128, 2, T], BF16)
            csnb = [aget([128, 2, 1024], BF16) for _ in range(4)]
            assert apos[0] + 2048 <= 11520
            wfs = aget([128, 2, 256])
            wfb = aget([128, 2, 256], BF16)
            junk = aget([128, 512], BF16)
            tmpf = aget([128, D])
            for kt_ in range(4):
                tk.dma('sp', csnb[kt_][:], csn_d.ap()[kt_], writes=['csnb%d' % kt_])
            assert True
            tk.dma('sp', wfs[:], wfn_d.ap()[l].rearrange("(c p) n -> p c n", p=128), writes=['wfs'])
            tk.op('pool', lambda e: e.tensor_copy(out=wfb[:], in_=wfs[:]), reads=['wfs'], writes=['wfb'])
            for t in range(NT):
                b = nb()
                for cs in range(2):
                    for ct in range(2):
                        tk.op('pe', lambda e, t=t, cs=cs, ct=ct: e.matmul(ps[b][:, cs * 256 + ct * 128:cs * 256 + ct * 128 + 128], lhsT=uT[:, ct, t * 128:(t + 1) * 128],
                                                                          rhs=C4S4[:, 3 + cs * 2 + ct, :], start=True, stop=True),
                              reads=['uT', 'cb16'], writes=[psk[b]])
                tk.op('act', lambda e, t=t: e.copy(out=ucs[:, t, :, :].rearrange("p a c -> p (a c)"), in_=ps[b][:, :]), reads=[psk[b]], writes=['ucs'])
            if stop('F0'):
                break
            yb = [nb() for _ in range(4)]
            for kt_ in range(8):
                ci = kt_ % 4
                if kt_ >= 4:
                    tk.dma('sp', csnb[ci][:], csn_d.ap()[kt_], writes=['csnb%d' % ci])
                for ct in range(2):
                    for g in range(2):
                        b = yb[ct * 2 + g]
                        for cs in range(2):
                            tk.op('pe', lambda e, kt_=kt_, ct=ct, g=g, cs=cs: e.matmul(ps[b][:, :], lhsT=ucs[:, kt_, cs, ct * 128:(ct + 1) * 128],
                                                                                       rhs=csnb[ci][:, cs, g * 512:(g + 1) * 512],
                                                                                       start=(kt_ == 0 and cs == 0), stop=(kt_ == 7 and cs == 1)),
                                  reads=['ucs', 'csnb%d' % ci], writes=[psk[b]])
            for ct in range(2):
                for g in range(2):
                    b = yb[ct * 2 + g]
                    tk.op('act', lambda e, ct=ct, g=g, b=b: e.copy(out=yT[:, ct, g * 512:(g + 1) * 512], in_=ps[b][:, :]), reads=[psk[b]], writes=['yT'])
            for t in range(NT):
                b = nb()
                for ct in range(2):
                    tk.op('pe', lambda e, t=t, ct=ct: e.matmul(ps[b][:, 0:256], lhsT=yT[:, ct, t * 128:(t + 1) * 128], rhs=wfb[:, ct, :], start=(ct == 0), stop=(ct == 1)),
                          reads=['yT', 'wfb'], writes=[psk[b]])
                tk.op('dve', lambda e, t=t: e.tensor_tensor(out=mixed[:, t, 768:1024], in0=ps[b][:, 0:256], in1=sgf[:, t, :], op=ALU.mult),
                      reads=[psk[b], 'sgf'], writes=['mixed%d' % t])

            if dbg and l == nl - 1:
                for t in range(NT):
                    tk.dma('sp', dbg_d.ap()[t * 128:(t + 1) * 128, :], mixed[:, t, :], reads=['mixed%d' % t])

            if stop('F1'):
                break
            w0 = next_w()
            w1 = next_w(prefetch=False)
            for t in range(NT):
                b = nb()
                for kc in range(8):
                    tk.op('pe', lambda e, t=t, kc=kc: e.transpose(out=psb16(b)[:, kc * 128:(kc + 1) * 128], in_=mixed[:, t, kc * 128:(kc + 1) * 128], identity=IDB),
                          reads=['mixed%d' % t, 'cb16'], writes=[psk[b]])
                tk.op('act', lambda e, t=t: e.copy(out=hT[:, :, t * 128:(t + 1) * 128], in_=psb16(b)[:, :].rearrange("p (k c) -> p k c", k=8)),
                      reads=[psk[b]], writes=['hT'])
            for t in range(NT):
                bb = [nb(), nb()]
                for hf, wi in enumerate((w0, w1)):
                    proj_tm(wi, 0, 512, t, bb[hf])
                    tk.op('act', lambda e, t=t, hf=hf: e.activation(out=junk[:], in_=ps[bb[hf]][:, :], func=AF.Square, accum_out=ssq[:, 8 + hf:9 + hf]),
                          reads=[psk[bb[hf]]], writes=['junk', 'ssq'])
                tk.op('dve', lambda e: e.tensor_tensor(out=rstd[:, 8:9], in0=ssq[:, 8:9], in1=ssq[:, 9:10], op=ALU.add), reads=['ssq'], writes=['rstd'])
                tk.op('dve', lambda e: e.tensor_scalar(out=rstd[:, 8:9], in0=rstd[:, 8:9], scalar1=1.0 / D, scalar2=EPS, op0=ALU.mult, op1=ALU.add),
                      reads=['rstd'], writes=['rstd'])
                tk.op('act', lambda e: e.activation(out=rstd[:, 8:9], in_=rstd[:, 8:9], func=AF.Ln), reads=['rstd'], writes=['rstd'])
                tk.op('act', lambda e: e.activation(out=rstd[:, 8:9], in_=rstd[:, 8:9], func=AF.Exp, scale=-0.5), reads=['rstd'], writes=['rstd'])
                for hf in range(2):
                    tk.op('dve', lambda e, hf=hf: e.scalar_tensor_tensor(out=tmpf[:, hf * 512:(hf + 1) * 512], in0=ps[bb[hf]][:, :], scalar=rstd[:, 8:9],
                                                                         in1=gg[:, hf * 512:(hf + 1) * 512], op0=ALU.mult, op1=ALU.mult),
                          reads=[psk[bb[hf]], 'rstd', 'gg'], writes=['tmpf'])
                tk.op('dve', lambda e, t=t: e.tensor_tensor(out=x_sb[:, t, :], in0=x_sb[:, t, :], in1=tmpf[:], op=ALU.add),
                      reads=['x%d' % t, 'tmpf'], writes=['x%d' % t])
            _issue(wstate['ptr'])

        for t in range(NT):
            tk.dma('sp', y_d.ap()[t * 128:(t + 1) * 128, :], x_sb[:, t, :], reads=['x%d' % t])
        tk.finish()
    return nc


def _consts(is_sample):
    cf32 = np.zeros((128, 7, 128), np.float32)
    p = np.arange(128)
    J2 = np.zeros((128, 128), np.float32)
    for a in range(2):
        for i in range(64):
            J2[a * 64 + i, a * 64 + 63 - i] = 1.0
    cf32[:, 0] = J2
    cf32[:, 1] = np.eye(128, dtype=np.float32)
    cm = np.zeros((128, 128), np.float32)
    if is_sample:
        qc = np.arange(64)
        c0 = np.clip(qc - 8, 0, 48)
        kc = np.arange(64)
        valid = (kc[:, None] >= c0[None, :]) & (kc[:, None] < c0[None, :] + 16)
        m = np.where(valid, 0.0, NEG).astype(np.float32)
        cm = np.tile(m, (2, 2))
    cf32[:, 2] = cm
    s = np.arange(32)[:, None]
    t = np.arange(32)[None, :]
    trf = (s <= t).astype(np.float32) - (s <= 15).astype(np.float32)
    trb = (s >= t).astype(np.float32) - (s >= 16).astype(np.float32)
    for a in range(4):
        cf32[a * 32:(a + 1) * 32, 3, a * 32:(a + 1) * 32] = trf
        cf32[a * 32:(a + 1) * 32, 4, a * 32:(a + 1) * 32] = trb
    sl = np.arange(128) % 32
    ch = np.arange(128) // 32
    selcols = np.zeros((128, 128), np.float32)
    for a in range(4):
        selcols[:, a * 2 + 0] = ((ch == a) & (sl <= 15))
        selcols[:, a * 2 + 1] = (ch == a)
        selcols[:, 8 + a * 2 + 0] = ((ch == 3 - a) & (sl >= 16))
        selcols[:, 8 + a * 2 + 1] = (ch == 3 - a)
        selcols[:, 18 + a] = (ch == a)
    selcols[:, 16] = (np.arange(128) < 64)
    selcols[:, 17] = (np.arange(128) >= 64)
    cf32[:, 5] = selcols
    cf32[0:64, 6, 0:64] = 1.0
    cf32[64:128, 6, 64:128] = 1.0
    rowb = np.zeros((128, 74), np.float32)
    for i, (j, kt) in enumerate(JK):
        for hf in range(2):
            for krl in range(2):
                if is_sample:
                    qr = 2 * j + hf
                    kr = 2 * kt + krl
                    r0 = int(np.clip(qr - 4, 0, 8))
                    ok = (r0 <= kr < r0 + 8)
                else:
                    ok = (kt // 2 == j // 2)
                rowb[krl * 64:(krl + 1) * 64, i * 2 + hf] = 0.0 if ok else NEG
    cb16 = np.zeros((128, 9, 128), np.float32)
    cb16[:, 7] = J2
    cb16[:, 8] = cm
    cb16[:, 0] = np.eye(128)
    mf = (s <= t).astype(np.float32)
    mb = (s >= t).astype(np.float32)
    z = np.zeros((64, 64), np.float32)
    for a in range(4):
        cb16[a * 32:(a + 1) * 32, 1, a * 32:(a + 1) * 32] = mf
        cb16[a * 32:(a + 1) * 32, 2, a * 32:(a + 1) * 32] = mb
    ang = 2 * np.pi * np.outer(np.arange(64), np.arange(64)) / 64
    c4 = np.cos(ang) / 8.0
    s4 = np.sin(ang) / 8.0
    for ct in range(2):
        cb16[:, 3 + ct] = np.block([[c4, z], [z, c4]])
        cb16[:, 5 + ct] = np.block([[s4, z], [z, s4]])
    n = 1024 if is_sample else 256
    idx = np.arange(n)
    a2 = 2 * np.pi * ((np.outer(idx, idx)) % n) / n
    cn = np.cos(a2) / np.sqrt(n)
    sn = -np.sin(a2) / np.sqrt(n)
    CN = np.zeros((1024, 1024), np.float64)
    SN = np.zeros((1024, 1024), np.float64)
    for i in range(1024 // n):
        CN[i * n:(i + 1) * n, i * n:(i + 1) * n] = cn
        SN[i * n:(i + 1) * n, i * n:(i + 1) * n] = sn
    csn = np.stack([CN.reshape(8, 128, 1024), SN.reshape(8, 128, 1024)], axis=2)
    return dict(cf32=cf32, rowbias=rowb, cb16=cb16.astype(ml_dtypes.bfloat16), csn=csn.astype(ml_dtypes.bfloat16))


def _in_maps(x_prompt, x_sample, cache_attn_k, cache_attn_v, state_hgrn, c, c_ctx,
             w_ada, b_ada, g_pre, w_in, rpb, lb_logits, g_hgrn, w_fnet, w_out, g_post):
    f = lambda a: np.ascontiguousarray(np.asarray(a, dtype=np.float32))
    shared = dict(w_ada=f(w_ada), b_ada=f(b_ada), g_pre=f(g_pre), w_in=f(w_in), lb_logits=f(lb_logits),
                  g_hgrn=f(g_hgrn), w_fnet=f(w_fnet), w_out=f(w_out), g_post=f(g_post))
    tp = np.zeros((NL, 8, 23, 127), np.float32)
    tp[:, :, 4:19, 48:79] = f(rpb)
    cs = _consts(True)
    cp = _consts(False)
    maps = []
    for i in range(8):
        m = dict(shared)
        if i < 4:
            m["x"] = f(x_sample[i])
            m["cvec"] = f(np.asarray(c[i]).reshape(8, 128).T)
            m["ctxk"] = f(np.asarray(cache_attn_k[i]).reshape(NL, 512, 512))
            m["ctxv"] = f(np.asarray(cache_attn_v[i]).reshape(NL, 512, 512))
            s = np.asarray(state_hgrn[i]).reshape(NL, 2, 2, 2, 64, 64)
            m["s0"] = f(s.transpose(0, 1, 3, 4, 2, 5).reshape(NL, 2, 128, 2, 64))
            m["flags"] = np.ones((128, 2), np.float32)
            m["tpad"] = tp
            m.update(cs)
        else:
            m["x"] = f(np.asarray(x_prompt[4 * (i - 4):4 * (i - 3)]).reshape(T, D))
            m["cvec"] = f(np.asarray(c_ctx).reshape(8, 128).T)
            m["ctxk"] = np.zeros((NL, 512, 512), np.float32)
            m["ctxv"] = np.zeros((NL, 512, 512), np.float32)
            m["s0"] = np.zeros((NL, 2, 128, 2, 64), np.float32)
            m["flags"] = np.zeros((128, 2), np.float32)
            m["tpad"] = np.zeros_like(tp)
            m.update(cp)
        maps.append(m)
    return maps


_NC_CACHE = {}


def kernel(**inputs):
    if 'nc' not in _NC_CACHE:
        _NC_CACHE['nc'] = build_nc()
    nc = _NC_CACHE['nc']
    maps = _in_maps(**inputs)
    res = run_bass_kernel_spmd(nc, maps, core_ids=list(range(8)))
    r = res.results
    y_sample = np.stack([r[i]["y"] for i in range(4)], axis=0).astype(np.float32)
    y_prompt = np.concatenate([r[i]["y"].reshape(4, 256, D) for i in range(4, 8)], axis=0).astype(np.float32)
    nk = np.concatenate([r[i]["newk"].reshape(NL, 4, 256, 8, 64).transpose(1, 0, 2, 3, 4) for i in range(4, 8)], axis=0)
    nv = np.concatenate([r[i]["newv"].reshape(NL, 4, 256, 8, 64).transpose(1, 0, 2, 3, 4) for i in range(4, 8)], axis=0)
    ns = np.concatenate([r[i]["news"].reshape(NL, 2, 4, 2, 64, 2, 64).transpose(2, 0, 1, 5, 3, 4, 6).reshape(4, NL, 2, 4, 64, 64)
                         for i in range(4, 8)], axis=0)
    return (y_prompt, y_sample, np.ascontiguousarray(nk, dtype=np.float32), np.ascontiguousarray(nv, dtype=np.float32),
            np.ascontiguousarray(ns, dtype=np.float32))
```

```python
import numpy as np
import ml_dtypes
from contextlib import ExitStack
import concourse.bass as bass
import concourse.mybir as mybir
from concourse.bass_utils import run_bass_kernel_spmd

F32 = mybir.dt.float32
BF16 = mybir.dt.bfloat16
AF = mybir.ActivationFunctionType
ALU = mybir.AluOpType
AX = mybir.AxisListType

NL = 4
D = 1024
T = 1024
NT = 8
EPS = 1e-6
NEG = -30000.0
KT = {0: [0, 1, 2, 3], 1: [0, 1, 2, 3], 2: [0, 1, 2, 3, 4], 3: [1, 2, 3, 4, 5],
      4: [2, 3, 4, 5, 6], 5: [3, 4, 5, 6, 7], 6: [4, 5, 6, 7], 7: [4, 5, 6, 7]}
JK = [(j, kt) for j in range(8) for kt in KT[j]]
JKI = {p: i for i, p in enumerate(JK)}
NDS = 24
NSW = 72


class TK:
    def __init__(s, nc, st):
        s.nc = nc
        s.E = {'pe': nc.tensor, 'act': nc.scalar, 'dve': nc.vector, 'pool': nc.gpsimd, 'sp': nc.sync}
        s.sem = {k: st.enter_context(nc.semaphore('s_' + k)) for k in ('pe', 'act', 'dve', 'pool')}
        s.cnt = {k: 0 for k in s.E}
        s.seen = {k: {} for k in s.E}
        s.lw = {}
        s.rd = {}
        s.dsems = [st.enter_context(nc.semaphore('d%d' % i)) for i in range(NDS)]
        s.dcnt = [0] * NDS
        s.dnext = 0
        s.swsems = [st.enter_context(nc.semaphore('w%d' % i)) for i in range(NSW)]
        s.swnext = 0
        s.swlow = 0

    def _wait(s, eng, key, val):
        if eng == 'pe' and key == 'pe':
            return
        if s.seen[eng].get(key, 0) >= val:
            return
        if isinstance(key, str):
            semobj = s.sem[key]
        elif key >= 1000:
            semobj = s.swsems[key - 1000]
        else:
            semobj = s.dsems[key]
        s.E[eng].wait_ge(semobj, val)
        s.seen[eng][key] = val

    def _deps(s, eng, reads, writes):
        for k in reads:
            w = s.lw.get(k)
            if w:
                s._wait(eng, *w)
            if k.startswith('ps'):
                for rk, rv in s.rd.get(k, {}).items():
                    if rk != eng:
                        s._wait(eng, rk, rv)
        for k in writes:
            w = s.lw.get(k)
            if w:
                s._wait(eng, *w)
            for rk, rv in s.rd.get(k, {}).items():
                s._wait(eng, rk, rv)

    def _book(s, tag, reads, writes):
        for k in reads:
            d = s.rd.setdefault(k, {})
            d[tag[0]] = max(d.get(tag[0], 0), tag[1])
        for k in writes:
            s.lw[k] = tag
            s.rd[k] = {}

    def op(s, eng, fn, reads=(), writes=(), war_only=()):
        s._deps(eng, reads, writes)
        inst = fn(s.E[eng])
        s.cnt[eng] += 1
        inst.then_inc(s.sem[eng], 1)
        s._book((eng, s.cnt[eng]), tuple(reads) + tuple(war_only), writes)

    def dma(s, q, out, in_, reads=(), writes=()):
        if q == 'pool':
            assert s.swnext < NSW, "out of one-shot semaphores"
            i = s.swnext
            s.swnext += 1
            s._deps(q, reads, writes)
            s.E[q].dma_start(out=out, in_=in_).then_inc(s.swsems[i], 16)
            s._book((1000 + i, 16), reads, writes)
            return
        i = s.dnext
        s.dnext = (s.dnext + 1) % NDS
        if s.dcnt[i] > 0:
            s._wait(q, i, s.dcnt[i])
        s._deps(q, reads, writes)
        s.dcnt[i] += 16
        s.E[q].dma_start(out=out, in_=in_).then_inc(s.dsems[i], 16)
        s._book((i, s.dcnt[i]), reads, writes)

    def barrier(s):
        engs = ('pe', 'act', 'dve', 'pool', 'sp')
        snap = dict(s.cnt)
        dsnap = list(s.dcnt)
        for e in engs:
            for o in ('pe', 'act', 'dve', 'pool'):
                if o != e and snap[o] > 0:
                    s._wait(e, o, snap[o])
            for i in range(NDS):
                if dsnap[i] > 0:
                    s._wait(e, i, dsnap[i])
            for i in range(s.swlow, s.swnext):
                s._wait(e, 1000 + i, 16)
        s.swlow = s.swnext

    def finish(s):
        for i in range(NDS):
            if s.dcnt[i] > 0:
                s._wait('sp', i, s.dcnt[i])
        for i in range(s.swnext):
            s._wait('sp', 1000 + i, 16)
        for k in ('pe', 'act', 'dve', 'pool'):
            if s.cnt[k] > 0:
                s._wait('sp', k, s.cnt[k])


def build_nc(nl=NL, dbg=False, upto=None):
    nc = bass.Bass("TRN2", target_bir_lowering=False)
    _order = ['M0', 'M1', 'M2', 'M3', 'M', 'A0', 'A1', 'A1a', 'A1b', 'A1c', 'A2', 'A', 'R0', 'R1', 'R2', 'R3', 'R', 'F0', 'F1', 'F']

    def stop(p):
        return upto is not None and _order.index(upto) <= _order.index(p)

    def din(name, shape, dt=F32):
        return nc.dram_tensor(name, list(shape), dt, kind="ExternalInput")

    def dout(name, shape, dt=F32):
        return nc.dram_tensor(name, list(shape), dt, kind="ExternalOutput")

    x_d = din("x", [T, D])
    cvec_d = din("cvec", [128, 8])
    ctxk_d = din("ctxk", [NL, 512, 512])
    ctxv_d = din("ctxv", [NL, 512, 512])
    s0_d = din("s0", [NL, 2, 128, 2, 64])
    flags_d = din("flags", [128, 2])
    wada_d = din("w_ada", [NL, D, 3 * D])
    bada_d = din("b_ada", [NL, 3 * D])
    gpre_d = din("g_pre", [NL, D])
    win_d = din("w_in", [NL, D, 3840])
    tpad_d = din("tpad", [NL, 8, 23, 127])
    lbl_d = din("lb_logits", [2, NL, 256])
    ghg_d = din("g_hgrn", [NL, 256])
    wfn_d = din("w_fnet", [NL, 256, 256])
    wout_d = din("w_out", [NL, D, D])
    gpost_d = din("g_post", [NL, D])
    cf32_d = din("cf32", [128, 7, 128])
    rowb_d = din("rowbias", [128, 74])
    cb16_d = din("cb16", [128, 9, 128], BF16)
    csn_d = din("csn", [8, 128, 2, 1024], BF16)

    y_d = dout("y", [T, D])
    nk_d = dout("newk", [NL, T, 512])
    nv_d = dout("newv", [NL, T, 512])
    ns_d = dout("news", [NL, 2, 4, 128, 2, 64])
    dbg_d = dout("dbgmixed", [T, D], BF16) if dbg else None

    with ExitStack() as st:
        def sb(name, shape, dt=F32):
            return st.enter_context(nc.sbuf_tensor(name, list(shape), dt))

        tk = TK(nc, st)
        x_sb = sb("x_sb", [128, NT, D])
        hT = sb("hT", [128, 8, T], BF16)
        mixed = sb("mixed", [128, NT, D], BF16)
        wst = [sb("wst0", [128, 8, 512], BF16)]
        wbf = [sb("wbf%d" % i, [128, 8, 512], BF16) for i in range(2)]
        gg = sb("gg", [128, D])
        modN = sb("modN", [128, 3 * D], BF16)
        screp = sb("screp", [128, 8, 128], BF16)
        brow = sb("brow", [1, 512])
        ones_row = sb("ones_row", [1, 128])
        csil = sb("csil", [128, 8])
        cf32 = sb("cf32s", [128, 7, 128])
        rowb = sb("rowbs", [128, 74])
        cb16 = sb("cb16s", [128, 9, 128], BF16)
        flags = sb("flagss", [128, 2])
        lbl = sb("lbl", [128, 2, 256])
        oml = sb("oml", [128, 2, 256])
        ghgB = sb("ghgB", [128, 256])
        ssq = sb("ssq", [128, 16])
        rstd = sb("rstd", [128, 16])
        ARW = 21120
        arena = sb("arena", [128, ARW])
        apos = [0]

        def areset():
            apos[0] = 0

        def aget(shape, dt=F32):
            n = 1
            for d_ in shape[1:]:
                n *= d_
            words = n if dt == F32 else (n + 1) // 2
            a0 = apos[0]
            apos[0] += words
            assert apos[0] <= ARW, ("arena overflow", apos[0])
            v = arena[:, a0:a0 + words]
            if dt != F32:
                v = v.bitcast(dt)
            if len(shape) == 3:
                v = v.rearrange("p (a b) -> p a b", a=shape[1])
            elif len(shape) == 4:
                v = v.rearrange("p (a b c) -> p a b c", a=shape[1], b=shape[2])
            return v

        psbig = [st.enter_context(nc.psum_tensor("psb%d" % i, [128, 1024], F32)) for i in range(4)]
        ps = [psbig[i // 2][:, (i % 2) * 512:(i % 2 + 1) * 512] for i in range(8)]
        psk = ["ps%d" % i for i in range(8)]
        bank_rr = [0]

        def nb(avoid=()):
            while True:
                b = bank_rr[0]
                bank_rr[0] = (b + 1) % 8
                if b not in avoid:
                    return b

        J2 = cf32[:, 0, :]
        IDF = cf32[:, 1, :]
        CMT = cf32[:, 2, :]
        TR = [cf32[:, 3, :], cf32[:, 4, :]]
        SEL = [cf32[:, 5, 0:8], cf32[:, 5, 8:16]]
        HM = cf32[:, 5, 16:18]
        QM = cf32[:, 5, 18:22]
        HME = cf32[:, 6, :].rearrange("p (a c) -> p a c", a=2)
        IDB = cb16[:, 0, :]
        TRI = [cb16[:, 1, :], cb16[:, 2, :]]
        C4S4 = cb16
        J2B = cb16[:, 7, :]
        CMTB = cb16[:, 8, :]

        tk.dma('sp', cf32[:], cf32_d.ap(), writes=['cf32'])
        tk.dma('sp', rowb[:], rowb_d.ap(), writes=['rowb'])
        tk.dma('sp', cb16[:], cb16_d.ap(), writes=['cb16'])
        tk.dma('sp', flags[:], flags_d.ap(), writes=['flags'])
        tk.dma('sp', csil[:], cvec_d.ap(), writes=['csil'])
        for t in range(NT):
            tk.dma('sp', x_sb[:, t, :], x_d.ap()[t * 128:(t + 1) * 128, :], writes=['x%d' % t])
        tk.op('pool', lambda e: e.memset(ones_row[:], 1.0), writes=['ones_row'])
        tk.op('act', lambda e: e.activation(out=csil[:], in_=csil[:], func=AF.Silu), reads=['csil'], writes=['csil'])

        wring = [0]

        wbring = [0]

        def load_w(src_ap, ncols):
            wi = wbring[0]
            wbring[0] ^= 1
            tk.dma('pool', wbf[wi][:, :, 0:ncols], src_ap.rearrange("(kc p) n -> p kc n", p=128), writes=['wbf%d' % wi])
            return wi

        wseq = []
        for l_ in range(nl):
            for (c0_, n_) in ((0, 512), (512, 512), (1024, 512), (1536, 512), (2048, 512), (2816, 512), (2560, 256), (3328, 512)):
                wseq.append((win_d, l_, c0_, n_))
            wseq.append((wout_d, l_, 0, 512))
            wseq.append((wout_d, l_, 512, 512))
        wstate = {'ptr': 0, 'loaded': {}}

        def _issue(i):
            if i < len(wseq) and i not in wstate['loaded']:
                d_, l_, c0_, n_ = wseq[i]
                wstate['loaded'][i] = load_w(d_.ap()[l_, :, c0_:c0_ + n_], n_)

        def next_w(prefetch=True):
            i = wstate['ptr']
            wstate['ptr'] += 1
            _issue(i)
            if prefetch:
                _issue(i + 1)
            return wstate['loaded'][i]

        def proj_tm(wi, c0, ncols, t, b):
            for kc in range(8):
                tk.op('pe', lambda e, kc=kc: e.matmul(ps[b][:, 0:ncols], lhsT=hT[:, kc, t * 128:(t + 1) * 128],
                                                     rhs=wbf[wi][:, kc, c0:c0 + ncols], start=(kc == 0), stop=(kc == 7)),
                      reads=['hT', 'wbf%d' % wi], writes=[psk[b]])

        def proj_fm(wi, ci, g, b):
            for kc in range(8):
                tk.op('pe', lambda e, kc=kc: e.matmul(ps[b][:, 0:512], lhsT=wbf[wi][:, kc, ci * 128:(ci + 1) * 128],
                                                     rhs=hT[:, kc, g * 512:(g + 1) * 512], start=(kc == 0), stop=(kc == 7)),
                      reads=['hT', 'wbf%d' % wi], writes=[psk[b]])

        def psb16(b):
            return ps[b].bitcast(BF16)

        def emit_mod_chunk(lm, ch, banks=None):
            mod_dma(lm, ch)
            mod_mm(lm, ch, banks)

        def mod_dma(lm, ch):
            tk.dma('pool', wst[0][:], wada_d.ap()[lm, :, ch * 512:(ch + 1) * 512].rearrange("(kc p) n -> p kc n", p=128), writes=['wst0'])
            tk.dma('sp', brow[:], bada_d.ap()[lm:lm + 1, ch * 512:(ch + 1) * 512], writes=['brow'])

        def mod_mm(lm, ch, banks=None):
            b = nb() if banks is None else banks[ch % len(banks)]
            for kc in range(8):
                tk.op('pe', lambda e, kc=kc: e.matmul(ps[b][:, :], lhsT=screp[:, kc, :], rhs=wst[0][:, kc, :], start=(kc == 0), stop=False),
                      reads=['screp', 'wst0'], writes=[psk[b]])
            tk.op('pe', lambda e: e.matmul(ps[b][:, :], lhsT=ones_row[0:1, :], rhs=brow[0:1, :], start=False, stop=True),
                  reads=['ones_row', 'brow'], writes=[psk[b]])
            tk.op('act', lambda e: e.copy(out=modN[:, ch * 512:(ch + 1) * 512], in_=ps[b][:, :]), reads=[psk[b]], writes=['modN'])

        tk.op('dve', lambda e: e.tensor_copy(out=screp[:], in_=csil[:].unsqueeze(2).broadcast_to([128, 8, 128])), reads=['csil'], writes=['screp'])
        for ch in range(6):
            emit_mod_chunk(0, ch)

        for l in range(nl):
            if l == 0:
                tk.barrier()
            areset()
            apos[0] = 11520
            gbc = aget([128, 2, D])
            lbt = aget([128, 2, NL, 256])
            junk = aget([128, D], BF16)
            tmpf = aget([128, D])
            hb = [aget([128, D], BF16) for _ in range(2)]
            modA = aget([128, D])
            tk.dma('sp', gbc[:, 0, :], bass.AP(gpre_d, l * D, [[0, 128], [1, D]]), writes=['gbc'])
            tk.dma('sp', gbc[:, 1, :], bass.AP(gpost_d, l * D, [[0, 128], [1, D]]), writes=['gbc'])
            tk.dma('sp', lbt[:].rearrange("p a l c -> p (a l c)"), bass.AP(lbl_d, 0, [[0, 128], [1, 2 * NL * 256]]), writes=['lbt'])
            if l == 0:
                tk.op('dve', lambda e: e.memset(lbl[:], 0.0), writes=['lbl'])
            else:
                mx = tmpf[:, 0:512].rearrange("p (a c) -> p a c", a=2)
                sm = tmpf[:, 512:1024].rearrange("p (a c) -> p a c", a=2)
                tk.op('dve', lambda e: e.tensor_tensor(out=mx, in0=lbt[:, :, 0, :], in1=lbt[:, :, 1, :], op=ALU.max),
                      reads=['lbt'], writes=['tmpf'])
                for l2 in range(2, NL):
                    tk.op('dve', lambda e, l2=l2: e.tensor_tensor(out=mx, in0=mx, in1=lbt[:, :, l2, :], op=ALU.max),
                          reads=['lbt', 'tmpf'], writes=['tmpf'])
                for l2 in range(NL):
                    tk.op('dve', lambda e, l2=l2: e.tensor_tensor(out=lbt[:, :, l2, :], in0=lbt[:, :, l2, :], in1=mx, op=ALU.subtract),
                          reads=['lbt', 'tmpf'], writes=['lbt'])
                tk.op('act', lambda e: e.activation(out=lbt[:].rearrange("p a l c -> p (a l c)"), in_=lbt[:].rearrange("p a l c -> p (a l c)"), func=AF.Exp),
                      reads=['lbt'], writes=['lbt'])
                tk.op('dve', lambda e: e.tensor_tensor(out=sm, in0=lbt[:, :, 0, :], in1=lbt[:, :, 1, :], op=ALU.add),
                      reads=['lbt'], writes=['tmpf'])
                for l2 in range(2, NL):
                    tk.op('dve', lambda e, l2=l2: e.tensor_tensor(out=sm, in0=sm, in1=lbt[:, :, l2, :], op=ALU.add),
                          reads=['lbt', 'tmpf'], writes=['tmpf'])
                tk.op('dve', lambda e: e.reciprocal(out=sm, in_=sm), reads=['tmpf'], writes=['tmpf'])
                tk.op('dve', lambda e: e.tensor_copy(out=lbl[:], in_=lbt[:, :, 1, :]), reads=['lbt'], writes=['lbl'])
                for l2 in range(2, l + 1):
                    tk.op('dve', lambda e, l2=l2: e.tensor_tensor(out=lbl[:], in0=lbl[:], in1=lbt[:, :, l2, :], op=ALU.add),
                          reads=['lbt', 'lbl'], writes=['lbl'])
                tk.op('dve', lambda e: e.tensor_tensor(out=lbl[:], in0=lbl[:], in1=sm, op=ALU.mult), reads=['lbl', 'tmpf'], writes=['lbl'])
            tk.op('dve', lambda e: e.tensor_scalar(out=oml[:], in0=lbl[:], scalar1=-0.5, scalar2=0.5, op0=ALU.mult, op1=ALU.add),
                  reads=['lbl'], writes=['oml'])
            tk.op('dve', lambda e: e.tensor_scalar(out=lbl[:], in0=lbl[:], scalar1=0.5, scalar2=0.5, op0=ALU.mult, op1=ALU.add),
                  reads=['lbl'], writes=['lbl'])
            if stop('M0'):
                break
            tk.op('dve', lambda e: e.scalar_tensor_tensor(out=modA[:], in0=modN[:, D:2 * D], scalar=1.0, in1=gbc[:, 0, :], op0=ALU.add, op1=ALU.mult),
                  reads=['modN', 'gbc'], writes=['modA'])
            tk.op('dve', lambda e: e.tensor_tensor(out=gg[:], in0=modN[:, 2 * D:3 * D], in1=gbc[:, 1, :], op=ALU.mult), reads=['modN', 'gbc'], writes=['gg'])
            if stop('M1'):
                break
            for t in range(NT):
                tk.op('act', lambda e, t=t: e.activation(out=junk[:], in_=x_sb[:, t, :], func=AF.Square, accum_out=ssq[:, t:t + 1]),
                      reads=['x%d' % t], writes=['junk', 'ssq'])
            tk.op('dve', lambda e: e.tensor_scalar(out=rstd[:, 0:8], in0=ssq[:, 0:8], scalar1=1.0 / D, scalar2=EPS, op0=ALU.mult, op1=ALU.add),
                  reads=['ssq'], writes=['rstd'])
            tk.op('act', lambda e: e.activation(out=rstd[:, 0:8], in_=rstd[:, 0:8], func=AF.Ln), reads=['rstd'], writes=['rstd'])
            tk.op('act', lambda e: e.activation(out=rstd[:, 0:8], in_=rstd[:, 0:8], func=AF.Exp, scale=-0.5), reads=['rstd'], writes=['rstd'])
            if stop('M2'):
                break
            for t in range(NT):
                hbi = t % 2
                tk.op('dve', lambda e, t=t: e.scalar_tensor_tensor(out=tmpf[:], in0=x_sb[:, t, :], scalar=rstd[:, t:t + 1], in1=modA[:],
                                                                   op0=ALU.mult, op1=ALU.mult),
                      reads=['x%d' % t, 'rstd', 'modA'], writes=['tmpf'])
                tk.op('dve', lambda e: e.tensor_tensor(out=hb[hbi][:], in0=tmpf[:], in1=modN[:, 0:D], op=ALU.add),
                      reads=['tmpf', 'modN'], writes=['hb%d' % hbi])
                if stop('M3'):
                    continue
                b = nb()
                for kc in range(8):
                    tk.op('pe', lambda e, kc=kc: e.transpose(out=psb16(b)[:, kc * 128:(kc + 1) * 128], in_=hb[hbi][:, kc * 128:(kc + 1) * 128], identity=IDB),
                          reads=['hb%d' % hbi, 'cb16'], writes=[psk[b]])
                tk.op('act', lambda e, t=t: e.copy(out=hT[:, :, t * 128:(t + 1) * 128], in_=psb16(b)[:, :].rearrange("p (k c) -> p k c", k=8)),
                      reads=[psk[b]], writes=['hT'])

            if stop('M'):
                break
            tk.barrier()
            areset()
            qT = aget([128, 4, T], BF16)
            kT = aget([128, 4, T], BF16)
            ckT = aget([128, 4, 512], BF16)
            vaug = aget([128, NT, 8, 66], BF16)
            cvaug = aget([128, 4, 8, 66], BF16)
            sga = aget([128, NT, 512], BF16)
            expT = aget([128, 7, 8, 128], BF16)
            Eb = [aget([128, 8, 128], BF16) for _ in range(3)]
            Pb = [aget([128, 8, 128], BF16) for _ in range(2)]
            hk = [Eb[0].bitcast(F32) if False else None, None]
            ost = [aget([128, 512]) for _ in range(2)]
            rden = aget([128, 8])
            otmp = aget([128, 8, 64])
            ckb = aget([128, 4, 512], BF16)
            hkA = aget([128, 8, 128])
            hkB = aget([128, 8, 128])
            hk = [hkA, hkB]
            tk.op('pool', lambda e: e.memset(vaug[:, :, :, 64:66], 1.0), writes=['vaug'])
            tk.op('dve', lambda e: e.tensor_copy(out=cvaug[:, :, :, 64:66].rearrange("p a b c -> p (a b) c"),
                                                 in_=flags[:, 0:1].unsqueeze(2).broadcast_to([128, 32, 2])),
                  reads=['flags'], writes=['cvaug'])
            def toep_dma(di):
                dl = di - 3
                hi = di % 2
                for qr in range(2):
                    for krl in range(2):
                        off = ((l * 8) * 23 + (2 * dl + krl - qr + 11)) * 127
                        src = bass.AP(tpad_d, off, [[1, 64], [23 * 127, 8], [1, 64]])
                        tk.dma('sp', hk[hi][qr * 64:(qr + 1) * 64, :, krl * 64:(krl + 1) * 64], src, writes=['hk%d_%d' % (hi, qr * 2 + krl)])

            def toep_mm(di):
                hi = di % 2
                bA = nb()
                bB = nb()
                for h in range(8):
                    b = bA if h % 2 == 0 else bB
                    o = ps[b][:, (h // 2) * 128:(h // 2 + 1) * 128]
                    tk.op('pe', lambda e, h=h, o=o: e.matmul(o, lhsT=hk[hi][:, h, :], rhs=J2, start=True, stop=False),
                          reads=['hk%d_%d' % (hi, x) for x in range(4)] + ['cf32'], writes=[psk[b]])
                    tk.op('pe', lambda e, o=o: e.matmul(o, lhsT=IDF, rhs=CMT, start=False, stop=True),
                          reads=['cf32'], writes=[psk[b]])
                for bi, b in enumerate((bA, bB)):
                    tk.op('act', lambda e, bi=bi, b=b: e.activation(out=expT[:, di, bi * 4:(bi + 1) * 4, :].rearrange("p a c -> p (a c)"),
                                                                    in_=ps[b][:, :], func=AF.Exp),
                          reads=[psk[b]], writes=['expT%d' % di])

            toep_dma(0)
            toep_dma(1)
            if stop('A0'):
                break
            ckk = 'ckb'
            tk.dma('pool', ckb[:], ctxk_d.ap()[l].rearrange("(c p) n -> p c n", p=128), writes=[ckk])
            for c in range(4):
                b = nb()
                for pr in range(4):
                    tk.op('pe', lambda e, pr=pr: e.transpose(out=psb16(b)[:, pr * 128:(pr + 1) * 128], in_=ckb[:, c, pr * 128:(pr + 1) * 128], identity=IDB),
                          reads=[ckk, 'cb16'], writes=[psk[b]])
                tk.op('act', lambda e, c=c: e.copy(out=ckT[:, :, c * 128:(c + 1) * 128], in_=psb16(b)[:, 0:512].rearrange("p (k c) -> p k c", k=4)),
                      reads=[psk[b]], writes=['ckT'])
            tk.dma('pool', ckb[:], ctxv_d.ap()[l].rearrange("(c p) n -> p c n", p=128), reads=[], writes=[ckk])
            tk.op('pool', lambda e: e.tensor_copy(out=cvaug[:, :, :, 0:64], in_=ckb[:].rearrange("p c (h d) -> p c h d", h=8)), reads=[ckk], writes=['cvaug'])
            if stop('A1'):
                break
            toep_mm(0)
            toep_dma(2)
            wi = next_w()
            for pr in range(4):
                for g in range(2):
                    b = nb()
                    proj_fm(wi, pr, g, b)
                    tk.op('act', lambda e, pr=pr, g=g: e.copy(out=qT[:, pr, g * 512:(g + 1) * 512], in_=ps[b][:, :]), reads=[psk[b]], writes=['qT%d_%d' % (pr, g)])
            if stop('A1a'):
                break
            toep_mm(1)
            toep_dma(3)
            wi = next_w()
            for pr in range(4):
                for g in range(2):
                    b = nb()
                    proj_fm(wi, pr, g, b)
                    tk.op('act', lambda e, pr=pr, g=g: e.copy(out=kT[:, pr, g * 512:(g + 1) * 512], in_=ps[b][:, :]), reads=[psk[b]], writes=['kTf%d_%d' % (pr, g)])
            toep_mm(2)
            toep_dma(4)
            for t in range(NT):
                b = nb()
                proj_tm(wi, 0, 512, t, b)
                oi = t % 2
                tk.op('dve', lambda e: e.tensor_copy(out=ost[oi][:], in_=ps[b][:, :]), reads=[psk[b]], writes=['ost%d' % oi])
                tk.dma('sp', nk_d.ap()[l, t * 128:(t + 1) * 128, :], ost[oi][:], reads=['ost%d' % oi])
            toep_mm(3)
            toep_dma(5)
            if stop('A1b'):
                break
            wi = next_w()
            for t in range(NT):
                b = nb()
                proj_tm(wi, 0, 512, t, b)
                oi = t % 2
                tk.op('dve', lambda e: e.tensor_copy(out=ost[oi][:], in_=ps[b][:, :]), reads=[psk[b]], writes=['ost%d' % oi])
                tk.op('act', lambda e, t=t: e.copy(out=vaug[:, t, :, 0:64], in_=ost[oi][:].rearrange("p (h d) -> p h d", h=8)),
                      reads=['ost%d' % oi], writes=['vaug%d' % t])
                tk.dma('sp', nv_d.ap()[l, t * 128:(t + 1) * 128, :], ost[oi][:], reads=['ost%d' % oi])
            if stop('A1c'):
                break
            toep_mm(4)
            toep_dma(6)
            wi = next_w()
            for t in range(NT):
                b = nb()
                proj_tm(wi, 0, 512, t, b)
                tk.op('act', lambda e, t=t: e.activation(out=sga[:, t, :], in_=ps[b][:, :], func=AF.Silu), reads=[psk[b]], writes=['sga%d' % t])
            toep_mm(5)
            toep_mm(6)
            if stop('A2'):
                break
            OA, OB = 6, 7
            spairs = [(0, 1), (2, 3), (4, 5)]
            allsteps = []
            for j in range(8):
                st_ = [('l', kt) for kt in KT[j]] + [('c', c) for c in range(4)]
                for si_, (kind, idx) in enumerate(st_):
                    allsteps.append((j, si_, len(st_), kind, idx))

            def emit_S(k):
                j, si_, ns_, kind, idx = allsteps[k]
                sA, sB = spairs[k % 3]
                for h in range(8):
                    b = sA if h % 2 == 0 else sB
                    r0 = (h % 2) * 64
                    ksrc = kT[r0:r0 + 64, h // 2, idx * 128:(idx + 1) * 128] if kind == 'l' else ckT[r0:r0 + 64, h // 2, idx * 128:(idx + 1) * 128]
                    tk.op('pe', lambda e, h=h, b=b, ksrc=ksrc, r0=r0: e.matmul(ps[b][:, (h // 2) * 128:(h // 2 + 1) * 128], lhsT=ksrc,
                                                                              rhs=qT[r0:r0 + 64, h // 2, j * 128:(j + 1) * 128], start=True, stop=True),
                          reads=['qT%d_%d' % (h // 2, j // 4), ('kTf%d_%d' % (h // 2, idx // 4)) if kind == 'l' else 'ckT'], writes=[psk[b]])

            def emit_rest(k):
                j, si_, ns_, kind, idx = allsteps[k]
                sA, sB = spairs[k % 3]
                sl = k % 3
                big = psbig[sA // 2]
                if kind == 'l':
                    jk = JKI[(j, idx)]
                    for hf in range(2):
                        tk.op('act', lambda e, hf=hf: e.activation(
                            out=Eb[sl][:, :, hf * 64:(hf + 1) * 64],
                            in_=big[:, :].rearrange("p (a c) -> p a c", a=8)[:, :, hf * 64:(hf + 1) * 64],
                            func=AF.Exp, scale=0.125, bias=rowb[:, jk * 2 + hf:jk * 2 + hf + 1]),
                            reads=[psk[sA], psk[sB], 'rowb'], writes=['Eb%d' % sl])
                    di = idx - j + 3
                    pl = k % 2
                    tk.op('dve', lambda e, di=di: e.tensor_tensor(out=Pb[pl][:], in0=Eb[sl][:], in1=expT[:, di, :, :], op=ALU.mult),
                          reads=['Eb%d' % sl, 'expT%d' % di], writes=['Pb%d' % pl])
                    lhs, lk = Pb[pl], 'Pb%d' % pl
                    vsrc, vk = vaug, 'vaug%d' % idx
                else:
                    tk.op('act', lambda e: e.activation(out=Eb[sl][:].rearrange("p a c -> p (a c)"), in_=big[:, :], func=AF.Exp, scale=0.125),
                          reads=[psk[sA], psk[sB]], writes=['Eb%d' % sl])
                    lhs, lk = Eb[sl], 'Eb%d' % sl
                    vsrc, vk = cvaug, 'cvaug'
                for e_ in range(8):
                    h = 2 * (e_ % 4) + e_ // 4
                    ob = OA if e_ < 4 else OB
                    tk.op('pe', lambda e, e_=e_, h=h, ob=ob, lhs=lhs, vsrc=vsrc: e.matmul(
                        ps[ob][:, (e_ % 4) * 66:(e_ % 4) * 66 + 66], lhsT=lhs[:, e_, :], rhs=vsrc[:, idx, h, :],
                        start=(si_ == 0 and e_ % 4 == 0), stop=(si_ == ns_ - 1), skip_group_check=True),
                        reads=[lk, vk] + (['vaug'] if kind == 'l' else []), writes=[psk[ob]])
                if si_ == ns_ - 1:
                    for bi, ob in enumerate((OA, OB)):
                        tk.op('dve', lambda e, bi=bi, ob=ob: e.reciprocal(out=rden[:, bi * 4:(bi + 1) * 4],
                                                                          in_=ps[ob][:, 0:264].rearrange("p (a c) -> p a c", a=4)[:, :, 64]),
                              reads=[psk[ob]], writes=['rden'])
                    for bi, ob in enumerate((OA, OB)):
                        tk.op('dve', lambda e, bi=bi, ob=ob: e.tensor_tensor(
                            out=otmp[:, bi:8:2, :], in0=ps[ob][:, 0:264].rearrange("p (a c) -> p a c", a=4)[:, :, 0:64],
                            in1=rden[:, bi * 4:(bi + 1) * 4].unsqueeze(2).broadcast_to([128, 4, 64]), op=ALU.mult),
                            reads=[psk[ob], 'rden'], writes=['otmp'])
                    tk.op('dve', lambda e: e.tensor_tensor(out=mixed[:, j, 0:512], in0=otmp[:].rearrange("p h d -> p (h d)"), in1=sga[:, j, :], op=ALU.mult),
                          reads=['otmp', 'sga%d' % j], writes=['mixed%d' % j])

            emit_S(0)
            emit_S(1)
            for k in range(len(allsteps)):
                if k + 2 < len(allsteps):
                    emit_S(k + 2)
                emit_rest(k)

            if stop('A'):
                break
            tk.barrier()
            areset()
            qh = aget([128, NT, 256], BF16)
            sgb = aget([128, NT, 256])
            qE = sgb.rearrange("p a b -> p (a b)")[:, 0:1024].bitcast(BF16).rearrange("p (a b) -> p a b", a=NT)
            kE = sgb.rearrange("p a b -> p (a b)")[:, 1024:2048].bitcast(BF16).rearrange("p (a b) -> p a b", a=NT)
            vh = aget([128, NT, 256], BF16)
            vhmF = aget([128, 4096])
            vhm = vhmF.bitcast(BF16).rearrange("p (q t c) -> p q t c", q=4, t=NT)
            A_ = vhmF[:, 0:2048].rearrange("p (e c) -> p e c", e=64)
            B_ = vhmF[:, 2048:4096].rearrange("p (e c) -> p e c", e=64)
            sgr = aget([128, NT, 256], BF16)
            fS = aget([128, 4096])
            fbuf = fS[:, 0:2048].rearrange("p (a b) -> p a b", a=NT)
            SinPm = fS.bitcast(BF16).rearrange("p (h c e) -> p h c e", h=4, c=32)
            lfbuf = aget([128, NT, 256])
            kETm = lfbuf.rearrange("p a b -> p (a b)").bitcast(BF16).rearrange("p (h t) -> p h t", h=4)
            osq = lfbuf
            tmpE = [aget([128, 512]) for _ in range(2)]
            qET = aget([128, 2, T], BF16)
            ATm = [aget([128, 4, 128], BF16) for _ in range(2)]
            osum = aget([128, NT, 256])
            gs3 = aget([128, 2, 32, 3])
            es3 = aget([128, 2, 32, 3])
            S0b = aget([128, 2, 64])
            nsb = aget([128, 4, 2, 64])
            hss = aget([128, 32])
            tk.dma('sp', ghgB[:], bass.AP(ghg_d, l * 256, [[0, 128], [1, 256]]), writes=['ghgB'])
            wi = next_w()
            for t in range(NT):
                b = nb()
                proj_tm(wi, 0, 512, t, b)
                tk.op('act', lambda e, t=t: e.activation(out=qh[:, t, :], in_=ps[b][:, 0:256], func=AF.Silu), reads=[psk[b]], writes=['qh'])
                tk.op('act', lambda e, t=t: e.activation(out=sgb[:, t, :], in_=ps[b][:, 256:512], func=AF.Tanh, scale=0.5), reads=[psk[b]], writes=['sQ'])
            wi = next_w()
            for t in range(NT):
                b = nb()
                proj_tm(wi, 0, 512, t, b)
                tk.op('act', lambda e, t=t: e.activation(out=sgr[:, t, :], in_=ps[b][:, 256:512], func=AF.Silu), reads=[psk[b]], writes=['sgr'])
                tk.op('dve', lambda e, t=t: e.tensor_copy(out=vh[:, t, :], in_=ps[b][:, 0:256]), reads=[psk[b]], writes=['vh'])

            if stop('R0'):
                break
            rstop = False
            for dr in range(2):
                if l + 1 < nl:
                    mod_dma(l + 1, 3 * dr)
                lb_bc = lbl[:, dr, :].unsqueeze(1).broadcast_to([128, NT, 256])
                oml_bc = oml[:, dr, :].unsqueeze(1).broadcast_to([128, NT, 256])
                tk.op('dve', lambda e: e.tensor_tensor(out=fbuf[:], in0=sgb[:], in1=oml_bc, op=ALU.mult), reads=['sQ', 'oml'], writes=['fS'])
                tk.op('dve', lambda e: e.tensor_tensor(out=fbuf[:], in0=fbuf[:], in1=lb_bc, op=ALU.add), reads=['fS', 'lbl'], writes=['fS'])
                tk.op('act', lambda e: e.activation(out=lfbuf[:].rearrange("p a b -> p (a b)"), in_=fbuf[:].rearrange("p a b -> p (a b)"), func=AF.Ln),
                      reads=['fS'], writes=['lK'])
                tk.op('dve', lambda e: e.tensor_scalar(out=fbuf[:], in0=fbuf[:], scalar1=-1.0, scalar2=1.0, op0=ALU.mult, op1=ALU.add),
                      reads=['fS'], writes=['fS'])
                bS = nb()
                for t in range(NT):
                    for pr in range(2):
                        tt_ = t if dr == 0 else 7 - t
                        tk.op('pe', lambda e, t=t, pr=pr, tt_=tt_: e.matmul(ps[bS][:, (pr * 8 + tt_) * 8:(pr * 8 + tt_) * 8 + 8], lhsT=lfbuf[:, t, pr * 128:(pr + 1) * 128],
                                                                   rhs=SEL[dr], start=True, stop=True),
                              reads=['lK', 'cf32'], writes=[psk[bS]])
                psS = ps[bS][:, 0:128].rearrange("p (a c r) -> p a c r", a=2, c=32)
                tk.op('dve', lambda e: e.tensor_copy(out=gs3[:, :, :, 0:2], in_=psS), reads=[psk[bS]], writes=['gs3'])
                tk.op('dve', lambda e: e.tensor_tensor(out=gs3[:, :, :, 2], in0=gs3[:, :, :, 1], in1=gs3[:, :, :, 0], op=ALU.subtract),
                      reads=['gs3'], writes=['gs3'])
                tk.op('act', lambda e: e.activation(out=es3[:].rearrange("p a c r -> p (a c r)"), in_=gs3[:].rearrange("p a c r -> p (a c r)"), func=AF.Exp),
                      reads=['gs3'], writes=['es3'])
                tk.op('dve', lambda e: e.tensor_scalar(out=es3[:, :, 8:32:8, 0:2], in0=es3[:, :, 8:32:8, 0:2], scalar1=flags[:, 1:2], scalar2=None, op0=ALU.mult),
                      reads=['es3', 'flags'], writes=['es3'])
                for tp in range(4):
                    b = nb()
                    for i in range(2):
                        t = 2 * tp + i
                        tk.op('pe', lambda e, t=t, i=i: e.matmul(ps[b][:, i * 256:(i + 1) * 256], lhsT=TR[dr], rhs=lfbuf[:, t, :], start=True, stop=True),
                              reads=['cf32', 'lK'], writes=[psk[b]])
                    tk.op('act', lambda e: e.activation(out=tmpE[0][:], in_=ps[b][:, :], func=AF.Exp), reads=[psk[b]], writes=['tmpE0'])
                    tk.op('act', lambda e: e.activation(out=tmpE[1][:], in_=ps[b][:, :], func=AF.Exp, scale=-1.0), reads=[psk[b]], writes=['tmpE1'])
                    tk.op('dve', lambda e, tp=tp: e.tensor_tensor(out=qE[:, 2 * tp:2 * tp + 2, :].rearrange("p a c -> p (a c)"),
                                                                  in0=qh[:, 2 * tp:2 * tp + 2, :].rearrange("p a c -> p (a c)"), in1=tmpE[0][:], op=ALU.mult),
                          reads=['qh', 'tmpE0'], writes=['sQ', 'qk%d' % tp])
                    tk.op('dve', lambda e, tp=tp: e.tensor_tensor(out=kE[:, 2 * tp:2 * tp + 2, :].rearrange("p a c -> p (a c)"),
                                                                  in0=fbuf[:, 2 * tp:2 * tp + 2, :].rearrange("p a c -> p (a c)"), in1=tmpE[1][:], op=ALU.mult),
                          reads=['fS', 'tmpE1'], writes=['sQ', 'qk%d' % tp])
                for cp in range(4):
                    tk.op('dve', lambda e, cp=cp: e.tensor_scalar(out=vhm[:, cp, :, :], in0=vh[:], scalar1=QM[:, cp:cp + 1], scalar2=None, op0=ALU.mult),
                          reads=['vh', 'cf32'], writes=['vhm'])
                tk._deps('act', [], ['lK'])
                for t in range(NT):
                    b = nb()
                    for pr in range(2):
                        tk.op('pe', lambda e, t=t, pr=pr: e.transpose(out=psb16(b)[:, pr * 128:(pr + 1) * 128], in_=qE[:, t, pr * 128:(pr + 1) * 128], identity=IDB),
                              reads=['qk%d' % (t // 2), 'cb16'], writes=[psk[b]], war_only=['sQ'])
                        tk.op('pe', lambda e, t=t, pr=pr: e.transpose(out=psb16(b)[:, (2 + pr) * 128:(3 + pr) * 128], in_=kE[:, t, pr * 128:(pr + 1) * 128], identity=IDB),
                              reads=['qk%d' % (t // 2), 'cb16'], writes=[psk[b]], war_only=['sQ'])
                    tk.op('act', lambda e, t=t: e.copy(out=qET[:, :, t * 128:(t + 1) * 128], in_=psb16(b)[:, 0:256].rearrange("p (a c) -> p a c", a=2)),
                          reads=[psk[b]], writes=['qET%d' % t])
                    for h in range(4):
                        tk.op('act', lambda e, t=t, h=h: e.activation(out=kETm[:, h, t * 128:(t + 1) * 128], in_=psb16(b)[:, (2 + h // 2) * 128:(3 + h // 2) * 128],
                                                                      func=AF.Copy, scale=HM[:, h % 2:h % 2 + 1]),
                              reads=[psk[b], 'cf32'], writes=['kT%d_%d' % (t, h)])
                if stop('R1'):
                    rstop = True
                    break
                tk.dma('sp', S0b[:], s0_d.ap()[l, dr], writes=['S0b'])
                kvb_all = [[nb() for _ in range(4)] for _ in range(2)]
                for pr in range(2):
                    kvb = kvb_all[pr]
                    for c in range(32):
                        cq = c if dr == 0 else 31 - c
                        b = kvb[cq // 8]
                        for h in (2 * pr, 2 * pr + 1):
                            o = ps[b][(h % 2) * 64:(h % 2) * 64 + 64, (cq % 8) * 64:(cq % 8) * 64 + 64]
                            tk.op('pe', lambda e, c=c, h=h, o=o: e.matmul(o, lhsT=kE[:, c // 4, h * 64:(h + 1) * 64], rhs=vhm[:, c % 4, c // 4, h * 64:(h + 1) * 64],
                                                                          start=True, stop=True),
                                  reads=['sQ', 'vhm'], writes=[psk[b]])
                for pr in range(2):
                    kvb = kvb_all[pr]
                    for g in range(4):
                        tk.op('dve', lambda e, g=g: e.tensor_tensor(out=B_[:, :, g * 8:(g + 1) * 8].rearrange("p e c -> p c e"),
                                                                    in0=ps[kvb[g]][:, :].rearrange("p (c e) -> p c e", c=8),
                                                                    in1=es3[:, pr, g * 8:(g + 1) * 8, 2].unsqueeze(2).broadcast_to([128, 8, 64]), op=ALU.mult),
                              reads=[psk[kvb[g]], 'es3'], writes=['vhm'])
                    if dr == 0 and pr == 0:
                        wi = next_w()
                        for t in range(NT):
                            b = kvb[t % 4]
                            proj_tm(wi, 0, 256, t, b)
                            tk.op('act', lambda e, t=t: e.activation(out=sgb[:, t, :], in_=ps[b][:, 0:256], func=AF.Tanh, scale=0.5), reads=[psk[b]], writes=['sQ'])
                    if dr == 1 and pr == 0:
                        wi = next_w()
                        uT_h = qh[:].rearrange("p a b -> p (a b)").rearrange("p (c t) -> p c t", c=2)
                        sgf_h = qE
                        hb_ = 0
                        for ci in range(2):
                            for g in range(2):
                                b = kvb[hb_ % 4]
                                hb_ += 1
                                proj_fm(wi, ci, g, b)
                                tk.op('act', lambda e, ci=ci, g=g: e.copy(out=uT_h[:, ci, g * 512:(g + 1) * 512], in_=ps[b][:, :]), reads=[psk[b]], writes=['qh'])
                        for t in range(NT):
                            b = kvb[hb_ % 4]
                            hb_ += 1
                            proj_tm(wi, 256, 256, t, b)
                            tk.op('act', lambda e, t=t: e.activation(out=sgf_h[:, t, :], in_=ps[b][:, 0:256], func=AF.Silu), reads=[psk[b]], writes=['sQ'])
                    tk.op('dve', lambda e: e.scalar_tensor_tensor(out=B_[:, :, 0], in0=S0b[:, pr, :], scalar=es3[:, pr, 0, 1:2], in1=B_[:, :, 0], op0=ALU.mult, op1=ALU.add),
                          reads=['S0b', 'es3', 'vhm'], writes=['vhm'])
                    tk.op('dve', lambda e: e.tensor_copy(out=A_[:], in_=es3[:, pr, :, 1].unsqueeze(1).broadcast_to([128, 64, 32])), reads=['es3', 'vhm'], writes=['vhm'])
                    tk.op('dve', lambda e: e.memset(A_[:, :, 0:1], 0.0), reads=['vhm'], writes=['vhm'])
                    tk.op('dve', lambda e: e.tensor_tensor_scan(out=B_[:].rearrange("p e c -> p (e c)"), data0=A_[:].rearrange("p e c -> p (e c)"),
                                                                data1=B_[:].rearrange("p e c -> p (e c)"), initial=0.0, op0=ALU.mult, op1=ALU.add),
                          reads=['vhm'], writes=['vhm'])
                    tk.op('dve', lambda e: e.tensor_copy(out=nsb[:, :, pr, :], in_=B_[:, :, 7:32:8].rearrange("p e k -> p k e")), reads=['vhm'], writes=['nsb'])
                    for h2 in range(2):
                        tk.op('dve', lambda e, h2=h2: e.scalar_tensor_tensor(out=SinPm[:, 2 * pr + h2, 1:32, :], in0=B_[:, :, 0:31].rearrange("p e c -> p c e"),
                                                                             scalar=HM[:, h2:h2 + 1], in1=es3[:, pr, 1:32, 0].unsqueeze(2).broadcast_to([128, 31, 64]),
                                                                             op0=ALU.mult, op1=ALU.mult),
                              reads=['vhm', 'es3', 'cf32'], writes=['fS'])
                        tk.op('dve', lambda e, h2=h2: e.scalar_tensor_tensor(out=SinPm[:, 2 * pr + h2, 0, :], in0=S0b[:, pr, :], scalar=HM[:, h2:h2 + 1],
                                                                             in1=es3[:, pr, 0, 0:1].broadcast_to([128, 64]), op0=ALU.mult, op1=ALU.mult),
                              reads=['S0b', 'es3', 'cf32'], writes=['fS'])
                nsb3 = nsb[:].rearrange("p s a e -> p s (a e)")
                if dr == 0:
                    tk.dma('sp', ns_d.ap()[l, 0].rearrange("s p a e -> p s (a e)"), nsb3, reads=['nsb'])
                else:
                    for k in range(4):
                        tk.dma('sp', ns_d.ap()[l, 1, 3 - k].rearrange("p a e -> p (a e)"), nsb3[:, k, :], reads=['nsb'])
                if l + 1 < nl:
                    mod_mm(l + 1, 3 * dr)
                    mod_dma(l + 1, 3 * dr + 1)
                if stop('R2'):
                    rstop = True
                    break
                for t in range(NT):
                    bA_ = nb()
                    ai = t % 2
                    for h in range(4):
                        tk.op('pe', lambda e, t=t, h=h: e.matmul(ps[bA_][:, h * 128:(h + 1) * 128], lhsT=kETm[:, h, t * 128:(t + 1) * 128],
                                                                 rhs=qET[:, h // 2, t * 128:(t + 1) * 128], start=True, stop=True),
                              reads=['kT%d_%d' % (t, h), 'qET%d' % t], writes=[psk[bA_]], war_only=['lK'])
                    tk.op('dve', lambda e: e.tensor_tensor(out=ATm[ai][:], in0=ps[bA_][:, :].rearrange("p (a c) -> p a c", a=4),
                                                           in1=TRI[dr].unsqueeze(1).broadcast_to([128, 4, 128]), op=ALU.mult),
                          reads=[psk[bA_], 'cb16'], writes=['ATm%d' % ai])
                    bO = nb()
                    for h in range(4):
                        tk.op('pe', lambda e, t=t, h=h: e.matmul(ps[bO][:, h * 64:(h + 1) * 64], lhsT=ATm[ai][:, h, :], rhs=vh[:, t, h * 64:(h + 1) * 64],
                                                                 start=True, stop=False, skip_group_check=True),
                              reads=['ATm%d' % ai, 'vh'], writes=[psk[bO]])
                        for cp in range(4):
                            tk.op('pe', lambda e, t=t, h=h, cp=cp: e.matmul(ps[bO][cp * 32:(cp + 1) * 32, h * 64:(h + 1) * 64],
                                                                            lhsT=qET[:, h // 2, t * 128 + cp * 32:t * 128 + cp * 32 + 32],
                                                                            rhs=SinPm[:, h, (4 * t + cp) if dr == 0 else 31 - (4 * t + cp), :], start=False, stop=(cp == 3), skip_group_check=True,
                                                                            tile_position=(0, cp * 32)),
                                  reads=['qET%d' % t, 'fS'], writes=[psk[bO]])
                    if l + 1 < nl and t == 3:
                        mod_mm(l + 1, 3 * dr + 1)
                        mod_dma(l + 1, 3 * dr + 2)
                    if l + 1 < nl and t == 7:
                        mod_mm(l + 1, 3 * dr + 2)
                    if dr == 0:
                        tk.op('act', lambda e, t=t: e.copy(out=osum[:, t, :], in_=ps[bO][:, 0:256]), reads=[psk[bO]], writes=['osum%d' % t])
                    else:
                        tk.op('dve', lambda e, t=t: e.tensor_tensor(out=osum[:, t, :], in0=osum[:, t, :], in1=ps[bO][:, 0:256], op=ALU.add),
                              reads=[psk[bO], 'osum%d' % t], writes=['osum%d' % t])
                if stop('R3'):
                    rstop = True
                    break
            if rstop:
                break
            tk.op('dve', lambda e: e.tensor_tensor(out=osq[:], in0=osum[:], in1=osum[:], op=ALU.mult), reads=['osum%d' % t_ for t_ in range(NT)], writes=['lK'])
            tk.op('dve', lambda e: e.tensor_reduce(out=hss[:], in_=osq[:].rearrange("p t (h d) -> p (t h) d", h=4), axis=AX.X, op=ALU.add),
                  reads=['lK'], writes=['hss'])
            tk.op('dve', lambda e: e.tensor_scalar(out=hss[:], in0=hss[:], scalar1=1.0 / 64, scalar2=EPS, op0=ALU.mult, op1=ALU.add), reads=['hss'], writes=['hss'])
            tk.op('act', lambda e: e.activation(out=hss[:], in_=hss[:], func=AF.Ln), reads=['hss'], writes=['hss'])
            tk.op('act', lambda e: e.activation(out=hss[:], in_=hss[:], func=AF.Exp, scale=-0.5), reads=['hss'], writes=['hss'])
            tk.op('dve', lambda e: e.tensor_tensor(out=osum[:].rearrange("p t (h d) -> p (t h) d", h=4), in0=osum[:].rearrange("p t (h d) -> p (t h) d", h=4),
                                                   in1=hss[:].unsqueeze(2).broadcast_to([128, 32, 64]), op=ALU.mult),
                  reads=['osum%d' % t_ for t_ in range(NT)] + ['hss'], writes=['osum%d' % t_ for t_ in range(NT)])
            tk.op('dve', lambda e: e.tensor_tensor(out=osum[:], in0=osum[:], in1=ghgB[:].unsqueeze(1).broadcast_to([128, NT, 256]), op=ALU.mult),
                  reads=['osum%d' % t_ for t_ in range(NT)] + ['ghgB'], writes=['osum%d' % t_ for t_ in range(NT)])
            tk.op('dve', lambda e: e.tensor_tensor(out=mixed[:, :, 512:768], in0=osum[:], in1=sgr[:], op=ALU.mult),
                  reads=['osum%d' % t_ for t_ in range(NT)] + ['sgr'], writes=['mixed%d' % j for j in range(8)])

            if stop('R'):
                break
            tk.barrier()
            areset()
            uT = aget([128, 2, T], BF16)
            sgf = aget([128, NT, 256], BF16)
            ucs = aget([128, NT, 2, 256], BF16)
            yT = aget([128, 2, T], BF16)
            csnb = [aget([128, 2, 1024], BF16) for _ in range(4)]
            assert apos[0] + 2048 <= 11520
            wfs = aget([128, 2, 256])
            wfb = aget([128, 2, 256], BF16)
            junk = aget([128, 512], BF16)
            tmpf = aget([128, D])
            for kt_ in range(4):
                tk.dma('sp', csnb[kt_][:], csn_d.ap()[kt_], writes=['csnb%d' % kt_])
            assert True
            tk.dma('sp', wfs[:], wfn_d.ap()[l].rearrange("(c p) n -> p c n", p=128), writes=['wfs'])
            tk.op('pool', lambda e: e.tensor_copy(out=wfb[:], in_=wfs[:]), reads=['wfs'], writes=['wfb'])
            for t in range(NT):
                b = nb()
                for cs in range(2):
                    for ct in range(2):
                        tk.op('pe', lambda e, t=t, cs=cs, ct=ct: e.matmul(ps[b][:, cs * 256 + ct * 128:cs * 256 + ct * 128 + 128], lhsT=uT[:, ct, t * 128:(t + 1) * 128],
                                                                          rhs=C4S4[:, 3 + cs * 2 + ct, :], start=True, stop=True),
                              reads=['uT', 'cb16'], writes=[psk[b]])
                tk.op('act', lambda e, t=t: e.copy(out=ucs[:, t, :, :].rearrange("p a c -> p (a c)"), in_=ps[b][:, :]), reads=[psk[b]], writes=['ucs%d' % t])
            if stop('F0'):
                break
            yb = [nb() for _ in range(4)]
            for kt_ in range(8):
                ci = kt_ % 4
                if kt_ >= 4:
                    tk.dma('sp', csnb[ci][:], csn_d.ap()[kt_], writes=['csnb%d' % ci])
                for ct in range(2):
                    for g in range(2):
                        b = yb[ct * 2 + g]
                        for cs in range(2):
                            tk.op('pe', lambda e, kt_=kt_, ct=ct, g=g, cs=cs: e.matmul(ps[b][:, :], lhsT=ucs[:, kt_, cs, ct * 128:(ct + 1) * 128],
                                                                                       rhs=csnb[ci][:, cs, g * 512:(g + 1) * 512],
                                                                                       start=(kt_ == 0 and cs == 0), stop=(kt_ == 7 and cs == 1)),
                                  reads=['ucs%d' % kt_, 'csnb%d' % ci], writes=[psk[b]])
            for ct in range(2):
                for g in range(2):
                    b = yb[ct * 2 + g]
                    tk.op('act', lambda e, ct=ct, g=g, b=b: e.copy(out=yT[:, ct, g * 512:(g + 1) * 512], in_=ps[b][:, :]), reads=[psk[b]], writes=['yT%d_%d' % (ct, g)])
            for t in range(NT):
                b = nb()
                for ct in range(2):
                    tk.op('pe', lambda e, t=t, ct=ct: e.matmul(ps[b][:, 0:256], lhsT=yT[:, ct, t * 128:(t + 1) * 128], rhs=wfb[:, ct, :], start=(ct == 0), stop=(ct == 1)),
                          reads=['yT%d_%d' % (ct, t // 4), 'wfb'], writes=[psk[b]])
                tk.op('dve', lambda e, t=t: e.tensor_tensor(out=mixed[:, t, 768:1024], in0=ps[b][:, 0:256], in1=sgf[:, t, :], op=ALU.mult),
                      reads=[psk[b], 'sgf'], writes=['mixed%d' % t])

            if dbg and l == nl - 1:
                for t in range(NT):
                    tk.dma('sp', dbg_d.ap()[t * 128:(t + 1) * 128, :], mixed[:, t, :], reads=['mixed%d' % t])

            if stop('F1'):
                break
            w0 = next_w()
            w1 = next_w(prefetch=False)
            for t in range(NT):
                b = nb()
                for kc in range(8):
                    tk.op('pe', lambda e, t=t, kc=kc: e.transpose(out=psb16(b)[:, kc * 128:(kc + 1) * 128], in_=mixed[:, t, kc * 128:(kc + 1) * 128], identity=IDB),
                          reads=['mixed%d' % t, 'cb16'], writes=[psk[b]])
                tk.op('act', lambda e, t=t: e.copy(out=hT[:, :, t * 128:(t + 1) * 128], in_=psb16(b)[:, :].rearrange("p (k c) -> p k c", k=8)),
                      reads=[psk[b]], writes=['hT'])
            for t in range(NT):
                bb = [nb(), nb()]
                for hf, wi in enumerate((w0, w1)):
                    proj_tm(wi, 0, 512, t, bb[hf])
                    tk.op('act', lambda e, t=t, hf=hf: e.activation(out=junk[:], in_=ps[bb[hf]][:, :], func=AF.Square, accum_out=ssq[:, 8 + hf:9 + hf]),
                          reads=[psk[bb[hf]]], writes=['junk', 'ssq'])
                tk.op('dve', lambda e: e.tensor_tensor(out=rstd[:, 8:9], in0=ssq[:, 8:9], in1=ssq[:, 9:10], op=ALU.add), reads=['ssq'], writes=['rstd'])
                tk.op('dve', lambda e: e.tensor_scalar(out=rstd[:, 8:9], in0=rstd[:, 8:9], scalar1=1.0 / D, scalar2=EPS, op0=ALU.mult, op1=ALU.add),
                      reads=['rstd'], writes=['rstd'])
                tk.op('act', lambda e: e.activation(out=rstd[:, 8:9], in_=rstd[:, 8:9], func=AF.Ln), reads=['rstd'], writes=['rstd'])
                tk.op('act', lambda e: e.activation(out=rstd[:, 8:9], in_=rstd[:, 8:9], func=AF.Exp, scale=-0.5), reads=['rstd'], writes=['rstd'])
                for hf in range(2):
                    tk.op('dve', lambda e, hf=hf: e.scalar_tensor_tensor(out=tmpf[:, hf * 512:(hf + 1) * 512], in0=ps[bb[hf]][:, :], scalar=rstd[:, 8:9],
                                                                         in1=gg[:, hf * 512:(hf + 1) * 512], op0=ALU.mult, op1=ALU.mult),
                          reads=[psk[bb[hf]], 'rstd', 'gg'], writes=['tmpf'])
                tk.op('dve', lambda e, t=t: e.tensor_tensor(out=x_sb[:, t, :], in0=x_sb[:, t, :], in1=tmpf[:], op=ALU.add),
                      reads=['x%d' % t, 'tmpf'], writes=['x%d' % t])
            _issue(wstate['ptr'])

        for t in range(NT):
            tk.dma('sp', y_d.ap()[t * 128:(t + 1) * 128, :], x_sb[:, t, :], reads=['x%d' % t])
        tk.finish()
    return nc


def _consts(is_sample):
    cf32 = np.zeros((128, 7, 128), np.float32)
    p = np.arange(128)
    J2 = np.zeros((128, 128), np.float32)
    for a in range(2):
        for i in range(64):
            J2[a * 64 + i, a * 64 + 63 - i] = 1.0
    cf32[:, 0] = J2
    cf32[:, 1] = np.eye(128, dtype=np.float32)
    cm = np.zeros((128, 128), np.float32)
    if is_sample:
        qc = np.arange(64)
        c0 = np.clip(qc - 8, 0, 48)
        kc = np.arange(64)
        valid = (kc[:, None] >= c0[None, :]) & (kc[:, None] < c0[None, :] + 16)
        m = np.where(valid, 0.0, NEG).astype(np.float32)
        cm = np.tile(m, (2, 2))
    cf32[:, 2] = cm
    s = np.arange(32)[:, None]
    t = np.arange(32)[None, :]
    trf = (s <= t).astype(np.float32) - (s <= 15).astype(np.float32)
    trb = (s >= t).astype(np.float32) - (s >= 16).astype(np.float32)
    for a in range(4):
        cf32[a * 32:(a + 1) * 32, 3, a * 32:(a + 1) * 32] = trf
        cf32[a * 32:(a + 1) * 32, 4, a * 32:(a + 1) * 32] = trb
    sl = np.arange(128) % 32
    ch = np.arange(128) // 32
    selcols = np.zeros((128, 128), np.float32)
    for a in range(4):
        selcols[:, a * 2 + 0] = ((ch == a) & (sl <= 15))
        selcols[:, a * 2 + 1] = (ch == a)
        selcols[:, 8 + a * 2 + 0] = ((ch == 3 - a) & (sl >= 16))
        selcols[:, 8 + a * 2 + 1] = (ch == 3 - a)
        selcols[:, 18 + a] = (ch == a)
    selcols[:, 16] = (np.arange(128) < 64)
    selcols[:, 17] = (np.arange(128) >= 64)
    cf32[:, 5] = selcols
    cf32[0:64, 6, 0:64] = 1.0
    cf32[64:128, 6, 64:128] = 1.0
    rowb = np.zeros((128, 74), np.float32)
    for i, (j, kt) in enumerate(JK):
        for hf in range(2):
            for krl in range(2):
                if is_sample:
                    qr = 2 * j + hf
                    kr = 2 * kt + krl
                    r0 = int(np.clip(qr - 4, 0, 8))
                    ok = (r0 <= kr < r0 + 8)
                else:
                    ok = (kt // 2 == j // 2)
                rowb[krl * 64:(krl + 1) * 64, i * 2 + hf] = 0.0 if ok else NEG
    cb16 = np.zeros((128, 9, 128), np.float32)
    cb16[:, 7] = J2
    cb16[:, 8] = cm
    cb16[:, 0] = np.eye(128)
    mf = (s <= t).astype(np.float32)
    mb = (s >= t).astype(np.float32)
    z = np.zeros((64, 64), np.float32)
    for a in range(4):
        cb16[a * 32:(a + 1) * 32, 1, a * 32:(a + 1) * 32] = mf
        cb16[a * 32:(a + 1) * 32, 2, a * 32:(a + 1) * 32] = mb
    ang = 2 * np.pi * np.outer(np.arange(64), np.arange(64)) / 64
    c4 = np.cos(ang) / 8.0
    s4 = np.sin(ang) / 8.0
    for ct in range(2):
        cb16[:, 3 + ct] = np.block([[c4, z], [z, c4]])
        cb16[:, 5 + ct] = np.block([[s4, z], [z, s4]])
    n = 1024 if is_sample else 256
    idx = np.arange(n)
    a2 = 2 * np.pi * ((np.outer(idx, idx)) % n) / n
    cn = np.cos(a2) / np.sqrt(n)
    sn = -np.sin(a2) / np.sqrt(n)
    CN = np.zeros((1024, 1024), np.float64)
    SN = np.zeros((1024, 1024), np.float64)
    for i in range(1024 // n):
        CN[i * n:(i + 1) * n, i * n:(i + 1) * n] = cn
        SN[i * n:(i + 1) * n, i * n:(i + 1) * n] = sn
    csn = np.stack([CN.reshape(8, 128, 1024), SN.reshape(8, 128, 1024)], axis=2)
    return dict(cf32=cf32, rowbias=rowb, cb16=cb16.astype(ml_dtypes.bfloat16), csn=csn.astype(ml_dtypes.bfloat16))


def _in_maps(x_prompt, x_sample, cache_attn_k, cache_attn_v, state_hgrn, c, c_ctx,
             w_ada, b_ada, g_pre, w_in, rpb, lb_logits, g_hgrn, w_fnet, w_out, g_post):
    f = lambda a: np.ascontiguousarray(np.asarray(a, dtype=np.float32))
    shared = dict(w_ada=f(w_ada), b_ada=f(b_ada), g_pre=f(g_pre), w_in=f(w_in), lb_logits=f(lb_logits),
                  g_hgrn=f(g_hgrn), w_fnet=f(w_fnet), w_out=f(w_out), g_post=f(g_post))
    tp = np.zeros((NL, 8, 23, 127), np.float32)
    tp[:, :, 4:19, 48:79] = f(rpb)
    cs = _consts(True)
    cp = _consts(False)
    maps = []
    for i in range(8):
        m = dict(shared)
        if i < 4:
            m["x"] = f(x_sample[i])
            m["cvec"] = f(np.asarray(c[i]).reshape(8, 128).T)
            m["ctxk"] = f(np.asarray(cache_attn_k[i]).reshape(NL, 512, 512))
            m["ctxv"] = f(np.asarray(cache_attn_v[i]).reshape(NL, 512, 512))
            s = np.asarray(state_hgrn[i]).reshape(NL, 2, 2, 2, 64, 64)
            m["s0"] = f(s.transpose(0, 1, 3, 4, 2, 5).reshape(NL, 2, 128, 2, 64))
            m["flags"] = np.ones((128, 2), np.float32)
            m["tpad"] = tp
            m.update(cs)
        else:
            m["x"] = f(np.asarray(x_prompt[4 * (i - 4):4 * (i - 3)]).reshape(T, D))
            m["cvec"] = f(np.asarray(c_ctx).reshape(8, 128).T)
            m["ctxk"] = np.zeros((NL, 512, 512), np.float32)
            m["ctxv"] = np.zeros((NL, 512, 512), np.float32)
            m["s0"] = np.zeros((NL, 2, 128, 2, 64), np.float32)
            m["flags"] = np.zeros((128, 2), np.float32)
            m["tpad"] = np.zeros_like(tp)
            m.update(cp)
        maps.append(m)
    return maps


_NC_CACHE = {}


def kernel(**inputs):
    if 'nc' not in _NC_CACHE:
        _NC_CACHE['nc'] = build_nc()
    nc = _NC_CACHE['nc']
    maps = _in_maps(**inputs)
    res = run_bass_kernel_spmd(nc, maps, core_ids=list(range(8)))
    r = res.results
    y_sample = np.stack([r[i]["y"] for i in range(4)], axis=0).astype(np.float32)
    y_prompt = np.concatenate([r[i]["y"].reshape(4, 256, D) for i in range(4, 8)], axis=0).astype(np.float32)
    nk = np.concatenate([r[i]["newk"].reshape(NL, 4, 256, 8, 64).transpose(1, 0, 2, 3, 4) for i in range(4, 8)], axis=0)
    nv = np.concatenate([r[i]["newv"].reshape(NL, 4, 256, 8, 64).transpose(1, 0, 2, 3, 4) for i in range(4, 8)], axis=0)
    ns = np.concatenate([r[i]["news"].reshape(NL, 2, 4, 2, 64, 2, 64).transpose(2, 0, 1, 5, 3, 4, 6).reshape(4, NL, 2, 4, 64, 64)
                         for i in range(4, 8)], axis=0)
    return (y_prompt, y_sample, np.ascontiguousarray(nk, dtype=np.float32), np.ascontiguousarray(nv, dtype=np.float32),
            np.ascontiguousarray(ns, dtype=np.float32))
```

```python
import numpy as np
import ml_dtypes
from contextlib import ExitStack
import concourse.bass as bass
import concourse.mybir as mybir
from concourse.bass_utils import run_bass_kernel_spmd

F32 = mybir.dt.float32
BF16 = mybir.dt.bfloat16
AF = mybir.ActivationFunctionType
ALU = mybir.AluOpType
AX = mybir.AxisListType

NL = 4
D = 1024
T = 1024
NT = 8
EPS = 1e-6
NEG = -30000.0
KT = {0: [0, 1, 2, 3], 1: [0, 1, 2, 3], 2: [0, 1, 2, 3, 4], 3: [1, 2, 3, 4, 5],
      4: [2, 3, 4, 5, 6], 5: [3, 4, 5, 6, 7], 6: [4, 5, 6, 7], 7: [4, 5, 6, 7]}
JK = [(j, kt) for j in range(8) for kt in KT[j]]
JKI = {p: i for i, p in enumerate(JK)}
NDS = 24
NSW = 72


class TK:
    def __init__(s, nc, st):
        s.nc = nc
        s.E = {'pe': nc.tensor, 'act': nc.scalar, 'dve': nc.vector, 'pool': nc.gpsimd, 'sp': nc.sync}
        s.sem = {k: st.enter_context(nc.semaphore('s_' + k)) for k in ('pe', 'act', 'dve', 'pool')}
        s.cnt = {k: 0 for k in s.E}
        s.seen = {k: {} for k in s.E}
        s.lw = {}
        s.rd = {}
        s.dsems = [st.enter_context(nc.semaphore('d%d' % i)) for i in range(NDS)]
        s.dcnt = [0] * NDS
        s.dnext = 0
        s.swsems = [st.enter_context(nc.semaphore('w%d' % i)) for i in range(NSW)]
        s.swnext = 0
        s.swlow = 0

    def _wait(s, eng, key, val):
        if eng == 'pe' and key == 'pe':
            return
        if s.seen[eng].get(key, 0) >= val:
            return
        if isinstance(key, str):
            semobj = s.sem[key]
        elif key >= 1000:
            semobj = s.swsems[key - 1000]
        else:
            semobj = s.dsems[key]
        s.E[eng].wait_ge(semobj, val)
        s.seen[eng][key] = val

    def _deps(s, eng, reads, writes):
        for k in reads:
            w = s.lw.get(k)
            if w:
                s._wait(eng, *w)
            if k.startswith('ps'):
                for rk, rv in s.rd.get(k, {}).items():
                    if rk != eng:
                        s._wait(eng, rk, rv)
        for k in writes:
            w = s.lw.get(k)
            if w:
                s._wait(eng, *w)
            for rk, rv in s.rd.get(k, {}).items():
                s._wait(eng, rk, rv)

    def _book(s, tag, reads, writes):
        for k in reads:
            d = s.rd.setdefault(k, {})
            d[tag[0]] = max(d.get(tag[0], 0), tag[1])
        for k in writes:
            s.lw[k] = tag
            s.rd[k] = {}

    def op(s, eng, fn, reads=(), writes=(), war_only=()):
        s._deps(eng, reads, writes)
        inst = fn(s.E[eng])
        s.cnt[eng] += 1
        inst.then_inc(s.sem[eng], 1)
        s._book((eng, s.cnt[eng]), tuple(reads) + tuple(war_only), writes)

    def dma(s, q, out, in_, reads=(), writes=()):
        if q == 'pool':
            assert s.swnext < NSW, "out of one-shot semaphores"
            i = s.swnext
            s.swnext += 1
            s._deps(q, reads, writes)
            s.E[q].dma_start(out=out, in_=in_).then_inc(s.swsems[i], 16)
            s._book((1000 + i, 16), reads, writes)
            return
        i = s.dnext
        s.dnext = (s.dnext + 1) % NDS
        if s.dcnt[i] > 0:
            s._wait(q, i, s.dcnt[i])
        s._deps(q, reads, writes)
        s.dcnt[i] += 16
        s.E[q].dma_start(out=out, in_=in_).then_inc(s.dsems[i], 16)
        s._book((i, s.dcnt[i]), reads, writes)

    def barrier(s):
        engs = ('pe', 'act', 'dve', 'pool', 'sp')
        snap = dict(s.cnt)
        dsnap = list(s.dcnt)
        for e in engs:
            for o in ('pe', 'act', 'dve', 'pool'):
                if o != e and snap[o] > 0:
                    s._wait(e, o, snap[o])
            for i in range(NDS):
                if dsnap[i] > 0:
                    s._wait(e, i, dsnap[i])
            for i in range(s.swlow, s.swnext):
                s._wait(e, 1000 + i, 16)
        s.swlow = s.swnext

    def finish(s):
        for i in range(NDS):
            if s.dcnt[i] > 0:
                s._wait('sp', i, s.dcnt[i])
        for i in range(s.swnext):
            s._wait('sp', 1000 + i, 16)
        for k in ('pe', 'act', 'dve', 'pool'):
            if s.cnt[k] > 0:
                s._wait('sp', k, s.cnt[k])


def build_nc(nl=NL, dbg=False, upto=None):
    nc = bass.Bass("TRN2", target_bir_lowering=False)
    _order = ['M0', 'M1', 'M2', 'M3', 'M', 'A0', 'A1', 'A1a', 'A1b', 'A1c', 'A2', 'A', 'R0', 'R1', 'R2', 'R3', 'R', 'F0', 'F1', 'F']

    def stop(p):
        return upto is not None and _order.index(upto) <= _order.index(p)

    def din(name, shape, dt=F32):
        return nc.dram_tensor(name, list(shape), dt, kind="ExternalInput")

    def dout(name, shape, dt=F32):
        return nc.dram_tensor(name, list(shape), dt, kind="ExternalOutput")

    x_d = din("x", [T, D])
    cvec_d = din("cvec", [128, 8])
    ctxk_d = din("ctxk", [NL, 512, 512])
    ctxv_d = din("ctxv", [NL, 512, 512])
    s0_d = din("s0", [NL, 2, 128, 2, 64])
    flags_d = din("flags", [128, 2])
    wada_d = din("w_ada", [NL, D, 3 * D])
    bada_d = din("b_ada", [NL, 3 * D])
    gpre_d = din("g_pre", [NL, D])
    win_d = din("w_in", [NL, D, 3840])
    tpad_d = din("tpad", [NL, 8, 23, 127])
    lbl_d = din("lb_logits", [2, NL, 256])
    ghg_d = din("g_hgrn", [NL, 256])
    wfn_d = din("w_fnet", [NL, 256, 256])
    wout_d = din("w_out", [NL, D, D])
    gpost_d = din("g_post", [NL, D])
    cf32_d = din("cf32", [128, 7, 128])
    rowb_d = din("rowbias", [128, 74])
    cb16_d = din("cb16", [128, 9, 128], BF16)
    csn_d = din("csn", [8, 128, 2, 1024], BF16)

    y_d = dout("y", [T, D])
    nk_d = dout("newk", [NL, T, 512])
    nv_d = dout("newv", [NL, T, 512])
    ns_d = dout("news", [NL, 2, 4, 128, 2, 64])
    dbg_d = dout("dbgmixed", [T, D], BF16) if dbg else None

    with ExitStack() as st:
        def sb(name, shape, dt=F32):
            return st.enter_context(nc.sbuf_tensor(name, list(shape), dt))

        tk = TK(nc, st)
        x_sb = sb("x_sb", [128, NT, D])
        hT = sb("hT", [128, 8, T], BF16)
        mixed = sb("mixed", [128, NT, D], BF16)
        wst = [sb("wst0", [128, 8, 512], BF16)]
        wbf = [sb("wbf%d" % i, [128, 8, 512], BF16) for i in range(2)]
        gg = sb("gg", [128, D])
        modN = sb("modN", [128, 3 * D], BF16)
        screp = sb("screp", [128, 8, 128], BF16)
        brow = sb("brow", [1, 512])
        ones_row = sb("ones_row", [1, 128])
        csil = sb("csil", [128, 8])
        cf32 = sb("cf32s", [128, 7, 128])
        rowb = sb("rowbs", [128, 74])
        cb16 = sb("cb16s", [128, 9, 128], BF16)
        flags = sb("flagss", [128, 2])
        lbl = sb("lbl", [128, 2, 256])
        oml = sb("oml", [128, 2, 256])
        ghgB = sb("ghgB", [128, 256])
        ssq = sb("ssq", [128, 16])
        rstd = sb("rstd", [128, 16])
        ARW = 21120
        arena = sb("arena", [128, ARW])
        apos = [0]

        def areset():
            apos[0] = 0

        def aget(shape, dt=F32):
            n = 1
            for d_ in shape[1:]:
                n *= d_
            words = n if dt == F32 else (n + 1) // 2
            a0 = apos[0]
            apos[0] += words
            assert apos[0] <= ARW, ("arena overflow", apos[0])
            v = arena[:, a0:a0 + words]
            if dt != F32:
                v = v.bitcast(dt)
            if len(shape) == 3:
                v = v.rearrange("p (a b) -> p a b", a=shape[1])
            elif len(shape) == 4:
                v = v.rearrange("p (a b c) -> p a b c", a=shape[1], b=shape[2])
            return v

        psbig = [st.enter_context(nc.psum_tensor("psb%d" % i, [128, 1024], F32)) for i in range(4)]
        ps = [psbig[i // 2][:, (i % 2) * 512:(i % 2 + 1) * 512] for i in range(8)]
        psk = ["ps%d" % i for i in range(8)]
        bank_rr = [0]

        def nb(avoid=()):
            while True:
                b = bank_rr[0]
                bank_rr[0] = (b + 1) % 8
                if b not in avoid:
                    return b

        J2 = cf32[:, 0, :]
        IDF = cf32[:, 1, :]
        CMT = cf32[:, 2, :]
        TR = [cf32[:, 3, :], cf32[:, 4, :]]
        SEL = [cf32[:, 5, 0:8], cf32[:, 5, 8:16]]
        HM = cf32[:, 5, 16:18]
        QM = cf32[:, 5, 18:22]
        HME = cf32[:, 6, :].rearrange("p (a c) -> p a c", a=2)
        IDB = cb16[:, 0, :]
        TRI = [cb16[:, 1, :], cb16[:, 2, :]]
        C4S4 = cb16
        J2B = cb16[:, 7, :]
        CMTB = cb16[:, 8, :]

        tk.dma('sp', cf32[:], cf32_d.ap(), writes=['cf32'])
        tk.dma('sp', rowb[:], rowb_d.ap(), writes=['rowb'])
        tk.dma('sp', cb16[:], cb16_d.ap(), writes=['cb16'])
        tk.dma('sp', flags[:], flags_d.ap(), writes=['flags'])
        tk.dma('sp', csil[:], cvec_d.ap(), writes=['csil'])
        for t in range(NT):
            tk.dma('sp', x_sb[:, t, :], x_d.ap()[t * 128:(t + 1) * 128, :], writes=['x%d' % t])
        tk.op('pool', lambda e: e.memset(ones_row[:], 1.0), writes=['ones_row'])
        tk.op('act', lambda e: e.activation(out=csil[:], in_=csil[:], func=AF.Silu), reads=['csil'], writes=['csil'])

        wring = [0]

        wbring = [0]

        def load_w(src_ap, ncols):
            wi = wbring[0]
            wbring[0] ^= 1
            tk.dma('pool', wbf[wi][:, :, 0:ncols], src_ap.rearrange("(kc p) n -> p kc n", p=128), writes=['wbf%d' % wi])
            return wi

        wseq = []
        for l_ in range(nl):
            for (c0_, n_) in ((0, 512), (512, 512), (1024, 512), (1536, 512), (2048, 512), (2816, 512), (2560, 256), (3328, 512)):
                wseq.append((win_d, l_, c0_, n_))
            wseq.append((wout_d, l_, 0, 512))
            wseq.append((wout_d, l_, 512, 512))
        wstate = {'ptr': 0, 'loaded': {}}

        def _issue(i):
            if i < len(wseq) and i not in wstate['loaded']:
                d_, l_, c0_, n_ = wseq[i]
                wstate['loaded'][i] = load_w(d_.ap()[l_, :, c0_:c0_ + n_], n_)

        def next_w(prefetch=True):
            i = wstate['ptr']
            wstate['ptr'] += 1
            _issue(i)
            if prefetch:
                _issue(i + 1)
            return wstate['loaded'][i]

        def proj_tm(wi, c0, ncols, t, b):
            for kc in range(8):
                tk.op('pe', lambda e, kc=kc: e.matmul(ps[b][:, 0:ncols], lhsT=hT[:, kc, t * 128:(t + 1) * 128],
                                                     rhs=wbf[wi][:, kc, c0:c0 + ncols], start=(kc == 0), stop=(kc == 7)),
                      reads=['hT', 'wbf%d' % wi], writes=[psk[b]])

        def proj_fm(wi, ci, g, b):
            for kc in range(8):
                tk.op('pe', lambda e, kc=kc: e.matmul(ps[b][:, 0:512], lhsT=wbf[wi][:, kc, ci * 128:(ci + 1) * 128],
                                                     rhs=hT[:, kc, g * 512:(g + 1) * 512], start=(kc == 0), stop=(kc == 7)),
                      reads=['hT', 'wbf%d' % wi], writes=[psk[b]])

        def psb16(b):
            return ps[b].bitcast(BF16)

        def emit_mod_chunk(lm, ch, banks=None):
            mod_dma(lm, ch)
            mod_mm(lm, ch, banks)

        def mod_dma(lm, ch):
            tk.dma('pool', wst[0][:], wada_d.ap()[lm, :, ch * 512:(ch + 1) * 512].rearrange("(kc p) n -> p kc n", p=128), writes=['wst0'])
            tk.dma('sp', brow[:], bada_d.ap()[lm:lm + 1, ch * 512:(ch + 1) * 512], writes=['brow'])

        def mod_mm(lm, ch, banks=None):
            b = nb() if banks is None else banks[ch % len(banks)]
            for kc in range(8):
                tk.op('pe', lambda e, kc=kc: e.matmul(ps[b][:, :], lhsT=screp[:, kc, :], rhs=wst[0][:, kc, :], start=(kc == 0), stop=False),
                      reads=['screp', 'wst0'], writes=[psk[b]])
            tk.op('pe', lambda e: e.matmul(ps[b][:, :], lhsT=ones_row[0:1, :], rhs=brow[0:1, :], start=False, stop=True),
                  reads=['ones_row', 'brow'], writes=[psk[b]])
            tk.op('act', lambda e: e.copy(out=modN[:, ch * 512:(ch + 1) * 512], in_=ps[b][:, :]), reads=[psk[b]], writes=['modN'])

        tk.op('dve', lambda e: e.tensor_copy(out=screp[:], in_=csil[:].unsqueeze(2).broadcast_to([128, 8, 128])), reads=['csil'], writes=['screp'])
        for ch in range(6):
            emit_mod_chunk(0, ch)

        for l in range(nl):
            if l == 0:
                tk.barrier()
            areset()
            apos[0] = 11520
            gbc = aget([128, 2, D])
            lbt = aget([128, 2, NL, 256])
            junk = aget([128, D], BF16)
            tmpf = aget([128, D])
            hb = [aget([128, D], BF16) for _ in range(2)]
            modA = aget([128, D])
            tk.dma('sp', gbc[:, 0, :], bass.AP(gpre_d, l * D, [[0, 128], [1, D]]), writes=['gbc'])
            tk.dma('sp', gbc[:, 1, :], bass.AP(gpost_d, l * D, [[0, 128], [1, D]]), writes=['gbc'])
            tk.dma('sp', lbt[:].rearrange("p a l c -> p (a l c)"), bass.AP(lbl_d, 0, [[0, 128], [1, 2 * NL * 256]]), writes=['lbt'])
            if l == 0:
                tk.op('dve', lambda e: e.memset(lbl[:], 0.0), writes=['lbl'])
            else:
                mx = tmpf[:, 0:512].rearrange("p (a c) -> p a c", a=2)
                sm = tmpf[:, 512:1024].rearrange("p (a c) -> p a c", a=2)
                tk.op('dve', lambda e: e.tensor_tensor(out=mx, in0=lbt[:, :, 0, :], in1=lbt[:, :, 1, :], op=ALU.max),
                      reads=['lbt'], writes=['tmpf'])
                for l2 in range(2, NL):
                    tk.op('dve', lambda e, l2=l2: e.tensor_tensor(out=mx, in0=mx, in1=lbt[:, :, l2, :], op=ALU.max),
                          reads=['lbt', 'tmpf'], writes=['tmpf'])
                for l2 in range(NL):
                    tk.op('dve', lambda e, l2=l2: e.tensor_tensor(out=lbt[:, :, l2, :], in0=lbt[:, :, l2, :], in1=mx, op=ALU.subtract),
                          reads=['lbt', 'tmpf'], writes=['lbt'])
                tk.op('act', lambda e: e.activation(out=lbt[:].rearrange("p a l c -> p (a l c)"), in_=lbt[:].rearrange("p a l c -> p (a l c)"), func=AF.Exp),
                      reads=['lbt'], writes=['lbt'])
                tk.op('dve', lambda e: e.tensor_tensor(out=sm, in0=lbt[:, :, 0, :], in1=lbt[:, :, 1, :], op=ALU.add),
                      reads=['lbt'], writes=['tmpf'])
                for l2 in range(2, NL):
                    tk.op('dve', lambda e, l2=l2: e.tensor_tensor(out=sm, in0=sm, in1=lbt[:, :, l2, :], op=ALU.add),
                          reads=['lbt', 'tmpf'], writes=['tmpf'])
                tk.op('dve', lambda e: e.reciprocal(out=sm, in_=sm), reads=['tmpf'], writes=['tmpf'])
                tk.op('dve', lambda e: e.tensor_copy(out=lbl[:], in_=lbt[:, :, 1, :]), reads=['lbt'], writes=['lbl'])
                for l2 in range(2, l + 1):
                    tk.op('dve', lambda e, l2=l2: e.tensor_tensor(out=lbl[:], in0=lbl[:], in1=lbt[:, :, l2, :], op=ALU.add),
                          reads=['lbt', 'lbl'], writes=['lbl'])
                tk.op('dve', lambda e: e.tensor_tensor(out=lbl[:], in0=lbl[:], in1=sm, op=ALU.mult), reads=['lbl', 'tmpf'], writes=['lbl'])
            tk.op('dve', lambda e: e.tensor_scalar(out=oml[:], in0=lbl[:], scalar1=-0.5, scalar2=0.5, op0=ALU.mult, op1=ALU.add),
                  reads=['lbl'], writes=['oml'])
            tk.op('dve', lambda e: e.tensor_scalar(out=lbl[:], in0=lbl[:], scalar1=0.5, scalar2=0.5, op0=ALU.mult, op1=ALU.add),
                  reads=['lbl'], writes=['lbl'])
            if stop('M0'):
                break
            tk.op('dve', lambda e: e.scalar_tensor_tensor(out=modA[:], in0=modN[:, D:2 * D], scalar=1.0, in1=gbc[:, 0, :], op0=ALU.add, op1=ALU.mult),
                  reads=['modN', 'gbc'], writes=['modA'])
            tk.op('dve', lambda e: e.tensor_tensor(out=gg[:], in0=modN[:, 2 * D:3 * D], in1=gbc[:, 1, :], op=ALU.mult), reads=['modN', 'gbc'], writes=['gg'])
            if stop('M1'):
                break
            for t in range(NT):
                tk.op('act', lambda e, t=t: e.activation(out=junk[:], in_=x_sb[:, t, :], func=AF.Square, accum_out=ssq[:, t:t + 1]),
                      reads=['x%d' % t], writes=['ssq%d' % t])
            tk.op('dve', lambda e: e.tensor_scalar(out=rstd[:, 0:8], in0=ssq[:, 0:8], scalar1=1.0 / D, scalar2=EPS, op0=ALU.mult, op1=ALU.add),
                  reads=['ssq%d' % t_ for t_ in range(NT)], writes=['rstd'])
            tk.op('act', lambda e: e.activation(out=rstd[:, 0:8], in_=rstd[:, 0:8], func=AF.Ln), reads=['rstd'], writes=['rstd'])
            tk.op('act', lambda e: e.activation(out=rstd[:, 0:8], in_=rstd[:, 0:8], func=AF.Exp, scale=-0.5), reads=['rstd'], writes=['rstd'])
            if stop('M2'):
                break
            for t in range(NT):
                hbi = t % 2
                tk.op('dve', lambda e, t=t: e.scalar_tensor_tensor(out=tmpf[:], in0=x_sb[:, t, :], scalar=rstd[:, t:t + 1], in1=modA[:],
                                                                   op0=ALU.mult, op1=ALU.mult),
                      reads=['x%d' % t, 'rstd', 'modA'], writes=['tmpf'])
                tk.op('dve', lambda e: e.tensor_tensor(out=hb[hbi][:], in0=tmpf[:], in1=modN[:, 0:D], op=ALU.add),
                      reads=['tmpf', 'modN'], writes=['hb%d' % hbi])
                if stop('M3'):
                    continue
                b = nb()
                for kc in range(8):
                    tk.op('pe', lambda e, kc=kc: e.transpose(out=psb16(b)[:, kc * 128:(kc + 1) * 128], in_=hb[hbi][:, kc * 128:(kc + 1) * 128], identity=IDB),
                          reads=['hb%d' % hbi, 'cb16'], writes=[psk[b]])
                tk.op('act', lambda e, t=t: e.copy(out=hT[:, :, t * 128:(t + 1) * 128], in_=psb16(b)[:, :].rearrange("p (k c) -> p k c", k=8)),
                      reads=[psk[b]], writes=['hT'])

            if stop('M'):
                break
            tk.barrier()
            areset()
            qT = aget([128, 4, T], BF16)
            kT = aget([128, 4, T], BF16)
            ckT = aget([128, 4, 512], BF16)
            vaug = aget([128, NT, 8, 66], BF16)
            cvaug = aget([128, 4, 8, 66], BF16)
            sga = aget([128, NT, 512], BF16)
            expT = aget([128, 7, 8, 128], BF16)
            Eb = [aget([128, 8, 128], BF16) for _ in range(3)]
            Pb = [aget([128, 8, 128], BF16) for _ in range(2)]
            hk = [Eb[0].bitcast(F32) if False else None, None]
            ost = [aget([128, 512]) for _ in range(2)]
            rden = aget([128, 8])
            otmp = aget([128, 8, 64])
            ckb = aget([128, 4, 512], BF16)
            hkA = aget([128, 8, 128])
            hkB = aget([128, 8, 128])
            hk = [hkA, hkB]
            tk.op('pool', lambda e: e.memset(vaug[:, :, :, 64:66], 1.0), writes=['vaug'])
            tk.op('dve', lambda e: e.tensor_copy(out=cvaug[:, :, :, 64:66].rearrange("p a b c -> p (a b) c"),
                                                 in_=flags[:, 0:1].unsqueeze(2).broadcast_to([128, 32, 2])),
                  reads=['flags'], writes=['cvaug'])
            def toep_dma(di):
                dl = di - 3
                hi = di % 2
                for qr in range(2):
                    for krl in range(2):
                        off = ((l * 8) * 23 + (2 * dl + krl - qr + 11)) * 127
                        src = bass.AP(tpad_d, off, [[1, 64], [23 * 127, 8], [1, 64]])
                        tk.dma('sp', hk[hi][qr * 64:(qr + 1) * 64, :, krl * 64:(krl + 1) * 64], src, writes=['hk%d_%d' % (hi, qr * 2 + krl)])

            def toep_mm(di):
                hi = di % 2
                bA = nb()
                bB = nb()
                for h in range(8):
                    b = bA if h % 2 == 0 else bB
                    o = ps[b][:, (h // 2) * 128:(h // 2 + 1) * 128]
                    tk.op('pe', lambda e, h=h, o=o: e.matmul(o, lhsT=hk[hi][:, h, :], rhs=J2, start=True, stop=False),
                          reads=['hk%d_%d' % (hi, x) for x in range(4)] + ['cf32'], writes=[psk[b]])
                    tk.op('pe', lambda e, o=o: e.matmul(o, lhsT=IDF, rhs=CMT, start=False, stop=True),
                          reads=['cf32'], writes=[psk[b]])
                for bi, b in enumerate((bA, bB)):
                    tk.op('act', lambda e, bi=bi, b=b: e.activation(out=expT[:, di, bi * 4:(bi + 1) * 4, :].rearrange("p a c -> p (a c)"),
                                                                    in_=ps[b][:, :], func=AF.Exp),
                          reads=[psk[b]], writes=['expT%d' % di])

            toep_dma(0)
            toep_dma(1)
            if stop('A0'):
                break
            ckk = 'ckb'
            tk.dma('pool', ckb[:], ctxk_d.ap()[l].rearrange("(c p) n -> p c n", p=128), writes=[ckk])
            for c in range(4):
                b = nb()
                for pr in range(4):
                    tk.op('pe', lambda e, pr=pr: e.transpose(out=psb16(b)[:, pr * 128:(pr + 1) * 128], in_=ckb[:, c, pr * 128:(pr + 1) * 128], identity=IDB),
                          reads=[ckk, 'cb16'], writes=[psk[b]])
                tk.op('act', lambda e, c=c: e.copy(out=ckT[:, :, c * 128:(c + 1) * 128], in_=psb16(b)[:, 0:512].rearrange("p (k c) -> p k c", k=4)),
                      reads=[psk[b]], writes=['ckT'])
            tk.dma('pool', ckb[:], ctxv_d.ap()[l].rearrange("(c p) n -> p c n", p=128), reads=[], writes=[ckk])
            tk.op('pool', lambda e: e.tensor_copy(out=cvaug[:, :, :, 0:64], in_=ckb[:].rearrange("p c (h d) -> p c h d", h=8)), reads=[ckk], writes=['cvaug'])
            if stop('A1'):
                break
            toep_mm(0)
            toep_dma(2)
            wi = next_w()
            for pr in range(4):
                for g in range(2):
                    b = nb()
                    proj_fm(wi, pr, g, b)
                    tk.op('act', lambda e, pr=pr, g=g: e.copy(out=qT[:, pr, g * 512:(g + 1) * 512], in_=ps[b][:, :]), reads=[psk[b]], writes=['qT%d_%d' % (pr, g)])
            if stop('A1a'):
                break
            toep_mm(1)
            toep_dma(3)
            wi = next_w()
            for pr in range(4):
                for g in range(2):
                    b = nb()
                    proj_fm(wi, pr, g, b)
                    tk.op('act', lambda e, pr=pr, g=g: e.copy(out=kT[:, pr, g * 512:(g + 1) * 512], in_=ps[b][:, :]), reads=[psk[b]], writes=['kTf%d_%d' % (pr, g)])
            toep_mm(2)
            toep_dma(4)
            for t in range(NT):
                b = nb()
                proj_tm(wi, 0, 512, t, b)
                oi = t % 2
                tk.op('dve', lambda e: e.tensor_copy(out=ost[oi][:], in_=ps[b][:, :]), reads=[psk[b]], writes=['ost%d' % oi])
                tk.dma('sp', nk_d.ap()[l, t * 128:(t + 1) * 128, :], ost[oi][:], reads=['ost%d' % oi])
            toep_mm(3)
            toep_dma(5)
            if stop('A1b'):
                break
            wi = next_w()
            for t in range(NT):
                b = nb()
                proj_tm(wi, 0, 512, t, b)
                oi = t % 2
                tk.op('dve', lambda e: e.tensor_copy(out=ost[oi][:], in_=ps[b][:, :]), reads=[psk[b]], writes=['ost%d' % oi])
                tk.op('act', lambda e, t=t: e.copy(out=vaug[:, t, :, 0:64], in_=ost[oi][:].rearrange("p (h d) -> p h d", h=8)),
                      reads=['ost%d' % oi], writes=['vaug%d' % t])
                tk.dma('sp', nv_d.ap()[l, t * 128:(t + 1) * 128, :], ost[oi][:], reads=['ost%d' % oi])
            if stop('A1c'):
                break
            toep_mm(4)
            toep_dma(6)
            wi = next_w()
            for t in range(NT):
                b = nb()
                proj_tm(wi, 0, 512, t, b)
                tk.op('act', lambda e, t=t: e.activation(out=sga[:, t, :], in_=ps[b][:, :], func=AF.Silu), reads=[psk[b]], writes=['sga%d' % t])
            toep_mm(5)
            toep_mm(6)
            if stop('A2'):
                break
            OA, OB = 6, 7
            spairs = [(0, 1), (2, 3), (4, 5)]
            allsteps = []
            for j in range(8):
                st_ = [('l', kt) for kt in KT[j]] + [('c', c) for c in range(4)]
                for si_, (kind, idx) in enumerate(st_):
                    allsteps.append((j, si_, len(st_), kind, idx))

            def emit_S(k):
                j, si_, ns_, kind, idx = allsteps[k]
                sA, sB = spairs[k % 3]
                for h in range(8):
                    b = sA if h % 2 == 0 else sB
                    r0 = (h % 2) * 64
                    ksrc = kT[r0:r0 + 64, h // 2, idx * 128:(idx + 1) * 128] if kind == 'l' else ckT[r0:r0 + 64, h // 2, idx * 128:(idx + 1) * 128]
                    tk.op('pe', lambda e, h=h, b=b, ksrc=ksrc, r0=r0: e.matmul(ps[b][:, (h // 2) * 128:(h // 2 + 1) * 128], lhsT=ksrc,
                                                                              rhs=qT[r0:r0 + 64, h // 2, j * 128:(j + 1) * 128], start=True, stop=True),
                          reads=['qT%d_%d' % (h // 2, j // 4), ('kTf%d_%d' % (h // 2, idx // 4)) if kind == 'l' else 'ckT'], writes=[psk[b]])

            def emit_rest(k):
                j, si_, ns_, kind, idx = allsteps[k]
                sA, sB = spairs[k % 3]
                sl = k % 3
                big = psbig[sA // 2]
                if kind == 'l':
                    jk = JKI[(j, idx)]
                    for hf in range(2):
                        tk.op('act', lambda e, hf=hf: e.activation(
                            out=Eb[sl][:, :, hf * 64:(hf + 1) * 64],
                            in_=big[:, :].rearrange("p (a c) -> p a c", a=8)[:, :, hf * 64:(hf + 1) * 64],
                            func=AF.Exp, scale=0.125, bias=rowb[:, jk * 2 + hf:jk * 2 + hf + 1]),
                            reads=[psk[sA], psk[sB], 'rowb'], writes=['Eb%d' % sl])
                    di = idx - j + 3
                    pl = k % 2
                    tk.op('dve', lambda e, di=di: e.tensor_tensor(out=Pb[pl][:], in0=Eb[sl][:], in1=expT[:, di, :, :], op=ALU.mult),
                          reads=['Eb%d' % sl, 'expT%d' % di], writes=['Pb%d' % pl])
                    lhs, lk = Pb[pl], 'Pb%d' % pl
                    vsrc, vk = vaug, 'vaug%d' % idx
                else:
                    tk.op('act', lambda e: e.activation(out=Eb[sl][:].rearrange("p a c -> p (a c)"), in_=big[:, :], func=AF.Exp, scale=0.125),
                          reads=[psk[sA], psk[sB]], writes=['Eb%d' % sl])
                    lhs, lk = Eb[sl], 'Eb%d' % sl
                    vsrc, vk = cvaug, 'cvaug'
                for e_ in range(8):
                    h = 2 * (e_ % 4) + e_ // 4
                    ob = OA if e_ < 4 else OB
                    tk.op('pe', lambda e, e_=e_, h=h, ob=ob, lhs=lhs, vsrc=vsrc: e.matmul(
                        ps[ob][:, (e_ % 4) * 66:(e_ % 4) * 66 + 66], lhsT=lhs[:, e_, :], rhs=vsrc[:, idx, h, :],
                        start=(si_ == 0 and e_ % 4 == 0), stop=(si_ == ns_ - 1), skip_group_check=True),
                        reads=[lk, vk] + (['vaug'] if kind == 'l' else []), writes=[psk[ob]])
                if si_ == ns_ - 1:
                    for bi, ob in enumerate((OA, OB)):
                        tk.op('dve', lambda e, bi=bi, ob=ob: e.reciprocal(out=rden[:, bi * 4:(bi + 1) * 4],
                                                                          in_=ps[ob][:, 0:264].rearrange("p (a c) -> p a c", a=4)[:, :, 64]),
                              reads=[psk[ob]], writes=['rden'])
                    for bi, ob in enumerate((OA, OB)):
                        tk.op('dve', lambda e, bi=bi, ob=ob: e.tensor_tensor(
                            out=otmp[:, bi:8:2, :], in0=ps[ob][:, 0:264].rearrange("p (a c) -> p a c", a=4)[:, :, 0:64],
                            in1=rden[:, bi * 4:(bi + 1) * 4].unsqueeze(2).broadcast_to([128, 4, 64]), op=ALU.mult),
                            reads=[psk[ob], 'rden'], writes=['otmp'])
                    tk.op('dve', lambda e: e.tensor_tensor(out=mixed[:, j, 0:512], in0=otmp[:].rearrange("p h d -> p (h d)"), in1=sga[:, j, :], op=ALU.mult),
                          reads=['otmp', 'sga%d' % j], writes=['mixed%d' % j])

            emit_S(0)
            emit_S(1)
            for k in range(len(allsteps)):
                if k + 2 < len(allsteps):
                    emit_S(k + 2)
                emit_rest(k)

            if stop('A'):
                break
            tk.barrier()
            areset()
            qh = aget([128, NT, 256], BF16)
            sgb = aget([128, NT, 256])
            qE = sgb.rearrange("p a b -> p (a b)")[:, 0:1024].bitcast(BF16).rearrange("p (a b) -> p a b", a=NT)
            kE = sgb.rearrange("p a b -> p (a b)")[:, 1024:2048].bitcast(BF16).rearrange("p (a b) -> p a b", a=NT)
            vh = aget([128, NT, 256], BF16)
            vhmF = aget([128, 4096])
            vhm = vhmF.bitcast(BF16).rearrange("p (q t c) -> p q t c", q=4, t=NT)
            A_ = vhmF[:, 0:2048].rearrange("p (e c) -> p e c", e=64)
            B_ = vhmF[:, 2048:4096].rearrange("p (e c) -> p e c", e=64)
            sgr = aget([128, NT, 256], BF16)
            fS = aget([128, 4096])
            fbuf = fS[:, 0:2048].rearrange("p (a b) -> p a b", a=NT)
            SinPm = fS.bitcast(BF16).rearrange("p (h c e) -> p h c e", h=4, c=32)
            lfbuf = aget([128, NT, 256])
            kETm = lfbuf.rearrange("p a b -> p (a b)").bitcast(BF16).rearrange("p (h t) -> p h t", h=4)
            osq = lfbuf
            tmpE = [aget([128, 512]) for _ in range(2)]
            qET = aget([128, 2, T], BF16)
            ATm = [aget([128, 4, 128], BF16) for _ in range(2)]
            osum = aget([128, NT, 256])
            gs3 = aget([128, 2, 32, 3])
            es3 = aget([128, 2, 32, 3])
            S0b = aget([128, 2, 64])
            nsb = aget([128, 4, 2, 64])
            hss = aget([128, 32])
            tk.dma('sp', ghgB[:], bass.AP(ghg_d, l * 256, [[0, 128], [1, 256]]), writes=['ghgB'])
            wi = next_w()
            for t in range(NT):
                b = nb()
                proj_tm(wi, 0, 512, t, b)
                tk.op('act', lambda e, t=t: e.activation(out=qh[:, t, :], in_=ps[b][:, 0:256], func=AF.Silu), reads=[psk[b]], writes=['qh'])
                tk.op('act', lambda e, t=t: e.activation(out=sgb[:, t, :], in_=ps[b][:, 256:512], func=AF.Tanh, scale=0.5), reads=[psk[b]], writes=['sQ'])
            wi = next_w()
            for t in range(NT):
                b = nb()
                proj_tm(wi, 0, 512, t, b)
                tk.op('act', lambda e, t=t: e.activation(out=sgr[:, t, :], in_=ps[b][:, 256:512], func=AF.Silu), reads=[psk[b]], writes=['sgr'])
                tk.op('dve', lambda e, t=t: e.tensor_copy(out=vh[:, t, :], in_=ps[b][:, 0:256]), reads=[psk[b]], writes=['vh'])

            if stop('R0'):
                break
            rstop = False
            for dr in range(2):
                if l + 1 < nl:
                    mod_dma(l + 1, 3 * dr)
                lb_bc = lbl[:, dr, :].unsqueeze(1).broadcast_to([128, NT, 256])
                oml_bc = oml[:, dr, :].unsqueeze(1).broadcast_to([128, NT, 256])
                tk.op('dve', lambda e: e.tensor_tensor(out=fbuf[:], in0=sgb[:], in1=oml_bc, op=ALU.mult), reads=['sQ', 'oml'], writes=['fS'])
                tk.op('dve', lambda e: e.tensor_tensor(out=fbuf[:], in0=fbuf[:], in1=lb_bc, op=ALU.add), reads=['fS', 'lbl'], writes=['fS'])
                tk.op('act', lambda e: e.activation(out=lfbuf[:].rearrange("p a b -> p (a b)"), in_=fbuf[:].rearrange("p a b -> p (a b)"), func=AF.Ln),
                      reads=['fS'], writes=['lK'])
                tk.op('dve', lambda e: e.tensor_scalar(out=fbuf[:], in0=fbuf[:], scalar1=-1.0, scalar2=1.0, op0=ALU.mult, op1=ALU.add),
                      reads=['fS'], writes=['fS'])
                bS = nb()
                for t in range(NT):
                    for pr in range(2):
                        tt_ = t if dr == 0 else 7 - t
                        tk.op('pe', lambda e, t=t, pr=pr, tt_=tt_: e.matmul(ps[bS][:, (pr * 8 + tt_) * 8:(pr * 8 + tt_) * 8 + 8], lhsT=lfbuf[:, t, pr * 128:(pr + 1) * 128],
                                                                   rhs=SEL[dr], start=True, stop=True),
                              reads=['lK', 'cf32'], writes=[psk[bS]])
                psS = ps[bS][:, 0:128].rearrange("p (a c r) -> p a c r", a=2, c=32)
                tk.op('dve', lambda e: e.tensor_copy(out=gs3[:, :, :, 0:2], in_=psS), reads=[psk[bS]], writes=['gs3'])
                tk.op('dve', lambda e: e.tensor_tensor(out=gs3[:, :, :, 2], in0=gs3[:, :, :, 1], in1=gs3[:, :, :, 0], op=ALU.subtract),
                      reads=['gs3'], writes=['gs3'])
                tk.op('act', lambda e: e.activation(out=es3[:].rearrange("p a c r -> p (a c r)"), in_=gs3[:].rearrange("p a c r -> p (a c r)"), func=AF.Exp),
                      reads=['gs3'], writes=['es3'])
                tk.op('dve', lambda e: e.tensor_scalar(out=es3[:, :, 8:32:8, 0:2], in0=es3[:, :, 8:32:8, 0:2], scalar1=flags[:, 1:2], scalar2=None, op0=ALU.mult),
                      reads=['es3', 'flags'], writes=['es3'])
                for tp in range(4):
                    b = nb()
                    for i in range(2):
                        t = 2 * tp + i
                        tk.op('pe', lambda e, t=t, i=i: e.matmul(ps[b][:, i * 256:(i + 1) * 256], lhsT=TR[dr], rhs=lfbuf[:, t, :], start=True, stop=True),
                              reads=['cf32', 'lK'], writes=[psk[b]])
                    tk.op('act', lambda e: e.activation(out=tmpE[0][:], in_=ps[b][:, :], func=AF.Exp), reads=[psk[b]], writes=['tmpE0'])
                    tk.op('act', lambda e: e.activation(out=tmpE[1][:], in_=ps[b][:, :], func=AF.Exp, scale=-1.0), reads=[psk[b]], writes=['tmpE1'])
                    tk.op('dve', lambda e, tp=tp: e.tensor_tensor(out=qE[:, 2 * tp:2 * tp + 2, :].rearrange("p a c -> p (a c)"),
                                                                  in0=qh[:, 2 * tp:2 * tp + 2, :].rearrange("p a c -> p (a c)"), in1=tmpE[0][:], op=ALU.mult),
                          reads=['qh', 'tmpE0'], writes=['sQ', 'qk%d' % tp])
                    tk.op('dve', lambda e, tp=tp: e.tensor_tensor(out=kE[:, 2 * tp:2 * tp + 2, :].rearrange("p a c -> p (a c)"),
                                                                  in0=fbuf[:, 2 * tp:2 * tp + 2, :].rearrange("p a c -> p (a c)"), in1=tmpE[1][:], op=ALU.mult),
                          reads=['fS', 'tmpE1'], writes=['sQ', 'qk%d' % tp])
                for cp in range(4):
                    tk.op('dve', lambda e, cp=cp: e.tensor_scalar(out=vhm[:, cp, :, :], in0=vh[:], scalar1=QM[:, cp:cp + 1], scalar2=None, op0=ALU.mult),
                          reads=['vh', 'cf32'], writes=['vhm'])
                tk._deps('act', [], ['lK'])
                for t in range(NT):
                    b = nb()
                    for pr in range(2):
                        tk.op('pe', lambda e, t=t, pr=pr: e.transpose(out=psb16(b)[:, pr * 128:(pr + 1) * 128], in_=qE[:, t, pr * 128:(pr + 1) * 128], identity=IDB),
                              reads=['qk%d' % (t // 2), 'cb16'], writes=[psk[b]], war_only=['sQ'])
                        tk.op('pe', lambda e, t=t, pr=pr: e.transpose(out=psb16(b)[:, (2 + pr) * 128:(3 + pr) * 128], in_=kE[:, t, pr * 128:(pr + 1) * 128], identity=IDB),
                              reads=['qk%d' % (t // 2), 'cb16'], writes=[psk[b]], war_only=['sQ'])
                    tk.op('act', lambda e, t=t: e.copy(out=qET[:, :, t * 128:(t + 1) * 128], in_=psb16(b)[:, 0:256].rearrange("p (a c) -> p a c", a=2)),
                          reads=[psk[b]], writes=['qET%d' % t])
                    for h in range(4):
                        tk.op('act', lambda e, t=t, h=h: e.activation(out=kETm[:, h, t * 128:(t + 1) * 128], in_=psb16(b)[:, (2 + h // 2) * 128:(3 + h // 2) * 128],
                                                                      func=AF.Copy, scale=HM[:, h % 2:h % 2 + 1]),
                              reads=[psk[b], 'cf32'], writes=['kT%d_%d' % (t, h)])
                if stop('R1'):
                    rstop = True
                    break
                tk.dma('sp', S0b[:], s0_d.ap()[l, dr], writes=['S0b'])
                kvb_all = [[nb() for _ in range(4)] for _ in range(2)]
                for pr in range(2):
                    kvb = kvb_all[pr]
                    for c in range(32):
                        cq = c if dr == 0 else 31 - c
                        b = kvb[cq // 8]
                        for h in (2 * pr, 2 * pr + 1):
                            o = ps[b][(h % 2) * 64:(h % 2) * 64 + 64, (cq % 8) * 64:(cq % 8) * 64 + 64]
                            tk.op('pe', lambda e, c=c, h=h, o=o: e.matmul(o, lhsT=kE[:, c // 4, h * 64:(h + 1) * 64], rhs=vhm[:, c % 4, c // 4, h * 64:(h + 1) * 64],
                                                                          start=True, stop=True),
                                  reads=['sQ', 'vhm'], writes=[psk[b]])
                for pr in range(2):
                    kvb = kvb_all[pr]
                    for g in range(4):
                        tk.op('dve', lambda e, g=g: e.tensor_tensor(out=B_[:, :, g * 8:(g + 1) * 8].rearrange("p e c -> p c e"),
                                                                    in0=ps[kvb[g]][:, :].rearrange("p (c e) -> p c e", c=8),
                                                                    in1=es3[:, pr, g * 8:(g + 1) * 8, 2].unsqueeze(2).broadcast_to([128, 8, 64]), op=ALU.mult),
                              reads=[psk[kvb[g]], 'es3'], writes=['vhm'])
                    if dr == 0 and pr == 0:
                        wi = next_w()
                        for t in range(NT):
                            b = kvb[t % 4]
                            proj_tm(wi, 0, 256, t, b)
                            tk.op('act', lambda e, t=t: e.activation(out=sgb[:, t, :], in_=ps[b][:, 0:256], func=AF.Tanh, scale=0.5), reads=[psk[b]], writes=['sQ'])
                    if dr == 1 and pr == 0:
                        wi = next_w()
                        uT_h = qh[:].rearrange("p a b -> p (a b)").rearrange("p (c t) -> p c t", c=2)
                        sgf_h = qE
                        hb_ = 0
                        for ci in range(2):
                            for g in range(2):
                                b = kvb[hb_ % 4]
                                hb_ += 1
                                proj_fm(wi, ci, g, b)
                                tk.op('act', lambda e, ci=ci, g=g: e.copy(out=uT_h[:, ci, g * 512:(g + 1) * 512], in_=ps[b][:, :]), reads=[psk[b]], writes=['qh'])
                        for t in range(NT):
                            b = kvb[hb_ % 4]
                            hb_ += 1
                            proj_tm(wi, 256, 256, t, b)
                            tk.op('act', lambda e, t=t: e.activation(out=sgf_h[:, t, :], in_=ps[b][:, 0:256], func=AF.Silu), reads=[psk[b]], writes=['sQ'])
                    tk.op('dve', lambda e: e.scalar_tensor_tensor(out=B_[:, :, 0], in0=S0b[:, pr, :], scalar=es3[:, pr, 0, 1:2], in1=B_[:, :, 0], op0=ALU.mult, op1=ALU.add),
                          reads=['S0b', 'es3', 'vhm'], writes=['vhm'])
                    tk.op('dve', lambda e: e.tensor_copy(out=A_[:], in_=es3[:, pr, :, 1].unsqueeze(1).broadcast_to([128, 64, 32])), reads=['es3', 'vhm'], writes=['vhm'])
                    tk.op('dve', lambda e: e.memset(A_[:, :, 0:1], 0.0), reads=['vhm'], writes=['vhm'])
                    tk.op('dve', lambda e: e.tensor_tensor_scan(out=B_[:].rearrange("p e c -> p (e c)"), data0=A_[:].rearrange("p e c -> p (e c)"),
                                                                data1=B_[:].rearrange("p e c -> p (e c)"), initial=0.0, op0=ALU.mult, op1=ALU.add),
                          reads=['vhm'], writes=['vhm'])
                    tk.op('dve', lambda e: e.tensor_copy(out=nsb[:, :, pr, :], in_=B_[:, :, 7:32:8].rearrange("p e k -> p k e")), reads=['vhm'], writes=['nsb'])
                    for h2 in range(2):
                        tk.op('dve', lambda e, h2=h2: e.scalar_tensor_tensor(out=SinPm[:, 2 * pr + h2, 1:32, :], in0=B_[:, :, 0:31].rearrange("p e c -> p c e"),
                                                                             scalar=HM[:, h2:h2 + 1], in1=es3[:, pr, 1:32, 0].unsqueeze(2).broadcast_to([128, 31, 64]),
                                                                             op0=ALU.mult, op1=ALU.mult),
                              reads=['vhm', 'es3', 'cf32'], writes=['fS'])
                        tk.op('dve', lambda e, h2=h2: e.scalar_tensor_tensor(out=SinPm[:, 2 * pr + h2, 0, :], in0=S0b[:, pr, :], scalar=HM[:, h2:h2 + 1],
                                                                             in1=es3[:, pr, 0, 0:1].broadcast_to([128, 64]), op0=ALU.mult, op1=ALU.mult),
                              reads=['S0b', 'es3', 'cf32'], writes=['fS'])
                nsb3 = nsb[:].rearrange("p s a e -> p s (a e)")
                if dr == 0:
                    tk.dma('sp', ns_d.ap()[l, 0].rearrange("s p a e -> p s (a e)"), nsb3, reads=['nsb'])
                else:
                    for k in range(4):
                        tk.dma('sp', ns_d.ap()[l, 1, 3 - k].rearrange("p a e -> p (a e)"), nsb3[:, k, :], reads=['nsb'])
                if l + 1 < nl:
                    mod_mm(l + 1, 3 * dr)
                    mod_dma(l + 1, 3 * dr + 1)
                if stop('R2'):
                    rstop = True
                    break
                for t in range(NT):
                    bA_ = nb()
                    ai = t % 2
                    for h in range(4):
                        tk.op('pe', lambda e, t=t, h=h: e.matmul(ps[bA_][:, h * 128:(h + 1) * 128], lhsT=kETm[:, h, t * 128:(t + 1) * 128],
                                                                 rhs=qET[:, h // 2, t * 128:(t + 1) * 128], start=True, stop=True),
                              reads=['kT%d_%d' % (t, h), 'qET%d' % t], writes=[psk[bA_]], war_only=['lK'])
                    tk.op('dve', lambda e: e.tensor_tensor(out=ATm[ai][:], in0=ps[bA_][:, :].rearrange("p (a c) -> p a c", a=4),
                                                           in1=TRI[dr].unsqueeze(1).broadcast_to([128, 4, 128]), op=ALU.mult),
                          reads=[psk[bA_], 'cb16'], writes=['ATm%d' % ai])
                    bO = nb()
                    for h in range(4):
                        tk.op('pe', lambda e, t=t, h=h: e.matmul(ps[bO][:, h * 64:(h + 1) * 64], lhsT=ATm[ai][:, h, :], rhs=vh[:, t, h * 64:(h + 1) * 64],
                                                                 start=True, stop=False, skip_group_check=True),
                              reads=['ATm%d' % ai, 'vh'], writes=[psk[bO]])
                        for cp in range(4):
                            tk.op('pe', lambda e, t=t, h=h, cp=cp: e.matmul(ps[bO][cp * 32:(cp + 1) * 32, h * 64:(h + 1) * 64],
                                                                            lhsT=qET[:, h // 2, t * 128 + cp * 32:t * 128 + cp * 32 + 32],
                                                                            rhs=SinPm[:, h, (4 * t + cp) if dr == 0 else 31 - (4 * t + cp), :], start=False, stop=(cp == 3), skip_group_check=True,
                                                                            tile_position=(0, cp * 32)),
                                  reads=['qET%d' % t, 'fS'], writes=[psk[bO]])
                    if l + 1 < nl and t == 3:
                        mod_mm(l + 1, 3 * dr + 1)
                        mod_dma(l + 1, 3 * dr + 2)
                    if l + 1 < nl and t == 7:
                        mod_mm(l + 1, 3 * dr + 2)
                    if dr == 0:
                        tk.op('act', lambda e, t=t: e.copy(out=osum[:, t, :], in_=ps[bO][:, 0:256]), reads=[psk[bO]], writes=['osum%d' % t])
                    else:
                        tk.op('dve', lambda e, t=t: e.tensor_tensor(out=osum[:, t, :], in0=osum[:, t, :], in1=ps[bO][:, 0:256], op=ALU.add),
                              reads=[psk[bO], 'osum%d' % t], writes=['osum%d' % t])
                if stop('R3'):
                    rstop = True
                    break
            if rstop:
                break
            tk.op('dve', lambda e: e.tensor_tensor(out=osq[:], in0=osum[:], in1=osum[:], op=ALU.mult), reads=['osum%d' % t_ for t_ in range(NT)], writes=['lK'])
            tk.op('dve', lambda e: e.tensor_reduce(out=hss[:], in_=osq[:].rearrange("p t (h d) -> p (t h) d", h=4), axis=AX.X, op=ALU.add),
                  reads=['lK'], writes=['hss'])
            tk.op('dve', lambda e: e.tensor_scalar(out=hss[:], in0=hss[:], scalar1=1.0 / 64, scalar2=EPS, op0=ALU.mult, op1=ALU.add), reads=['hss'], writes=['hss'])
            tk.op('act', lambda e: e.activation(out=hss[:], in_=hss[:], func=AF.Ln), reads=['hss'], writes=['hss'])
            tk.op('act', lambda e: e.activation(out=hss[:], in_=hss[:], func=AF.Exp, scale=-0.5), reads=['hss'], writes=['hss'])
            tk.op('dve', lambda e: e.tensor_tensor(out=osum[:].rearrange("p t (h d) -> p (t h) d", h=4), in0=osum[:].rearrange("p t (h d) -> p (t h) d", h=4),
                                                   in1=hss[:].unsqueeze(2).broadcast_to([128, 32, 64]), op=ALU.mult),
                  reads=['osum%d' % t_ for t_ in range(NT)] + ['hss'], writes=['osum%d' % t_ for t_ in range(NT)])
            tk.op('dve', lambda e: e.tensor_tensor(out=osum[:], in0=osum[:], in1=ghgB[:].unsqueeze(1).broadcast_to([128, NT, 256]), op=ALU.mult),
                  reads=['osum%d' % t_ for t_ in range(NT)] + ['ghgB'], writes=['osum%d' % t_ for t_ in range(NT)])
            tk.op('dve', lambda e: e.tensor_tensor(out=mixed[:, :, 512:768], in0=osum[:], in1=sgr[:], op=ALU.mult),
                  reads=['osum%d' % t_ for t_ in range(NT)] + ['sgr'], writes=['mixed%d' % j for j in range(8)])

            if stop('R'):
                break
            tk.barrier()
            areset()
            uT = aget([128, 2, T], BF16)
            sgf = aget([128, NT, 256], BF16)
            ucs = aget([128, NT, 2, 256], BF16)
            yT = aget([128, 2, T], BF16)
            csnb = [aget([128, 2, 1024], BF16) for _ in range(4)]
            assert apos[0] + 2048 <= 11520
            wfs = aget([128, 2, 256])
            wfb = aget([128, 2, 256], BF16)
            junk = aget([128, 512], BF16)
            tmpf = aget([128, D])
            for kt_ in range(4):
                tk.dma('sp', csnb[kt_][:], csn_d.ap()[kt_], writes=['csnb%d' % kt_])
            assert True
            tk.dma('sp', wfs[:], wfn_d.ap()[l].rearrange("(c p) n -> p c n", p=128), writes=['wfs'])
            tk.op('pool', lambda e: e.tensor_copy(out=wfb[:], in_=wfs[:]), reads=['wfs'], writes=['wfb'])
            for t in range(NT):
                b = nb()
                for cs in range(2):
                    for ct in range(2):
                        tk.op('pe', lambda e, t=t, cs=cs, ct=ct: e.matmul(ps[b][:, cs * 256 + ct * 128:cs * 256 + ct * 128 + 128], lhsT=uT[:, ct, t * 128:(t + 1) * 128],
                                                                          rhs=C4S4[:, 3 + cs * 2 + ct, :], start=True, stop=True),
                              reads=['uT', 'cb16'], writes=[psk[b]])
                tk.op('act', lambda e, t=t: e.copy(out=ucs[:, t, :, :].rearrange("p a c -> p (a c)"), in_=ps[b][:, :]), reads=[psk[b]], writes=['ucs%d' % t])
            if stop('F0'):
                break
            yb = [nb() for _ in range(4)]
            for kt_ in range(8):
                ci = kt_ % 4
                if kt_ >= 4:
                    tk.dma('sp', csnb[ci][:], csn_d.ap()[kt_], writes=['csnb%d' % ci])
                for ct in range(2):
                    for g in range(2):
                        b = yb[ct * 2 + g]
                        for cs in range(2):
                            tk.op('pe', lambda e, kt_=kt_, ct=ct, g=g, cs=cs: e.matmul(ps[b][:, :], lhsT=ucs[:, kt_, cs, ct * 128:(ct + 1) * 128],
                                                                                       rhs=csnb[ci][:, cs, g * 512:(g + 1) * 512],
                                                                                       start=(kt_ == 0 and cs == 0), stop=(kt_ == 7 and cs == 1)),
                                  reads=['ucs%d' % kt_, 'csnb%d' % ci], writes=[psk[b]])
            for ct in range(2):
                for g in range(2):
                    b = yb[ct * 2 + g]
                    tk.op('act', lambda e, ct=ct, g=g, b=b: e.copy(out=yT[:, ct, g * 512:(g + 1) * 512], in_=ps[b][:, :]), reads=[psk[b]], writes=['yT%d_%d' % (ct, g)])
            for t in range(NT):
                b = nb()
                for ct in range(2):
                    tk.op('pe', lambda e, t=t, ct=ct: e.matmul(ps[b][:, 0:256], lhsT=yT[:, ct, t * 128:(t + 1) * 128], rhs=wfb[:, ct, :], start=(ct == 0), stop=(ct == 1)),
                          reads=['yT%d_%d' % (ct, t // 4), 'wfb'], writes=[psk[b]])
                tk.op('dve', lambda e, t=t: e.tensor_tensor(out=mixed[:, t, 768:1024], in0=ps[b][:, 0:256], in1=sgf[:, t, :], op=ALU.mult),
                      reads=[psk[b], 'sgf'], writes=['mixed%d' % t])

            if dbg and l == nl - 1:
                for t in range(NT):
                    tk.dma('sp', dbg_d.ap()[t * 128:(t + 1) * 128, :], mixed[:, t, :], reads=['mixed%d' % t])

            if stop('F1'):
                break
            w0 = next_w()
            w1 = next_w(prefetch=False)
            for t in range(NT):
                b = nb()
                for kc in range(8):
                    tk.op('pe', lambda e, t=t, kc=kc: e.transpose(out=psb16(b)[:, kc * 128:(kc + 1) * 128], in_=mixed[:, t, kc * 128:(kc + 1) * 128], identity=IDB),
                          reads=['mixed%d' % t, 'cb16'], writes=[psk[b]])
                tk.op('act', lambda e, t=t: e.copy(out=hT[:, :, t * 128:(t + 1) * 128], in_=psb16(b)[:, :].rearrange("p (k c) -> p k c", k=8)),
                      reads=[psk[b]], writes=['hT'])
            for t in range(NT):
                bb = [nb(), nb()]
                for hf, wi in enumerate((w0, w1)):
                    proj_tm(wi, 0, 512, t, bb[hf])
                    tk.op('act', lambda e, t=t, hf=hf: e.activation(out=junk[:], in_=ps[bb[hf]][:, :], func=AF.Square, accum_out=ssq[:, 8 + hf:9 + hf]),
                          reads=[psk[bb[hf]]], writes=['ssq%d' % (8 + hf)])
                tk.op('dve', lambda e: e.tensor_tensor(out=rstd[:, 8:9], in0=ssq[:, 8:9], in1=ssq[:, 9:10], op=ALU.add), reads=['ssq8', 'ssq9'], writes=['rstd'])
                tk.op('dve', lambda e: e.tensor_scalar(out=rstd[:, 8:9], in0=rstd[:, 8:9], scalar1=1.0 / D, scalar2=EPS, op0=ALU.mult, op1=ALU.add),
                      reads=['rstd'], writes=['rstd'])
                tk.op('act', lambda e: e.activation(out=rstd[:, 8:9], in_=rstd[:, 8:9], func=AF.Ln), reads=['rstd'], writes=['rstd'])
                tk.op('act', lambda e: e.activation(out=rstd[:, 8:9], in_=rstd[:, 8:9], func=AF.Exp, scale=-0.5), reads=['rstd'], writes=['rstd'])
                for hf in range(2):
                    tk.op('dve', lambda e, hf=hf: e.scalar_tensor_tensor(out=tmpf[:, hf * 512:(hf + 1) * 512], in0=ps[bb[hf]][:, :], scalar=rstd[:, 8:9],
                                                                         in1=gg[:, hf * 512:(hf + 1) * 512], op0=ALU.mult, op1=ALU.mult),
                          reads=[psk[bb[hf]], 'rstd', 'gg'], writes=['tmpf'])
                tk.op('dve', lambda e, t=t: e.tensor_tensor(out=x_sb[:, t, :], in0=x_sb[:, t, :], in1=tmpf[:], op=ALU.add),
                      reads=['x%d' % t, 'tmpf'], writes=['x%d' % t])
            _issue(wstate['ptr'])

        for t in range(NT):
            tk.dma('sp', y_d.ap()[t * 128:(t + 1) * 128, :], x_sb[:, t, :], reads=['x%d' % t])
        tk.finish()
    return nc


def _consts(is_sample):
    cf32 = np.zeros((128, 7, 128), np.float32)
    p = np.arange(128)
    J2 = np.zeros((128, 128), np.float32)
    for a in range(2):
        for i in range(64):
            J2[a * 64 + i, a * 64 + 63 - i] = 1.0
    cf32[:, 0] = J2
    cf32[:, 1] = np.eye(128, dtype=np.float32)
    cm = np.zeros((128, 128), np.float32)
    if is_sample:
        qc = np.arange(64)
        c0 = np.clip(qc - 8, 0, 48)
        kc = np.arange(64)
        valid = (kc[:, None] >= c0[None, :]) & (kc[:, None] < c0[None, :] + 16)
        m = np.where(valid, 0.0, NEG).astype(np.float32)
        cm = np.tile(m, (2, 2))
    cf32[:, 2] = cm
    s = np.arange(32)[:, None]
    t = np.arange(32)[None, :]
    trf = (s <= t).astype(np.float32) - (s <= 15).astype(np.float32)
    trb = (s >= t).astype(np.float32) - (s >= 16).astype(np.float32)
    for a in range(4):
        cf32[a * 32:(a + 1) * 32, 3, a * 32:(a + 1) * 32] = trf
        cf32[a * 32:(a + 1) * 32, 4, a * 32:(a + 1) * 32] = trb
    sl = np.arange(128) % 32
    ch = np.arange(128) // 32
    selcols = np.zeros((128, 128), np.float32)
    for a in range(4):
        selcols[:, a * 2 + 0] = ((ch == a) & (sl <= 15))
        selcols[:, a * 2 + 1] = (ch == a)
        selcols[:, 8 + a * 2 + 0] = ((ch == 3 - a) & (sl >= 16))
        selcols[:, 8 + a * 2 + 1] = (ch == 3 - a)
        selcols[:, 18 + a] = (ch == a)
    selcols[:, 16] = (np.arange(128) < 64)
    selcols[:, 17] = (np.arange(128) >= 64)
    cf32[:, 5] = selcols
    cf32[0:64, 6, 0:64] = 1.0
    cf32[64:128, 6, 64:128] = 1.0
    rowb = np.zeros((128, 74), np.float32)
    for i, (j, kt) in enumerate(JK):
        for hf in range(2):
            for krl in range(2):
                if is_sample:
                    qr = 2 * j + hf
                    kr = 2 * kt + krl
                    r0 = int(np.clip(qr - 4, 0, 8))
                    ok = (r0 <= kr < r0 + 8)
                else:
                    ok = (kt // 2 == j // 2)
                rowb[krl * 64:(krl + 1) * 64, i * 2 + hf] = 0.0 if ok else NEG
    cb16 = np.zeros((128, 9, 128), np.float32)
    cb16[:, 7] = J2
    cb16[:, 8] = cm
    cb16[:, 0] = np.eye(128)
    mf = (s <= t).astype(np.float32)
    mb = (s >= t).astype(np.float32)
    z = np.zeros((64, 64), np.float32)
    for a in range(4):
        cb16[a * 32:(a + 1) * 32, 1, a * 32:(a + 1) * 32] = mf
        cb16[a * 32:(a + 1) * 32, 2, a * 32:(a + 1) * 32] = mb
    ang = 2 * np.pi * np.outer(np.arange(64), np.arange(64)) / 64
    c4 = np.cos(ang) / 8.0
    s4 = np.sin(ang) / 8.0
    for ct in range(2):
        cb16[:, 3 + ct] = np.block([[c4, z], [z, c4]])
        cb16[:, 5 + ct] = np.block([[s4, z], [z, s4]])
    n = 1024 if is_sample else 256
    idx = np.arange(n)
    a2 = 2 * np.pi * ((np.outer(idx, idx)) % n) / n
    cn = np.cos(a2) / np.sqrt(n)
    sn = -np.sin(a2) / np.sqrt(n)
    CN = np.zeros((1024, 1024), np.float64)
    SN = np.zeros((1024, 1024), np.float64)
    for i in range(1024 // n):
        CN[i * n:(i + 1) * n, i * n:(i + 1) * n] = cn
        SN[i * n:(i + 1) * n, i * n:(i + 1) * n] = sn
    csn = np.stack([CN.reshape(8, 128, 1024), SN.reshape(8, 128, 1024)], axis=2)
    return dict(cf32=cf32, rowbias=rowb, cb16=cb16.astype(ml_dtypes.bfloat16), csn=csn.astype(ml_dtypes.bfloat16))


def _in_maps(x_prompt, x_sample, cache_attn_k, cache_attn_v, state_hgrn, c, c_ctx,
             w_ada, b_ada, g_pre, w_in, rpb, lb_logits, g_hgrn, w_fnet, w_out, g_post):
    f = lambda a: np.ascontiguousarray(np.asarray(a, dtype=np.float32))
    shared = dict(w_ada=f(w_ada), b_ada=f(b_ada), g_pre=f(g_pre), w_in=f(w_in), lb_logits=f(lb_logits),
                  g_hgrn=f(g_hgrn), w_fnet=f(w_fnet), w_out=f(w_out), g_post=f(g_post))
    tp = np.zeros((NL, 8, 23, 127), np.float32)
    tp[:, :, 4:19, 48:79] = f(rpb)
    cs = _consts(True)
    cp = _consts(False)
    maps = []
    for i in range(8):
        m = dict(shared)
        if i < 4:
            m["x"] = f(x_sample[i])
            m["cvec"] = f(np.asarray(c[i]).reshape(8, 128).T)
            m["ctxk"] = f(np.asarray(cache_attn_k[i]).reshape(NL, 512, 512))
            m["ctxv"] = f(np.asarray(cache_attn_v[i]).reshape(NL, 512, 512))
            s = np.asarray(state_hgrn[i]).reshape(NL, 2, 2, 2, 64, 64)
            m["s0"] = f(s.transpose(0, 1, 3, 4, 2, 5).reshape(NL, 2, 128, 2, 64))
            m["flags"] = np.ones((128, 2), np.float32)
            m["tpad"] = tp
            m.update(cs)
        else:
            m["x"] = f(np.asarray(x_prompt[4 * (i - 4):4 * (i - 3)]).reshape(T, D))
            m["cvec"] = f(np.asarray(c_ctx).reshape(8, 128).T)
            m["ctxk"] = np.zeros((NL, 512, 512), np.float32)
            m["ctxv"] = np.zeros((NL, 512, 512), np.float32)
            m["s0"] = np.zeros((NL, 2, 128, 2, 64), np.float32)
            m["flags"] = np.zeros((128, 2), np.float32)
            m["tpad"] = np.zeros_like(tp)
            m.update(cp)
        maps.append(m)
    return maps


_NC_CACHE = {}


def kernel(**inputs):
    if 'nc' not in _NC_CACHE:
        _NC_CACHE['nc'] = build_nc()
    nc = _NC_CACHE['nc']
    maps = _in_maps(**inputs)
    res = run_bass_kernel_spmd(nc, maps, core_ids=list(range(8)))
    r = res.results
    y_sample = np.stack([r[i]["y"] for i in range(4)], axis=0).astype(np.float32)
    y_prompt = np.concatenate([r[i]["y"].reshape(4, 256, D) for i in range(4, 8)], axis=0).astype(np.float32)
    nk = np.concatenate([r[i]["newk"].reshape(NL, 4, 256, 8, 64).transpose(1, 0, 2, 3, 4) for i in range(4, 8)], axis=0)
    nv = np.concatenate([r[i]["newv"].reshape(NL, 4, 256, 8, 64).transpose(1, 0, 2, 3, 4) for i in range(4, 8)], axis=0)
    ns = np.concatenate([r[i]["news"].reshape(NL, 2, 4, 2, 64, 2, 64).transpose(2, 0, 1, 5, 3, 4, 6).reshape(4, NL, 2, 4, 64, 64)
                         for i in range(4, 8)], axis=0)
    return (y_prompt, y_sample, np.ascontiguousarray(nk, dtype=np.float32), np.ascontiguousarray(nv, dtype=np.float32),
            np.ascontiguousarray(ns, dtype=np.float32))
```

```python
import numpy as np
import ml_dtypes
from contextlib import ExitStack
import concourse.bass as bass
import concourse.mybir as mybir
from concourse.bass_utils import run_bass_kernel_spmd

F32 = mybir.dt.float32
BF16 = mybir.dt.bfloat16
AF = mybir.ActivationFunctionType
ALU = mybir.AluOpType
AX = mybir.AxisListType

NL = 4
D = 1024
T = 1024
NT = 8
EPS = 1e-6
NEG = -30000.0
KT = {0: [0, 1, 2, 3], 1: [0, 1, 2, 3], 2: [0, 1, 2, 3, 4], 3: [1, 2, 3, 4, 5],
      4: [2, 3, 4, 5, 6], 5: [3, 4, 5, 6, 7], 6: [4, 5, 6, 7], 7: [4, 5, 6, 7]}
JK = [(j, kt) for j in range(8) for kt in KT[j]]
JKI = {p: i for i, p in enumerate(JK)}
NDS = 24
NSW = 72


class TK:
    def __init__(s, nc, st):
        s.nc = nc
        s.E = {'pe': nc.tensor, 'act': nc.scalar, 'dve': nc.vector, 'pool': nc.gpsimd, 'sp': nc.sync}
        s.sem = {k: st.enter_context(nc.semaphore('s_' + k)) for k in ('pe', 'act', 'dve', 'pool')}
        s.cnt = {k: 0 for k in s.E}
        s.seen = {k: {} for k in s.E}
        s.lw = {}
        s.rd = {}
        s.dsems = [st.enter_context(nc.semaphore('d%d' % i)) for i in range(NDS)]
        s.dcnt = [0] * NDS
        s.dnext = 0
        s.swsems = [st.enter_context(nc.semaphore('w%d' % i)) for i in range(NSW)]
        s.swnext = 0
        s.swlow = 0

    def _wait(s, eng, key, val):
        if eng == 'pe' and key == 'pe':
            return
        if s.seen[eng].get(key, 0) >= val:
            return
        if isinstance(key, str):
            semobj = s.sem[key]
        elif key >= 1000:
            semobj = s.swsems[key - 1000]
        else:
            semobj = s.dsems[key]
        s.E[eng].wait_ge(semobj, val)
        s.seen[eng][key] = val

    def _deps(s, eng, reads, writes):
        for k in reads:
            w = s.lw.get(k)
            if w:
                s._wait(eng, *w)
            if k.startswith('ps'):
                for rk, rv in s.rd.get(k, {}).items():
                    if rk != eng:
                        s._wait(eng, rk, rv)
        for k in writes:
            w = s.lw.get(k)
            if w:
                s._wait(eng, *w)
            for rk, rv in s.rd.get(k, {}).items():
                s._wait(eng, rk, rv)

    def _book(s, tag, reads, writes):
        for k in reads:
            d = s.rd.setdefault(k, {})
            d[tag[0]] = max(d.get(tag[0], 0), tag[1])
        for k in writes:
            s.lw[k] = tag
            s.rd[k] = {}

    def op(s, eng, fn, reads=(), writes=(), war_only=()):
        s._deps(eng, reads, writes)
        inst = fn(s.E[eng])
        s.cnt[eng] += 1
        inst.then_inc(s.sem[eng], 1)
        s._book((eng, s.cnt[eng]), tuple(reads) + tuple(war_only), writes)

    def dma(s, q, out, in_, reads=(), writes=()):
        if q == 'pool':
            assert s.swnext < NSW, "out of one-shot semaphores"
            i = s.swnext
            s.swnext += 1
            s._deps(q, reads, writes)
            s.E[q].dma_start(out=out, in_=in_).then_inc(s.swsems[i], 16)
            s._book((1000 + i, 16), reads, writes)
            return
        i = s.dnext
        s.dnext = (s.dnext + 1) % NDS
        if s.dcnt[i] > 0:
            s._wait(q, i, s.dcnt[i])
        s._deps(q, reads, writes)
        s.dcnt[i] += 16
        s.E[q].dma_start(out=out, in_=in_).then_inc(s.dsems[i], 16)
        s._book((i, s.dcnt[i]), reads, writes)

    def barrier(s):
        engs = ('pe', 'act', 'dve', 'pool', 'sp')
        snap = dict(s.cnt)
        dsnap = list(s.dcnt)
        for e in engs:
            for o in ('pe', 'act', 'dve', 'pool'):
                if o != e and snap[o] > 0:
                    s._wait(e, o, snap[o])
            for i in range(NDS):
                if dsnap[i] > 0:
                    s._wait(e, i, dsnap[i])
            for i in range(s.swlow, s.swnext):
                s._wait(e, 1000 + i, 16)
        s.swlow = s.swnext

    def finish(s):
        for i in range(NDS):
            if s.dcnt[i] > 0:
                s._wait('sp', i, s.dcnt[i])
        for i in range(s.swnext):
            s._wait('sp', 1000 + i, 16)
        for k in ('pe', 'act', 'dve', 'pool'):
            if s.cnt[k] > 0:
                s._wait('sp', k, s.cnt[k])


def build_nc(nl=NL, dbg=False, upto=None):
    nc = bass.Bass("TRN2", target_bir_lowering=False)
    _order = ['M0', 'M1', 'M2', 'M3', 'M', 'A0', 'A1', 'A1a', 'A1b', 'A1c', 'A2', 'A', 'R0', 'R1', 'R2', 'R3', 'R', 'F0', 'F1', 'F']

    def stop(p):
        return upto is not None and _order.index(upto) <= _order.index(p)

    def din(name, shape, dt=F32):
        return nc.dram_tensor(name, list(shape), dt, kind="ExternalInput")

    def dout(name, shape, dt=F32):
        return nc.dram_tensor(name, list(shape), dt, kind="ExternalOutput")

    x_d = din("x", [T, D])
    cvec_d = din("cvec", [128, 8])
    ctxk_d = din("ctxk", [NL, 512, 512])
    ctxv_d = din("ctxv", [NL, 512, 512])
    s0_d = din("s0", [NL, 2, 128, 2, 64])
    flags_d = din("flags", [128, 2])
    wada_d = din("w_ada", [NL, D, 3 * D])
    bada_d = din("b_ada", [NL, 3 * D])
    gpre_d = din("g_pre", [NL, D])
    win_d = din("w_in", [NL, D, 3840])
    tpad_d = din("tpad", [NL, 8, 23, 127])
    lbl_d = din("lb_logits", [2, NL, 256])
    ghg_d = din("g_hgrn", [NL, 256])
    wfn_d = din("w_fnet", [NL, 256, 256])
    wout_d = din("w_out", [NL, D, D])
    gpost_d = din("g_post", [NL, D])
    cf32_d = din("cf32", [128, 7, 128])
    rowb_d = din("rowbias", [128, 74])
    cb16_d = din("cb16", [128, 9, 128], BF16)
    csn_d = din("csn", [8, 128, 2, 1024], BF16)

    y_d = dout("y", [T, D])
    nk_d = dout("newk", [NL, T, 512])
    nv_d = dout("newv", [NL, T, 512])
    ns_d = dout("news", [NL, 2, 4, 128, 2, 64])
    dbg_d = dout("dbgmixed", [T, D], BF16) if dbg else None

    with ExitStack() as st:
        def sb(name, shape, dt=F32):
            return st.enter_context(nc.sbuf_tensor(name, list(shape), dt))

        tk = TK(nc, st)
        x_sb = sb("x_sb", [128, NT, D])
        hT = sb("hT", [128, 8, T], BF16)
        mixed = sb("mixed", [128, NT, D], BF16)
        wst = [sb("wst0", [128, 8, 512], BF16)]
        wbf = [sb("wbf%d" % i, [128, 8, 512], BF16) for i in range(2)]
        gg = sb("gg", [128, D])
        modN = sb("modN", [128, 3 * D], BF16)
        screp = sb("screp", [128, 8, 128], BF16)
        brow = sb("brow", [1, 512])
        ones_row = sb("ones_row", [1, 128])
        csil = sb("csil", [128, 8])
        cf32 = sb("cf32s", [128, 7, 128])
        rowb = sb("rowbs", [128, 74])
        cb16 = sb("cb16s", [128, 9, 128], BF16)
        flags = sb("flagss", [128, 2])
        lbl = sb("lbl", [128, 2, 256])
        oml = sb("oml", [128, 2, 256])
        ghgB = sb("ghgB", [128, 256])
        ssq = sb("ssq", [128, 16])
        rstd = sb("rstd", [128, 16])
        ARW = 21120
        arena = sb("arena", [128, ARW])
        apos = [0]

        def areset():
            apos[0] = 0

        def aget(shape, dt=F32):
            n = 1
            for d_ in shape[1:]:
                n *= d_
            words = n if dt == F32 else (n + 1) // 2
            a0 = apos[0]
            apos[0] += words
            assert apos[0] <= ARW, ("arena overflow", apos[0])
            v = arena[:, a0:a0 + words]
            if dt != F32:
                v = v.bitcast(dt)
            if len(shape) == 3:
                v = v.rearrange("p (a b) -> p a b", a=shape[1])
            elif len(shape) == 4:
                v = v.rearrange("p (a b c) -> p a b c", a=shape[1], b=shape[2])
            return v

        psbig = [st.enter_context(nc.psum_tensor("psb%d" % i, [128, 1024], F32)) for i in range(4)]
        ps = [psbig[i // 2][:, (i % 2) * 512:(i % 2 + 1) * 512] for i in range(8)]
        psk = ["ps%d" % i for i in range(8)]
        bank_rr = [0]

        def nb(avoid=()):
            while True:
                b = bank_rr[0]
                bank_rr[0] = (b + 1) % 8
                if b not in avoid:
                    return b

        J2 = cf32[:, 0, :]
        IDF = cf32[:, 1, :]
        CMT = cf32[:, 2, :]
        TR = [cf32[:, 3, :], cf32[:, 4, :]]
        SEL = [cf32[:, 5, 0:8], cf32[:, 5, 8:16]]
        HM = cf32[:, 5, 16:18]
        QM = cf32[:, 5, 18:22]
        HME = cf32[:, 6, :].rearrange("p (a c) -> p a c", a=2)
        IDB = cb16[:, 0, :]
        TRI = [cb16[:, 1, :], cb16[:, 2, :]]
        C4S4 = cb16
        J2B = cb16[:, 7, :]
        CMTB = cb16[:, 8, :]

        tk.dma('sp', cf32[:], cf32_d.ap(), writes=['cf32'])
        tk.dma('sp', rowb[:], rowb_d.ap(), writes=['rowb'])
        tk.dma('sp', cb16[:], cb16_d.ap(), writes=['cb16'])
        tk.dma('sp', flags[:], flags_d.ap(), writes=['flags'])
        tk.dma('sp', csil[:], cvec_d.ap(), writes=['csil'])
        for t in range(NT):
            tk.dma('sp', x_sb[:, t, :], x_d.ap()[t * 128:(t + 1) * 128, :], writes=['x%d' % t])
        tk.op('pool', lambda e: e.memset(ones_row[:], 1.0), writes=['ones_row'])
        tk.op('act', lambda e: e.activation(out=csil[:], in_=csil[:], func=AF.Silu), reads=['csil'], writes=['csil'])

        wring = [0]

        wbring = [0]

        def load_w(src_ap, ncols):
            wi = wbring[0]
            wbring[0] ^= 1
            tk.dma('pool', wbf[wi][:, :, 0:ncols], src_ap.rearrange("(kc p) n -> p kc n", p=128), writes=['wbf%d' % wi])
            return wi

        wseq = []
        for l_ in range(nl):
            for (c0_, n_) in ((0, 512), (512, 512), (1024, 512), (1536, 512), (2048, 512), (2816, 512), (2560, 256), (3328, 512)):
                wseq.append((win_d, l_, c0_, n_))
            wseq.append((wout_d, l_, 0, 512))
            wseq.append((wout_d, l_, 512, 512))
        wstate = {'ptr': 0, 'loaded': {}}

        def _issue(i):
            if i < len(wseq) and i not in wstate['loaded']:
                d_, l_, c0_, n_ = wseq[i]
                wstate['loaded'][i] = load_w(d_.ap()[l_, :, c0_:c0_ + n_], n_)

        def next_w(prefetch=True):
            i = wstate['ptr']
            wstate['ptr'] += 1
            _issue(i)
            if prefetch:
                _issue(i + 1)
            return wstate['loaded'][i]

        def proj_tm(wi, c0, ncols, t, b):
            for kc in range(8):
                tk.op('pe', lambda e, kc=kc: e.matmul(ps[b][:, 0:ncols], lhsT=hT[:, kc, t * 128:(t + 1) * 128],
                                                     rhs=wbf[wi][:, kc, c0:c0 + ncols], start=(kc == 0), stop=(kc == 7)),
                      reads=['hT%d' % t, 'wbf%d' % wi], writes=[psk[b]])

        def proj_fm(wi, ci, g, b):
            for kc in range(8):
                tk.op('pe', lambda e, kc=kc: e.matmul(ps[b][:, 0:512], lhsT=wbf[wi][:, kc, ci * 128:(ci + 1) * 128],
                                                     rhs=hT[:, kc, g * 512:(g + 1) * 512], start=(kc == 0), stop=(kc == 7)),
                      reads=['hT%d' % x_ for x_ in range(4 * g, 4 * g + 4)] + ['wbf%d' % wi], writes=[psk[b]])

        def psb16(b):
            return ps[b].bitcast(BF16)

        def emit_mod_chunk(lm, ch, banks=None):
            mod_dma(lm, ch)
            mod_mm(lm, ch, banks)

        def mod_dma(lm, ch):
            tk.dma('pool', wst[0][:], wada_d.ap()[lm, :, ch * 512:(ch + 1) * 512].rearrange("(kc p) n -> p kc n", p=128), writes=['wst0'])
            tk.dma('sp', brow[:], bada_d.ap()[lm:lm + 1, ch * 512:(ch + 1) * 512], writes=['brow'])

        def mod_mm(lm, ch, banks=None):
            b = nb() if banks is None else banks[ch % len(banks)]
            for kc in range(8):
                tk.op('pe', lambda e, kc=kc: e.matmul(ps[b][:, :], lhsT=screp[:, kc, :], rhs=wst[0][:, kc, :], start=(kc == 0), stop=False),
                      reads=['screp', 'wst0'], writes=[psk[b]])
            tk.op('pe', lambda e: e.matmul(ps[b][:, :], lhsT=ones_row[0:1, :], rhs=brow[0:1, :], start=False, stop=True),
                  reads=['ones_row', 'brow'], writes=[psk[b]])
            tk.op('act', lambda e: e.copy(out=modN[:, ch * 512:(ch + 1) * 512], in_=ps[b][:, :]), reads=[psk[b]], writes=['modN'])

        tk.op('dve', lambda e: e.tensor_copy(out=screp[:], in_=csil[:].unsqueeze(2).broadcast_to([128, 8, 128])), reads=['csil'], writes=['screp'])
        for ch in range(6):
            emit_mod_chunk(0, ch)

        for l in range(nl):
            if l == 0:
                tk.barrier()
            areset()
            apos[0] = 11520
            gbc = aget([128, 2, D])
            lbt = aget([128, 2, NL, 256])
            junk = aget([128, D], BF16)
            tmpf = aget([128, D])
            hb = [aget([128, D], BF16) for _ in range(2)]
            modA = aget([128, D])
            tk.dma('sp', gbc[:, 0, :], bass.AP(gpre_d, l * D, [[0, 128], [1, D]]), writes=['gbc'])
            tk.dma('sp', gbc[:, 1, :], bass.AP(gpost_d, l * D, [[0, 128], [1, D]]), writes=['gbc'])
            tk.dma('sp', lbt[:].rearrange("p a l c -> p (a l c)"), bass.AP(lbl_d, 0, [[0, 128], [1, 2 * NL * 256]]), writes=['lbt'])
            if l == 0:
                tk.op('dve', lambda e: e.memset(lbl[:], 0.0), writes=['lbl'])
            else:
                mx = tmpf[:, 0:512].rearrange("p (a c) -> p a c", a=2)
                sm = tmpf[:, 512:1024].rearrange("p (a c) -> p a c", a=2)
                tk.op('dve', lambda e: e.tensor_tensor(out=mx, in0=lbt[:, :, 0, :], in1=lbt[:, :, 1, :], op=ALU.max),
                      reads=['lbt'], writes=['tmpf'])
                for l2 in range(2, NL):
                    tk.op('dve', lambda e, l2=l2: e.tensor_tensor(out=mx, in0=mx, in1=lbt[:, :, l2, :], op=ALU.max),
                          reads=['lbt', 'tmpf'], writes=['tmpf'])
                for l2 in range(NL):
                    tk.op('dve', lambda e, l2=l2: e.tensor_tensor(out=lbt[:, :, l2, :], in0=lbt[:, :, l2, :], in1=mx, op=ALU.subtract),
                          reads=['lbt', 'tmpf'], writes=['lbt'])
                tk.op('act', lambda e: e.activation(out=lbt[:].rearrange("p a l c -> p (a l c)"), in_=lbt[:].rearrange("p a l c -> p (a l c)"), func=AF.Exp),
                      reads=['lbt'], writes=['lbt'])
                tk.op('dve', lambda e: e.tensor_tensor(out=sm, in0=lbt[:, :, 0, :], in1=lbt[:, :, 1, :], op=ALU.add),
                      reads=['lbt'], writes=['tmpf'])
                for l2 in range(2, NL):
                    tk.op('dve', lambda e, l2=l2: e.tensor_tensor(out=sm, in0=sm, in1=lbt[:, :, l2, :], op=ALU.add),
                          reads=['lbt', 'tmpf'], writes=['tmpf'])
                tk.op('dve', lambda e: e.reciprocal(out=sm, in_=sm), reads=['tmpf'], writes=['tmpf'])
                tk.op('dve', lambda e: e.tensor_copy(out=lbl[:], in_=lbt[:, :, 1, :]), reads=['lbt'], writes=['lbl'])
                for l2 in range(2, l + 1):
                    tk.op('dve', lambda e, l2=l2: e.tensor_tensor(out=lbl[:], in0=lbl[:], in1=lbt[:, :, l2, :], op=ALU.add),
                          reads=['lbt', 'lbl'], writes=['lbl'])
                tk.op('dve', lambda e: e.tensor_tensor(out=lbl[:], in0=lbl[:], in1=sm, op=ALU.mult), reads=['lbl', 'tmpf'], writes=['lbl'])
            tk.op('dve', lambda e: e.tensor_scalar(out=oml[:], in0=lbl[:], scalar1=-0.5, scalar2=0.5, op0=ALU.mult, op1=ALU.add),
                  reads=['lbl'], writes=['oml'])
            tk.op('dve', lambda e: e.tensor_scalar(out=lbl[:], in0=lbl[:], scalar1=0.5, scalar2=0.5, op0=ALU.mult, op1=ALU.add),
                  reads=['lbl'], writes=['lbl'])
            if stop('M0'):
                break
            tk.op('dve', lambda e: e.scalar_tensor_tensor(out=modA[:], in0=modN[:, D:2 * D], scalar=1.0, in1=gbc[:, 0, :], op0=ALU.add, op1=ALU.mult),
                  reads=['modN', 'gbc'], writes=['modA'])
            tk.op('dve', lambda e: e.tensor_tensor(out=gg[:], in0=modN[:, 2 * D:3 * D], in1=gbc[:, 1, :], op=ALU.mult), reads=['modN', 'gbc'], writes=['gg'])
            if stop('M1'):
                break
            for t in range(NT):
                tk.op('act', lambda e, t=t: e.activation(out=junk[:], in_=x_sb[:, t, :], func=AF.Square, accum_out=ssq[:, t:t + 1]),
                      reads=['x%d' % t], writes=['junk', 'ssq'])
            tk.op('dve', lambda e: e.tensor_scalar(out=rstd[:, 0:8], in0=ssq[:, 0:8], scalar1=1.0 / D, scalar2=EPS, op0=ALU.mult, op1=ALU.add),
                  reads=['ssq'], writes=['rstd'])
            tk.op('act', lambda e: e.activation(out=rstd[:, 0:8], in_=rstd[:, 0:8], func=AF.Ln), reads=['rstd'], writes=['rstd'])
            tk.op('act', lambda e: e.activation(out=rstd[:, 0:8], in_=rstd[:, 0:8], func=AF.Exp, scale=-0.5), reads=['rstd'], writes=['rstd'])
            if stop('M2'):
                break
            for t in range(NT):
                hbi = t % 2
                tk.op('dve', lambda e, t=t: e.scalar_tensor_tensor(out=tmpf[:], in0=x_sb[:, t, :], scalar=rstd[:, t:t + 1], in1=modA[:],
                                                                   op0=ALU.mult, op1=ALU.mult),
                      reads=['x%d' % t, 'rstd', 'modA'], writes=['tmpf'])
                tk.op('dve', lambda e: e.tensor_tensor(out=hb[hbi][:], in0=tmpf[:], in1=modN[:, 0:D], op=ALU.add),
                      reads=['tmpf', 'modN'], writes=['hb%d' % hbi])
                if stop('M3'):
                    continue
                b = nb()
                for kc in range(8):
                    tk.op('pe', lambda e, kc=kc: e.transpose(out=psb16(b)[:, kc * 128:(kc + 1) * 128], in_=hb[hbi][:, kc * 128:(kc + 1) * 128], identity=IDB),
                          reads=['hb%d' % hbi, 'cb16'], writes=[psk[b]])
                tk.op('act', lambda e, t=t: e.copy(out=hT[:, :, t * 128:(t + 1) * 128], in_=psb16(b)[:, :].rearrange("p (k c) -> p k c", k=8)),
                      reads=[psk[b]], writes=['hT%d' % t])

            if stop('M'):
                break
            tk.barrier()
            areset()
            qT = aget([128, 4, T], BF16)
            kT = aget([128, 4, T], BF16)
            ckT = aget([128, 4, 512], BF16)
            vaug = aget([128, NT, 8, 66], BF16)
            cvaug = aget([128, 4, 8, 66], BF16)
            sga = aget([128, NT, 512], BF16)
            expT = aget([128, 7, 8, 128], BF16)
            Eb = [aget([128, 8, 128], BF16) for _ in range(3)]
            Pb = [aget([128, 8, 128], BF16) for _ in range(2)]
            hk = [Eb[0].bitcast(F32) if False else None, None]
            ost = [aget([128, 512]) for _ in range(2)]
            rden = aget([128, 8])
            otmp = aget([128, 8, 64])
            ckb = aget([128, 4, 512], BF16)
            hkA = aget([128, 8, 128])
            hkB = aget([128, 8, 128])
            hk = [hkA, hkB]
            tk.op('pool', lambda e: e.memset(vaug[:, :, :, 64:66], 1.0), writes=['vaug'])
            tk.op('dve', lambda e: e.tensor_copy(out=cvaug[:, :, :, 64:66].rearrange("p a b c -> p (a b) c"),
                                                 in_=flags[:, 0:1].unsqueeze(2).broadcast_to([128, 32, 2])),
                  reads=['flags'], writes=['cvaug'])
            def toep_dma(di):
                dl = di - 3
                hi = di % 2
                for qr in range(2):
                    for krl in range(2):
                        off = ((l * 8) * 23 + (2 * dl + krl - qr + 11)) * 127
                        src = bass.AP(tpad_d, off, [[1, 64], [23 * 127, 8], [1, 64]])
                        tk.dma('sp', hk[hi][qr * 64:(qr + 1) * 64, :, krl * 64:(krl + 1) * 64], src, writes=['hk%d_%d' % (hi, qr * 2 + krl)])

            def toep_mm(di):
                hi = di % 2
                bA = nb()
                bB = nb()
                for h in range(8):
                    b = bA if h % 2 == 0 else bB
                    o = ps[b][:, (h // 2) * 128:(h // 2 + 1) * 128]
                    tk.op('pe', lambda e, h=h, o=o: e.matmul(o, lhsT=hk[hi][:, h, :], rhs=J2, start=True, stop=False),
                          reads=['hk%d_%d' % (hi, x) for x in range(4)] + ['cf32'], writes=[psk[b]])
                    tk.op('pe', lambda e, o=o: e.matmul(o, lhsT=IDF, rhs=CMT, start=False, stop=True),
                          reads=['cf32'], writes=[psk[b]])
                for bi, b in enumerate((bA, bB)):
                    tk.op('act', lambda e, bi=bi, b=b: e.activation(out=expT[:, di, bi * 4:(bi + 1) * 4, :].rearrange("p a c -> p (a c)"),
                                                                    in_=ps[b][:, :], func=AF.Exp),
                          reads=[psk[b]], writes=['expT%d' % di])

            toep_dma(0)
            toep_dma(1)
            if stop('A0'):
                break
            ckk = 'ckb'
            tk.dma('pool', ckb[:], ctxk_d.ap()[l].rearrange("(c p) n -> p c n", p=128), writes=[ckk])
            for c in range(4):
                b = nb()
                for pr in range(4):
                    tk.op('pe', lambda e, pr=pr: e.transpose(out=psb16(b)[:, pr * 128:(pr + 1) * 128], in_=ckb[:, c, pr * 128:(pr + 1) * 128], identity=IDB),
                          reads=[ckk, 'cb16'], writes=[psk[b]])
                tk.op('act', lambda e, c=c: e.copy(out=ckT[:, :, c * 128:(c + 1) * 128], in_=psb16(b)[:, 0:512].rearrange("p (k c) -> p k c", k=4)),
                      reads=[psk[b]], writes=['ckT'])
            tk.dma('pool', ckb[:], ctxv_d.ap()[l].rearrange("(c p) n -> p c n", p=128), reads=[], writes=[ckk])
            tk.op('pool', lambda e: e.tensor_copy(out=cvaug[:, :, :, 0:64], in_=ckb[:].rearrange("p c (h d) -> p c h d", h=8)), reads=[ckk], writes=['cvaug'])
            if stop('A1'):
                break
            toep_mm(0)
            toep_dma(2)
            wi = next_w()
            for pr in range(4):
                for g in range(2):
                    b = nb()
                    proj_fm(wi, pr, g, b)
                    tk.op('act', lambda e, pr=pr, g=g: e.copy(out=qT[:, pr, g * 512:(g + 1) * 512], in_=ps[b][:, :]), reads=[psk[b]], writes=['qT%d_%d' % (pr, g)])
            if stop('A1a'):
                break
            toep_mm(1)
            toep_dma(3)
            wi = next_w()
            for pr in range(4):
                for g in range(2):
                    b = nb()
                    proj_fm(wi, pr, g, b)
                    tk.op('act', lambda e, pr=pr, g=g: e.copy(out=kT[:, pr, g * 512:(g + 1) * 512], in_=ps[b][:, :]), reads=[psk[b]], writes=['kTf%d_%d' % (pr, g)])
            toep_mm(2)
            toep_dma(4)
            for t in range(NT):
                b = nb()
                proj_tm(wi, 0, 512, t, b)
                oi = t % 2
                tk.op('dve', lambda e: e.tensor_copy(out=ost[oi][:], in_=ps[b][:, :]), reads=[psk[b]], writes=['ost%d' % oi])
                tk.dma('sp', nk_d.ap()[l, t * 128:(t + 1) * 128, :], ost[oi][:], reads=['ost%d' % oi])
            toep_mm(3)
            toep_dma(5)
            if stop('A1b'):
                break
            wi = next_w()
            for t in range(NT):
                b = nb()
                proj_tm(wi, 0, 512, t, b)
                oi = t % 2
                tk.op('dve', lambda e: e.tensor_copy(out=ost[oi][:], in_=ps[b][:, :]), reads=[psk[b]], writes=['ost%d' % oi])
                tk.op('act', lambda e, t=t: e.copy(out=vaug[:, t, :, 0:64], in_=ost[oi][:].rearrange("p (h d) -> p h d", h=8)),
                      reads=['ost%d' % oi], writes=['vaug%d' % t])
                tk.dma('sp', nv_d.ap()[l, t * 128:(t + 1) * 128, :], ost[oi][:], reads=['ost%d' % oi])
            if stop('A1c'):
                break
            toep_mm(4)
            toep_dma(6)
            wi = next_w()
            for t in range(NT):
                b = nb()
                proj_tm(wi, 0, 512, t, b)
                tk.op('act', lambda e, t=t: e.activation(out=sga[:, t, :], in_=ps[b][:, :], func=AF.Silu), reads=[psk[b]], writes=['sga%d' % t])
            toep_mm(5)
            toep_mm(6)
            if stop('A2'):
                break
            OA, OB = 6, 7
            spairs = [(0, 1), (2, 3), (4, 5)]
            allsteps = []
            for j in range(8):
                st_ = [('l', kt) for kt in KT[j]] + [('c', c) for c in range(4)]
                for si_, (kind, idx) in enumerate(st_):
                    allsteps.append((j, si_, len(st_), kind, idx))

            def emit_S(k):
                j, si_, ns_, kind, idx = allsteps[k]
                sA, sB = spairs[k % 3]
                for h in range(8):
                    b = sA if h % 2 == 0 else sB
                    r0 = (h % 2) * 64
                    ksrc = kT[r0:r0 + 64, h // 2, idx * 128:(idx + 1) * 128] if kind == 'l' else ckT[r0:r0 + 64, h // 2, idx * 128:(idx + 1) * 128]
                    tk.op('pe', lambda e, h=h, b=b, ksrc=ksrc, r0=r0: e.matmul(ps[b][:, (h // 2) * 128:(h // 2 + 1) * 128], lhsT=ksrc,
                                                                              rhs=qT[r0:r0 + 64, h // 2, j * 128:(j + 1) * 128], start=True, stop=True),
                          reads=['qT%d_%d' % (h // 2, j // 4), ('kTf%d_%d' % (h // 2, idx // 4)) if kind == 'l' else 'ckT'], writes=[psk[b]])

            def emit_rest(k):
                j, si_, ns_, kind, idx = allsteps[k]
                sA, sB = spairs[k % 3]
                sl = k % 3
                big = psbig[sA // 2]
                if kind == 'l':
                    jk = JKI[(j, idx)]
                    for hf in range(2):
                        tk.op('act', lambda e, hf=hf: e.activation(
                            out=Eb[sl][:, :, hf * 64:(hf + 1) * 64],
                            in_=big[:, :].rearrange("p (a c) -> p a c", a=8)[:, :, hf * 64:(hf + 1) * 64],
                            func=AF.Exp, scale=0.125, bias=rowb[:, jk * 2 + hf:jk * 2 + hf + 1]),
                            reads=[psk[sA], psk[sB], 'rowb'], writes=['Eb%d' % sl])
                    di = idx - j + 3
                    pl = k % 2
                    tk.op('dve', lambda e, di=di: e.tensor_tensor(out=Pb[pl][:], in0=Eb[sl][:], in1=expT[:, di, :, :], op=ALU.mult),
                          reads=['Eb%d' % sl, 'expT%d' % di], writes=['Pb%d' % pl])
                    lhs, lk = Pb[pl], 'Pb%d' % pl
                    vsrc, vk = vaug, 'vaug%d' % idx
                else:
                    tk.op('act', lambda e: e.activation(out=Eb[sl][:].rearrange("p a c -> p (a c)"), in_=big[:, :], func=AF.Exp, scale=0.125),
                          reads=[psk[sA], psk[sB]], writes=['Eb%d' % sl])
                    lhs, lk = Eb[sl], 'Eb%d' % sl
                    vsrc, vk = cvaug, 'cvaug'
                for e_ in range(8):
                    h = 2 * (e_ % 4) + e_ // 4
                    ob = OA if e_ < 4 else OB
                    tk.op('pe', lambda e, e_=e_, h=h, ob=ob, lhs=lhs, vsrc=vsrc: e.matmul(
                        ps[ob][:, (e_ % 4) * 66:(e_ % 4) * 66 + 66], lhsT=lhs[:, e_, :], rhs=vsrc[:, idx, h, :],
                        start=(si_ == 0 and e_ % 4 == 0), stop=(si_ == ns_ - 1), skip_group_check=True),
                        reads=[lk, vk] + (['vaug'] if kind == 'l' else []), writes=[psk[ob]])
                if si_ == ns_ - 1:
                    for bi, ob in enumerate((OA, OB)):
                        tk.op('dve', lambda e, bi=bi, ob=ob: e.reciprocal(out=rden[:, bi * 4:(bi + 1) * 4],
                                                                          in_=ps[ob][:, 0:264].rearrange("p (a c) -> p a c", a=4)[:, :, 64]),
                              reads=[psk[ob]], writes=['rden'])
                    for bi, ob in enumerate((OA, OB)):
                        tk.op('dve', lambda e, bi=bi, ob=ob: e.tensor_tensor(
                            out=otmp[:, bi:8:2, :], in0=ps[ob][:, 0:264].rearrange("p (a c) -> p a c", a=4)[:, :, 0:64],
                            in1=rden[:, bi * 4:(bi + 1) * 4].unsqueeze(2).broadcast_to([128, 4, 64]), op=ALU.mult),
                            reads=[psk[ob], 'rden'], writes=['otmp'])
                    tk.op('dve', lambda e: e.tensor_tensor(out=mixed[:, j, 0:512], in0=otmp[:].rearrange("p h d -> p (h d)"), in1=sga[:, j, :], op=ALU.mult),
                          reads=['otmp', 'sga%d' % j], writes=['mixed%d' % j])

            emit_S(0)
            emit_S(1)
            for k in range(len(allsteps)):
                if k + 2 < len(allsteps):
                    emit_S(k + 2)
                emit_rest(k)

            if stop('A'):
                break
            tk.barrier()
            areset()
            qh = aget([128, NT, 256], BF16)
            sgb = aget([128, NT, 256])
            qE = sgb.rearrange("p a b -> p (a b)")[:, 0:1024].bitcast(BF16).rearrange("p (a b) -> p a b", a=NT)
            kE = sgb.rearrange("p a b -> p (a b)")[:, 1024:2048].bitcast(BF16).rearrange("p (a b) -> p a b", a=NT)
            vh = aget([128, NT, 256], BF16)
            vhmF = aget([128, 4096])
            vhm = vhmF.bitcast(BF16).rearrange("p (q t c) -> p q t c", q=4, t=NT)
            A_ = vhmF[:, 0:2048].rearrange("p (e c) -> p e c", e=64)
            B_ = vhmF[:, 2048:4096].rearrange("p (e c) -> p e c", e=64)
            sgr = aget([128, NT, 256], BF16)
            fS = aget([128, 4096])
            fbuf = fS[:, 0:2048].rearrange("p (a b) -> p a b", a=NT)
            SinPm = fS.bitcast(BF16).rearrange("p (h c e) -> p h c e", h=4, c=32)
            lfbuf = aget([128, NT, 256])
            kETm = lfbuf.rearrange("p a b -> p (a b)").bitcast(BF16).rearrange("p (h t) -> p h t", h=4)
            osq = lfbuf
            tmpE = [aget([128, 512]) for _ in range(2)]
            qET = aget([128, 2, T], BF16)
            ATm = [aget([128, 4, 128], BF16) for _ in range(2)]
            osum = aget([128, NT, 256])
            gs3 = aget([128, 2, 32, 3])
            es3 = aget([128, 2, 32, 3])
            S0b = aget([128, 2, 64])
            nsb = aget([128, 4, 2, 64])
            hss = aget([128, 32])
            tk.dma('sp', ghgB[:], bass.AP(ghg_d, l * 256, [[0, 128], [1, 256]]), writes=['ghgB'])
            wi = next_w()
            for t in range(NT):
                b = nb()
                proj_tm(wi, 0, 512, t, b)
                tk.op('act', lambda e, t=t: e.activation(out=qh[:, t, :], in_=ps[b][:, 0:256], func=AF.Silu), reads=[psk[b]], writes=['qh'])
                tk.op('act', lambda e, t=t: e.activation(out=sgb[:, t, :], in_=ps[b][:, 256:512], func=AF.Tanh, scale=0.5), reads=[psk[b]], writes=['sQ'])
            wi = next_w()
            for t in range(NT):
                b = nb()
                proj_tm(wi, 0, 512, t, b)
                tk.op('act', lambda e, t=t: e.activation(out=sgr[:, t, :], in_=ps[b][:, 256:512], func=AF.Silu), reads=[psk[b]], writes=['sgr'])
                tk.op('dve', lambda e, t=t: e.tensor_copy(out=vh[:, t, :], in_=ps[b][:, 0:256]), reads=[psk[b]], writes=['vh'])

            if stop('R0'):
                break
            rstop = False
            for dr in range(2):
                if l + 1 < nl:
                    mod_dma(l + 1, 3 * dr)
                lb_bc = lbl[:, dr, :].unsqueeze(1).broadcast_to([128, NT, 256])
                oml_bc = oml[:, dr, :].unsqueeze(1).broadcast_to([128, NT, 256])
                tk.op('dve', lambda e: e.tensor_tensor(out=fbuf[:], in0=sgb[:], in1=oml_bc, op=ALU.mult), reads=['sQ', 'oml'], writes=['fS'])
                tk.op('dve', lambda e: e.tensor_tensor(out=fbuf[:], in0=fbuf[:], in1=lb_bc, op=ALU.add), reads=['fS', 'lbl'], writes=['fS'])
                tk.op('act', lambda e: e.activation(out=lfbuf[:].rearrange("p a b -> p (a b)"), in_=fbuf[:].rearrange("p a b -> p (a b)"), func=AF.Ln),
                      reads=['fS'], writes=['lK'])
                tk.op('dve', lambda e: e.tensor_scalar(out=fbuf[:], in0=fbuf[:], scalar1=-1.0, scalar2=1.0, op0=ALU.mult, op1=ALU.add),
                      reads=['fS'], writes=['fS'])
                bS = nb()
                for t in range(NT):
                    for pr in range(2):
                        tt_ = t if dr == 0 else 7 - t
                        tk.op('pe', lambda e, t=t, pr=pr, tt_=tt_: e.matmul(ps[bS][:, (pr * 8 + tt_) * 8:(pr * 8 + tt_) * 8 + 8], lhsT=lfbuf[:, t, pr * 128:(pr + 1) * 128],
                                                                   rhs=SEL[dr], start=True, stop=True),
                              reads=['lK', 'cf32'], writes=[psk[bS]])
                psS = ps[bS][:, 0:128].rearrange("p (a c r) -> p a c r", a=2, c=32)
                tk.op('dve', lambda e: e.tensor_copy(out=gs3[:, :, :, 0:2], in_=psS), reads=[psk[bS]], writes=['gs3'])
                tk.op('dve', lambda e: e.tensor_tensor(out=gs3[:, :, :, 2], in0=gs3[:, :, :, 1], in1=gs3[:, :, :, 0], op=ALU.subtract),
                      reads=['gs3'], writes=['gs3'])
                tk.op('act', lambda e: e.activation(out=es3[:].rearrange("p a c r -> p (a c r)"), in_=gs3[:].rearrange("p a c r -> p (a c r)"), func=AF.Exp),
                      reads=['gs3'], writes=['es3'])
                tk.op('dve', lambda e: e.tensor_scalar(out=es3[:, :, 8:32:8, 0:2], in0=es3[:, :, 8:32:8, 0:2], scalar1=flags[:, 1:2], scalar2=None, op0=ALU.mult),
                      reads=['es3', 'flags'], writes=['es3'])
                for tp in range(4):
                    b = nb()
                    for i in range(2):
                        t = 2 * tp + i
                        tk.op('pe', lambda e, t=t, i=i: e.matmul(ps[b][:, i * 256:(i + 1) * 256], lhsT=TR[dr], rhs=lfbuf[:, t, :], start=True, stop=True),
                              reads=['cf32', 'lK'], writes=[psk[b]])
                    tk.op('act', lambda e: e.activation(out=tmpE[0][:], in_=ps[b][:, :], func=AF.Exp), reads=[psk[b]], writes=['tmpE0'])
                    tk.op('act', lambda e: e.activation(out=tmpE[1][:], in_=ps[b][:, :], func=AF.Exp, scale=-1.0), reads=[psk[b]], writes=['tmpE1'])
                    tk.op('dve', lambda e, tp=tp: e.tensor_tensor(out=qE[:, 2 * tp:2 * tp + 2, :].rearrange("p a c -> p (a c)"),
                                                                  in0=qh[:, 2 * tp:2 * tp + 2, :].rearrange("p a c -> p (a c)"), in1=tmpE[0][:], op=ALU.mult),
                          reads=['qh', 'tmpE0'], writes=['sQ', 'qk%d' % tp])
                    tk.op('dve', lambda e, tp=tp: e.tensor_tensor(out=kE[:, 2 * tp:2 * tp + 2, :].rearrange("p a c -> p (a c)"),
                                                                  in0=fbuf[:, 2 * tp:2 * tp + 2, :].rearrange("p a c -> p (a c)"), in1=tmpE[1][:], op=ALU.mult),
                          reads=['fS', 'tmpE1'], writes=['sQ', 'qk%d' % tp])
                for cp in range(4):
                    tk.op('dve', lambda e, cp=cp: e.tensor_scalar(out=vhm[:, cp, :, :], in0=vh[:], scalar1=QM[:, cp:cp + 1], scalar2=None, op0=ALU.mult),
                          reads=['vh', 'cf32'], writes=['vhm'])
                tk._deps('act', [], ['lK'])
                for t in range(NT):
                    b = nb()
                    for pr in range(2):
                        tk.op('pe', lambda e, t=t, pr=pr: e.transpose(out=psb16(b)[:, pr * 128:(pr + 1) * 128], in_=qE[:, t, pr * 128:(pr + 1) * 128], identity=IDB),
                              reads=['qk%d' % (t // 2), 'cb16'], writes=[psk[b]], war_only=['sQ'])
                        tk.op('pe', lambda e, t=t, pr=pr: e.transpose(out=psb16(b)[:, (2 + pr) * 128:(3 + pr) * 128], in_=kE[:, t, pr * 128:(pr + 1) * 128], identity=IDB),
                              reads=['qk%d' % (t // 2), 'cb16'], writes=[psk[b]], war_only=['sQ'])
                    tk.op('act', lambda e, t=t: e.copy(out=qET[:, :, t * 128:(t + 1) * 128], in_=psb16(b)[:, 0:256].rearrange("p (a c) -> p a c", a=2)),
                          reads=[psk[b]], writes=['qET%d' % t])
                    for h in range(4):
                        tk.op('act', lambda e, t=t, h=h: e.activation(out=kETm[:, h, t * 128:(t + 1) * 128], in_=psb16(b)[:, (2 + h // 2) * 128:(3 + h // 2) * 128],
                                                                      func=AF.Copy, scale=HM[:, h % 2:h % 2 + 1]),
                              reads=[psk[b], 'cf32'], writes=['kT%d_%d' % (t, h)])
                if stop('R1'):
                    rstop = True
                    break
                tk.dma('sp', S0b[:], s0_d.ap()[l, dr], writes=['S0b'])
                kvb_all = [[nb() for _ in range(4)] for _ in range(2)]
                for pr in range(2):
                    kvb = kvb_all[pr]
                    for c in range(32):
                        cq = c if dr == 0 else 31 - c
                        b = kvb[cq // 8]
                        for h in (2 * pr, 2 * pr + 1):
                            o = ps[b][(h % 2) * 64:(h % 2) * 64 + 64, (cq % 8) * 64:(cq % 8) * 64 + 64]
                            tk.op('pe', lambda e, c=c, h=h, o=o: e.matmul(o, lhsT=kE[:, c // 4, h * 64:(h + 1) * 64], rhs=vhm[:, c % 4, c // 4, h * 64:(h + 1) * 64],
                                                                          start=True, stop=True),
                                  reads=['sQ', 'vhm'], writes=[psk[b]])
                for pr in range(2):
                    kvb = kvb_all[pr]
                    for g in range(4):
                        tk.op('dve', lambda e, g=g: e.tensor_tensor(out=B_[:, :, g * 8:(g + 1) * 8].rearrange("p e c -> p c e"),
                                                                    in0=ps[kvb[g]][:, :].rearrange("p (c e) -> p c e", c=8),
                                                                    in1=es3[:, pr, g * 8:(g + 1) * 8, 2].unsqueeze(2).broadcast_to([128, 8, 64]), op=ALU.mult),
                              reads=[psk[kvb[g]], 'es3'], writes=['vhm'])
                    if dr == 0 and pr == 0:
                        wi = next_w()
                        for t in range(NT):
                            b = kvb[t % 4]
                            proj_tm(wi, 0, 256, t, b)
                            tk.op('act', lambda e, t=t: e.activation(out=sgb[:, t, :], in_=ps[b][:, 0:256], func=AF.Tanh, scale=0.5), reads=[psk[b]], writes=['sQ'])
                    if dr == 1 and pr == 0:
                        wi = next_w()
                        uT_h = qh[:].rearrange("p a b -> p (a b)").rearrange("p (c t) -> p c t", c=2)
                        sgf_h = qE
                        hb_ = 0
                        for ci in range(2):
                            for g in range(2):
                                b = kvb[hb_ % 4]
                                hb_ += 1
                                proj_fm(wi, ci, g, b)
                                tk.op('act', lambda e, ci=ci, g=g: e.copy(out=uT_h[:, ci, g * 512:(g + 1) * 512], in_=ps[b][:, :]), reads=[psk[b]], writes=['qh'])
                        for t in range(NT):
                            b = kvb[hb_ % 4]
                            hb_ += 1
                            proj_tm(wi, 256, 256, t, b)
                            tk.op('act', lambda e, t=t: e.activation(out=sgf_h[:, t, :], in_=ps[b][:, 0:256], func=AF.Silu), reads=[psk[b]], writes=['sQ'])
                    tk.op('dve', lambda e: e.scalar_tensor_tensor(out=B_[:, :, 0], in0=S0b[:, pr, :], scalar=es3[:, pr, 0, 1:2], in1=B_[:, :, 0], op0=ALU.mult, op1=ALU.add),
                          reads=['S0b', 'es3', 'vhm'], writes=['vhm'])
                    tk.op('dve', lambda e: e.tensor_copy(out=A_[:], in_=es3[:, pr, :, 1].unsqueeze(1).broadcast_to([128, 64, 32])), reads=['es3', 'vhm'], writes=['vhm'])
                    tk.op('dve', lambda e: e.memset(A_[:, :, 0:1], 0.0), reads=['vhm'], writes=['vhm'])
                    tk.op('dve', lambda e: e.tensor_tensor_scan(out=B_[:].rearrange("p e c -> p (e c)"), data0=A_[:].rearrange("p e c -> p (e c)"),
                                                                data1=B_[:].rearrange("p e c -> p (e c)"), initial=0.0, op0=ALU.mult, op1=ALU.add),
                          reads=['vhm'], writes=['vhm'])
                    tk.op('dve', lambda e: e.tensor_copy(out=nsb[:, :, pr, :], in_=B_[:, :, 7:32:8].rearrange("p e k -> p k e")), reads=['vhm'], writes=['nsb'])
                    for h2 in range(2):
                        tk.op('dve', lambda e, h2=h2: e.scalar_tensor_tensor(out=SinPm[:, 2 * pr + h2, 1:32, :], in0=B_[:, :, 0:31].rearrange("p e c -> p c e"),
                                                                             scalar=HM[:, h2:h2 + 1], in1=es3[:, pr, 1:32, 0].unsqueeze(2).broadcast_to([128, 31, 64]),
                                                                             op0=ALU.mult, op1=ALU.mult),
                              reads=['vhm', 'es3', 'cf32'], writes=['fS'])
                        tk.op('dve', lambda e, h2=h2: e.scalar_tensor_tensor(out=SinPm[:, 2 * pr + h2, 0, :], in0=S0b[:, pr, :], scalar=HM[:, h2:h2 + 1],
                                                                             in1=es3[:, pr, 0, 0:1].broadcast_to([128, 64]), op0=ALU.mult, op1=ALU.mult),
                              reads=['S0b', 'es3', 'cf32'], writes=['fS'])
                nsb3 = nsb[:].rearrange("p s a e -> p s (a e)")
                if dr == 0:
                    tk.dma('sp', ns_d.ap()[l, 0].rearrange("s p a e -> p s (a e)"), nsb3, reads=['nsb'])
                else:
                    for k in range(4):
                        tk.dma('sp', ns_d.ap()[l, 1, 3 - k].rearrange("p a e -> p (a e)"), nsb3[:, k, :], reads=['nsb'])
                if l + 1 < nl:
                    mod_mm(l + 1, 3 * dr)
                    mod_dma(l + 1, 3 * dr + 1)
                if stop('R2'):
                    rstop = True
                    break
                for t in range(NT):
                    bA_ = nb()
                    ai = t % 2
                    for h in range(4):
                        tk.op('pe', lambda e, t=t, h=h: e.matmul(ps[bA_][:, h * 128:(h + 1) * 128], lhsT=kETm[:, h, t * 128:(t + 1) * 128],
                                                                 rhs=qET[:, h // 2, t * 128:(t + 1) * 128], start=True, stop=True),
                              reads=['kT%d_%d' % (t, h), 'qET%d' % t], writes=[psk[bA_]], war_only=['lK'])
                    tk.op('dve', lambda e: e.tensor_tensor(out=ATm[ai][:], in0=ps[bA_][:, :].rearrange("p (a c) -> p a c", a=4),
                                                           in1=TRI[dr].unsqueeze(1).broadcast_to([128, 4, 128]), op=ALU.mult),
                          reads=[psk[bA_], 'cb16'], writes=['ATm%d' % ai])
                    bO = nb()
                    for h in range(4):
                        tk.op('pe', lambda e, t=t, h=h: e.matmul(ps[bO][:, h * 64:(h + 1) * 64], lhsT=ATm[ai][:, h, :], rhs=vh[:, t, h * 64:(h + 1) * 64],
                                                                 start=True, stop=False, skip_group_check=True),
                              reads=['ATm%d' % ai, 'vh'], writes=[psk[bO]])
                        for cp in range(4):
                            tk.op('pe', lambda e, t=t, h=h, cp=cp: e.matmul(ps[bO][cp * 32:(cp + 1) * 32, h * 64:(h + 1) * 64],
                                                                            lhsT=qET[:, h // 2, t * 128 + cp * 32:t * 128 + cp * 32 + 32],
                                                                            rhs=SinPm[:, h, (4 * t + cp) if dr == 0 else 31 - (4 * t + cp), :], start=False, stop=(cp == 3), skip_group_check=True,
                                                                            tile_position=(0, cp * 32)),
                                  reads=['qET%d' % t, 'fS'], writes=[psk[bO]])
                    if l + 1 < nl and t == 3:
                        mod_mm(l + 1, 3 * dr + 1)
                        mod_dma(l + 1, 3 * dr + 2)
                    if l + 1 < nl and t == 7:
                        mod_mm(l + 1, 3 * dr + 2)
                    if dr == 0:
                        tk.op('act', lambda e, t=t: e.copy(out=osum[:, t, :], in_=ps[bO][:, 0:256]), reads=[psk[bO]], writes=['osum%d' % t])
                    else:
                        tk.op('dve', lambda e, t=t: e.tensor_tensor(out=osum[:, t, :], in0=osum[:, t, :], in1=ps[bO][:, 0:256], op=ALU.add),
                              reads=[psk[bO], 'osum%d' % t], writes=['osum%d' % t])
                if stop('R3'):
                    rstop = True
                    break
            if rstop:
                break
            tk.op('dve', lambda e: e.tensor_tensor(out=osq[:], in0=osum[:], in1=osum[:], op=ALU.mult), reads=['osum%d' % t_ for t_ in range(NT)], writes=['lK'])
            tk.op('dve', lambda e: e.tensor_reduce(out=hss[:], in_=osq[:].rearrange("p t (h d) -> p (t h) d", h=4), axis=AX.X, op=ALU.add),
                  reads=['lK'], writes=['hss'])
            tk.op('dve', lambda e: e.tensor_scalar(out=hss[:], in0=hss[:], scalar1=1.0 / 64, scalar2=EPS, op0=ALU.mult, op1=ALU.add), reads=['hss'], writes=['hss'])
            tk.op('act', lambda e: e.activation(out=hss[:], in_=hss[:], func=AF.Ln), reads=['hss'], writes=['hss'])
            tk.op('act', lambda e: e.activation(out=hss[:], in_=hss[:], func=AF.Exp, scale=-0.5), reads=['hss'], writes=['hss'])
            tk.op('dve', lambda e: e.tensor_tensor(out=osum[:].rearrange("p t (h d) -> p (t h) d", h=4), in0=osum[:].rearrange("p t (h d) -> p (t h) d", h=4),
                                                   in1=hss[:].unsqueeze(2).broadcast_to([128, 32, 64]), op=ALU.mult),
                  reads=['osum%d' % t_ for t_ in range(NT)] + ['hss'], writes=['osum%d' % t_ for t_ in range(NT)])
            tk.op('dve', lambda e: e.tensor_tensor(out=osum[:], in0=osum[:], in1=ghgB[:].unsqueeze(1).broadcast_to([128, NT, 256]), op=ALU.mult),
                  reads=['osum%d' % t_ for t_ in range(NT)] + ['ghgB'], writes=['osum%d' % t_ for t_ in range(NT)])
            tk.op('dve', lambda e: e.tensor_tensor(out=mixed[:, :, 512:768], in0=osum[:], in1=sgr[:], op=ALU.mult),
                  reads=['osum%d' % t_ for t_ in range(NT)] + ['sgr'], writes=['mixed%d' % j for j in range(8)])

            if stop('R'):
                break
            tk.barrier()
            areset()
            uT = aget([128, 2, T], BF16)
            sgf = aget([128, NT, 256], BF16)
            ucs = aget([128, NT, 2, 256], BF16)
            yT = aget([128, 2, T], BF16)
            csnb = [aget([128, 2, 1024], BF16) for _ in range(4)]
            assert apos[0] + 2048 <= 11520
            wfs = aget([128, 2, 256])
            wfb = aget([128, 2, 256], BF16)
            junk = aget([128, 512], BF16)
            tmpf = aget([128, D])
            for kt_ in range(4):
                tk.dma('sp', csnb[kt_][:], csn_d.ap()[kt_], writes=['csnb%d' % kt_])
            assert True
            tk.dma('sp', wfs[:], wfn_d.ap()[l].rearrange("(c p) n -> p c n", p=128), writes=['wfs'])
            tk.op('pool', lambda e: e.tensor_copy(out=wfb[:], in_=wfs[:]), reads=['wfs'], writes=['wfb'])
            for t in range(NT):
                b = nb()
                for cs in range(2):
                    for ct in range(2):
                        tk.op('pe', lambda e, t=t, cs=cs, ct=ct: e.matmul(ps[b][:, cs * 256 + ct * 128:cs * 256 + ct * 128 + 128], lhsT=uT[:, ct, t * 128:(t + 1) * 128],
                                                                          rhs=C4S4[:, 3 + cs * 2 + ct, :], start=True, stop=True),
                              reads=['uT', 'cb16'], writes=[psk[b]])
                tk.op('act', lambda e, t=t: e.copy(out=ucs[:, t, :, :].rearrange("p a c -> p (a c)"), in_=ps[b][:, :]), reads=[psk[b]], writes=['ucs%d' % t])
            if stop('F0'):
                break
            yb = [nb() for _ in range(4)]
            for kt_ in range(8):
                ci = kt_ % 4
                if kt_ >= 4:
                    tk.dma('sp', csnb[ci][:], csn_d.ap()[kt_], writes=['csnb%d' % ci])
                for ct in range(2):
                    for g in range(2):
                        b = yb[ct * 2 + g]
                        for cs in range(2):
                            tk.op('pe', lambda e, kt_=kt_, ct=ct, g=g, cs=cs: e.matmul(ps[b][:, :], lhsT=ucs[:, kt_, cs, ct * 128:(ct + 1) * 128],
                                                                                       rhs=csnb[ci][:, cs, g * 512:(g + 1) * 512],
                                                                                       start=(kt_ == 0 and cs == 0), stop=(kt_ == 7 and cs == 1)),
                                  reads=['ucs%d' % kt_, 'csnb%d' % ci], writes=[psk[b]])
            for ct in range(2):
                for g in range(2):
                    b = yb[ct * 2 + g]
                    tk.op('act', lambda e, ct=ct, g=g, b=b: e.copy(out=yT[:, ct, g * 512:(g + 1) * 512], in_=ps[b][:, :]), reads=[psk[b]], writes=['yT%d_%d' % (ct, g)])
            for t in range(NT):
                b = nb()
                for ct in range(2):
                    tk.op('pe', lambda e, t=t, ct=ct: e.matmul(ps[b][:, 0:256], lhsT=yT[:, ct, t * 128:(t + 1) * 128], rhs=wfb[:, ct, :], start=(ct == 0), stop=(ct == 1)),
                          reads=['yT%d_%d' % (ct, t // 4), 'wfb'], writes=[psk[b]])
                tk.op('dve', lambda e, t=t: e.tensor_tensor(out=mixed[:, t, 768:1024], in0=ps[b][:, 0:256], in1=sgf[:, t, :], op=ALU.mult),
                      reads=[psk[b], 'sgf'], writes=['mixed%d' % t])

            if dbg and l == nl - 1:
                for t in range(NT):
                    tk.dma('sp', dbg_d.ap()[t * 128:(t + 1) * 128, :], mixed[:, t, :], reads=['mixed%d' % t])

            if stop('F1'):
                break
            w0 = next_w()
            w1 = next_w(prefetch=False)
            for t in range(NT):
                b = nb()
                for kc in range(8):
                    tk.op('pe', lambda e, t=t, kc=kc: e.transpose(out=psb16(b)[:, kc * 128:(kc + 1) * 128], in_=mixed[:, t, kc * 128:(kc + 1) * 128], identity=IDB),
                          reads=['mixed%d' % t, 'cb16'], writes=[psk[b]])
                tk.op('act', lambda e, t=t: e.copy(out=hT[:, :, t * 128:(t + 1) * 128], in_=psb16(b)[:, :].rearrange("p (k c) -> p k c", k=8)),
                      reads=[psk[b]], writes=['hT%d' % t])
            for t in range(NT):
                bb = [nb(), nb()]
                for hf, wi in enumerate((w0, w1)):
                    proj_tm(wi, 0, 512, t, bb[hf])
                    tk.op('act', lambda e, t=t, hf=hf: e.activation(out=junk[:], in_=ps[bb[hf]][:, :], func=AF.Square, accum_out=ssq[:, 8 + hf:9 + hf]),
                          reads=[psk[bb[hf]]], writes=['junk', 'ssq'])
                tk.op('dve', lambda e: e.tensor_tensor(out=rstd[:, 8:9], in0=ssq[:, 8:9], in1=ssq[:, 9:10], op=ALU.add), reads=['ssq'], writes=['rstd'])
                tk.op('dve', lambda e: e.tensor_scalar(out=rstd[:, 8:9], in0=rstd[:, 8:9], scalar1=1.0 / D, scalar2=EPS, op0=ALU.mult, op1=ALU.add),
                      reads=['rstd'], writes=['rstd'])
                tk.op('act', lambda e: e.activation(out=rstd[:, 8:9], in_=rstd[:, 8:9], func=AF.Ln), reads=['rstd'], writes=['rstd'])
                tk.op('act', lambda e: e.activation(out=rstd[:, 8:9], in_=rstd[:, 8:9], func=AF.Exp, scale=-0.5), reads=['rstd'], writes=['rstd'])
                for hf in range(2):
                    tk.op('dve', lambda e, hf=hf: e.scalar_tensor_tensor(out=tmpf[:, hf * 512:(hf + 1) * 512], in0=ps[bb[hf]][:, :], scalar=rstd[:, 8:9],
                                                                         in1=gg[:, hf * 512:(hf + 1) * 512], op0=ALU.mult, op1=ALU.mult),
                          reads=[psk[bb[hf]], 'rstd', 'gg'], writes=['tmpf'])
                tk.op('dve', lambda e, t=t: e.tensor_tensor(out=x_sb[:, t, :], in0=x_sb[:, t, :], in1=tmpf[:], op=ALU.add),
                      reads=['x%d' % t, 'tmpf'], writes=['x%d' % t])
            _issue(wstate['ptr'])

        for t in range(NT):
            tk.dma('sp', y_d.ap()[t * 128:(t + 1) * 128, :], x_sb[:, t, :], reads=['x%d' % t])
        tk.finish()
    return nc


def _consts(is_sample):
    cf32 = np.zeros((128, 7, 128), np.float32)
    p = np.arange(128)
    J2 = np.zeros((128, 128), np.float32)
    for a in range(2):
        for i in range(64):
            J2[a * 64 + i, a * 64 + 63 - i] = 1.0
    cf32[:, 0] = J2
    cf32[:, 1] = np.eye(128, dtype=np.float32)
    cm = np.zeros((128, 128), np.float32)
    if is_sample:
        qc = np.arange(64)
        c0 = np.clip(qc - 8, 0, 48)
        kc = np.arange(64)
        valid = (kc[:, None] >= c0[None, :]) & (kc[:, None] < c0[None, :] + 16)
        m = np.where(valid, 0.0, NEG).astype(np.float32)
        cm = np.tile(m, (2, 2))
    cf32[:, 2] = cm
    s = np.arange(32)[:, None]
    t = np.arange(32)[None, :]
    trf = (s <= t).astype(np.float32) - (s <= 15).astype(np.float32)
    trb = (s >= t).astype(np.float32) - (s >= 16).astype(np.float32)
    for a in range(4):
        cf32[a * 32:(a + 1) * 32, 3, a * 32:(a + 1) * 32] = trf
        cf32[a * 32:(a + 1) * 32, 4, a * 32:(a + 1) * 32] = trb
    sl = np.arange(128) % 32
    ch = np.arange(128) // 32
    selcols = np.zeros((128, 128), np.float32)
    for a in range(4):
        selcols[:, a * 2 + 0] = ((ch == a) & (sl <= 15))
        selcols[:, a * 2 + 1] = (ch == a)
        selcols[:, 8 + a * 2 + 0] = ((ch == 3 - a) & (sl >= 16))
        selcols[:, 8 + a * 2 + 1] = (ch == 3 - a)
        selcols[:, 18 + a] = (ch == a)
    selcols[:, 16] = (np.arange(128) < 64)
    selcols[:, 17] = (np.arange(128) >= 64)
    cf32[:, 5] = selcols
    cf32[0:64, 6, 0:64] = 1.0
    cf32[64:128, 6, 64:128] = 1.0
    rowb = np.zeros((128, 74), np.float32)
    for i, (j, kt) in enumerate(JK):
        for hf in range(2):
            for krl in range(2):
                if is_sample:
                    qr = 2 * j + hf
                    kr = 2 * kt + krl
                    r0 = int(np.clip(qr - 4, 0, 8))
                    ok = (r0 <= kr < r0 + 8)
                else:
                    ok = (kt // 2 == j // 2)
                rowb[krl * 64:(krl + 1) * 64, i * 2 + hf] = 0.0 if ok else NEG
    cb16 = np.zeros((128, 9, 128), np.float32)
    cb16[:, 7] = J2
    cb16[:, 8] = cm
    cb16[:, 0] = np.eye(128)
    mf = (s <= t).astype(np.float32)
    mb = (s >= t).astype(np.float32)
    z = np.zeros((64, 64), np.float32)
    for a in range(4):
        cb16[a * 32:(a + 1) * 32, 1, a * 32:(a + 1) * 32] = mf
        cb16[a * 32:(a + 1) * 32, 2, a * 32:(a + 1) * 32] = mb
    ang = 2 * np.pi * np.outer(np.arange(64), np.arange(64)) / 64
    c4 = np.cos(ang) / 8.0
    s4 = np.sin(ang) / 8.0
    for ct in range(2):
        cb16[:, 3 + ct] = np.block([[c4, z], [z, c4]])
        cb16[:, 5 + ct] = np.block([[s4, z], [z, s4]])
    n = 1024 if is_sample else 256
    idx = np.arange(n)
    a2 = 2 * np.pi * ((np.outer(idx, idx)) % n) / n
    cn = np.cos(a2) / np.sqrt(n)
    sn = -np.sin(a2) / np.sqrt(n)
    CN = np.zeros((1024, 1024), np.float64)
    SN = np.zeros((1024, 1024), np.float64)
    for i in range(1024 // n):
        CN[i * n:(i + 1) * n, i * n:(i + 1) * n] = cn
        SN[i * n:(i + 1) * n, i * n:(i + 1) * n] = sn
    csn = np.stack([CN.reshape(8, 128, 1024), SN.reshape(8, 128, 1024)], axis=2)
    return dict(cf32=cf32, rowbias=rowb, cb16=cb16.astype(ml_dtypes.bfloat16), csn=csn.astype(ml_dtypes.bfloat16))


def _in_maps(x_prompt, x_sample, cache_attn_k, cache_attn_v, state_hgrn, c, c_ctx,
             w_ada, b_ada, g_pre, w_in, rpb, lb_logits, g_hgrn, w_fnet, w_out, g_post):
    f = lambda a: np.ascontiguousarray(np.asarray(a, dtype=np.float32))
    shared = dict(w_ada=f(w_ada), b_ada=f(b_ada), g_pre=f(g_pre), w_in=f(w_in), lb_logits=f(lb_logits),
                  g_hgrn=f(g_hgrn), w_fnet=f(w_fnet), w_out=f(w_out), g_post=f(g_post))
    tp = np.zeros((NL, 8, 23, 127), np.float32)
    tp[:, :, 4:19, 48:79] = f(rpb)
    cs = _consts(True)
    cp = _consts(False)
    maps = []
    for i in range(8):
        m = dict(shared)
        if i < 4:
            m["x"] = f(x_sample[i])
            m["cvec"] = f(np.asarray(c[i]).reshape(8, 128).T)
            m["ctxk"] = f(np.asarray(cache_attn_k[i]).reshape(NL, 512, 512))
            m["ctxv"] = f(np.asarray(cache_attn_v[i]).reshape(NL, 512, 512))
            s = np.asarray(state_hgrn[i]).reshape(NL, 2, 2, 2, 64, 64)
            m["s0"] = f(s.transpose(0, 1, 3, 4, 2, 5).reshape(NL, 2, 128, 2, 64))
            m["flags"] = np.ones((128, 2), np.float32)
            m["tpad"] = tp
            m.update(cs)
        else:
            m["x"] = f(np.asarray(x_prompt[4 * (i - 4):4 * (i - 3)]).reshape(T, D))
            m["cvec"] = f(np.asarray(c_ctx).reshape(8, 128).T)
            m["ctxk"] = np.zeros((NL, 512, 512), np.float32)
            m["ctxv"] = np.zeros((NL, 512, 512), np.float32)
            m["s0"] = np.zeros((NL, 2, 128, 2, 64), np.float32)
            m["flags"] = np.zeros((128, 2), np.float32)
            m["tpad"] = np.zeros_like(tp)
            m.update(cp)
        maps.append(m)
    return maps


_NC_CACHE = {}


def kernel(**inputs):
    if 'nc' not in _NC_CACHE:
        _NC_CACHE['nc'] = build_nc()
    nc = _NC_CACHE['nc']
    maps = _in_maps(**inputs)
    res = run_bass_kernel_spmd(nc, maps, core_ids=list(range(8)))
    r = res.results
    y_sample = np.stack([r[i]["y"] for i in range(4)], axis=0).astype(np.float32)
    y_prompt = np.concatenate([r[i]["y"].reshape(4, 256, D) for i in range(4, 8)], axis=0).astype(np.float32)
    nk = np.concatenate([r[i]["newk"].reshape(NL, 4, 256, 8, 64).transpose(1, 0, 2, 3, 4) for i in range(4, 8)], axis=0)
    nv = np.concatenate([r[i]["newv"].reshape(NL, 4, 256, 8, 64).transpose(1, 0, 2, 3, 4) for i in range(4, 8)], axis=0)
    ns = np.concatenate([r[i]["news"].reshape(NL, 2, 4, 2, 64, 2, 64).transpose(2, 0, 1, 5, 3, 4, 6).reshape(4, NL, 2, 4, 64, 64)
                         for i in range(4, 8)], axis=0)
    return (y_prompt, y_sample, np.ascontiguousarray(nk, dtype=np.float32), np.ascontiguousarray(nv, dtype=np.float32),
            np.ascontiguousarray(ns, dtype=np.float32))
```

```python
import numpy as np
import ml_dtypes
from contextlib import ExitStack
import concourse.bass as bass
import concourse.mybir as mybir
from concourse.bass_utils import run_bass_kernel_spmd

F32 = mybir.dt.float32
BF16 = mybir.dt.bfloat16
AF = mybir.ActivationFunctionType
ALU = mybir.AluOpType
AX = mybir.AxisListType

NL = 4
D = 1024
T = 1024
NT = 8
EPS = 1e-6
NEG = -30000.0
KT = {0: [0, 1, 2, 3], 1: [0, 1, 2, 3], 2: [0, 1, 2, 3, 4], 3: [1, 2, 3, 4, 5],
      4: [2, 3, 4, 5, 6], 5: [3, 4, 5, 6, 7], 6: [4, 5, 6, 7], 7: [4, 5, 6, 7]}
JK = [(j, kt) for j in range(8) for kt in KT[j]]
JKI = {p: i for i, p in enumerate(JK)}
NDS = 24
NSW = 72


class TK:
    def __init__(s, nc, st):
        s.nc = nc
        s.E = {'pe': nc.tensor, 'act': nc.scalar, 'dve': nc.vector, 'pool': nc.gpsimd, 'sp': nc.sync}
        s.sem = {k: st.enter_context(nc.semaphore('s_' + k)) for k in ('pe', 'act', 'dve', 'pool')}
        s.cnt = {k: 0 for k in s.E}
        s.seen = {k: {} for k in s.E}
        s.lw = {}
        s.rd = {}
        s.dsems = [st.enter_context(nc.semaphore('d%d' % i)) for i in range(NDS)]
        s.dcnt = [0] * NDS
        s.dnext = 0
        s.swsems = [st.enter_context(nc.semaphore('w%d' % i)) for i in range(NSW)]
        s.swnext = 0
        s.swlow = 0

    def _wait(s, eng, key, val):
        if eng == 'pe' and key == 'pe':
            return
        if s.seen[eng].get(key, 0) >= val:
            return
        if isinstance(key, str):
            semobj = s.sem[key]
        elif key >= 1000:
            semobj = s.swsems[key - 1000]
        else:
            semobj = s.dsems[key]
        s.E[eng].wait_ge(semobj, val)
        s.seen[eng][key] = val

    def _deps(s, eng, reads, writes):
        for k in reads:
            w = s.lw.get(k)
            if w:
                s._wait(eng, *w)
            if k.startswith('ps'):
                for rk, rv in s.rd.get(k, {}).items():
                    if rk != eng:
                        s._wait(eng, rk, rv)
        for k in writes:
            w = s.lw.get(k)
            if w:
                s._wait(eng, *w)
            for rk, rv in s.rd.get(k, {}).items():
                s._wait(eng, rk, rv)

    def _book(s, tag, reads, writes):
        for k in reads:
            d = s.rd.setdefault(k, {})
            d[tag[0]] = max(d.get(tag[0], 0), tag[1])
        for k in writes:
            s.lw[k] = tag
            s.rd[k] = {}

    def op(s, eng, fn, reads=(), writes=(), war_only=()):
        s._deps(eng, reads, writes)
        inst = fn(s.E[eng])
        s.cnt[eng] += 1
        inst.then_inc(s.sem[eng], 1)
        s._book((eng, s.cnt[eng]), tuple(reads) + tuple(war_only), writes)

    def dma(s, q, out, in_, reads=(), writes=()):
        if q == 'pool':
            assert s.swnext < NSW, "out of one-shot semaphores"
            i = s.swnext
            s.swnext += 1
            s._deps(q, reads, writes)
            s.E[q].dma_start(out=out, in_=in_).then_inc(s.swsems[i], 16)
            s._book((1000 + i, 16), reads, writes)
            return
        i = s.dnext
        s.dnext = (s.dnext + 1) % NDS
        if s.dcnt[i] > 0:
            s._wait(q, i, s.dcnt[i])
        s._deps(q, reads, writes)
        s.dcnt[i] += 16
        s.E[q].dma_start(out=out, in_=in_).then_inc(s.dsems[i], 16)
        s._book((i, s.dcnt[i]), reads, writes)

    def barrier(s):
        engs = ('pe', 'act', 'dve', 'pool', 'sp')
        snap = dict(s.cnt)
        dsnap = list(s.dcnt)
        for e in engs:
            for o in ('pe', 'act', 'dve', 'pool'):
                if o != e and snap[o] > 0:
                    s._wait(e, o, snap[o])
            for i in range(NDS):
                if dsnap[i] > 0:
                    s._wait(e, i, dsnap[i])
            for i in range(s.swlow, s.swnext):
                s._wait(e, 1000 + i, 16)
        s.swlow = s.swnext

    def finish(s):
        for i in range(NDS):
            if s.dcnt[i] > 0:
                s._wait('sp', i, s.dcnt[i])
        for i in range(s.swnext):
            s._wait('sp', 1000 + i, 16)
        for k in ('pe', 'act', 'dve', 'pool'):
            if s.cnt[k] > 0:
                s._wait('sp', k, s.cnt[k])


def build_nc(nl=NL, dbg=False, upto=None):
    nc = bass.Bass("TRN2", target_bir_lowering=False)
    _order = ['M0', 'M1', 'M2', 'M3', 'M', 'A0', 'A1', 'A1a', 'A1b', 'A1c', 'A2', 'A', 'R0', 'R1', 'R2', 'R3', 'R', 'F0', 'F1', 'F']

    def stop(p):
        return upto is not None and _order.index(upto) <= _order.index(p)

    def din(name, shape, dt=F32):
        return nc.dram_tensor(name, list(shape), dt, kind="ExternalInput")

    def dout(name, shape, dt=F32):
        return nc.dram_tensor(name, list(shape), dt, kind="ExternalOutput")

    x_d = din("x", [T, D])
    cvec_d = din("cvec", [128, 8])
    ctxk_d = din("ctxk", [NL, 512, 512])
    ctxv_d = din("ctxv", [NL, 512, 512])
    s0_d = din("s0", [NL, 2, 128, 2, 64])
    flags_d = din("flags", [128, 2])
    wada_d = din("w_ada", [NL, D, 3 * D])
    bada_d = din("b_ada", [NL, 3 * D])
    gpre_d = din("g_pre", [NL, D])
    win_d = din("w_in", [NL, D, 3840])
    tpad_d = din("tpad", [NL, 8, 23, 127])
    lbl_d = din("lb_logits", [2, NL, 256])
    ghg_d = din("g_hgrn", [NL, 256])
    wfn_d = din("w_fnet", [NL, 256, 256])
    wout_d = din("w_out", [NL, D, D])
    gpost_d = din("g_post", [NL, D])
    cf32_d = din("cf32", [128, 7, 128])
    rowb_d = din("rowbias", [128, 74])
    cb16_d = din("cb16", [128, 9, 128], BF16)
    csn_d = din("csn", [8, 128, 2, 1024], BF16)

    y_d = dout("y", [T, D])
    nk_d = dout("newk", [NL, T, 512])
    nv_d = dout("newv", [NL, T, 512])
    ns_d = dout("news", [NL, 2, 4, 128, 2, 64])
    dbg_d = dout("dbgmixed", [T, D], BF16) if dbg else None

    with ExitStack() as st:
        def sb(name, shape, dt=F32):
            return st.enter_context(nc.sbuf_tensor(name, list(shape), dt))

        tk = TK(nc, st)
        x_sb = sb("x_sb", [128, NT, D])
        hT = sb("hT", [128, 8, T], BF16)
        mixed = sb("mixed", [128, NT, D], BF16)
        wst = [sb("wst0", [128, 8, 512], BF16)]
        wbf = [sb("wbf%d" % i, [128, 8, 512], BF16) for i in range(2)]
        gg = sb("gg", [128, D])
        modN = sb("modN", [128, 3 * D], BF16)
        screp = sb("screp", [128, 8, 128], BF16)
        brow = sb("brow", [1, 512])
        ones_row = sb("ones_row", [1, 128])
        csil = sb("csil", [128, 8])
        cf32 = sb("cf32s", [128, 7, 128])
        rowb = sb("rowbs", [128, 74])
        cb16 = sb("cb16s", [128, 9, 128], BF16)
        flags = sb("flagss", [128, 2])
        lbl = sb("lbl", [128, 2, 256])
        oml = sb("oml", [128, 2, 256])
        ghgB = sb("ghgB", [128, 256])
        ssq = sb("ssq", [128, 16])
        rstd = sb("rstd", [128, 16])
        ARW = 21120
        arena = sb("arena", [128, ARW])
        apos = [0]

        def areset():
            apos[0] = 0

        def aget(shape, dt=F32):
            n = 1
            for d_ in shape[1:]:
                n *= d_
            words = n if dt == F32 else (n + 1) // 2
            a0 = apos[0]
            apos[0] += words
            assert apos[0] <= ARW, ("arena overflow", apos[0])
            v = arena[:, a0:a0 + words]
            if dt != F32:
                v = v.bitcast(dt)
            if len(shape) == 3:
                v = v.rearrange("p (a b) -> p a b", a=shape[1])
            elif len(shape) == 4:
                v = v.rearrange("p (a b c) -> p a b c", a=shape[1], b=shape[2])
            return v

        psbig = [st.enter_context(nc.psum_tensor("psb%d" % i, [128, 1024], F32)) for i in range(4)]
        ps = [psbig[i // 2][:, (i % 2) * 512:(i % 2 + 1) * 512] for i in range(8)]
        psk = ["ps%d" % i for i in range(8)]
        bank_rr = [0]

        def nb(avoid=()):
            while True:
                b = bank_rr[0]
                bank_rr[0] = (b + 1) % 8
                if b not in avoid:
                    return b

        J2 = cf32[:, 0, :]
        IDF = cf32[:, 1, :]
        CMT = cf32[:, 2, :]
        TR = [cf32[:, 3, :], cf32[:, 4, :]]
        SEL = [cf32[:, 5, 0:8], cf32[:, 5, 8:16]]
        HM = cf32[:, 5, 16:18]
        QM = cf32[:, 5, 18:22]
        HME = cf32[:, 6, :].rearrange("p (a c) -> p a c", a=2)
        IDB = cb16[:, 0, :]
        TRI = [cb16[:, 1, :], cb16[:, 2, :]]
        C4S4 = cb16
        J2B = cb16[:, 7, :]
        CMTB = cb16[:, 8, :]

        tk.dma('sp', cf32[:], cf32_d.ap(), writes=['cf32'])
        tk.dma('sp', rowb[:], rowb_d.ap(), writes=['rowb'])
        tk.dma('sp', cb16[:], cb16_d.ap(), writes=['cb16'])
        tk.dma('sp', flags[:], flags_d.ap(), writes=['flags'])
        tk.dma('sp', csil[:], cvec_d.ap(), writes=['csil'])
        for t in range(NT):
            tk.dma('sp', x_sb[:, t, :], x_d.ap()[t * 128:(t + 1) * 128, :], writes=['x%d' % t])
        tk.op('pool', lambda e: e.memset(ones_row[:], 1.0), writes=['ones_row'])
        tk.op('act', lambda e: e.activation(out=csil[:], in_=csil[:], func=AF.Silu), reads=['csil'], writes=['csil'])

        wring = [0]

        wbring = [0]

        def load_w(src_ap, ncols):
            wi = wbring[0]
            wbring[0] ^= 1
            tk.dma('pool', wbf[wi][:, :, 0:ncols], src_ap.rearrange("(kc p) n -> p kc n", p=128), writes=['wbf%d' % wi])
            return wi

        wseq = []
        for l_ in range(nl):
            for (c0_, n_) in ((0, 512), (512, 512), (1024, 512), (1536, 512), (2048, 512), (2816, 512), (2560, 256), (3328, 512)):
                wseq.append((win_d, l_, c0_, n_))
            wseq.append((wout_d, l_, 0, 512))
            wseq.append((wout_d, l_, 512, 512))
        wstate = {'ptr': 0, 'loaded': {}}

        def _issue(i):
            if i < len(wseq) and i not in wstate['loaded']:
                d_, l_, c0_, n_ = wseq[i]
                wstate['loaded'][i] = load_w(d_.ap()[l_, :, c0_:c0_ + n_], n_)

        def next_w(prefetch=True):
            i = wstate['ptr']
            wstate['ptr'] += 1
            _issue(i)
            if prefetch:
                _issue(i + 1)
            return wstate['loaded'][i]

        def proj_tm(wi, c0, ncols, t, b):
            for kc in range(8):
                tk.op('pe', lambda e, kc=kc: e.matmul(ps[b][:, 0:ncols], lhsT=hT[:, kc, t * 128:(t + 1) * 128],
                                                     rhs=wbf[wi][:, kc, c0:c0 + ncols], start=(kc == 0), stop=(kc == 7)),
                      reads=['hT%d' % t, 'wbf%d' % wi], writes=[psk[b]])

        def proj_fm(wi, ci, g, b):
            for kc in range(8):
                tk.op('pe', lambda e, kc=kc: e.matmul(ps[b][:, 0:512], lhsT=wbf[wi][:, kc, ci * 128:(ci + 1) * 128],
                                                     rhs=hT[:, kc, g * 512:(g + 1) * 512], start=(kc == 0), stop=(kc == 7)),
                      reads=['hT%d' % x_ for x_ in range(4 * g, 4 * g + 4)] + ['wbf%d' % wi], writes=[psk[b]])

        def psb16(b):
            return ps[b].bitcast(BF16)

        def emit_mod_chunk(lm, ch, banks=None):
            mod_dma(lm, ch)
            mod_mm(lm, ch, banks)

        def mod_dma(lm, ch):
            tk.dma('pool', wst[0][:], wada_d.ap()[lm, :, ch * 512:(ch + 1) * 512].rearrange("(kc p) n -> p kc n", p=128), writes=['wst0'])
            tk.dma('sp', brow[:], bada_d.ap()[lm:lm + 1, ch * 512:(ch + 1) * 512], writes=['brow'])

        def mod_mm(lm, ch, banks=None):
            b = nb() if banks is None else banks[ch % len(banks)]
            for kc in range(8):
                tk.op('pe', lambda e, kc=kc: e.matmul(ps[b][:, :], lhsT=screp[:, kc, :], rhs=wst[0][:, kc, :], start=(kc == 0), stop=False),
                      reads=['screp', 'wst0'], writes=[psk[b]])
            tk.op('pe', lambda e: e.matmul(ps[b][:, :], lhsT=ones_row[0:1, :], rhs=brow[0:1, :], start=False, stop=True),
                  reads=['ones_row', 'brow'], writes=[psk[b]])
            tk.op('act', lambda e: e.copy(out=modN[:, ch * 512:(ch + 1) * 512], in_=ps[b][:, :]), reads=[psk[b]], writes=['modN'])

        tk.op('dve', lambda e: e.tensor_copy(out=screp[:], in_=csil[:].unsqueeze(2).broadcast_to([128, 8, 128])), reads=['csil'], writes=['screp'])
        for ch in range(6):
            emit_mod_chunk(0, ch)

        for l in range(nl):
            if l == 0:
                tk.barrier()
            areset()
            apos[0] = 11520
            gbc = aget([128, 2, D])
            lbt = aget([128, 2, NL, 256])
            junk = aget([128, D], BF16)
            tmpf = aget([128, D])
            hb = [aget([128, D], BF16) for _ in range(2)]
            modA = aget([128, D])
            tk.dma('sp', gbc[:, 0, :], bass.AP(gpre_d, l * D, [[0, 128], [1, D]]), writes=['gbc'])
            tk.dma('sp', gbc[:, 1, :], bass.AP(gpost_d, l * D, [[0, 128], [1, D]]), writes=['gbc'])
            tk.dma('sp', lbt[:].rearrange("p a l c -> p (a l c)"), bass.AP(lbl_d, 0, [[0, 128], [1, 2 * NL * 256]]), writes=['lbt'])
            if l == 0:
                tk.op('dve', lambda e: e.memset(lbl[:], 0.0), writes=['lbl'])
            else:
                mx = tmpf[:, 0:512].rearrange("p (a c) -> p a c", a=2)
                sm = tmpf[:, 512:1024].rearrange("p (a c) -> p a c", a=2)
                tk.op('dve', lambda e: e.tensor_tensor(out=mx, in0=lbt[:, :, 0, :], in1=lbt[:, :, 1, :], op=ALU.max),
                      reads=['lbt'], writes=['tmpf'])
                for l2 in range(2, NL):
                    tk.op('dve', lambda e, l2=l2: e.tensor_tensor(out=mx, in0=mx, in1=lbt[:, :, l2, :], op=ALU.max),
                          reads=['lbt', 'tmpf'], writes=['tmpf'])
                for l2 in range(NL):
                    tk.op('dve', lambda e, l2=l2: e.tensor_tensor(out=lbt[:, :, l2, :], in0=lbt[:, :, l2, :], in1=mx, op=ALU.subtract),
                          reads=['lbt', 'tmpf'], writes=['lbt'])
                tk.op('act', lambda e: e.activation(out=lbt[:].rearrange("p a l c -> p (a l c)"), in_=lbt[:].rearrange("p a l c -> p (a l c)"), func=AF.Exp),
                      reads=['lbt'], writes=['lbt'])
                tk.op('dve', lambda e: e.tensor_tensor(out=sm, in0=lbt[:, :, 0, :], in1=lbt[:, :, 1, :], op=ALU.add),
                      reads=['lbt'], writes=['tmpf'])
                for l2 in range(2, NL):
                    tk.op('dve', lambda e, l2=l2: e.tensor_tensor(out=sm, in0=sm, in1=lbt[:, :, l2, :], op=ALU.add),
                          reads=['lbt', 'tmpf'], writes=['tmpf'])
                tk.op('dve', lambda e: e.reciprocal(out=sm, in_=sm), reads=['tmpf'], writes=['tmpf'])
                tk.op('dve', lambda e: e.tensor_copy(out=lbl[:], in_=lbt[:, :, 1, :]), reads=['lbt'], writes=['lbl'])
                for l2 in range(2, l + 1):
                    tk.op('dve', lambda e, l2=l2: e.tensor_tensor(out=lbl[:], in0=lbl[:], in1=lbt[:, :, l2, :], op=ALU.add),
                          reads=['lbt', 'lbl'], writes=['lbl'])
                tk.op('dve', lambda e: e.tensor_tensor(out=lbl[:], in0=lbl[:], in1=sm, op=ALU.mult), reads=['lbl', 'tmpf'], writes=['lbl'])
            tk.op('dve', lambda e: e.tensor_scalar(out=oml[:], in0=lbl[:], scalar1=-0.5, scalar2=0.5, op0=ALU.mult, op1=ALU.add),
                  reads=['lbl'], writes=['oml'])
            tk.op('dve', lambda e: e.tensor_scalar(out=lbl[:], in0=lbl[:], scalar1=0.5, scalar2=0.5, op0=ALU.mult, op1=ALU.add),
                  reads=['lbl'], writes=['lbl'])
            if stop('M0'):
                break
            tk.op('dve', lambda e: e.scalar_tensor_tensor(out=modA[:], in0=modN[:, D:2 * D], scalar=1.0, in1=gbc[:, 0, :], op0=ALU.add, op1=ALU.mult),
                  reads=['modN', 'gbc'], writes=['modA'])
            tk.op('dve', lambda e: e.tensor_tensor(out=gg[:], in0=modN[:, 2 * D:3 * D], in1=gbc[:, 1, :], op=ALU.mult), reads=['modN', 'gbc'], writes=['gg'])
            if stop('M1'):
                break
            for t in range(NT):
                tk.op('act', lambda e, t=t: e.activation(out=junk[:], in_=x_sb[:, t, :], func=AF.Square, accum_out=ssq[:, t:t + 1]),
                      reads=['x%d' % t], writes=['junk', 'ssq'])
            tk.op('dve', lambda e: e.tensor_scalar(out=rstd[:, 0:8], in0=ssq[:, 0:8], scalar1=1.0 / D, scalar2=EPS, op0=ALU.mult, op1=ALU.add),
                  reads=['ssq'], writes=['rstd'])
            tk.op('act', lambda e: e.activation(out=rstd[:, 0:8], in_=rstd[:, 0:8], func=AF.Ln), reads=['rstd'], writes=['rstd'])
            tk.op('act', lambda e: e.activation(out=rstd[:, 0:8], in_=rstd[:, 0:8], func=AF.Exp, scale=-0.5), reads=['rstd'], writes=['rstd'])
            if stop('M2'):
                break
            for t in range(NT):
                hbi = t % 2
                tk.op('dve', lambda e, t=t: e.scalar_tensor_tensor(out=tmpf[:], in0=x_sb[:, t, :], scalar=rstd[:, t:t + 1], in1=modA[:],
                                                                   op0=ALU.mult, op1=ALU.mult),
                      reads=['x%d' % t, 'rstd', 'modA'], writes=['tmpf'])
                tk.op('dve', lambda e: e.tensor_tensor(out=hb[hbi][:], in0=tmpf[:], in1=modN[:, 0:D], op=ALU.add),
                      reads=['tmpf', 'modN'], writes=['hb%d' % hbi])
                if stop('M3'):
                    continue
                b = nb()
                for kc in range(8):
                    tk.op('pe', lambda e, kc=kc: e.transpose(out=psb16(b)[:, kc * 128:(kc + 1) * 128], in_=hb[hbi][:, kc * 128:(kc + 1) * 128], identity=IDB),
                          reads=['hb%d' % hbi, 'cb16'], writes=[psk[b]])
                tk.op('act', lambda e, t=t: e.copy(out=hT[:, :, t * 128:(t + 1) * 128], in_=psb16(b)[:, :].rearrange("p (k c) -> p k c", k=8)),
                      reads=[psk[b]], writes=['hT%d' % t])

            if stop('M'):
                break
            tk.barrier()
            areset()
            qT = aget([128, 4, T], BF16)
            kT = aget([128, 4, T], BF16)
            ckT = aget([128, 4, 512], BF16)
            vaug = aget([128, NT, 8, 66], BF16)
            cvaug = aget([128, 4, 8, 66], BF16)
            sga = aget([128, NT, 512], BF16)
            expT = aget([128, 7, 8, 128], BF16)
            Eb = [aget([128, 8, 128], BF16) for _ in range(3)]
            Pb = [aget([128, 8, 128], BF16) for _ in range(2)]
            hk = [Eb[0].bitcast(F32) if False else None, None]
            ost = [aget([128, 512]) for _ in range(2)]
            rden = aget([128, 8])
            otmp = aget([128, 8, 64])
            ckb = aget([128, 4, 512], BF16)
            hkA = aget([128, 8, 128])
            hkB = aget([128, 8, 128])
            hk = [hkA, hkB]
            tk.op('pool', lambda e: e.memset(vaug[:, :, :, 64:66], 1.0), writes=['vaug'])
            tk.op('dve', lambda e: e.tensor_copy(out=cvaug[:, :, :, 64:66].rearrange("p a b c -> p (a b) c"),
                                                 in_=flags[:, 0:1].unsqueeze(2).broadcast_to([128, 32, 2])),
                  reads=['flags'], writes=['cvaug'])
            def toep_dma(di):
                dl = di - 3
                hi = di % 2
                for qr in range(2):
                    for krl in range(2):
                        off = ((l * 8) * 23 + (2 * dl + krl - qr + 11)) * 127
                        src = bass.AP(tpad_d, off, [[1, 64], [23 * 127, 8], [1, 64]])
                        tk.dma('sp', hk[hi][qr * 64:(qr + 1) * 64, :, krl * 64:(krl + 1) * 64], src, writes=['hk%d_%d' % (hi, qr * 2 + krl)])

            def toep_mm(di):
                hi = di % 2
                bA = nb()
                bB = nb()
                for h in range(8):
                    b = bA if h % 2 == 0 else bB
                    o = ps[b][:, (h // 2) * 128:(h // 2 + 1) * 128]
                    tk.op('pe', lambda e, h=h, o=o: e.matmul(o, lhsT=hk[hi][:, h, :], rhs=J2, start=True, stop=False),
                          reads=['hk%d_%d' % (hi, x) for x in range(4)] + ['cf32'], writes=[psk[b]])
                    tk.op('pe', lambda e, o=o: e.matmul(o, lhsT=IDF, rhs=CMT, start=False, stop=True),
                          reads=['cf32'], writes=[psk[b]])
                for bi, b in enumerate((bA, bB)):
                    tk.op('act', lambda e, bi=bi, b=b: e.activation(out=expT[:, di, bi * 4:(bi + 1) * 4, :].rearrange("p a c -> p (a c)"),
                                                                    in_=ps[b][:, :], func=AF.Exp),
                          reads=[psk[b]], writes=['expT%d' % di])

            toep_dma(0)
            toep_dma(1)
            if stop('A0'):
                break
            ckk = 'ckb'
            tk.dma('pool', ckb[:], ctxk_d.ap()[l].rearrange("(c p) n -> p c n", p=128), writes=[ckk])
            for c in range(4):
                b = nb()
                for pr in range(4):
                    tk.op('pe', lambda e, pr=pr: e.transpose(out=psb16(b)[:, pr * 128:(pr + 1) * 128], in_=ckb[:, c, pr * 128:(pr + 1) * 128], identity=IDB),
                          reads=[ckk, 'cb16'], writes=[psk[b]])
                tk.op('act', lambda e, c=c: e.copy(out=ckT[:, :, c * 128:(c + 1) * 128], in_=psb16(b)[:, 0:512].rearrange("p (k c) -> p k c", k=4)),
                      reads=[psk[b]], writes=['ckT'])
            tk.dma('pool', ckb[:], ctxv_d.ap()[l].rearrange("(c p) n -> p c n", p=128), reads=[], writes=[ckk])
            tk.op('pool', lambda e: e.tensor_copy(out=cvaug[:, :, :, 0:64], in_=ckb[:].rearrange("p c (h d) -> p c h d", h=8)), reads=[ckk], writes=['cvaug'])
            if stop('A1'):
                break
            toep_mm(0)
            toep_dma(2)
            wi = next_w()
            for pr in range(4):
                for g in range(2):
                    b = nb()
                    proj_fm(wi, pr, g, b)
                    tk.op('act', lambda e, pr=pr, g=g: e.copy(out=qT[:, pr, g * 512:(g + 1) * 512], in_=ps[b][:, :]), reads=[psk[b]], writes=['qT%d_%d' % (pr, g)])
            if stop('A1a'):
                break
            toep_mm(1)
            toep_dma(3)
            wi = next_w()
            for pr in range(4):
                for g in range(2):
                    b = nb()
                    proj_fm(wi, pr, g, b)
                    tk.op('act', lambda e, pr=pr, g=g: e.copy(out=kT[:, pr, g * 512:(g + 1) * 512], in_=ps[b][:, :]), reads=[psk[b]], writes=['kTf%d_%d' % (pr, g)])
            toep_mm(2)
            toep_dma(4)
            for t in range(NT):
                b = nb()
                proj_tm(wi, 0, 512, t, b)
                oi = t % 2
                tk.op('dve', lambda e: e.tensor_copy(out=ost[oi][:], in_=ps[b][:, :]), reads=[psk[b]], writes=['ost%d' % oi])
                tk.dma('sp', nk_d.ap()[l, t * 128:(t + 1) * 128, :], ost[oi][:], reads=['ost%d' % oi])
            toep_mm(3)
            toep_dma(5)
            if stop('A1b'):
                break
            wi = next_w()
            for t in range(NT):
                b = nb()
                proj_tm(wi, 0, 512, t, b)
                oi = t % 2
                tk.op('dve', lambda e: e.tensor_copy(out=ost[oi][:], in_=ps[b][:, :]), reads=[psk[b]], writes=['ost%d' % oi])
                tk.op('act', lambda e, t=t: e.copy(out=vaug[:, t, :, 0:64], in_=ost[oi][:].rearrange("p (h d) -> p h d", h=8)),
                      reads=['ost%d' % oi], writes=['vaug%d' % t])
                tk.dma('sp', nv_d.ap()[l, t * 128:(t + 1) * 128, :], ost[oi][:], reads=['ost%d' % oi])
            if stop('A1c'):
                break
            toep_mm(4)
            toep_dma(6)
            wi = next_w()
            for t in range(NT):
                b = nb()
                proj_tm(wi, 0, 512, t, b)
                tk.op('act', lambda e, t=t: e.activation(out=sga[:, t, :], in_=ps[b][:, :], func=AF.Silu), reads=[psk[b]], writes=['sga%d' % t])
            toep_mm(5)
            toep_mm(6)
            if stop('A2'):
                break
            OA, OB = 6, 7
            spairs = [(0, 1), (2, 3), (4, 5)]
            allsteps = []
            for j in range(8):
                st_ = [('l', kt) for kt in KT[j]] + [('c', c) for c in range(4)]
                for si_, (kind, idx) in enumerate(st_):
                    allsteps.append((j, si_, len(st_), kind, idx))

            def emit_S(k):
                j, si_, ns_, kind, idx = allsteps[k]
                sA, sB = spairs[k % 3]
                for h in range(8):
                    b = sA if h % 2 == 0 else sB
                    r0 = (h % 2) * 64
                    ksrc = kT[r0:r0 + 64, h // 2, idx * 128:(idx + 1) * 128] if kind == 'l' else ckT[r0:r0 + 64, h // 2, idx * 128:(idx + 1) * 128]
                    tk.op('pe', lambda e, h=h, b=b, ksrc=ksrc, r0=r0: e.matmul(ps[b][:, (h // 2) * 128:(h // 2 + 1) * 128], lhsT=ksrc,
                                                                              rhs=qT[r0:r0 + 64, h // 2, j * 128:(j + 1) * 128], start=True, stop=True),
                          reads=['qT%d_%d' % (h // 2, j // 4), ('kTf%d_%d' % (h // 2, idx // 4)) if kind == 'l' else 'ckT'], writes=[psk[b]])

            def emit_rest(k):
                j, si_, ns_, kind, idx = allsteps[k]
                sA, sB = spairs[k % 3]
                sl = k % 3
                big = psbig[sA // 2]
                if kind == 'l':
                    jk = JKI[(j, idx)]
                    for hf in range(2):
                        tk.op('act', lambda e, hf=hf: e.activation(
                            out=Eb[sl][:, :, hf * 64:(hf + 1) * 64],
                            in_=big[:, :].rearrange("p (a c) -> p a c", a=8)[:, :, hf * 64:(hf + 1) * 64],
                            func=AF.Exp, scale=0.125, bias=rowb[:, jk * 2 + hf:jk * 2 + hf + 1]),
                            reads=[psk[sA], psk[sB], 'rowb'], writes=['Eb%d' % sl])
                    di = idx - j + 3
                    pl = k % 2
                    tk.op('dve', lambda e, di=di: e.tensor_tensor(out=Pb[pl][:], in0=Eb[sl][:], in1=expT[:, di, :, :], op=ALU.mult),
                          reads=['Eb%d' % sl, 'expT%d' % di], writes=['Pb%d' % pl])
                    lhs, lk = Pb[pl], 'Pb%d' % pl
                    vsrc, vk = vaug, 'vaug%d' % idx
                else:
                    tk.op('act', lambda e: e.activation(out=Eb[sl][:].rearrange("p a c -> p (a c)"), in_=big[:, :], func=AF.Exp, scale=0.125),
                          reads=[psk[sA], psk[sB]], writes=['Eb%d' % sl])
                    lhs, lk = Eb[sl], 'Eb%d' % sl
                    vsrc, vk = cvaug, 'cvaug'
                for e_ in range(8):
                    h = 2 * (e_ % 4) + e_ // 4
                    ob = OA if e_ < 4 else OB
                    tk.op('pe', lambda e, e_=e_, h=h, ob=ob, lhs=lhs, vsrc=vsrc: e.matmul(
                        ps[ob][:, (e_ % 4) * 66:(e_ % 4) * 66 + 66], lhsT=lhs[:, e_, :], rhs=vsrc[:, idx, h, :],
                        start=(si_ == 0 and e_ % 4 == 0), stop=(si_ == ns_ - 1), skip_group_check=True),
                        reads=[lk, vk] + (['vaug'] if kind == 'l' else []), writes=[psk[ob]])
                if si_ == ns_ - 1:
                    for bi, ob in enumerate((OA, OB)):
                        tk.op('dve', lambda e, bi=bi, ob=ob: e.reciprocal(out=rden[:, bi * 4:(bi + 1) * 4],
                                                                          in_=ps[ob][:, 0:264].rearrange("p (a c) -> p a c", a=4)[:, :, 64]),
                              reads=[psk[ob]], writes=['rden'])
                    for bi, ob in enumerate((OA, OB)):
                        tk.op('dve', lambda e, bi=bi, ob=ob: e.tensor_tensor(
                            out=otmp[:, bi:8:2, :], in0=ps[ob][:, 0:264].rearrange("p (a c) -> p a c", a=4)[:, :, 0:64],
                            in1=rden[:, bi * 4:(bi + 1) * 4].unsqueeze(2).broadcast_to([128, 4, 64]), op=ALU.mult),
                            reads=[psk[ob], 'rden'], writes=['otmp'])
                    tk.op('dve', lambda e: e.tensor_tensor(out=mixed[:, j, 0:512], in0=otmp[:].rearrange("p h d -> p (h d)"), in1=sga[:, j, :], op=ALU.mult),
                          reads=['otmp', 'sga%d' % j], writes=['mixed%d' % j])

            emit_S(0)
            emit_S(1)
            for k in range(len(allsteps)):
                if k + 2 < len(allsteps):
                    emit_S(k + 2)
                emit_rest(k)

            if stop('A'):
                break
            tk.barrier()
            areset()
            qh = aget([128, NT, 256], BF16)
            sgb = aget([128, NT, 256])
            qE = sgb.rearrange("p a b -> p (a b)")[:, 0:1024].bitcast(BF16).rearrange("p (a b) -> p a b", a=NT)
            kE = sgb.rearrange("p a b -> p (a b)")[:, 1024:2048].bitcast(BF16).rearrange("p (a b) -> p a b", a=NT)
            vh = aget([128, NT, 256], BF16)
            vhmF = aget([128, 4096])
            vhm = vhmF.bitcast(BF16).rearrange("p (q t c) -> p q t c", q=4, t=NT)
            A_ = vhmF[:, 0:2048].rearrange("p (e c) -> p e c", e=64)
            B_ = vhmF[:, 2048:4096].rearrange("p (e c) -> p e c", e=64)
            sgr = aget([128, NT, 256], BF16)
            fS = aget([128, 4096])
            fbuf = fS[:, 0:2048].rearrange("p (a b) -> p a b", a=NT)
            SinPm = fS.bitcast(BF16).rearrange("p (h c e) -> p h c e", h=4, c=32)
            lfbuf = aget([128, NT, 256])
            kETm = lfbuf.rearrange("p a b -> p (a b)").bitcast(BF16).rearrange("p (h t) -> p h t", h=4)
            osq = lfbuf
            tmpE = [aget([128, 512]) for _ in range(2)]
            qET = aget([128, 2, T], BF16)
            ATm = [aget([128, 4, 128], BF16) for _ in range(2)]
            osum = aget([128, NT, 256])
            gs3 = aget([128, 2, 32, 3])
            es3 = aget([128, 2, 32, 3])
            S0b = aget([128, 2, 64])
            nsb = aget([128, 4, 2, 64])
            hss = aget([128, 32])
            tk.dma('sp', ghgB[:], bass.AP(ghg_d, l * 256, [[0, 128], [1, 256]]), writes=['ghgB'])
            wi = next_w()
            for t in range(NT):
                b = nb()
                proj_tm(wi, 0, 512, t, b)
                tk.op('act', lambda e, t=t: e.activation(out=qh[:, t, :], in_=ps[b][:, 0:256], func=AF.Silu), reads=[psk[b]], writes=['qh%d' % t])
                tk.op('act', lambda e, t=t: e.activation(out=sgb[:, t, :], in_=ps[b][:, 256:512], func=AF.Tanh, scale=0.5), reads=[psk[b]], writes=['sQ'])
            wi = next_w()
            for t in range(NT):
                b = nb()
                proj_tm(wi, 0, 512, t, b)
                tk.op('act', lambda e, t=t: e.activation(out=sgr[:, t, :], in_=ps[b][:, 256:512], func=AF.Silu), reads=[psk[b]], writes=['sgr%d' % t])
                tk.op('dve', lambda e, t=t: e.tensor_copy(out=vh[:, t, :], in_=ps[b][:, 0:256]), reads=[psk[b]], writes=['vh%d' % t])

            if stop('R0'):
                break
            rstop = False
            for dr in range(2):
                if l + 1 < nl:
                    mod_dma(l + 1, 3 * dr)
                lb_bc = lbl[:, dr, :].unsqueeze(1).broadcast_to([128, NT, 256])
                oml_bc = oml[:, dr, :].unsqueeze(1).broadcast_to([128, NT, 256])
                tk.op('dve', lambda e: e.tensor_tensor(out=fbuf[:], in0=sgb[:], in1=oml_bc, op=ALU.mult), reads=['sQ', 'oml'], writes=['fS'])
                tk.op('dve', lambda e: e.tensor_tensor(out=fbuf[:], in0=fbuf[:], in1=lb_bc, op=ALU.add), reads=['fS', 'lbl'], writes=['fS'])
                tk.op('act', lambda e: e.activation(out=lfbuf[:].rearrange("p a b -> p (a b)"), in_=fbuf[:].rearrange("p a b -> p (a b)"), func=AF.Ln),
                      reads=['fS'], writes=['lK'])
                tk.op('dve', lambda e: e.tensor_scalar(out=fbuf[:], in0=fbuf[:], scalar1=-1.0, scalar2=1.0, op0=ALU.mult, op1=ALU.add),
                      reads=['fS'], writes=['fS'])
                bS = nb()
                for t in range(NT):
                    for pr in range(2):
                        tt_ = t if dr == 0 else 7 - t
                        tk.op('pe', lambda e, t=t, pr=pr, tt_=tt_: e.matmul(ps[bS][:, (pr * 8 + tt_) * 8:(pr * 8 + tt_) * 8 + 8], lhsT=lfbuf[:, t, pr * 128:(pr + 1) * 128],
                                                                   rhs=SEL[dr], start=True, stop=True),
                              reads=['lK', 'cf32'], writes=[psk[bS]])
                psS = ps[bS][:, 0:128].rearrange("p (a c r) -> p a c r", a=2, c=32)
                tk.op('dve', lambda e: e.tensor_copy(out=gs3[:, :, :, 0:2], in_=psS), reads=[psk[bS]], writes=['gs3'])
                tk.op('dve', lambda e: e.tensor_tensor(out=gs3[:, :, :, 2], in0=gs3[:, :, :, 1], in1=gs3[:, :, :, 0], op=ALU.subtract),
                      reads=['gs3'], writes=['gs3'])
                tk.op('act', lambda e: e.activation(out=es3[:].rearrange("p a c r -> p (a c r)"), in_=gs3[:].rearrange("p a c r -> p (a c r)"), func=AF.Exp),
                      reads=['gs3'], writes=['es3'])
                tk.op('dve', lambda e: e.tensor_scalar(out=es3[:, :, 8:32:8, 0:2], in0=es3[:, :, 8:32:8, 0:2], scalar1=flags[:, 1:2], scalar2=None, op0=ALU.mult),
                      reads=['es3', 'flags'], writes=['es3'])
                for tp in range(4):
                    b = nb()
                    for i in range(2):
                        t = 2 * tp + i
                        tk.op('pe', lambda e, t=t, i=i: e.matmul(ps[b][:, i * 256:(i + 1) * 256], lhsT=TR[dr], rhs=lfbuf[:, t, :], start=True, stop=True),
                              reads=['cf32', 'lK'], writes=[psk[b]])
                    tk.op('act', lambda e: e.activation(out=tmpE[0][:], in_=ps[b][:, :], func=AF.Exp), reads=[psk[b]], writes=['tmpE0'])
                    tk.op('act', lambda e: e.activation(out=tmpE[1][:], in_=ps[b][:, :], func=AF.Exp, scale=-1.0), reads=[psk[b]], writes=['tmpE1'])
                    tk.op('dve', lambda e, tp=tp: e.tensor_tensor(out=qE[:, 2 * tp:2 * tp + 2, :].rearrange("p a c -> p (a c)"),
                                                                  in0=qh[:, 2 * tp:2 * tp + 2, :].rearrange("p a c -> p (a c)"), in1=tmpE[0][:], op=ALU.mult),
                          reads=['qh%d' % (2 * tp), 'qh%d' % (2 * tp + 1), 'tmpE0'], writes=['sQ', 'qk%d' % tp])
                    tk.op('dve', lambda e, tp=tp: e.tensor_tensor(out=kE[:, 2 * tp:2 * tp + 2, :].rearrange("p a c -> p (a c)"),
                                                                  in0=fbuf[:, 2 * tp:2 * tp + 2, :].rearrange("p a c -> p (a c)"), in1=tmpE[1][:], op=ALU.mult),
                          reads=['fS', 'tmpE1'], writes=['sQ', 'qk%d' % tp])
                for cp in range(4):
                    tk.op('dve', lambda e, cp=cp: e.tensor_scalar(out=vhm[:, cp, :, :], in0=vh[:], scalar1=QM[:, cp:cp + 1], scalar2=None, op0=ALU.mult),
                          reads=['vh%d' % t_ for t_ in range(NT)] + ['cf32'], writes=['vhm'])
                tk._deps('act', [], ['lK'])
                for t in range(NT):
                    b = nb()
                    for pr in range(2):
                        tk.op('pe', lambda e, t=t, pr=pr: e.transpose(out=psb16(b)[:, pr * 128:(pr + 1) * 128], in_=qE[:, t, pr * 128:(pr + 1) * 128], identity=IDB),
                              reads=['qk%d' % (t // 2), 'cb16'], writes=[psk[b]], war_only=['sQ'])
                        tk.op('pe', lambda e, t=t, pr=pr: e.transpose(out=psb16(b)[:, (2 + pr) * 128:(3 + pr) * 128], in_=kE[:, t, pr * 128:(pr + 1) * 128], identity=IDB),
                              reads=['qk%d' % (t // 2), 'cb16'], writes=[psk[b]], war_only=['sQ'])
                    tk.op('act', lambda e, t=t: e.copy(out=qET[:, :, t * 128:(t + 1) * 128], in_=psb16(b)[:, 0:256].rearrange("p (a c) -> p a c", a=2)),
                          reads=[psk[b]], writes=['qET%d' % t])
                    for h in range(4):
                        tk.op('act', lambda e, t=t, h=h: e.activation(out=kETm[:, h, t * 128:(t + 1) * 128], in_=psb16(b)[:, (2 + h // 2) * 128:(3 + h // 2) * 128],
                                                                      func=AF.Copy, scale=HM[:, h % 2:h % 2 + 1]),
                              reads=[psk[b], 'cf32'], writes=['kT%d_%d' % (t, h)])
                if stop('R1'):
                    rstop = True
                    break
                tk.dma('sp', S0b[:], s0_d.ap()[l, dr], writes=['S0b'])
                kvb_all = [[nb() for _ in range(4)] for _ in range(2)]
                for pr in range(2):
                    kvb = kvb_all[pr]
                    for c in range(32):
                        cq = c if dr == 0 else 31 - c
                        b = kvb[cq // 8]
                        for h in (2 * pr, 2 * pr + 1):
                            o = ps[b][(h % 2) * 64:(h % 2) * 64 + 64, (cq % 8) * 64:(cq % 8) * 64 + 64]
                            tk.op('pe', lambda e, c=c, h=h, o=o: e.matmul(o, lhsT=kE[:, c // 4, h * 64:(h + 1) * 64], rhs=vhm[:, c % 4, c // 4, h * 64:(h + 1) * 64],
                                                                          start=True, stop=True),
                                  reads=['sQ', 'vhm'], writes=[psk[b]])
                for pr in range(2):
                    kvb = kvb_all[pr]
                    for g in range(4):
                        tk.op('dve', lambda e, g=g: e.tensor_tensor(out=B_[:, :, g * 8:(g + 1) * 8].rearrange("p e c -> p c e"),
                                                                    in0=ps[kvb[g]][:, :].rearrange("p (c e) -> p c e", c=8),
                                                                    in1=es3[:, pr, g * 8:(g + 1) * 8, 2].unsqueeze(2).broadcast_to([128, 8, 64]), op=ALU.mult),
                              reads=[psk[kvb[g]], 'es3'], writes=['vhm'])
                    if dr == 0 and pr == 0:
                        wi = next_w()
                        for t in range(NT):
                            b = kvb[t % 4]
                            proj_tm(wi, 0, 256, t, b)
                            tk.op('act', lambda e, t=t: e.activation(out=sgb[:, t, :], in_=ps[b][:, 0:256], func=AF.Tanh, scale=0.5), reads=[psk[b]], writes=['sQ'])
                    if dr == 1 and pr == 0:
                        wi = next_w()
                        uT_h = qh[:].rearrange("p a b -> p (a b)").rearrange("p (c t) -> p c t", c=2)
                        sgf_h = qE
                        hb_ = 0
                        for ci in range(2):
                            for g in range(2):
                                b = kvb[hb_ % 4]
                                hb_ += 1
                                proj_fm(wi, ci, g, b)
                                tk.op('act', lambda e, ci=ci, g=g: e.copy(out=uT_h[:, ci, g * 512:(g + 1) * 512], in_=ps[b][:, :]), reads=[psk[b]], writes=['qh%d' % t_ for t_ in range(NT)])
                        for t in range(NT):
                            b = kvb[hb_ % 4]
                            hb_ += 1
                            proj_tm(wi, 256, 256, t, b)
                            tk.op('act', lambda e, t=t: e.activation(out=sgf_h[:, t, :], in_=ps[b][:, 0:256], func=AF.Silu), reads=[psk[b]], writes=['sQ'])
                    tk.op('dve', lambda e: e.scalar_tensor_tensor(out=B_[:, :, 0], in0=S0b[:, pr, :], scalar=es3[:, pr, 0, 1:2], in1=B_[:, :, 0], op0=ALU.mult, op1=ALU.add),
                          reads=['S0b', 'es3', 'vhm'], writes=['vhm'])
                    tk.op('dve', lambda e: e.tensor_copy(out=A_[:], in_=es3[:, pr, :, 1].unsqueeze(1).broadcast_to([128, 64, 32])), reads=['es3', 'vhm'], writes=['vhm'])
                    tk.op('dve', lambda e: e.memset(A_[:, :, 0:1], 0.0), reads=['vhm'], writes=['vhm'])
                    tk.op('dve', lambda e: e.tensor_tensor_scan(out=B_[:].rearrange("p e c -> p (e c)"), data0=A_[:].rearrange("p e c -> p (e c)"),
                                                                data1=B_[:].rearrange("p e c -> p (e c)"), initial=0.0, op0=ALU.mult, op1=ALU.add),
                          reads=['vhm'], writes=['vhm'])
                    tk.op('dve', lambda e: e.tensor_copy(out=nsb[:, :, pr, :], in_=B_[:, :, 7:32:8].rearrange("p e k -> p k e")), reads=['vhm'], writes=['nsb'])
                    for h2 in range(2):
                        tk.op('dve', lambda e, h2=h2: e.scalar_tensor_tensor(out=SinPm[:, 2 * pr + h2, 1:32, :], in0=B_[:, :, 0:31].rearrange("p e c -> p c e"),
                                                                             scalar=HM[:, h2:h2 + 1], in1=es3[:, pr, 1:32, 0].unsqueeze(2).broadcast_to([128, 31, 64]),
                                                                             op0=ALU.mult, op1=ALU.mult),
                              reads=['vhm', 'es3', 'cf32'], writes=['fS'])
                        tk.op('dve', lambda e, h2=h2: e.scalar_tensor_tensor(out=SinPm[:, 2 * pr + h2, 0, :], in0=S0b[:, pr, :], scalar=HM[:, h2:h2 + 1],
                                                                             in1=es3[:, pr, 0, 0:1].broadcast_to([128, 64]), op0=ALU.mult, op1=ALU.mult),
                              reads=['S0b', 'es3', 'cf32'], writes=['fS'])
                nsb3 = nsb[:].rearrange("p s a e -> p s (a e)")
                if dr == 0:
                    tk.dma('sp', ns_d.ap()[l, 0].rearrange("s p a e -> p s (a e)"), nsb3, reads=['nsb'])
                else:
                    for k in range(4):
                        tk.dma('sp', ns_d.ap()[l, 1, 3 - k].rearrange("p a e -> p (a e)"), nsb3[:, k, :], reads=['nsb'])
                if l + 1 < nl:
                    mod_mm(l + 1, 3 * dr)
                    mod_dma(l + 1, 3 * dr + 1)
                if stop('R2'):
                    rstop = True
                    break
                for t in range(NT):
                    bA_ = nb()
                    ai = t % 2
                    for h in range(4):
                        tk.op('pe', lambda e, t=t, h=h: e.matmul(ps[bA_][:, h * 128:(h + 1) * 128], lhsT=kETm[:, h, t * 128:(t + 1) * 128],
                                                                 rhs=qET[:, h // 2, t * 128:(t + 1) * 128], start=True, stop=True),
                              reads=['kT%d_%d' % (t, h), 'qET%d' % t], writes=[psk[bA_]], war_only=['lK'])
                    tk.op('dve', lambda e: e.tensor_tensor(out=ATm[ai][:], in0=ps[bA_][:, :].rearrange("p (a c) -> p a c", a=4),
                                                           in1=TRI[dr].unsqueeze(1).broadcast_to([128, 4, 128]), op=ALU.mult),
                          reads=[psk[bA_], 'cb16'], writes=['ATm%d' % ai])
                    bO = nb()
                    for h in range(4):
                        tk.op('pe', lambda e, t=t, h=h: e.matmul(ps[bO][:, h * 64:(h + 1) * 64], lhsT=ATm[ai][:, h, :], rhs=vh[:, t, h * 64:(h + 1) * 64],
                                                                 start=True, stop=False, skip_group_check=True),
                              reads=['ATm%d' % ai, 'vh%d' % t], writes=[psk[bO]])
                        for cp in range(4):
                            tk.op('pe', lambda e, t=t, h=h, cp=cp: e.matmul(ps[bO][cp * 32:(cp + 1) * 32, h * 64:(h + 1) * 64],
                                                                            lhsT=qET[:, h // 2, t * 128 + cp * 32:t * 128 + cp * 32 + 32],
                                                                            rhs=SinPm[:, h, (4 * t + cp) if dr == 0 else 31 - (4 * t + cp), :], start=False, stop=(cp == 3), skip_group_check=True,
                                                                            tile_position=(0, cp * 32)),
                                  reads=['qET%d' % t, 'fS'], writes=[psk[bO]])
                    if l + 1 < nl and t == 3:
                        mod_mm(l + 1, 3 * dr + 1)
                        mod_dma(l + 1, 3 * dr + 2)
                    if l + 1 < nl and t == 7:
                        mod_mm(l + 1, 3 * dr + 2)
                    if dr == 0:
                        tk.op('act', lambda e, t=t: e.copy(out=osum[:, t, :], in_=ps[bO][:, 0:256]), reads=[psk[bO]], writes=['osum%d' % t])
                    else:
                        tk.op('dve', lambda e, t=t: e.tensor_tensor(out=osum[:, t, :], in0=osum[:, t, :], in1=ps[bO][:, 0:256], op=ALU.add),
                              reads=[psk[bO], 'osum%d' % t], writes=['osum%d' % t])
                if stop('R3'):
                    rstop = True
                    break
            if rstop:
                break
            tk.op('dve', lambda e: e.tensor_tensor(out=osq[:], in0=osum[:], in1=osum[:], op=ALU.mult), reads=['osum%d' % t_ for t_ in range(NT)], writes=['lK'])
            tk.op('dve', lambda e: e.tensor_reduce(out=hss[:], in_=osq[:].rearrange("p t (h d) -> p (t h) d", h=4), axis=AX.X, op=ALU.add),
                  reads=['lK'], writes=['hss'])
            tk.op('dve', lambda e: e.tensor_scalar(out=hss[:], in0=hss[:], scalar1=1.0 / 64, scalar2=EPS, op0=ALU.mult, op1=ALU.add), reads=['hss'], writes=['hss'])
            tk.op('act', lambda e: e.activation(out=hss[:], in_=hss[:], func=AF.Ln), reads=['hss'], writes=['hss'])
            tk.op('act', lambda e: e.activation(out=hss[:], in_=hss[:], func=AF.Exp, scale=-0.5), reads=['hss'], writes=['hss'])
            tk.op('dve', lambda e: e.tensor_tensor(out=osum[:].rearrange("p t (h d) -> p (t h) d", h=4), in0=osum[:].rearrange("p t (h d) -> p (t h) d", h=4),
                                                   in1=hss[:].unsqueeze(2).broadcast_to([128, 32, 64]), op=ALU.mult),
                  reads=['osum%d' % t_ for t_ in range(NT)] + ['hss'], writes=['osum%d' % t_ for t_ in range(NT)])
            tk.op('dve', lambda e: e.tensor_tensor(out=osum[:], in0=osum[:], in1=ghgB[:].unsqueeze(1).broadcast_to([128, NT, 256]), op=ALU.mult),
                  reads=['osum%d' % t_ for t_ in range(NT)] + ['ghgB'], writes=['osum%d' % t_ for t_ in range(NT)])
            tk.op('dve', lambda e: e.tensor_tensor(out=mixed[:, :, 512:768], in0=osum[:], in1=sgr[:], op=ALU.mult),
                  reads=['osum%d' % t_ for t_ in range(NT)] + ['sgr%d' % t_ for t_ in range(NT)], writes=['mixed%d' % j for j in range(8)])

            if stop('R'):
                break
            tk.barrier()
            areset()
            uT = aget([128, 2, T], BF16)
            sgf = aget([128, NT, 256], BF16)
            ucs = aget([128, NT, 2, 256], BF16)
            yT = aget([128, 2, T], BF16)
            csnb = [aget([128, 2, 1024], BF16) for _ in range(4)]
            assert apos[0] + 2048 <= 11520
            wfs = aget([128, 2, 256])
            wfb = aget([128, 2, 256], BF16)
            junk = aget([128, 512], BF16)
            tmpf = aget([128, D])
            for kt_ in range(4):
                tk.dma('sp', csnb[kt_][:], csn_d.ap()[kt_], writes=['csnb%d' % kt_])
            assert True
            tk.dma('sp', wfs[:], wfn_d.ap()[l].rearrange("(c p) n -> p c n", p=128), writes=['wfs'])
            tk.op('pool', lambda e: e.tensor_copy(out=wfb[:], in_=wfs[:]), reads=['wfs'], writes=['wfb'])
            for t in range(NT):
                b = nb()
                for cs in range(2):
                    for ct in range(2):
                        tk.op('pe', lambda e, t=t, cs=cs, ct=ct: e.matmul(ps[b][:, cs * 256 + ct * 128:cs * 256 + ct * 128 + 128], lhsT=uT[:, ct, t * 128:(t + 1) * 128],
                                                                          rhs=C4S4[:, 3 + cs * 2 + ct, :], start=True, stop=True),
                              reads=['uT', 'cb16'], writes=[psk[b]])
                tk.op('act', lambda e, t=t: e.copy(out=ucs[:, t, :, :].rearrange("p a c -> p (a c)"), in_=ps[b][:, :]), reads=[psk[b]], writes=['ucs%d' % t])
            if stop('F0'):
                break
            yb = [nb() for _ in range(4)]
            for kt_ in range(8):
                ci = kt_ % 4
                if kt_ >= 4:
                    tk.dma('sp', csnb[ci][:], csn_d.ap()[kt_], writes=['csnb%d' % ci])
                for ct in range(2):
                    for g in range(2):
                        b = yb[ct * 2 + g]
                        for cs in range(2):
                            tk.op('pe', lambda e, kt_=kt_, ct=ct, g=g, cs=cs: e.matmul(ps[b][:, :], lhsT=ucs[:, kt_, cs, ct * 128:(ct + 1) * 128],
                                                                                       rhs=csnb[ci][:, cs, g * 512:(g + 1) * 512],
                                                                                       start=(kt_ == 0 and cs == 0), stop=(kt_ == 7 and cs == 1)),
                                  reads=['ucs%d' % kt_, 'csnb%d' % ci], writes=[psk[b]])
            for ct in range(2):
                for g in range(2):
                    b = yb[ct * 2 + g]
                    tk.op('act', lambda e, ct=ct, g=g, b=b: e.copy(out=yT[:, ct, g * 512:(g + 1) * 512], in_=ps[b][:, :]), reads=[psk[b]], writes=['yT%d_%d' % (ct, g)])
            for t in range(NT):
                b = nb()
                for ct in range(2):
                    tk.op('pe', lambda e, t=t, ct=ct: e.matmul(ps[b][:, 0:256], lhsT=yT[:, ct, t * 128:(t + 1) * 128], rhs=wfb[:, ct, :], start=(ct == 0), stop=(ct == 1)),
                          reads=['yT%d_%d' % (ct, t // 4), 'wfb'], writes=[psk[b]])
                tk.op('dve', lambda e, t=t: e.tensor_tensor(out=mixed[:, t, 768:1024], in0=ps[b][:, 0:256], in1=sgf[:, t, :], op=ALU.mult),
                      reads=[psk[b], 'sgf'], writes=['mixed%d' % t])

            if dbg and l == nl - 1:
                for t in range(NT):
                    tk.dma('sp', dbg_d.ap()[t * 128:(t + 1) * 128, :], mixed[:, t, :], reads=['mixed%d' % t])

            if stop('F1'):
                break
            w0 = next_w()
            w1 = next_w(prefetch=False)
            for t in range(NT):
                b = nb()
                for kc in range(8):
                    tk.op('pe', lambda e, t=t, kc=kc: e.transpose(out=psb16(b)[:, kc * 128:(kc + 1) * 128], in_=mixed[:, t, kc * 128:(kc + 1) * 128], identity=IDB),
                          reads=['mixed%d' % t, 'cb16'], writes=[psk[b]])
                tk.op('act', lambda e, t=t: e.copy(out=hT[:, :, t * 128:(t + 1) * 128], in_=psb16(b)[:, :].rearrange("p (k c) -> p k c", k=8)),
                      reads=[psk[b]], writes=['hT%d' % t])
            for t in range(NT):
                bb = [nb(), nb()]
                for hf, wi in enumerate((w0, w1)):
                    proj_tm(wi, 0, 512, t, bb[hf])
                    tk.op('act', lambda e, t=t, hf=hf: e.activation(out=junk[:], in_=ps[bb[hf]][:, :], func=AF.Square, accum_out=ssq[:, 8 + hf:9 + hf]),
                          reads=[psk[bb[hf]]], writes=['junk', 'ssq'])
                tk.op('dve', lambda e: e.tensor_tensor(out=rstd[:, 8:9], in0=ssq[:, 8:9], in1=ssq[:, 9:10], op=ALU.add), reads=['ssq'], writes=['rstd'])
                tk.op('dve', lambda e: e.tensor_scalar(out=rstd[:, 8:9], in0=rstd[:, 8:9], scalar1=1.0 / D, scalar2=EPS, op0=ALU.mult, op1=ALU.add),
                      reads=['rstd'], writes=['rstd'])
                tk.op('act', lambda e: e.activation(out=rstd[:, 8:9], in_=rstd[:, 8:9], func=AF.Ln), reads=['rstd'], writes=['rstd'])
                tk.op('act', lambda e: e.activation(out=rstd[:, 8:9], in_=rstd[:, 8:9], func=AF.Exp, scale=-0.5), reads=['rstd'], writes=['rstd'])
                for hf in range(2):
                    tk.op('dve', lambda e, hf=hf: e.scalar_tensor_tensor(out=tmpf[:, hf * 512:(hf + 1) * 512], in0=ps[bb[hf]][:, :], scalar=rstd[:, 8:9],
                                                                         in1=gg[:, hf * 512:(hf + 1) * 512], op0=ALU.mult, op1=ALU.mult),
                          reads=[psk[bb[hf]], 'rstd', 'gg'], writes=['tmpf'])
                tk.op('dve', lambda e, t=t: e.tensor_tensor(out=x_sb[:, t, :], in0=x_sb[:, t, :], in1=tmpf[:], op=ALU.add),
                      reads=['x%d' % t, 'tmpf'], writes=['x%d' % t])
            _issue(wstate['ptr'])

        for t in range(NT):
            tk.dma('sp', y_d.ap()[t * 128:(t + 1) * 128, :], x_sb[:, t, :], reads=['x%d' % t])
        tk.finish()
    return nc


def _consts(is_sample):
    cf32 = np.zeros((128, 7, 128), np.float32)
    p = np.arange(128)
    J2 = np.zeros((128, 128), np.float32)
    for a in range(2):
        for i in range(64):
            J2[a * 64 + i, a * 64 + 63 - i] = 1.0
    cf32[:, 0] = J2
    cf32[:, 1] = np.eye(128, dtype=np.float32)
    cm = np.zeros((128, 128), np.float32)
    if is_sample:
        qc = np.arange(64)
        c0 = np.clip(qc - 8, 0, 48)
        kc = np.arange(64)
        valid = (kc[:, None] >= c0[None, :]) & (kc[:, None] < c0[None, :] + 16)
        m = np.where(valid, 0.0, NEG).astype(np.float32)
        cm = np.tile(m, (2, 2))
    cf32[:, 2] = cm
    s = np.arange(32)[:, None]
    t = np.arange(32)[None, :]
    trf = (s <= t).astype(np.float32) - (s <= 15).astype(np.float32)
    trb = (s >= t).astype(np.float32) - (s >= 16).astype(np.float32)
    for a in range(4):
        cf32[a * 32:(a + 1) * 32, 3, a * 32:(a + 1) * 32] = trf
        cf32[a * 32:(a + 1) * 32, 4, a * 32:(a + 1) * 32] = trb
    sl = np.arange(128) % 32
    ch = np.arange(128) // 32
    selcols = np.zeros((128, 128), np.float32)
    for a in range(4):
        selcols[:, a * 2 + 0] = ((ch == a) & (sl <= 15))
        selcols[:, a * 2 + 1] = (ch == a)
        selcols[:, 8 + a * 2 + 0] = ((ch == 3 - a) & (sl >= 16))
        selcols[:, 8 + a * 2 + 1] = (ch == 3 - a)
        selcols[:, 18 + a] = (ch == a)
    selcols[:, 16] = (np.arange(128) < 64)
    selcols[:, 17] = (np.arange(128) >= 64)
    cf32[:, 5] = selcols
    cf32[0:64, 6, 0:64] = 1.0
    cf32[64:128, 6, 64:128] = 1.0
    rowb = np.zeros((128, 74), np.float32)
    for i, (j, kt) in enumerate(JK):
        for hf in range(2):
            for krl in range(2):
                if is_sample:
                    qr = 2 * j + hf
                    kr = 2 * kt + krl
                    r0 = int(np.clip(qr - 4, 0, 8))
                    ok = (r0 <= kr < r0 + 8)
                else:
                    ok = (kt // 2 == j // 2)
                rowb[krl * 64:(krl + 1) * 64, i * 2 + hf] = 0.0 if ok else NEG
    cb16 = np.zeros((128, 9, 128), np.float32)
    cb16[:, 7] = J2
    cb16[:, 8] = cm
    cb16[:, 0] = np.eye(128)
    mf = (s <= t).astype(np.float32)
    mb = (s >= t).astype(np.float32)
    z = np.zeros((64, 64), np.float32)
    for a in range(4):
        cb16[a * 32:(a + 1) * 32, 1, a * 32:(a + 1) * 32] = mf
        cb16[a * 32:(a + 1) * 32, 2, a * 32:(a + 1) * 32] = mb
    ang = 2 * np.pi * np.outer(np.arange(64), np.arange(64)) / 64
    c4 = np.cos(ang) / 8.0
    s4 = np.sin(ang) / 8.0
    for ct in range(2):
        cb16[:, 3 + ct] = np.block([[c4, z], [z, c4]])
        cb16[:, 5 + ct] = np.block([[s4, z], [z, s4]])
    n = 1024 if is_sample else 256
    idx = np.arange(n)
    a2 = 2 * np.pi * ((np.outer(idx, idx)) % n) / n
    cn = np.cos(a2) / np.sqrt(n)
    sn = -np.sin(a2) / np.sqrt(n)
    CN = np.zeros((1024, 1024), np.float64)
    SN = np.zeros((1024, 1024), np.float64)
    for i in range(1024 // n):
        CN[i * n:(i + 1) * n, i * n:(i + 1) * n] = cn
        SN[i * n:(i + 1) * n, i * n:(i + 1) * n] = sn
    csn = np.stack([CN.reshape(8, 128, 1024), SN.reshape(8, 128, 1024)], axis=2)
    return dict(cf32=cf32, rowbias=rowb, cb16=cb16.astype(ml_dtypes.bfloat16), csn=csn.astype(ml_dtypes.bfloat16))


def _in_maps(x_prompt, x_sample, cache_attn_k, cache_attn_v, state_hgrn, c, c_ctx,
             w_ada, b_ada, g_pre, w_in, rpb, lb_logits, g_hgrn, w_fnet, w_out, g_post):
    f = lambda a: np.ascontiguousarray(np.asarray(a, dtype=np.float32))
    shared = dict(w_ada=f(w_ada), b_ada=f(b_ada), g_pre=f(g_pre), w_in=f(w_in), lb_logits=f(lb_logits),
                  g_hgrn=f(g_hgrn), w_fnet=f(w_fnet), w_out=f(w_out), g_post=f(g_post))
    tp = np.zeros((NL, 8, 23, 127), np.float32)
    tp[:, :, 4:19, 48:79] = f(rpb)
    cs = _consts(True)
    cp = _consts(False)
    maps = []
    for i in range(8):
        m = dict(shared)
        if i < 4:
            m["x"] = f(x_sample[i])
            m["cvec"] = f(np.asarray(c[i]).reshape(8, 128).T)
            m["ctxk"] = f(np.asarray(cache_attn_k[i]).reshape(NL, 512, 512))
            m["ctxv"] = f(np.asarray(cache_attn_v[i]).reshape(NL, 512, 512))
            s = np.asarray(state_hgrn[i]).reshape(NL, 2, 2, 2, 64, 64)
            m["s0"] = f(s.transpose(0, 1, 3, 4, 2, 5).reshape(NL, 2, 128, 2, 64))
            m["flags"] = np.ones((128, 2), np.float32)
            m["tpad"] = tp
            m.update(cs)
        else:
            m["x"] = f(np.asarray(x_prompt[4 * (i - 4):4 * (i - 3)]).reshape(T, D))
            m["cvec"] = f(np.asarray(c_ctx).reshape(8, 128).T)
            m["ctxk"] = np.zeros((NL, 512, 512), np.float32)
            m["ctxv"] = np.zeros((NL, 512, 512), np.float32)
            m["s0"] = np.zeros((NL, 2, 128, 2, 64), np.float32)
            m["flags"] = np.zeros((128, 2), np.float32)
            m["tpad"] = np.zeros_like(tp)
            m.update(cp)
        maps.append(m)
    return maps


_NC_CACHE = {}


def kernel(**inputs):
    if 'nc' not in _NC_CACHE:
        _NC_CACHE['nc'] = build_nc()
    nc = _NC_CACHE['nc']
    maps = _in_maps(**inputs)
    res = run_bass_kernel_spmd(nc, maps, core_ids=list(range(8)))
    r = res.results
    y_sample = np.stack([r[i]["y"] for i in range(4)], axis=0).astype(np.float32)
    y_prompt = np.concatenate([r[i]["y"].reshape(4, 256, D) for i in range(4, 8)], axis=0).astype(np.float32)
    nk = np.concatenate([r[i]["newk"].reshape(NL, 4, 256, 8, 64).transpose(1, 0, 2, 3, 4) for i in range(4, 8)], axis=0)
    nv = np.concatenate([r[i]["newv"].reshape(NL, 4, 256, 8, 64).transpose(1, 0, 2, 3, 4) for i in range(4, 8)], axis=0)
    ns = np.concatenate([r[i]["news"].reshape(NL, 2, 4, 2, 64, 2, 64).transpose(2, 0, 1, 5, 3, 4, 6).reshape(4, NL, 2, 4, 64, 64)
                         for i in range(4, 8)], axis=0)
    return (y_prompt, y_sample, np.ascontiguousarray(nk, dtype=np.float32), np.ascontiguousarray(nv, dtype=np.float32),
            np.ascontiguousarray(ns, dtype=np.float32))
```

```python
import numpy as np
import ml_dtypes
from contextlib import ExitStack
import concourse.bass as bass
import concourse.mybir as mybir
from concourse.bass_utils import run_bass_kernel_spmd

F32 = mybir.dt.float32
BF16 = mybir.dt.bfloat16
AF = mybir.ActivationFunctionType
ALU = mybir.AluOpType
AX = mybir.AxisListType

NL = 4
D = 1024
T = 1024
NT = 8
EPS = 1e-6
NEG = -30000.0
KT = {0: [0, 1, 2, 3], 1: [0, 1, 2, 3], 2: [0, 1, 2, 3, 4], 3: [1, 2, 3, 4, 5],
      4: [2, 3, 4, 5, 6], 5: [3, 4, 5, 6, 7], 6: [4, 5, 6, 7], 7: [4, 5, 6, 7]}
JK = [(j, kt) for j in range(8) for kt in KT[j]]
JKI = {p: i for i, p in enumerate(JK)}
NDS = 24
NSW = 72


class TK:
    def __init__(s, nc, st):
        s.nc = nc
        s.E = {'pe': nc.tensor, 'act': nc.scalar, 'dve': nc.vector, 'pool': nc.gpsimd, 'sp': nc.sync}
        s.sem = {k: st.enter_context(nc.semaphore('s_' + k)) for k in ('pe', 'act', 'dve', 'pool')}
        s.cnt = {k: 0 for k in s.E}
        s.seen = {k: {} for k in s.E}
        s.lw = {}
        s.rd = {}
        s.dsems = [st.enter_context(nc.semaphore('d%d' % i)) for i in range(NDS)]
        s.dcnt = [0] * NDS
        s.dnext = 0
        s.swsems = [st.enter_context(nc.semaphore('w%d' % i)) for i in range(NSW)]
        s.swnext = 0
        s.swlow = 0

    def _wait(s, eng, key, val):
        if eng == 'pe' and key == 'pe':
            return
        if s.seen[eng].get(key, 0) >= val:
            return
        if isinstance(key, str):
            semobj = s.sem[key]
        elif key >= 1000:
            semobj = s.swsems[key - 1000]
        else:
            semobj = s.dsems[key]
        s.E[eng].wait_ge(semobj, val)
        s.seen[eng][key] = val

    def _deps(s, eng, reads, writes):
        for k in reads:
            w = s.lw.get(k)
            if w:
                s._wait(eng, *w)
            if k.startswith('ps'):
                for rk, rv in s.rd.get(k, {}).items():
                    if rk != eng:
                        s._wait(eng, rk, rv)
        for k in writes:
            w = s.lw.get(k)
            if w:
                s._wait(eng, *w)
            for rk, rv in s.rd.get(k, {}).items():
                s._wait(eng, rk, rv)

    def _book(s, tag, reads, writes):
        for k in reads:
            d = s.rd.setdefault(k, {})
            d[tag[0]] = max(d.get(tag[0], 0), tag[1])
        for k in writes:
            s.lw[k] = tag
            s.rd[k] = {}

    def op(s, eng, fn, reads=(), writes=(), war_only=()):
        s._deps(eng, reads, writes)
        inst = fn(s.E[eng])
        s.cnt[eng] += 1
        inst.then_inc(s.sem[eng], 1)
        s._book((eng, s.cnt[eng]), tuple(reads) + tuple(war_only), writes)

    def dma(s, q, out, in_, reads=(), writes=()):
        if q == 'pool':
            assert s.swnext < NSW, "out of one-shot semaphores"
            i = s.swnext
            s.swnext += 1
            s._deps(q, reads, writes)
            s.E[q].dma_start(out=out, in_=in_).then_inc(s.swsems[i], 16)
            s._book((1000 + i, 16), reads, writes)
            return
        i = s.dnext
        s.dnext = (s.dnext + 1) % NDS
        if s.dcnt[i] > 0:
            s._wait(q, i, s.dcnt[i])
        s._deps(q, reads, writes)
        s.dcnt[i] += 16
        s.E[q].dma_start(out=out, in_=in_).then_inc(s.dsems[i], 16)
        s._book((i, s.dcnt[i]), reads, writes)

    def barrier(s):
        engs = ('pe', 'act', 'dve', 'pool', 'sp')
        snap = dict(s.cnt)
        dsnap = list(s.dcnt)
        for e in engs:
            for o in ('pe', 'act', 'dve', 'pool'):
                if o != e and snap[o] > 0:
                    s._wait(e, o, snap[o])
            for i in range(NDS):
                if dsnap[i] > 0:
                    s._wait(e, i, dsnap[i])
            for i in range(s.swlow, s.swnext):
                s._wait(e, 1000 + i, 16)
        s.swlow = s.swnext

    def finish(s):
        for i in range(NDS):
            if s.dcnt[i] > 0:
                s._wait('sp', i, s.dcnt[i])
        for i in range(s.swnext):
            s._wait('sp', 1000 + i, 16)
        for k in ('pe', 'act', 'dve', 'pool'):
            if s.cnt[k] > 0:
                s._wait('sp', k, s.cnt[k])


def build_nc(nl=NL, dbg=False, upto=None):
    nc = bass.Bass("TRN2", target_bir_lowering=False)
    _order = ['M0', 'M1', 'M2', 'M3', 'M', 'A0', 'A1', 'A1a', 'A1b', 'A1c', 'A2', 'A', 'R0', 'R1', 'R2', 'R3', 'R', 'F0', 'F1', 'F']

    def stop(p):
        return upto is not None and _order.index(upto) <= _order.index(p)

    def din(name, shape, dt=F32):
        return nc.dram_tensor(name, list(shape), dt, kind="ExternalInput")

    def dout(name, shape, dt=F32):
        return nc.dram_tensor(name, list(shape), dt, kind="ExternalOutput")

    x_d = din("x", [T, D])
    cvec_d = din("cvec", [128, 8])
    ctxk_d = din("ctxk", [NL, 512, 512])
    ctxv_d = din("ctxv", [NL, 512, 512])
    s0_d = din("s0", [NL, 2, 128, 2, 64])
    flags_d = din("flags", [128, 2])
    wada_d = din("w_ada", [NL, D, 3 * D])
    bada_d = din("b_ada", [NL, 3 * D])
    gpre_d = din("g_pre", [NL, D])
    win_d = din("w_in", [NL, D, 3840])
    tpad_d = din("tpad", [NL, 8, 23, 127])
    lbl_d = din("lb_logits", [2, NL, 256])
    ghg_d = din("g_hgrn", [NL, 256])
    wfn_d = din("w_fnet", [NL, 256, 256])
    wout_d = din("w_out", [NL, D, D])
    gpost_d = din("g_post", [NL, D])
    cf32_d = din("cf32", [128, 7, 128])
    rowb_d = din("rowbias", [128, 74])
    cb16_d = din("cb16", [128, 9, 128], BF16)
    csn_d = din("csn", [8, 128, 2, 1024], BF16)

    y_d = dout("y", [T, D])
    nk_d = dout("newk", [NL, T, 512])
    nv_d = dout("newv", [NL, T, 512])
    ns_d = dout("news", [NL, 2, 4, 128, 2, 64])
    dbg_d = dout("dbgmixed", [T, D], BF16) if dbg else None

    with ExitStack() as st:
        def sb(name, shape, dt=F32):
            return st.enter_context(nc.sbuf_tensor(name, list(shape), dt))

        tk = TK(nc, st)
        x_sb = sb("x_sb", [128, NT, D])
        hT = sb("hT", [128, 8, T], BF16)
        mixed = sb("mixed", [128, NT, D], BF16)
        wst = [sb("wst0", [128, 8, 512], BF16)]
        wbf = [sb("wbf%d" % i, [128, 8, 512], BF16) for i in range(2)]
        gg = sb("gg", [128, D])
        modN = sb("modN", [128, 3 * D], BF16)
        screp = sb("screp", [128, 8, 128], BF16)
        brow = sb("brow", [1, 512])
        ones_row = sb("ones_row", [1, 128])
        csil = sb("csil", [128, 8])
        cf32 = sb("cf32s", [128, 7, 128])
        rowb = sb("rowbs", [128, 74])
        cb16 = sb("cb16s", [128, 9, 128], BF16)
        flags = sb("flagss", [128, 2])
        lbl = sb("lbl", [128, 2, 256])
        oml = sb("oml", [128, 2, 256])
        ghgB = sb("ghgB", [128, 256])
        ssq = sb("ssq", [128, 16])
        rstd = sb("rstd", [128, 16])
        ARW = 21120
        arena = sb("arena", [128, ARW])
        apos = [0]

        def areset():
            apos[0] = 0

        def aget(shape, dt=F32):
            n = 1
            for d_ in shape[1:]:
                n *= d_
            words = n if dt == F32 else (n + 1) // 2
            a0 = apos[0]
            apos[0] += words
            assert apos[0] <= ARW, ("arena overflow", apos[0])
            v = arena[:, a0:a0 + words]
            if dt != F32:
                v = v.bitcast(dt)
            if len(shape) == 3:
                v = v.rearrange("p (a b) -> p a b", a=shape[1])
            elif len(shape) == 4:
                v = v.rearrange("p (a b c) -> p a b c", a=shape[1], b=shape[2])
            return v

        psbig = [st.enter_context(nc.psum_tensor("psb%d" % i, [128, 1024], F32)) for i in range(4)]
        ps = [psbig[i // 2][:, (i % 2) * 512:(i % 2 + 1) * 512] for i in range(8)]
        psk = ["ps%d" % i for i in range(8)]
        bank_rr = [0]

        def nb(avoid=()):
            while True:
                b = bank_rr[0]
                bank_rr[0] = (b + 1) % 8
                if b not in avoid:
                    return b

        J2 = cf32[:, 0, :]
        IDF = cf32[:, 1, :]
        CMT = cf32[:, 2, :]
        TR = [cf32[:, 3, :], cf32[:, 4, :]]
        SEL = [cf32[:, 5, 0:8], cf32[:, 5, 8:16]]
        HM = cf32[:, 5, 16:18]
        QM = cf32[:, 5, 18:22]
        HME = cf32[:, 6, :].rearrange("p (a c) -> p a c", a=2)
        IDB = cb16[:, 0, :]
        TRI = [cb16[:, 1, :], cb16[:, 2, :]]
        C4S4 = cb16
        J2B = cb16[:, 7, :]
        CMTB = cb16[:, 8, :]

        tk.dma('sp', cf32[:], cf32_d.ap(), writes=['cf32'])
        tk.dma('sp', rowb[:], rowb_d.ap(), writes=['rowb'])
        tk.dma('sp', cb16[:], cb16_d.ap(), writes=['cb16'])
        tk.dma('sp', flags[:], flags_d.ap(), writes=['flags'])
        tk.dma('sp', csil[:], cvec_d.ap(), writes=['csil'])
        for t in range(NT):
            tk.dma('sp', x_sb[:, t, :], x_d.ap()[t * 128:(t + 1) * 128, :], writes=['x%d' % t])
        tk.op('pool', lambda e: e.memset(ones_row[:], 1.0), writes=['ones_row'])
        tk.op('act', lambda e: e.activation(out=csil[:], in_=csil[:], func=AF.Silu), reads=['csil'], writes=['csil'])

        wring = [0]

        wbring = [0]

        def load_w(src_ap, ncols):
            wi = wbring[0]
            wbring[0] ^= 1
            tk.dma('pool', wbf[wi][:, :, 0:ncols], src_ap.rearrange("(kc p) n -> p kc n", p=128), writes=['wbf%d' % wi])
            return wi

        wseq = []
        for l_ in range(nl):
            for (c0_, n_) in ((0, 512), (512, 512), (1024, 512), (1536, 512), (2048, 512), (2816, 512), (2560, 256), (3328, 512)):
                wseq.append((win_d, l_, c0_, n_))
            wseq.append((wout_d, l_, 0, 512))
            wseq.append((wout_d, l_, 512, 512))
        wstate = {'ptr': 0, 'loaded': {}}

        def _issue(i):
            if i < len(wseq) and i not in wstate['loaded']:
                d_, l_, c0_, n_ = wseq[i]
                wstate['loaded'][i] = load_w(d_.ap()[l_, :, c0_:c0_ + n_], n_)

        def next_w(prefetch=True):
            i = wstate['ptr']
            wstate['ptr'] += 1
            _issue(i)
            if prefetch:
                _issue(i + 1)
            return wstate['loaded'][i]

        def proj_tm(wi, c0, ncols, t, b):
            for kc in range(8):
                tk.op('pe', lambda e, kc=kc: e.matmul(ps[b][:, 0:ncols], lhsT=hT[:, kc, t * 128:(t + 1) * 128],
                                                     rhs=wbf[wi][:, kc, c0:c0 + ncols], start=(kc == 0), stop=(kc == 7)),
                      reads=['hT%d' % t, 'wbf%d' % wi], writes=[psk[b]])

        def proj_fm(wi, ci, g, b):
            for kc in range(8):
                tk.op('pe', lambda e, kc=kc: e.matmul(ps[b][:, 0:512], lhsT=wbf[wi][:, kc, ci * 128:(ci + 1) * 128],
                                                     rhs=hT[:, kc, g * 512:(g + 1) * 512], start=(kc == 0), stop=(kc == 7)),
                      reads=['hT%d' % x_ for x_ in range(4 * g, 4 * g + 4)] + ['wbf%d' % wi], writes=[psk[b]])

        def psb16(b):
            return ps[b].bitcast(BF16)

        def emit_mod_chunk(lm, ch, banks=None):
            mod_dma(lm, ch)
            mod_mm(lm, ch, banks)

        def mod_dma(lm, ch):
            tk.dma('pool', wst[0][:], wada_d.ap()[lm, :, ch * 512:(ch + 1) * 512].rearrange("(kc p) n -> p kc n", p=128), writes=['wst0'])
            tk.dma('sp', brow[:], bada_d.ap()[lm:lm + 1, ch * 512:(ch + 1) * 512], writes=['brow'])

        def mod_mm(lm, ch, banks=None):
            b = nb() if banks is None else banks[ch % len(banks)]
            for kc in range(8):
                tk.op('pe', lambda e, kc=kc: e.matmul(ps[b][:, :], lhsT=screp[:, kc, :], rhs=wst[0][:, kc, :], start=(kc == 0), stop=False),
                      reads=['screp', 'wst0'], writes=[psk[b]])
            tk.op('pe', lambda e: e.matmul(ps[b][:, :], lhsT=ones_row[0:1, :], rhs=brow[0:1, :], start=False, stop=True),
                  reads=['ones_row', 'brow'], writes=[psk[b]])
            tk.op('act', lambda e: e.copy(out=modN[:, ch * 512:(ch + 1) * 512], in_=ps[b][:, :]), reads=[psk[b]], writes=['modN'])

        tk.op('dve', lambda e: e.tensor_copy(out=screp[:], in_=csil[:].unsqueeze(2).broadcast_to([128, 8, 128])), reads=['csil'], writes=['screp'])
        for ch in range(6):
            emit_mod_chunk(0, ch)

        for l in range(nl):
            if l == 0:
                tk.barrier()
            areset()
            apos[0] = 11520
            gbc = aget([128, 2, D])
            lbt = aget([128, 2, NL, 256])
            junk = aget([128, D], BF16)
            tmpf = aget([128, D])
            hb = [aget([128, D], BF16) for _ in range(2)]
            modA = aget([128, D])
            tk.dma('sp', gbc[:, 0, :], bass.AP(gpre_d, l * D, [[0, 128], [1, D]]), writes=['gbc'])
            tk.dma('sp', gbc[:, 1, :], bass.AP(gpost_d, l * D, [[0, 128], [1, D]]), writes=['gbc'])
            tk.dma('sp', lbt[:].rearrange("p a l c -> p (a l c)"), bass.AP(lbl_d, 0, [[0, 128], [1, 2 * NL * 256]]), writes=['lbt'])
            if l == 0:
                tk.op('dve', lambda e: e.memset(lbl[:], 0.0), writes=['lbl'])
            else:
                mx = tmpf[:, 0:512].rearrange("p (a c) -> p a c", a=2)
                sm = tmpf[:, 512:1024].rearrange("p (a c) -> p a c", a=2)
                tk.op('dve', lambda e: e.tensor_tensor(out=mx, in0=lbt[:, :, 0, :], in1=lbt[:, :, 1, :], op=ALU.max),
                      reads=['lbt'], writes=['tmpf'])
                for l2 in range(2, NL):
                    tk.op('dve', lambda e, l2=l2: e.tensor_tensor(out=mx, in0=mx, in1=lbt[:, :, l2, :], op=ALU.max),
                          reads=['lbt', 'tmpf'], writes=['tmpf'])
                for l2 in range(NL):
                    tk.op('dve', lambda e, l2=l2: e.tensor_tensor(out=lbt[:, :, l2, :], in0=lbt[:, :, l2, :], in1=mx, op=ALU.subtract),
                          reads=['lbt', 'tmpf'], writes=['lbt'])
                tk.op('act', lambda e: e.activation(out=lbt[:].rearrange("p a l c -> p (a l c)"), in_=lbt[:].rearrange("p a l c -> p (a l c)"), func=AF.Exp),
                      reads=['lbt'], writes=['lbt'])
                tk.op('dve', lambda e: e.tensor_tensor(out=sm, in0=lbt[:, :, 0, :], in1=lbt[:, :, 1, :], op=ALU.add),
                      reads=['lbt'], writes=['tmpf'])
                for l2 in range(2, NL):
                    tk.op('dve', lambda e, l2=l2: e.tensor_tensor(out=sm, in0=sm, in1=lbt[:, :, l2, :], op=ALU.add),
                          reads=['lbt', 'tmpf'], writes=['tmpf'])
                tk.op('dve', lambda e: e.reciprocal(out=sm, in_=sm), reads=['tmpf'], writes=['tmpf'])
                tk.op('dve', lambda e: e.tensor_copy(out=lbl[:], in_=lbt[:, :, 1, :]), reads=['lbt'], writes=['lbl'])
                for l2 in range(2, l + 1):
                    tk.op('dve', lambda e, l2=l2: e.tensor_tensor(out=lbl[:], in0=lbl[:], in1=lbt[:, :, l2, :], op=ALU.add),
                          reads=['lbt', 'lbl'], writes=['lbl'])
                tk.op('dve', lambda e: e.tensor_tensor(out=lbl[:], in0=lbl[:], in1=sm, op=ALU.mult), reads=['lbl', 'tmpf'], writes=['lbl'])
            tk.op('dve', lambda e: e.tensor_scalar(out=oml[:], in0=lbl[:], scalar1=-0.5, scalar2=0.5, op0=ALU.mult, op1=ALU.add),
                  reads=['lbl'], writes=['oml'])
            tk.op('dve', lambda e: e.tensor_scalar(out=lbl[:], in0=lbl[:], scalar1=0.5, scalar2=0.5, op0=ALU.mult, op1=ALU.add),
                  reads=['lbl'], writes=['lbl'])
            if stop('M0'):
                break
            tk.op('dve', lambda e: e.scalar_tensor_tensor(out=modA[:], in0=modN[:, D:2 * D], scalar=1.0, in1=gbc[:, 0, :], op0=ALU.add, op1=ALU.mult),
                  reads=['modN', 'gbc'], writes=['modA'])
            tk.op('dve', lambda e: e.tensor_tensor(out=gg[:], in0=modN[:, 2 * D:3 * D], in1=gbc[:, 1, :], op=ALU.mult), reads=['modN', 'gbc'], writes=['gg'])
            if stop('M1'):
                break
            for t in range(NT):
                tk.op('act', lambda e, t=t: e.activation(out=junk[:], in_=x_sb[:, t, :], func=AF.Square, accum_out=ssq[:, t:t + 1]),
                      reads=['x%d' % t], writes=['junk', 'ssq'])
            tk.op('dve', lambda e: e.tensor_scalar(out=rstd[:, 0:8], in0=ssq[:, 0:8], scalar1=1.0 / D, scalar2=EPS, op0=ALU.mult, op1=ALU.add),
                  reads=['ssq'], writes=['rstd'])
            tk.op('act', lambda e: e.activation(out=rstd[:, 0:8], in_=rstd[:, 0:8], func=AF.Ln), reads=['rstd'], writes=['rstd'])
            tk.op('act', lambda e: e.activation(out=rstd[:, 0:8], in_=rstd[:, 0:8], func=AF.Exp, scale=-0.5), reads=['rstd'], writes=['rstd'])
            if stop('M2'):
                break
            for t in range(NT):
                hbi = t % 2
                tk.op('dve', lambda e, t=t: e.scalar_tensor_tensor(out=tmpf[:], in0=x_sb[:, t, :], scalar=rstd[:, t:t + 1], in1=modA[:],
                                                                   op0=ALU.mult, op1=ALU.mult),
                      reads=['x%d' % t, 'rstd', 'modA'], writes=['tmpf'])
                tk.op('dve', lambda e: e.tensor_tensor(out=hb[hbi][:], in0=tmpf[:], in1=modN[:, 0:D], op=ALU.add),
                      reads=['tmpf', 'modN'], writes=['hb%d' % hbi])
                if stop('M3'):
                    continue
                b = nb()
                for kc in range(8):
                    tk.op('pe', lambda e, kc=kc: e.transpose(out=psb16(b)[:, kc * 128:(kc + 1) * 128], in_=hb[hbi][:, kc * 128:(kc + 1) * 128], identity=IDB),
                          reads=['hb%d' % hbi, 'cb16'], writes=[psk[b]])
                tk.op('act', lambda e, t=t: e.copy(out=hT[:, :, t * 128:(t + 1) * 128], in_=psb16(b)[:, :].rearrange("p (k c) -> p k c", k=8)),
                      reads=[psk[b]], writes=['hT%d' % t])

            if stop('M'):
                break
            tk.barrier()
            areset()
            qT = aget([128, 4, T], BF16)
            kT = aget([128, 4, T], BF16)
            ckT = aget([128, 4, 512], BF16)
            vaug = aget([128, NT, 8, 66], BF16)
            cvaug = aget([128, 4, 8, 66], BF16)
            sga = aget([128, NT, 512], BF16)
            expT = aget([128, 7, 8, 128], BF16)
            Eb = [aget([128, 8, 128], BF16) for _ in range(3)]
            Pb = [aget([128, 8, 128], BF16) for _ in range(2)]
            hk = [Eb[0].bitcast(F32) if False else None, None]
            ost = [aget([128, 512]) for _ in range(2)]
            rden = aget([128, 8])
            otmp = aget([128, 8, 64])
            ckb = aget([128, 4, 512], BF16)
            hkA = aget([128, 8, 128])
            hkB = aget([128, 8, 128])
            hk = [hkA, hkB]
            tk.op('pool', lambda e: e.memset(vaug[:, :, :, 64:66], 1.0), writes=['vaug'])
            tk.op('dve', lambda e: e.tensor_copy(out=cvaug[:, :, :, 64:66].rearrange("p a b c -> p (a b) c"),
                                                 in_=flags[:, 0:1].unsqueeze(2).broadcast_to([128, 32, 2])),
                  reads=['flags'], writes=['cvaug'])
            def toep_dma(di):
                dl = di - 3
                hi = di % 2
                for qr in range(2):
                    for krl in range(2):
                        off = ((l * 8) * 23 + (2 * dl + krl - qr + 11)) * 127
                        src = bass.AP(tpad_d, off, [[1, 64], [23 * 127, 8], [1, 64]])
                        tk.dma('sp', hk[hi][qr * 64:(qr + 1) * 64, :, krl * 64:(krl + 1) * 64], src, writes=['hk%d_%d' % (hi, qr * 2 + krl)])

            def toep_mm(di):
                hi = di % 2
                bA = nb()
                bB = nb()
                for h in range(8):
                    b = bA if h % 2 == 0 else bB
                    o = ps[b][:, (h // 2) * 128:(h // 2 + 1) * 128]
                    tk.op('pe', lambda e, h=h, o=o: e.matmul(o, lhsT=hk[hi][:, h, :], rhs=J2, start=True, stop=False),
                          reads=['hk%d_%d' % (hi, x) for x in range(4)] + ['cf32'], writes=[psk[b]])
                    tk.op('pe', lambda e, o=o: e.matmul(o, lhsT=IDF, rhs=CMT, start=False, stop=True),
                          reads=['cf32'], writes=[psk[b]])
                for bi, b in enumerate((bA, bB)):
                    tk.op('act', lambda e, bi=bi, b=b: e.activation(out=expT[:, di, bi * 4:(bi + 1) * 4, :].rearrange("p a c -> p (a c)"),
                                                                    in_=ps[b][:, :], func=AF.Exp),
                          reads=[psk[b]], writes=['expT%d' % di])

            toep_dma(0)
            toep_dma(1)
            if stop('A0'):
                break
            ckk = 'ckb'
            tk.dma('pool', ckb[:], ctxk_d.ap()[l].rearrange("(c p) n -> p c n", p=128), writes=[ckk])
            for c in range(4):
                b = nb()
                for pr in range(4):
                    tk.op('pe', lambda e, pr=pr: e.transpose(out=psb16(b)[:, pr * 128:(pr + 1) * 128], in_=ckb[:, c, pr * 128:(pr + 1) * 128], identity=IDB),
                          reads=[ckk, 'cb16'], writes=[psk[b]])
                tk.op('act', lambda e, c=c: e.copy(out=ckT[:, :, c * 128:(c + 1) * 128], in_=psb16(b)[:, 0:512].rearrange("p (k c) -> p k c", k=4)),
                      reads=[psk[b]], writes=['ckT'])
            tk.dma('pool', ckb[:], ctxv_d.ap()[l].rearrange("(c p) n -> p c n", p=128), reads=[], writes=[ckk])
            tk.op('pool', lambda e: e.tensor_copy(out=cvaug[:, :, :, 0:64], in_=ckb[:].rearrange("p c (h d) -> p c h d", h=8)), reads=[ckk], writes=['cvaug'])
            if stop('A1'):
                break
            toep_mm(0)
            toep_dma(2)
            wi = next_w()
            for pr in range(4):
                for g in range(2):
                    b = nb()
                    proj_fm(wi, pr, g, b)
                    tk.op('act', lambda e, pr=pr, g=g: e.copy(out=qT[:, pr, g * 512:(g + 1) * 512], in_=ps[b][:, :]), reads=[psk[b]], writes=['qT%d_%d' % (pr, g)])
            if stop('A1a'):
                break
            toep_mm(1)
            toep_dma(3)
            wi = next_w()
            for pr in range(4):
                for g in range(2):
                    b = nb()
                    proj_fm(wi, pr, g, b)
                    tk.op('act', lambda e, pr=pr, g=g: e.copy(out=kT[:, pr, g * 512:(g + 1) * 512], in_=ps[b][:, :]), reads=[psk[b]], writes=['kTf%d_%d' % (pr, g)])
            toep_mm(2)
            toep_dma(4)
            for t in range(NT):
                b = nb()
                proj_tm(wi, 0, 512, t, b)
                oi = t % 2
                tk.op('dve', lambda e: e.tensor_copy(out=ost[oi][:], in_=ps[b][:, :]), reads=[psk[b]], writes=['ost%d' % oi])
                tk.dma('sp', nk_d.ap()[l, t * 128:(t + 1) * 128, :], ost[oi][:], reads=['ost%d' % oi])
            toep_mm(3)
            toep_dma(5)
            if stop('A1b'):
                break
            wi = next_w()
            for t in range(NT):
                b = nb()
                proj_tm(wi, 0, 512, t, b)
                oi = t % 2
                tk.op('dve', lambda e: e.tensor_copy(out=ost[oi][:], in_=ps[b][:, :]), reads=[psk[b]], writes=['ost%d' % oi])
                tk.op('act', lambda e, t=t: e.copy(out=vaug[:, t, :, 0:64], in_=ost[oi][:].rearrange("p (h d) -> p h d", h=8)),
                      reads=['ost%d' % oi], writes=['vaug%d' % t])
                tk.dma('sp', nv_d.ap()[l, t * 128:(t + 1) * 128, :], ost[oi][:], reads=['ost%d' % oi])
            if stop('A1c'):
                break
            toep_mm(4)
            toep_dma(6)
            wi = next_w()
            for t in range(NT):
                b = nb()
                proj_tm(wi, 0, 512, t, b)
                tk.op('act', lambda e, t=t: e.activation(out=sga[:, t, :], in_=ps[b][:, :], func=AF.Silu), reads=[psk[b]], writes=['sga%d' % t])
            toep_mm(5)
            toep_mm(6)
            if stop('A2'):
                break
            OA, OB = 6, 7
            spairs = [(0, 1), (2, 3), (4, 5)]
            allsteps = []
            for j in range(8):
                st_ = [('l', kt) for kt in KT[j]] + [('c', c) for c in range(4)]
                for si_, (kind, idx) in enumerate(st_):
                    allsteps.append((j, si_, len(st_), kind, idx))

            def emit_S(k):
                j, si_, ns_, kind, idx = allsteps[k]
                sA, sB = spairs[k % 3]
                for h in range(8):
                    b = sA if h % 2 == 0 else sB
                    r0 = (h % 2) * 64
                    ksrc = kT[r0:r0 + 64, h // 2, idx * 128:(idx + 1) * 128] if kind == 'l' else ckT[r0:r0 + 64, h // 2, idx * 128:(idx + 1) * 128]
                    tk.op('pe', lambda e, h=h, b=b, ksrc=ksrc, r0=r0: e.matmul(ps[b][:, (h // 2) * 128:(h // 2 + 1) * 128], lhsT=ksrc,
                                                                              rhs=qT[r0:r0 + 64, h // 2, j * 128:(j + 1) * 128], start=True, stop=True),
                          reads=['qT%d_%d' % (h // 2, j // 4), ('kTf%d_%d' % (h // 2, idx // 4)) if kind == 'l' else 'ckT'], writes=[psk[b]])

            def emit_rest(k):
                j, si_, ns_, kind, idx = allsteps[k]
                sA, sB = spairs[k % 3]
                sl = k % 3
                big = psbig[sA // 2]
                if kind == 'l':
                    jk = JKI[(j, idx)]
                    for hf in range(2):
                        tk.op('act', lambda e, hf=hf: e.activation(
                            out=Eb[sl][:, :, hf * 64:(hf + 1) * 64],
                            in_=big[:, :].rearrange("p (a c) -> p a c", a=8)[:, :, hf * 64:(hf + 1) * 64],
                            func=AF.Exp, scale=0.125, bias=rowb[:, jk * 2 + hf:jk * 2 + hf + 1]),
                            reads=[psk[sA], psk[sB], 'rowb'], writes=['Eb%d' % sl])
                    di = idx - j + 3
                    pl = k % 2
                    tk.op('dve', lambda e, di=di: e.tensor_tensor(out=Pb[pl][:], in0=Eb[sl][:], in1=expT[:, di, :, :], op=ALU.mult),
                          reads=['Eb%d' % sl, 'expT%d' % di], writes=['Pb%d' % pl])
                    lhs, lk = Pb[pl], 'Pb%d' % pl
                    vsrc, vk = vaug, 'vaug%d' % idx
                else:
                    tk.op('act', lambda e: e.activation(out=Eb[sl][:].rearrange("p a c -> p (a c)"), in_=big[:, :], func=AF.Exp, scale=0.125),
                          reads=[psk[sA], psk[sB]], writes=['Eb%d' % sl])
                    lhs, lk = Eb[sl], 'Eb%d' % sl
                    vsrc, vk = cvaug, 'cvaug'
                for e_ in range(8):
                    h = 2 * (e_ % 4) + e_ // 4
                    ob = OA if e_ < 4 else OB
                    tk.op('pe', lambda e, e_=e_, h=h, ob=ob, lhs=lhs, vsrc=vsrc: e.matmul(
                        ps[ob][:, (e_ % 4) * 66:(e_ % 4) * 66 + 66], lhsT=lhs[:, e_, :], rhs=vsrc[:, idx, h, :],
                        start=(si_ == 0 and e_ % 4 == 0), stop=(si_ == ns_ - 1), skip_group_check=True),
                        reads=[lk, vk] + (['vaug'] if kind == 'l' else []), writes=[psk[ob]])
                if si_ == ns_ - 1:
                    for bi, ob in enumerate((OA, OB)):
                        tk.op('dve', lambda e, bi=bi, ob=ob: e.reciprocal(out=rden[:, bi * 4:(bi + 1) * 4],
                                                                          in_=ps[ob][:, 0:264].rearrange("p (a c) -> p a c", a=4)[:, :, 64]),
                              reads=[psk[ob]], writes=['rden%d' % bi])
                    for bi, ob in enumerate((OA, OB)):
                        tk.op('dve', lambda e, bi=bi, ob=ob: e.tensor_tensor(
                            out=otmp[:, bi:8:2, :], in0=ps[ob][:, 0:264].rearrange("p (a c) -> p a c", a=4)[:, :, 0:64],
                            in1=rden[:, bi * 4:(bi + 1) * 4].unsqueeze(2).broadcast_to([128, 4, 64]), op=ALU.mult),
                            reads=[psk[ob], 'rden%d' % bi], writes=['otmp%d' % bi])
                    tk.op('dve', lambda e: e.tensor_tensor(out=mixed[:, j, 0:512], in0=otmp[:].rearrange("p h d -> p (h d)"), in1=sga[:, j, :], op=ALU.mult),
                          reads=['otmp0', 'otmp1', 'sga%d' % j], writes=['mixed%d' % j])

            emit_S(0)
            emit_S(1)
            for k in range(len(allsteps)):
                if k + 2 < len(allsteps):
                    emit_S(k + 2)
                emit_rest(k)

            if stop('A'):
                break
            tk.barrier()
            areset()
            qh = aget([128, NT, 256], BF16)
            sgb = aget([128, NT, 256])
            qE = sgb.rearrange("p a b -> p (a b)")[:, 0:1024].bitcast(BF16).rearrange("p (a b) -> p a b", a=NT)
            kE = sgb.rearrange("p a b -> p (a b)")[:, 1024:2048].bitcast(BF16).rearrange("p (a b) -> p a b", a=NT)
            vh = aget([128, NT, 256], BF16)
            vhmF = aget([128, 4096])
            vhm = vhmF.bitcast(BF16).rearrange("p (q t c) -> p q t c", q=4, t=NT)
            A_ = vhmF[:, 0:2048].rearrange("p (e c) -> p e c", e=64)
            B_ = vhmF[:, 2048:4096].rearrange("p (e c) -> p e c", e=64)
            sgr = aget([128, NT, 256], BF16)
            fS = aget([128, 4096])
            fbuf = fS[:, 0:2048].rearrange("p (a b) -> p a b", a=NT)
            SinPm = fS.bitcast(BF16).rearrange("p (h c e) -> p h c e", h=4, c=32)
            lfbuf = aget([128, NT, 256])
            kETm = lfbuf.rearrange("p a b -> p (a b)").bitcast(BF16).rearrange("p (h t) -> p h t", h=4)
            osq = lfbuf
            tmpE = [aget([128, 512]) for _ in range(2)]
            qET = aget([128, 2, T], BF16)
            ATm = [aget([128, 4, 128], BF16) for _ in range(2)]
            osum = aget([128, NT, 256])
            gs3 = aget([128, 2, 32, 3])
            es3 = aget([128, 2, 32, 3])
            S0b = aget([128, 2, 64])
            nsb = aget([128, 4, 2, 64])
            hss = aget([128, 32])
            tk.dma('sp', ghgB[:], bass.AP(ghg_d, l * 256, [[0, 128], [1, 256]]), writes=['ghgB'])
            wi = next_w()
            for t in range(NT):
                b = nb()
                proj_tm(wi, 0, 512, t, b)
                tk.op('act', lambda e, t=t: e.activation(out=qh[:, t, :], in_=ps[b][:, 0:256], func=AF.Silu), reads=[psk[b]], writes=['qh%d' % t])
                tk.op('act', lambda e, t=t: e.activation(out=sgb[:, t, :], in_=ps[b][:, 256:512], func=AF.Tanh, scale=0.5), reads=[psk[b]], writes=['sQ'])
            wi = next_w()
            for t in range(NT):
                b = nb()
                proj_tm(wi, 0, 512, t, b)
                tk.op('act', lambda e, t=t: e.activation(out=sgr[:, t, :], in_=ps[b][:, 256:512], func=AF.Silu), reads=[psk[b]], writes=['sgr%d' % t])
                tk.op('dve', lambda e, t=t: e.tensor_copy(out=vh[:, t, :], in_=ps[b][:, 0:256]), reads=[psk[b]], writes=['vh%d' % t])

            if stop('R0'):
                break
            rstop = False
            for dr in range(2):
                if l + 1 < nl:
                    mod_dma(l + 1, 3 * dr)
                lb_bc = lbl[:, dr, :].unsqueeze(1).broadcast_to([128, NT, 256])
                oml_bc = oml[:, dr, :].unsqueeze(1).broadcast_to([128, NT, 256])
                tk.op('dve', lambda e: e.tensor_tensor(out=fbuf[:], in0=sgb[:], in1=oml_bc, op=ALU.mult), reads=['sQ', 'oml'], writes=['fS'])
                tk.op('dve', lambda e: e.tensor_tensor(out=fbuf[:], in0=fbuf[:], in1=lb_bc, op=ALU.add), reads=['fS', 'lbl'], writes=['fS'])
                tk.op('act', lambda e: e.activation(out=lfbuf[:].rearrange("p a b -> p (a b)"), in_=fbuf[:].rearrange("p a b -> p (a b)"), func=AF.Ln),
                      reads=['fS'], writes=['lK'])
                tk.op('dve', lambda e: e.tensor_scalar(out=fbuf[:], in0=fbuf[:], scalar1=-1.0, scalar2=1.0, op0=ALU.mult, op1=ALU.add),
                      reads=['fS'], writes=['fS'])
                bS = nb()
                for t in range(NT):
                    for pr in range(2):
                        tt_ = t if dr == 0 else 7 - t
                        tk.op('pe', lambda e, t=t, pr=pr, tt_=tt_: e.matmul(ps[bS][:, (pr * 8 + tt_) * 8:(pr * 8 + tt_) * 8 + 8], lhsT=lfbuf[:, t, pr * 128:(pr + 1) * 128],
                                                                   rhs=SEL[dr], start=True, stop=True),
                              reads=['lK', 'cf32'], writes=[psk[bS]])
                psS = ps[bS][:, 0:128].rearrange("p (a c r) -> p a c r", a=2, c=32)
                tk.op('dve', lambda e: e.tensor_copy(out=gs3[:, :, :, 0:2], in_=psS), reads=[psk[bS]], writes=['gs3'])
                tk.op('dve', lambda e: e.tensor_tensor(out=gs3[:, :, :, 2], in0=gs3[:, :, :, 1], in1=gs3[:, :, :, 0], op=ALU.subtract),
                      reads=['gs3'], writes=['gs3'])
                tk.op('act', lambda e: e.activation(out=es3[:].rearrange("p a c r -> p (a c r)"), in_=gs3[:].rearrange("p a c r -> p (a c r)"), func=AF.Exp),
                      reads=['gs3'], writes=['es3'])
                tk.op('dve', lambda e: e.tensor_scalar(out=es3[:, :, 8:32:8, 0:2], in0=es3[:, :, 8:32:8, 0:2], scalar1=flags[:, 1:2], scalar2=None, op0=ALU.mult),
                      reads=['es3', 'flags'], writes=['es3'])
                for tp in range(4):
                    b = nb()
                    for i in range(2):
                        t = 2 * tp + i
                        tk.op('pe', lambda e, t=t, i=i: e.matmul(ps[b][:, i * 256:(i + 1) * 256], lhsT=TR[dr], rhs=lfbuf[:, t, :], start=True, stop=True),
                              reads=['cf32', 'lK'], writes=[psk[b]])
                    tk.op('act', lambda e: e.activation(out=tmpE[0][:], in_=ps[b][:, :], func=AF.Exp), reads=[psk[b]], writes=['tmpE0'])
                    tk.op('act', lambda e: e.activation(out=tmpE[1][:], in_=ps[b][:, :], func=AF.Exp, scale=-1.0), reads=[psk[b]], writes=['tmpE1'])
                    tk.op('dve', lambda e, tp=tp: e.tensor_tensor(out=qE[:, 2 * tp:2 * tp + 2, :].rearrange("p a c -> p (a c)"),
                                                                  in0=qh[:, 2 * tp:2 * tp + 2, :].rearrange("p a c -> p (a c)"), in1=tmpE[0][:], op=ALU.mult),
                          reads=['qh%d' % (2 * tp), 'qh%d' % (2 * tp + 1), 'tmpE0'], writes=['sQ', 'qk%d' % tp])
                    tk.op('dve', lambda e, tp=tp: e.tensor_tensor(out=kE[:, 2 * tp:2 * tp + 2, :].rearrange("p a c -> p (a c)"),
                                                                  in0=fbuf[:, 2 * tp:2 * tp + 2, :].rearrange("p a c -> p (a c)"), in1=tmpE[1][:], op=ALU.mult),
                          reads=['fS', 'tmpE1'], writes=['sQ', 'qk%d' % tp])
                for cp in range(4):
                    tk.op('dve', lambda e, cp=cp: e.tensor_scalar(out=vhm[:, cp, :, :], in0=vh[:], scalar1=QM[:, cp:cp + 1], scalar2=None, op0=ALU.mult),
                          reads=['vh%d' % t_ for t_ in range(NT)] + ['cf32'], writes=['vhm'])
                tk._deps('act', [], ['lK'])
                for t in range(NT):
                    b = nb()
                    for pr in range(2):
                        tk.op('pe', lambda e, t=t, pr=pr: e.transpose(out=psb16(b)[:, pr * 128:(pr + 1) * 128], in_=qE[:, t, pr * 128:(pr + 1) * 128], identity=IDB),
                              reads=['qk%d' % (t // 2), 'cb16'], writes=[psk[b]], war_only=['sQ'])
                        tk.op('pe', lambda e, t=t, pr=pr: e.transpose(out=psb16(b)[:, (2 + pr) * 128:(3 + pr) * 128], in_=kE[:, t, pr * 128:(pr + 1) * 128], identity=IDB),
                              reads=['qk%d' % (t // 2), 'cb16'], writes=[psk[b]], war_only=['sQ'])
                    tk.op('act', lambda e, t=t: e.copy(out=qET[:, :, t * 128:(t + 1) * 128], in_=psb16(b)[:, 0:256].rearrange("p (a c) -> p a c", a=2)),
                          reads=[psk[b]], writes=['qET%d' % t])
                    for h in range(4):
                        tk.op('act', lambda e, t=t, h=h: e.activation(out=kETm[:, h, t * 128:(t + 1) * 128], in_=psb16(b)[:, (2 + h // 2) * 128:(3 + h // 2) * 128],
                                                                      func=AF.Copy, scale=HM[:, h % 2:h % 2 + 1]),
                              reads=[psk[b], 'cf32'], writes=['kT%d_%d' % (t, h)])
                if stop('R1'):
                    rstop = True
                    break
                tk.dma('sp', S0b[:], s0_d.ap()[l, dr], writes=['S0b'])
                kvb_all = [[nb() for _ in range(4)] for _ in range(2)]
                for pr in range(2):
                    kvb = kvb_all[pr]
                    for c in range(32):
                        cq = c if dr == 0 else 31 - c
                        b = kvb[cq // 8]
                        for h in (2 * pr, 2 * pr + 1):
                            o = ps[b][(h % 2) * 64:(h % 2) * 64 + 64, (cq % 8) * 64:(cq % 8) * 64 + 64]
                            tk.op('pe', lambda e, c=c, h=h, o=o: e.matmul(o, lhsT=kE[:, c // 4, h * 64:(h + 1) * 64], rhs=vhm[:, c % 4, c // 4, h * 64:(h + 1) * 64],
                                                                          start=True, stop=True),
                                  reads=['sQ', 'vhm'], writes=[psk[b]])
                for pr in range(2):
                    kvb = kvb_all[pr]
                    for g in range(4):
                        tk.op('dve', lambda e, g=g: e.tensor_tensor(out=B_[:, :, g * 8:(g + 1) * 8].rearrange("p e c -> p c e"),
                                                                    in0=ps[kvb[g]][:, :].rearrange("p (c e) -> p c e", c=8),
                                                                    in1=es3[:, pr, g * 8:(g + 1) * 8, 2].unsqueeze(2).broadcast_to([128, 8, 64]), op=ALU.mult),
                              reads=[psk[kvb[g]], 'es3'], writes=['vhm'])
                    if dr == 0 and pr == 0:
                        wi = next_w()
                        for t in range(NT):
                            b = kvb[t % 4]
                            proj_tm(wi, 0, 256, t, b)
                            tk.op('act', lambda e, t=t: e.activation(out=sgb[:, t, :], in_=ps[b][:, 0:256], func=AF.Tanh, scale=0.5), reads=[psk[b]], writes=['sQ'])
                    if dr == 1 and pr == 0:
                        wi = next_w()
                        uT_h = qh[:].rearrange("p a b -> p (a b)").rearrange("p (c t) -> p c t", c=2)
                        sgf_h = qE
                        hb_ = 0
                        for ci in range(2):
                            for g in range(2):
                                b = kvb[hb_ % 4]
                                hb_ += 1
                                proj_fm(wi, ci, g, b)
                                tk.op('act', lambda e, ci=ci, g=g: e.copy(out=uT_h[:, ci, g * 512:(g + 1) * 512], in_=ps[b][:, :]), reads=[psk[b]], writes=['qh%d' % t_ for t_ in range(NT)])
                        for t in range(NT):
                            b = kvb[hb_ % 4]
                            hb_ += 1
                            proj_tm(wi, 256, 256, t, b)
                            tk.op('act', lambda e, t=t: e.activation(out=sgf_h[:, t, :], in_=ps[b][:, 0:256], func=AF.Silu), reads=[psk[b]], writes=['sQ'])
                    tk.op('dve', lambda e: e.scalar_tensor_tensor(out=B_[:, :, 0], in0=S0b[:, pr, :], scalar=es3[:, pr, 0, 1:2], in1=B_[:, :, 0], op0=ALU.mult, op1=ALU.add),
                          reads=['S0b', 'es3', 'vhm'], writes=['vhm'])
                    tk.op('dve', lambda e: e.tensor_copy(out=A_[:], in_=es3[:, pr, :, 1].unsqueeze(1).broadcast_to([128, 64, 32])), reads=['es3', 'vhm'], writes=['vhm'])
                    tk.op('dve', lambda e: e.memset(A_[:, :, 0:1], 0.0), reads=['vhm'], writes=['vhm'])
                    tk.op('dve', lambda e: e.tensor_tensor_scan(out=B_[:].rearrange("p e c -> p (e c)"), data0=A_[:].rearrange("p e c -> p (e c)"),
                                                                data1=B_[:].rearrange("p e c -> p (e c)"), initial=0.0, op0=ALU.mult, op1=ALU.add),
                          reads=['vhm'], writes=['vhm'])
                    tk.op('dve', lambda e: e.tensor_copy(out=nsb[:, :, pr, :], in_=B_[:, :, 7:32:8].rearrange("p e k -> p k e")), reads=['vhm'], writes=['nsb'])
                    for h2 in range(2):
                        tk.op('dve', lambda e, h2=h2: e.scalar_tensor_tensor(out=SinPm[:, 2 * pr + h2, 1:32, :], in0=B_[:, :, 0:31].rearrange("p e c -> p c e"),
                                                                             scalar=HM[:, h2:h2 + 1], in1=es3[:, pr, 1:32, 0].unsqueeze(2).broadcast_to([128, 31, 64]),
                                                                             op0=ALU.mult, op1=ALU.mult),
                              reads=['vhm', 'es3', 'cf32'], writes=['fS'])
                        tk.op('dve', lambda e, h2=h2: e.scalar_tensor_tensor(out=SinPm[:, 2 * pr + h2, 0, :], in0=S0b[:, pr, :], scalar=HM[:, h2:h2 + 1],
                                                                             in1=es3[:, pr, 0, 0:1].broadcast_to([128, 64]), op0=ALU.mult, op1=ALU.mult),
                              reads=['S0b', 'es3', 'cf32'], writes=['fS'])
                nsb3 = nsb[:].rearrange("p s a e -> p s (a e)")
                if dr == 0:
                    tk.dma('sp', ns_d.ap()[l, 0].rearrange("s p a e -> p s (a e)"), nsb3, reads=['nsb'])
                else:
                    for k in range(4):
                        tk.dma('sp', ns_d.ap()[l, 1, 3 - k].rearrange("p a e -> p (a e)"), nsb3[:, k, :], reads=['nsb'])
                if l + 1 < nl:
                    mod_mm(l + 1, 3 * dr)
                    mod_dma(l + 1, 3 * dr + 1)
                if stop('R2'):
                    rstop = True
                    break
                for t in range(NT):
                    bA_ = nb()
                    ai = t % 2
                    for h in range(4):
                        tk.op('pe', lambda e, t=t, h=h: e.matmul(ps[bA_][:, h * 128:(h + 1) * 128], lhsT=kETm[:, h, t * 128:(t + 1) * 128],
                                                                 rhs=qET[:, h // 2, t * 128:(t + 1) * 128], start=True, stop=True),
                              reads=['kT%d_%d' % (t, h), 'qET%d' % t], writes=[psk[bA_]], war_only=['lK'])
                    tk.op('dve', lambda e: e.tensor_tensor(out=ATm[ai][:], in0=ps[bA_][:, :].rearrange("p (a c) -> p a c", a=4),
                                                           in1=TRI[dr].unsqueeze(1).broadcast_to([128, 4, 128]), op=ALU.mult),
                          reads=[psk[bA_], 'cb16'], writes=['ATm%d' % ai])
                    bO = nb()
                    for h in range(4):
                        tk.op('pe', lambda e, t=t, h=h: e.matmul(ps[bO][:, h * 64:(h + 1) * 64], lhsT=ATm[ai][:, h, :], rhs=vh[:, t, h * 64:(h + 1) * 64],
                                                                 start=True, stop=False, skip_group_check=True),
                              reads=['ATm%d' % ai, 'vh%d' % t], writes=[psk[bO]])
                        for cp in range(4):
                            tk.op('pe', lambda e, t=t, h=h, cp=cp: e.matmul(ps[bO][cp * 32:(cp + 1) * 32, h * 64:(h + 1) * 64],
                                                                            lhsT=qET[:, h // 2, t * 128 + cp * 32:t * 128 + cp * 32 + 32],
                                                                            rhs=SinPm[:, h, (4 * t + cp) if dr == 0 else 31 - (4 * t + cp), :], start=False, stop=(cp == 3), skip_group_check=True,
                                                                            tile_position=(0, cp * 32)),
                                  reads=['qET%d' % t, 'fS'], writes=[psk[bO]])
                    if l + 1 < nl and t == 3:
                        mod_mm(l + 1, 3 * dr + 1)
                        mod_dma(l + 1, 3 * dr + 2)
                    if l + 1 < nl and t == 7:
                        mod_mm(l + 1, 3 * dr + 2)
                    if dr == 0:
                        tk.op('act', lambda e, t=t: e.copy(out=osum[:, t, :], in_=ps[bO][:, 0:256]), reads=[psk[bO]], writes=['osum%d' % t])
                    else:
                        tk.op('dve', lambda e, t=t: e.tensor_tensor(out=osum[:, t, :], in0=osum[:, t, :], in1=ps[bO][:, 0:256], op=ALU.add),
                              reads=[psk[bO], 'osum%d' % t], writes=['osum%d' % t])
                if stop('R3'):
                    rstop = True
                    break
            if rstop:
                break
            tk.op('dve', lambda e: e.tensor_tensor(out=osq[:], in0=osum[:], in1=osum[:], op=ALU.mult), reads=['osum%d' % t_ for t_ in range(NT)], writes=['lK'])
            tk.op('dve', lambda e: e.tensor_reduce(out=hss[:], in_=osq[:].rearrange("p t (h d) -> p (t h) d", h=4), axis=AX.X, op=ALU.add),
                  reads=['lK'], writes=['hss'])
            tk.op('dve', lambda e: e.tensor_scalar(out=hss[:], in0=hss[:], scalar1=1.0 / 64, scalar2=EPS, op0=ALU.mult, op1=ALU.add), reads=['hss'], writes=['hss'])
            tk.op('act', lambda e: e.activation(out=hss[:], in_=hss[:], func=AF.Ln), reads=['hss'], writes=['hss'])
            tk.op('act', lambda e: e.activation(out=hss[:], in_=hss[:], func=AF.Exp, scale=-0.5), reads=['hss'], writes=['hss'])
            tk.op('dve', lambda e: e.tensor_tensor(out=osum[:].rearrange("p t (h d) -> p (t h) d", h=4), in0=osum[:].rearrange("p t (h d) -> p (t h) d", h=4),
                                                   in1=hss[:].unsqueeze(2).broadcast_to([128, 32, 64]), op=ALU.mult),
                  reads=['osum%d' % t_ for t_ in range(NT)] + ['hss'], writes=['osum%d' % t_ for t_ in range(NT)])
            tk.op('dve', lambda e: e.tensor_tensor(out=osum[:], in0=osum[:], in1=ghgB[:].unsqueeze(1).broadcast_to([128, NT, 256]), op=ALU.mult),
                  reads=['osum%d' % t_ for t_ in range(NT)] + ['ghgB'], writes=['osum%d' % t_ for t_ in range(NT)])
            tk.op('dve', lambda e: e.tensor_tensor(out=mixed[:, :, 512:768], in0=osum[:], in1=sgr[:], op=ALU.mult),
                  reads=['osum%d' % t_ for t_ in range(NT)] + ['sgr%d' % t_ for t_ in range(NT)], writes=['mixed%d' % j for j in range(8)])

            if stop('R'):
                break
            tk.barrier()
            areset()
            uT = aget([128, 2, T], BF16)
            sgf = aget([128, NT, 256], BF16)
            ucs = aget([128, NT, 2, 256], BF16)
            yT = aget([128, 2, T], BF16)
            csnb = [aget([128, 2, 1024], BF16) for _ in range(4)]
            assert apos[0] + 2048 <= 11520
            wfs = aget([128, 2, 256])
            wfb = aget([128, 2, 256], BF16)
            junk = aget([128, 512], BF16)
            tmpf = aget([128, D])
            for kt_ in range(4):
                tk.dma('sp', csnb[kt_][:], csn_d.ap()[kt_], writes=['csnb%d' % kt_])
            assert True
            tk.dma('sp', wfs[:], wfn_d.ap()[l].rearrange("(c p) n -> p c n", p=128), writes=['wfs'])
            tk.op('pool', lambda e: e.tensor_copy(out=wfb[:], in_=wfs[:]), reads=['wfs'], writes=['wfb'])
            for t in range(NT):
                b = nb()
                for cs in range(2):
                    for ct in range(2):
                        tk.op('pe', lambda e, t=t, cs=cs, ct=ct: e.matmul(ps[b][:, cs * 256 + ct * 128:cs * 256 + ct * 128 + 128], lhsT=uT[:, ct, t * 128:(t + 1) * 128],
                                                                          rhs=C4S4[:, 3 + cs * 2 + ct, :], start=True, stop=True),
                              reads=['uT', 'cb16'], writes=[psk[b]])
                tk.op('act', lambda e, t=t: e.copy(out=ucs[:, t, :, :].rearrange("p a c -> p (a c)"), in_=ps[b][:, :]), reads=[psk[b]], writes=['ucs%d' % t])
            if stop('F0'):
                break
            yb = [nb() for _ in range(4)]
            for kt_ in range(8):
                ci = kt_ % 4
                if kt_ >= 4:
                    tk.dma('sp', csnb[ci][:], csn_d.ap()[kt_], writes=['csnb%d' % ci])
                for ct in range(2):
                    for g in range(2):
                        b = yb[ct * 2 + g]
                        for cs in range(2):
                            tk.op('pe', lambda e, kt_=kt_, ct=ct, g=g, cs=cs: e.matmul(ps[b][:, :], lhsT=ucs[:, kt_, cs, ct * 128:(ct + 1) * 128],
                                                                                       rhs=csnb[ci][:, cs, g * 512:(g + 1) * 512],
                                                                                       start=(kt_ == 0 and cs == 0), stop=(kt_ == 7 and cs == 1)),
                                  reads=['ucs%d' % kt_, 'csnb%d' % ci], writes=[psk[b]])
            for ct in range(2):
                for g in range(2):
                    b = yb[ct * 2 + g]
                    tk.op('act', lambda e, ct=ct, g=g, b=b: e.copy(out=yT[:, ct, g * 512:(g + 1) * 512], in_=ps[b][:, :]), reads=[psk[b]], writes=['yT%d_%d' % (ct, g)])
            for t in range(NT):
                b = nb()
                for ct in range(2):
                    tk.op('pe', lambda e, t=t, ct=ct: e.matmul(ps[b][:, 0:256], lhsT=yT[:, ct, t * 128:(t + 1) * 128], rhs=wfb[:, ct, :], start=(ct == 0), stop=(ct == 1)),
                          reads=['yT%d_%d' % (ct, t // 4), 'wfb'], writes=[psk[b]])
                tk.op('dve', lambda e, t=t: e.tensor_tensor(out=mixed[:, t, 768:1024], in0=ps[b][:, 0:256], in1=sgf[:, t, :], op=ALU.mult),
                      reads=[psk[b], 'sgf'], writes=['mixed%d' % t])

            if dbg and l == nl - 1:
                for t in range(NT):
                    tk.dma('sp', dbg_d.ap()[t * 128:(t + 1) * 128, :], mixed[:, t, :], reads=['mixed%d' % t])

            if stop('F1'):
                break
            w0 = next_w()
            w1 = next_w(prefetch=False)
            for t in range(NT):
                b = nb()
                for kc in range(8):
                    tk.op('pe', lambda e, t=t, kc=kc: e.transpose(out=psb16(b)[:, kc * 128:(kc + 1) * 128], in_=mixed[:, t, kc * 128:(kc + 1) * 128], identity=IDB),
                          reads=['mixed%d' % t, 'cb16'], writes=[psk[b]])
                tk.op('act', lambda e, t=t: e.copy(out=hT[:, :, t * 128:(t + 1) * 128], in_=psb16(b)[:, :].rearrange("p (k c) -> p k c", k=8)),
                      reads=[psk[b]], writes=['hT%d' % t])
            for t in range(NT):
                bb = [nb(), nb()]
                for hf, wi in enumerate((w0, w1)):
                    proj_tm(wi, 0, 512, t, bb[hf])
                    tk.op('act', lambda e, t=t, hf=hf: e.activation(out=junk[:], in_=ps[bb[hf]][:, :], func=AF.Square, accum_out=ssq[:, 8 + hf:9 + hf]),
                          reads=[psk[bb[hf]]], writes=['junk', 'ssq'])
                tk.op('dve', lambda e: e.tensor_tensor(out=rstd[:, 8:9], in0=ssq[:, 8:9], in1=ssq[:, 9:10], op=ALU.add), reads=['ssq'], writes=['rstd'])
                tk.op('dve', lambda e: e.tensor_scalar(out=rstd[:, 8:9], in0=rstd[:, 8:9], scalar1=1.0 / D, scalar2=EPS, op0=ALU.mult, op1=ALU.add),
                      reads=['rstd'], writes=['rstd'])
                tk.op('act', lambda e: e.activation(out=rstd[:, 8:9], in_=rstd[:, 8:9], func=AF.Ln), reads=['rstd'], writes=['rstd'])
                tk.op('act', lambda e: e.activation(out=rstd[:, 8:9], in_=rstd[:, 8:9], func=AF.Exp, scale=-0.5), reads=['rstd'], writes=['rstd'])
                for hf in range(2):
                    tk.op('dve', lambda e, hf=hf: e.scalar_tensor_tensor(out=tmpf[:, hf * 512:(hf + 1) * 512], in0=ps[bb[hf]][:, :], scalar=rstd[:, 8:9],
                                                                         in1=gg[:, hf * 512:(hf + 1) * 512], op0=ALU.mult, op1=ALU.mult),
                          reads=[psk[bb[hf]], 'rstd', 'gg'], writes=['tmpf'])
                tk.op('dve', lambda e, t=t: e.tensor_tensor(out=x_sb[:, t, :], in0=x_sb[:, t, :], in1=tmpf[:], op=ALU.add),
                      reads=['x%d' % t, 'tmpf'], writes=['x%d' % t])
            _issue(wstate['ptr'])

        for t in range(NT):
            tk.dma('sp', y_d.ap()[t * 128:(t + 1) * 128, :], x_sb[:, t, :], reads=['x%d' % t])
        tk.finish()
    return nc


def _consts(is_sample):
    cf32 = np.zeros((128, 7, 128), np.float32)
    p = np.arange(128)
    J2 = np.zeros((128, 128), np.float32)
    for a in range(2):
        for i in range(64):
            J2[a * 64 + i, a * 64 + 63 - i] = 1.0
    cf32[:, 0] = J2
    cf32[:, 1] = np.eye(128, dtype=np.float32)
    cm = np.zeros((128, 128), np.float32)
    if is_sample:
        qc = np.arange(64)
        c0 = np.clip(qc - 8, 0, 48)
        kc = np.arange(64)
        valid = (kc[:, None] >= c0[None, :]) & (kc[:, None] < c0[None, :] + 16)
        m = np.where(valid, 0.0, NEG).astype(np.float32)
        cm = np.tile(m, (2, 2))
    cf32[:, 2] = cm
    s = np.arange(32)[:, None]
    t = np.arange(32)[None, :]
    trf = (s <= t).astype(np.float32) - (s <= 15).astype(np.float32)
    trb = (s >= t).astype(np.float32) - (s >= 16).astype(np.float32)
    for a in range(4):
        cf32[a * 32:(a + 1) * 32, 3, a * 32:(a + 1) * 32] = trf
        cf32[a * 32:(a + 1) * 32, 4, a * 32:(a + 1) * 32] = trb
    sl = np.arange(128) % 32
    ch = np.arange(128) // 32
    selcols = np.zeros((128, 128), np.float32)
    for a in range(4):
        selcols[:, a * 2 + 0] = ((ch == a) & (sl <= 15))
        selcols[:, a * 2 + 1] = (ch == a)
        selcols[:, 8 + a * 2 + 0] = ((ch == 3 - a) & (sl >= 16))
        selcols[:, 8 + a * 2 + 1] = (ch == 3 - a)
        selcols[:, 18 + a] = (ch == a)
    selcols[:, 16] = (np.arange(128) < 64)
    selcols[:, 17] = (np.arange(128) >= 64)
    cf32[:, 5] = selcols
    cf32[0:64, 6, 0:64] = 1.0
    cf32[64:128, 6, 64:128] = 1.0
    rowb = np.zeros((128, 74), np.float32)
    for i, (j, kt) in enumerate(JK):
        for hf in range(2):
            for krl in range(2):
                if is_sample:
                    qr = 2 * j + hf
                    kr = 2 * kt + krl
                    r0 = int(np.clip(qr - 4, 0, 8))
                    ok = (r0 <= kr < r0 + 8)
                else:
                    ok = (kt // 2 == j // 2)
                rowb[krl * 64:(krl + 1) * 64, i * 2 + hf] = 0.0 if ok else NEG
    cb16 = np.zeros((128, 9, 128), np.float32)
    cb16[:, 7] = J2
    cb16[:, 8] = cm
    cb16[:, 0] = np.eye(128)
    mf = (s <= t).astype(np.float32)
    mb = (s >= t).astype(np.float32)
    z = np.zeros((64, 64), np.float32)
    for a in range(4):
        cb16[a * 32:(a + 1) * 32, 1, a * 32:(a + 1) * 32] = mf
        cb16[a * 32:(a + 1) * 32, 2, a * 32:(a + 1) * 32] = mb
    ang = 2 * np.pi * np.outer(np.arange(64), np.arange(64)) / 64
    c4 = np.cos(ang) / 8.0
    s4 = np.sin(ang) / 8.0
    for ct in range(2):
        cb16[:, 3 + ct] = np.block([[c4, z], [z, c4]])
        cb16[:, 5 + ct] = np.block([[s4, z], [z, s4]])
    n = 1024 if is_sample else 256
    idx = np.arange(n)
    a2 = 2 * np.pi * ((np.outer(idx, idx)) % n) / n
    cn = np.cos(a2) / np.sqrt(n)
    sn = -np.sin(a2) / np.sqrt(n)
    CN = np.zeros((1024, 1024), np.float64)
    SN = np.zeros((1024, 1024), np.float64)
    for i in range(1024 // n):
        CN[i * n:(i + 1) * n, i * n:(i + 1) * n] = cn
        SN[i * n:(i + 1) * n, i * n:(i + 1) * n] = sn
    csn = np.stack([CN.reshape(8, 128, 1024), SN.reshape(8, 128, 1024)], axis=2)
    return dict(cf32=cf32, rowbias=rowb, cb16=cb16.astype(ml_dtypes.bfloat16), csn=csn.astype(ml_dtypes.bfloat16))


def _in_maps(x_prompt, x_sample, cache_attn_k, cache_attn_v, state_hgrn, c, c_ctx,
             w_ada, b_ada, g_pre, w_in, rpb, lb_logits, g_hgrn, w_fnet, w_out, g_post):
    f = lambda a: np.ascontiguousarray(np.asarray(a, dtype=np.float32))
    shared = dict(w_ada=f(w_ada), b_ada=f(b_ada), g_pre=f(g_pre), w_in=f(w_in), lb_logits=f(lb_logits),
                  g_hgrn=f(g_hgrn), w_fnet=f(w_fnet), w_out=f(w_out), g_post=f(g_post))
    tp = np.zeros((NL, 8, 23, 127), np.float32)
    tp[:, :, 4:19, 48:79] = f(rpb)
    cs = _consts(True)
    cp = _consts(False)
    maps = []
    for i in range(8):
        m = dict(shared)
        if i < 4:
            m["x"] = f(x_sample[i])
            m["cvec"] = f(np.asarray(c[i]).reshape(8, 128).T)
            m["ctxk"] = f(np.asarray(cache_attn_k[i]).reshape(NL, 512, 512))
            m["ctxv"] = f(np.asarray(cache_attn_v[i]).reshape(NL, 512, 512))
            s = np.asarray(state_hgrn[i]).reshape(NL, 2, 2, 2, 64, 64)
            m["s0"] = f(s.transpose(0, 1, 3, 4, 2, 5).reshape(NL, 2, 128, 2, 64))
            m["flags"] = np.ones((128, 2), np.float32)
            m["tpad"] = tp
            m.update(cs)
        else:
            m["x"] = f(np.asarray(x_prompt[4 * (i - 4):4 * (i - 3)]).reshape(T, D))
            m["cvec"] = f(np.asarray(c_ctx).reshape(8, 128).T)
            m["ctxk"] = np.zeros((NL, 512, 512), np.float32)
            m["ctxv"] = np.zeros((NL, 512, 512), np.float32)
            m["s0"] = np.zeros((NL, 2, 128, 2, 64), np.float32)
            m["flags"] = np.zeros((128, 2), np.float32)
            m["tpad"] = np.zeros_like(tp)
            m.update(cp)
        maps.append(m)
    return maps


_NC_CACHE = {}


def kernel(**inputs):
    if 'nc' not in _NC_CACHE:
        _NC_CACHE['nc'] = build_nc()
    nc = _NC_CACHE['nc']
    maps = _in_maps(**inputs)
    res = run_bass_kernel_spmd(nc, maps, core_ids=list(range(8)))
    r = res.results
    y_sample = np.stack([r[i]["y"] for i in range(4)], axis=0).astype(np.float32)
    y_prompt = np.concatenate([r[i]["y"].reshape(4, 256, D) for i in range(4, 8)], axis=0).astype(np.float32)
    nk = np.concatenate([r[i]["newk"].reshape(NL, 4, 256, 8, 64).transpose(1, 0, 2, 3, 4) for i in range(4, 8)], axis=0)
    nv = np.concatenate([r[i]["newv"].reshape(NL, 4, 256, 8, 64).transpose(1, 0, 2, 3, 4) for i in range(4, 8)], axis=0)
    ns = np.concatenate([r[i]["news"].reshape(NL, 2, 4, 2, 64, 2, 64).transpose(2, 0, 1, 5, 3, 4, 6).reshape(4, NL, 2, 4, 64, 64)
                         for i in range(4, 8)], axis=0)
    return (y_prompt, y_sample, np.ascontiguousarray(nk, dtype=np.float32), np.ascontiguousarray(nv, dtype=np.float32),
            np.ascontiguousarray(ns, dtype=np.float32))
```

```python
import numpy as np
import ml_dtypes
from contextlib import ExitStack
import concourse.bass as bass
import concourse.mybir as mybir
from concourse.bass_utils import run_bass_kernel_spmd

F32 = mybir.dt.float32
BF16 = mybir.dt.bfloat16
AF = mybir.ActivationFunctionType
ALU = mybir.AluOpType
AX = mybir.AxisListType

NL = 4
D = 1024
T = 1024
NT = 8
EPS = 1e-6
NEG = -30000.0
KT = {0: [0, 1, 2, 3], 1: [0, 1, 2, 3], 2: [0, 1, 2, 3, 4], 3: [1, 2, 3, 4, 5],
      4: [2, 3, 4, 5, 6], 5: [3, 4, 5, 6, 7], 6: [4, 5, 6, 7], 7: [4, 5, 6, 7]}
JK = [(j, kt) for j in range(8) for kt in KT[j]]
JKI = {p: i for i, p in enumerate(JK)}
NDS = 24
NSW = 72


class TK:
    def __init__(s, nc, st):
        s.nc = nc
        s.E = {'pe': nc.tensor, 'act': nc.scalar, 'dve': nc.vector, 'pool': nc.gpsimd, 'sp': nc.sync}
        s.sem = {k: st.enter_context(nc.semaphore('s_' + k)) for k in ('pe', 'act', 'dve', 'pool')}
        s.cnt = {k: 0 for k in s.E}
        s.seen = {k: {} for k in s.E}
        s.lw = {}
        s.rd = {}
        s.dsems = [st.enter_context(nc.semaphore('d%d' % i)) for i in range(NDS)]
        s.dcnt = [0] * NDS
        s.dnext = 0
        s.swsems = [st.enter_context(nc.semaphore('w%d' % i)) for i in range(NSW)]
        s.swnext = 0
        s.swlow = 0

    def _wait(s, eng, key, val):
        if eng == 'pe' and key == 'pe':
            return
        if s.seen[eng].get(key, 0) >= val:
            return
        if isinstance(key, str):
            semobj = s.sem[key]
        elif key >= 1000:
            semobj = s.swsems[key - 1000]
        else:
            semobj = s.dsems[key]
        s.E[eng].wait_ge(semobj, val)
        s.seen[eng][key] = val

    def _deps(s, eng, reads, writes):
        for k in reads:
            w = s.lw.get(k)
            if w:
                s._wait(eng, *w)
            if k.startswith('ps'):
                for rk, rv in s.rd.get(k, {}).items():
                    if rk != eng:
                        s._wait(eng, rk, rv)
        for k in writes:
            w = s.lw.get(k)
            if w:
                s._wait(eng, *w)
            for rk, rv in s.rd.get(k, {}).items():
                s._wait(eng, rk, rv)

    def _book(s, tag, reads, writes):
        for k in reads:
            d = s.rd.setdefault(k, {})
            d[tag[0]] = max(d.get(tag[0], 0), tag[1])
        for k in writes:
            s.lw[k] = tag
            s.rd[k] = {}

    def op(s, eng, fn, reads=(), writes=(), war_only=()):
        s._deps(eng, reads, writes)
        inst = fn(s.E[eng])
        s.cnt[eng] += 1
        inst.then_inc(s.sem[eng], 1)
        s._book((eng, s.cnt[eng]), tuple(reads) + tuple(war_only), writes)

    def dma(s, q, out, in_, reads=(), writes=()):
        if q == 'pool':
            assert s.swnext < NSW, "out of one-shot semaphores"
            i = s.swnext
            s.swnext += 1
            s._deps(q, reads, writes)
            s.E[q].dma_start(out=out, in_=in_).then_inc(s.swsems[i], 16)
            s._book((1000 + i, 16), reads, writes)
            return
        i = s.dnext
        s.dnext = (s.dnext + 1) % NDS
        if s.dcnt[i] > 0:
            s._wait(q, i, s.dcnt[i])
        s._deps(q, reads, writes)
        s.dcnt[i] += 16
        s.E[q].dma_start(out=out, in_=in_).then_inc(s.dsems[i], 16)
        s._book((i, s.dcnt[i]), reads, writes)

    def barrier(s):
        engs = ('pe', 'act', 'dve', 'pool', 'sp')
        snap = dict(s.cnt)
        dsnap = list(s.dcnt)
        for e in engs:
            for o in ('pe', 'act', 'dve', 'pool'):
                if o != e and snap[o] > 0:
                    s._wait(e, o, snap[o])
            for i in range(NDS):
                if dsnap[i] > 0:
                    s._wait(e, i, dsnap[i])
            for i in range(s.swlow, s.swnext):
                s._wait(e, 1000 + i, 16)
        s.swlow = s.swnext

    def finish(s):
        for i in range(NDS):
            if s.dcnt[i] > 0:
                s._wait('sp', i, s.dcnt[i])
        for i in range(s.swnext):
            s._wait('sp', 1000 + i, 16)
        for k in ('pe', 'act', 'dve', 'pool'):
            if s.cnt[k] > 0:
                s._wait('sp', k, s.cnt[k])


def build_nc(nl=NL, dbg=False, upto=None):
    nc = bass.Bass("TRN2", target_bir_lowering=False)
    _order = ['M0', 'M1', 'M2', 'M3', 'M', 'A0', 'A1', 'A1a', 'A1b', 'A1c', 'A2', 'A', 'R0', 'R1', 'R2', 'R3', 'R', 'F0', 'F1', 'F']

    def stop(p):
        return upto is not None and _order.index(upto) <= _order.index(p)

    def din(name, shape, dt=F32):
        return nc.dram_tensor(name, list(shape), dt, kind="ExternalInput")

    def dout(name, shape, dt=F32):
        return nc.dram_tensor(name, list(shape), dt, kind="ExternalOutput")

    x_d = din("x", [T, D])
    cvec_d = din("cvec", [128, 8])
    ctxk_d = din("ctxk", [NL, 512, 512])
    ctxv_d = din("ctxv", [NL, 512, 512])
    s0_d = din("s0", [NL, 2, 128, 2, 64])
    flags_d = din("flags", [128, 2])
    wada_d = din("w_ada", [NL, D, 3 * D])
    bada_d = din("b_ada", [NL, 3 * D])
    gpre_d = din("g_pre", [NL, D])
    win_d = din("w_in", [NL, D, 3840])
    tpad_d = din("tpad", [NL, 8, 23, 127])
    lbl_d = din("lb_logits", [2, NL, 256])
    ghg_d = din("g_hgrn", [NL, 256])
    wfn_d = din("w_fnet", [NL, 256, 256])
    wout_d = din("w_out", [NL, D, D])
    gpost_d = din("g_post", [NL, D])
    cf32_d = din("cf32", [128, 7, 128])
    rowb_d = din("rowbias", [128, 74])
    cb16_d = din("cb16", [128, 9, 128], BF16)
    csn_d = din("csn", [8, 128, 2, 1024], BF16)

    y_d = dout("y", [T, D])
    nk_d = dout("newk", [NL, T, 512])
    nv_d = dout("newv", [NL, T, 512])
    ns_d = dout("news", [NL, 2, 4, 128, 2, 64])
    dbg_d = dout("dbgmixed", [T, D], BF16) if dbg else None

    with ExitStack() as st:
        def sb(name, shape, dt=F32):
            return st.enter_context(nc.sbuf_tensor(name, list(shape), dt))

        tk = TK(nc, st)
        x_sb = sb("x_sb", [128, NT, D])
        hT = sb("hT", [128, 8, T], BF16)
        mixed = sb("mixed", [128, NT, D], BF16)
        wst = [sb("wst0", [128, 8, 512], BF16)]
        wbf = [sb("wbf%d" % i, [128, 8, 512], BF16) for i in range(2)]
        gg = sb("gg", [128, D])
        modN = sb("modN", [128, 3 * D], BF16)
        screp = sb("screp", [128, 8, 128], BF16)
        brow = sb("brow", [1, 512])
        ones_row = sb("ones_row", [1, 128])
        csil = sb("csil", [128, 8])
        cf32 = sb("cf32s", [128, 7, 128])
        rowb = sb("rowbs", [128, 74])
        cb16 = sb("cb16s", [128, 9, 128], BF16)
        flags = sb("flagss", [128, 2])
        lbl = sb("lbl", [128, 2, 256])
        oml = sb("oml", [128, 2, 256])
        ghgB = sb("ghgB", [128, 256])
        ssq = sb("ssq", [128, 16])
        rstd = sb("rstd", [128, 16])
        ARW = 21120
        arena = sb("arena", [128, ARW])
        apos = [0]

        def areset():
            apos[0] = 0

        def aget(shape, dt=F32):
            n = 1
            for d_ in shape[1:]:
                n *= d_
            words = n if dt == F32 else (n + 1) // 2
            a0 = apos[0]
            apos[0] += words
            assert apos[0] <= ARW, ("arena overflow", apos[0])
            v = arena[:, a0:a0 + words]
            if dt != F32:
                v = v.bitcast(dt)
            if len(shape) == 3:
                v = v.rearrange("p (a b) -> p a b", a=shape[1])
            elif len(shape) == 4:
                v = v.rearrange("p (a b c) -> p a b c", a=shape[1], b=shape[2])
            return v

        psbig = [st.enter_context(nc.psum_tensor("psb%d" % i, [128, 1024], F32)) for i in range(4)]
        ps = [psbig[i // 2][:, (i % 2) * 512:(i % 2 + 1) * 512] for i in range(8)]
        psk = ["ps%d" % i for i in range(8)]
        bank_rr = [0]

        def nb(avoid=()):
            while True:
                b = bank_rr[0]
                bank_rr[0] = (b + 1) % 8
                if b not in avoid:
                    return b

        J2 = cf32[:, 0, :]
        IDF = cf32[:, 1, :]
        CMT = cf32[:, 2, :]
        TR = [cf32[:, 3, :], cf32[:, 4, :]]
        SEL = [cf32[:, 5, 0:8], cf32[:, 5, 8:16]]
        HM = cf32[:, 5, 16:18]
        QM = cf32[:, 5, 18:22]
        HME = cf32[:, 6, :].rearrange("p (a c) -> p a c", a=2)
        IDB = cb16[:, 0, :]
        TRI = [cb16[:, 1, :], cb16[:, 2, :]]
        C4S4 = cb16
        J2B = cb16[:, 7, :]
        CMTB = cb16[:, 8, :]

        tk.dma('sp', cf32[:], cf32_d.ap(), writes=['cf32'])
        tk.dma('sp', rowb[:], rowb_d.ap(), writes=['rowb'])
        tk.dma('sp', cb16[:], cb16_d.ap(), writes=['cb16'])
        tk.dma('sp', flags[:], flags_d.ap(), writes=['flags'])
        tk.dma('sp', csil[:], cvec_d.ap(), writes=['csil'])
        for t in range(NT):
            tk.dma('sp', x_sb[:, t, :], x_d.ap()[t * 128:(t + 1) * 128, :], writes=['x%d' % t])
        tk.op('pool', lambda e: e.memset(ones_row[:], 1.0), writes=['ones_row'])
        tk.op('act', lambda e: e.activation(out=csil[:], in_=csil[:], func=AF.Silu), reads=['csil'], writes=['csil'])

        wring = [0]

        wbring = [0]

        def load_w(src_ap, ncols):
            wi = wbring[0]
            wbring[0] ^= 1
            tk.dma('pool', wbf[wi][:, :, 0:ncols], src_ap.rearrange("(kc p) n -> p kc n", p=128), writes=['wbf%d' % wi])
            return wi

        wseq = []
        for l_ in range(nl):
            for (c0_, n_) in ((0, 512), (512, 512), (1024, 512), (1536, 512), (2048, 512), (2816, 512), (2560, 256), (3328, 512)):
                wseq.append((win_d, l_, c0_, n_))
            wseq.append((wout_d, l_, 0, 512))
            wseq.append((wout_d, l_, 512, 512))
        wstate = {'ptr': 0, 'loaded': {}}

        def _issue(i):
            if i < len(wseq) and i not in wstate['loaded']:
                d_, l_, c0_, n_ = wseq[i]
                wstate['loaded'][i] = load_w(d_.ap()[l_, :, c0_:c0_ + n_], n_)

        def next_w(prefetch=True):
            i = wstate['ptr']
            wstate['ptr'] += 1
            _issue(i)
            if prefetch:
                _issue(i + 1)
            return wstate['loaded'][i]

        def proj_tm(wi, c0, ncols, t, b):
            for kc in range(8):
                tk.op('pe', lambda e, kc=kc: e.matmul(ps[b][:, 0:ncols], lhsT=hT[:, kc, t * 128:(t + 1) * 128],
                                                     rhs=wbf[wi][:, kc, c0:c0 + ncols], start=(kc == 0), stop=(kc == 7)),
                      reads=['hT%d' % t, 'wbf%d' % wi], writes=[psk[b]])

        def proj_fm(wi, ci, g, b):
            for kc in range(8):
                tk.op('pe', lambda e, kc=kc: e.matmul(ps[b][:, 0:512], lhsT=wbf[wi][:, kc, ci * 128:(ci + 1) * 128],
                                                     rhs=hT[:, kc, g * 512:(g + 1) * 512], start=(kc == 0), stop=(kc == 7)),
                      reads=['hT%d' % x_ for x_ in range(4 * g, 4 * g + 4)] + ['wbf%d' % wi], writes=[psk[b]])

        def psb16(b):
            return ps[b].bitcast(BF16)

        def emit_mod_chunk(lm, ch, banks=None):
            mod_dma(lm, ch)
            mod_mm(lm, ch, banks)

        def mod_dma(lm, ch):
            tk.dma('pool', wst[0][:], wada_d.ap()[lm, :, ch * 512:(ch + 1) * 512].rearrange("(kc p) n -> p kc n", p=128), writes=['wst0'])
            tk.dma('sp', brow[:], bada_d.ap()[lm:lm + 1, ch * 512:(ch + 1) * 512], writes=['brow'])

        def mod_mm(lm, ch, banks=None):
            b = nb() if banks is None else banks[ch % len(banks)]
            for kc in range(8):
                tk.op('pe', lambda e, kc=kc: e.matmul(ps[b][:, :], lhsT=screp[:, kc, :], rhs=wst[0][:, kc, :], start=(kc == 0), stop=False),
                      reads=['screp', 'wst0'], writes=[psk[b]])
            tk.op('pe', lambda e: e.matmul(ps[b][:, :], lhsT=ones_row[0:1, :], rhs=brow[0:1, :], start=False, stop=True),
                  reads=['ones_row', 'brow'], writes=[psk[b]])
            tk.op('act', lambda e: e.copy(out=modN[:, ch * 512:(ch + 1) * 512], in_=ps[b][:, :]), reads=[psk[b]], writes=['modN'])

        tk.op('dve', lambda e: e.tensor_copy(out=screp[:], in_=csil[:].unsqueeze(2).broadcast_to([128, 8, 128])), reads=['csil'], writes=['screp'])
        for ch in range(6):
            emit_mod_chunk(0, ch)

        for l in range(nl):
            if l == 0:
                tk.barrier()
            areset()
            apos[0] = 11520
            gbc = aget([128, 2, D])
            lbt = aget([128, 2, NL, 256])
            junk = aget([128, D], BF16)
            tmpf = aget([128, D])
            hb = [aget([128, D], BF16) for _ in range(2)]
            modA = aget([128, D])
            tk.dma('sp', gbc[:, 0, :], bass.AP(gpre_d, l * D, [[0, 128], [1, D]]), writes=['gbc'])
            tk.dma('sp', gbc[:, 1, :], bass.AP(gpost_d, l * D, [[0, 128], [1, D]]), writes=['gbc'])
            tk.dma('sp', lbt[:].rearrange("p a l c -> p (a l c)"), bass.AP(lbl_d, 0, [[0, 128], [1, 2 * NL * 256]]), writes=['lbt'])
            if l == 0:
                tk.op('dve', lambda e: e.memset(lbl[:], 0.0), writes=['lbl'])
            else:
                mx = tmpf[:, 0:512].rearrange("p (a c) -> p a c", a=2)
                sm = tmpf[:, 512:1024].rearrange("p (a c) -> p a c", a=2)
                tk.op('dve', lambda e: e.tensor_tensor(out=mx, in0=lbt[:, :, 0, :], in1=lbt[:, :, 1, :], op=ALU.max),
                      reads=['lbt'], writes=['tmpf'])
                for l2 in range(2, NL):
                    tk.op('dve', lambda e, l2=l2: e.tensor_tensor(out=mx, in0=mx, in1=lbt[:, :, l2, :], op=ALU.max),
                          reads=['lbt', 'tmpf'], writes=['tmpf'])
                for l2 in range(NL):
                    tk.op('dve', lambda e, l2=l2: e.tensor_tensor(out=lbt[:, :, l2, :], in0=lbt[:, :, l2, :], in1=mx, op=ALU.subtract),
                          reads=['lbt', 'tmpf'], writes=['lbt'])
                tk.op('act', lambda e: e.activation(out=lbt[:].rearrange("p a l c -> p (a l c)"), in_=lbt[:].rearrange("p a l c -> p (a l c)"), func=AF.Exp),
                      reads=['lbt'], writes=['lbt'])
                tk.op('dve', lambda e: e.tensor_tensor(out=sm, in0=lbt[:, :, 0, :], in1=lbt[:, :, 1, :], op=ALU.add),
                      reads=['lbt'], writes=['tmpf'])
                for l2 in range(2, NL):
                    tk.op('dve', lambda e, l2=l2: e.tensor_tensor(out=sm, in0=sm, in1=lbt[:, :, l2, :], op=ALU.add),
                          reads=['lbt', 'tmpf'], writes=['tmpf'])
                tk.op('dve', lambda e: e.reciprocal(out=sm, in_=sm), reads=['tmpf'], writes=['tmpf'])
                tk.op('dve', lambda e: e.tensor_copy(out=lbl[:], in_=lbt[:, :, 1, :]), reads=['lbt'], writes=['lbl'])
                for l2 in range(2, l + 1):
                    tk.op('dve', lambda e, l2=l2: e.tensor_tensor(out=lbl[:], in0=lbl[:], in1=lbt[:, :, l2, :], op=ALU.add),
                          reads=['lbt', 'lbl'], writes=['lbl'])
                tk.op('dve', lambda e: e.tensor_tensor(out=lbl[:], in0=lbl[:], in1=sm, op=ALU.mult), reads=['lbl', 'tmpf'], writes=['lbl'])
            tk.op('dve', lambda e: e.tensor_scalar(out=oml[:], in0=lbl[:], scalar1=-0.5, scalar2=0.5, op0=ALU.mult, op1=ALU.add),
                  reads=['lbl'], writes=['oml'])
            tk.op('dve', lambda e: e.tensor_scalar(out=lbl[:], in0=lbl[:], scalar1=0.5, scalar2=0.5, op0=ALU.mult, op1=ALU.add),
                  reads=['lbl'], writes=['lbl'])
            if stop('M0'):
                break
            tk.op('dve', lambda e: e.scalar_tensor_tensor(out=modA[:], in0=modN[:, D:2 * D], scalar=1.0, in1=gbc[:, 0, :], op0=ALU.add, op1=ALU.mult),
                  reads=['modN', 'gbc'], writes=['modA'])
            tk.op('dve', lambda e: e.tensor_tensor(out=gg[:], in0=modN[:, 2 * D:3 * D], in1=gbc[:, 1, :], op=ALU.mult), reads=['modN', 'gbc'], writes=['gg'])
            if stop('M1'):
                break
            for t in range(NT):
                tk.op('act', lambda e, t=t: e.activation(out=junk[:], in_=x_sb[:, t, :], func=AF.Square, accum_out=ssq[:, t:t + 1]),
                      reads=['x%d' % t], writes=['junk', 'ssq'])
            tk.op('dve', lambda e: e.tensor_scalar(out=rstd[:, 0:8], in0=ssq[:, 0:8], scalar1=1.0 / D, scalar2=EPS, op0=ALU.mult, op1=ALU.add),
                  reads=['ssq'], writes=['rstd'])
            tk.op('act', lambda e: e.activation(out=rstd[:, 0:8], in_=rstd[:, 0:8], func=AF.Ln), reads=['rstd'], writes=['rstd'])
            tk.op('act', lambda e: e.activation(out=rstd[:, 0:8], in_=rstd[:, 0:8], func=AF.Exp, scale=-0.5), reads=['rstd'], writes=['rstd'])
            if stop('M2'):
                break
            for t in range(NT):
                hbi = t % 2
                tk.op('dve', lambda e, t=t: e.scalar_tensor_tensor(out=tmpf[:], in0=x_sb[:, t, :], scalar=rstd[:, t:t + 1], in1=modA[:],
                                                                   op0=ALU.mult, op1=ALU.mult),
                      reads=['x%d' % t, 'rstd', 'modA'], writes=['tmpf'])
                tk.op('dve', lambda e: e.tensor_tensor(out=hb[hbi][:], in0=tmpf[:], in1=modN[:, 0:D], op=ALU.add),
                      reads=['tmpf', 'modN'], writes=['hb%d' % hbi])
                if stop('M3'):
                    continue
                b = nb()
                for kc in range(8):
                    tk.op('pe', lambda e, kc=kc: e.transpose(out=psb16(b)[:, kc * 128:(kc + 1) * 128], in_=hb[hbi][:, kc * 128:(kc + 1) * 128], identity=IDB),
                          reads=['hb%d' % hbi, 'cb16'], writes=[psk[b]])
                tk.op('act', lambda e, t=t: e.copy(out=hT[:, :, t * 128:(t + 1) * 128], in_=psb16(b)[:, :].rearrange("p (k c) -> p k c", k=8)),
                      reads=[psk[b]], writes=['hT%d' % t])

            if stop('M'):
                break
            tk.barrier()
            areset()
            qT = aget([128, 4, T], BF16)
            kT = aget([128, 4, T], BF16)
            ckT = aget([128, 4, 512], BF16)
            vaug = aget([128, NT, 8, 66], BF16)
            cvaug = aget([128, 4, 8, 66], BF16)
            sga = aget([128, NT, 512], BF16)
            expT = aget([128, 7, 8, 128], BF16)
            Eb = [aget([128, 8, 128], BF16) for _ in range(3)]
            Pb = [aget([128, 8, 128], BF16) for _ in range(2)]
            hk = [Eb[0].bitcast(F32) if False else None, None]
            ost = [aget([128, 512]) for _ in range(2)]
            rden = aget([128, 8])
            otmp = aget([128, 8, 64])
            ckb = aget([128, 4, 512], BF16)
            hkA = aget([128, 8, 128])
            hkB = aget([128, 8, 128])
            hk = [hkA, hkB]
            tk.op('pool', lambda e: e.memset(vaug[:, :, :, 64:66], 1.0), writes=['vaug'])
            tk.op('dve', lambda e: e.tensor_copy(out=cvaug[:, :, :, 64:66].rearrange("p a b c -> p (a b) c"),
                                                 in_=flags[:, 0:1].unsqueeze(2).broadcast_to([128, 32, 2])),
                  reads=['flags'], writes=['cvaug'])
            def toep_dma(di):
                dl = di - 3
                hi = di % 2
                for qr in range(2):
                    for krl in range(2):
                        off = ((l * 8) * 23 + (2 * dl + krl - qr + 11)) * 127
                        src = bass.AP(tpad_d, off, [[1, 64], [23 * 127, 8], [1, 64]])
                        tk.dma('sp', hk[hi][qr * 64:(qr + 1) * 64, :, krl * 64:(krl + 1) * 64], src, writes=['hk%d_%d' % (hi, qr * 2 + krl)])

            def toep_mm(di):
                hi = di % 2
                bA = nb()
                bB = nb()
                for h in range(8):
                    b = bA if h % 2 == 0 else bB
                    o = ps[b][:, (h // 2) * 128:(h // 2 + 1) * 128]
                    tk.op('pe', lambda e, h=h, o=o: e.matmul(o, lhsT=hk[hi][:, h, :], rhs=J2, start=True, stop=False),
                          reads=['hk%d_%d' % (hi, x) for x in range(4)] + ['cf32'], writes=[psk[b]])
                    tk.op('pe', lambda e, o=o: e.matmul(o, lhsT=IDF, rhs=CMT, start=False, stop=True),
                          reads=['cf32'], writes=[psk[b]])
                for bi, b in enumerate((bA, bB)):
                    tk.op('act', lambda e, bi=bi, b=b: e.activation(out=expT[:, di, bi * 4:(bi + 1) * 4, :].rearrange("p a c -> p (a c)"),
                                                                    in_=ps[b][:, :], func=AF.Exp),
                          reads=[psk[b]], writes=['expT%d' % di])

            toep_dma(0)
            toep_dma(1)
            if stop('A0'):
                break
            ckk = 'ckb'
            tk.dma('pool', ckb[:], ctxk_d.ap()[l].rearrange("(c p) n -> p c n", p=128), writes=[ckk])
            for c in range(4):
                b = nb()
                for pr in range(4):
                    tk.op('pe', lambda e, pr=pr: e.transpose(out=psb16(b)[:, pr * 128:(pr + 1) * 128], in_=ckb[:, c, pr * 128:(pr + 1) * 128], identity=IDB),
                          reads=[ckk, 'cb16'], writes=[psk[b]])
                tk.op('act', lambda e, c=c: e.copy(out=ckT[:, :, c * 128:(c + 1) * 128], in_=psb16(b)[:, 0:512].rearrange("p (k c) -> p k c", k=4)),
                      reads=[psk[b]], writes=['ckT'])
            tk.dma('pool', ckb[:], ctxv_d.ap()[l].rearrange("(c p) n -> p c n", p=128), reads=[], writes=[ckk])
            tk.op('pool', lambda e: e.tensor_copy(out=cvaug[:, :, :, 0:64], in_=ckb[:].rearrange("p c (h d) -> p c h d", h=8)), reads=[ckk], writes=['cvaug'])
            if stop('A1'):
                break
            toep_mm(0)
            toep_dma(2)
            wi = next_w()
            for pr in range(4):
                for g in range(2):
                    b = nb()
                    proj_fm(wi, pr, g, b)
                    tk.op('act', lambda e, pr=pr, g=g: e.copy(out=qT[:, pr, g * 512:(g + 1) * 512], in_=ps[b][:, :]), reads=[psk[b]], writes=['qT%d_%d' % (pr, g)])
            if stop('A1a'):
                break
            toep_mm(1)
            toep_dma(3)
            wi = next_w()
            for pr in range(4):
                for g in range(2):
                    b = nb()
                    proj_fm(wi, pr, g, b)
                    tk.op('act', lambda e, pr=pr, g=g: e.copy(out=kT[:, pr, g * 512:(g + 1) * 512], in_=ps[b][:, :]), reads=[psk[b]], writes=['kTf%d_%d' % (pr, g)])
            toep_mm(2)
            toep_dma(4)
            for t in range(NT):
                b = nb()
                proj_tm(wi, 0, 512, t, b)
                oi = t % 2
                tk.op('dve', lambda e: e.tensor_copy(out=ost[oi][:], in_=ps[b][:, :]), reads=[psk[b]], writes=['ost%d' % oi])
                tk.dma('sp', nk_d.ap()[l, t * 128:(t + 1) * 128, :], ost[oi][:], reads=['ost%d' % oi])
            toep_mm(3)
            toep_dma(5)
            if stop('A1b'):
                break
            wi = next_w()
            for t in range(NT):
                b = nb()
                proj_tm(wi, 0, 512, t, b)
                oi = t % 2
                tk.op('dve', lambda e: e.tensor_copy(out=ost[oi][:], in_=ps[b][:, :]), reads=[psk[b]], writes=['ost%d' % oi])
                tk.op('act', lambda e, t=t: e.copy(out=vaug[:, t, :, 0:64], in_=ost[oi][:].rearrange("p (h d) -> p h d", h=8)),
                      reads=['ost%d' % oi], writes=['vaug%d' % t])
                tk.dma('sp', nv_d.ap()[l, t * 128:(t + 1) * 128, :], ost[oi][:], reads=['ost%d' % oi])
            if stop('A1c'):
                break
            toep_mm(4)
            toep_dma(6)
            wi = next_w()
            for t in range(NT):
                b = nb()
                proj_tm(wi, 0, 512, t, b)
                tk.op('act', lambda e, t=t: e.activation(out=sga[:, t, :], in_=ps[b][:, :], func=AF.Silu), reads=[psk[b]], writes=['sga%d' % t])
            toep_mm(5)
            toep_mm(6)
            if stop('A2'):
                break
            OA, OB = 6, 7
            spairs = [(0, 1), (2, 3), (4, 5)]
            allsteps = []
            for j in range(8):
                st_ = [('l', kt) for kt in KT[j]] + [('c', c) for c in range(4)]
                for si_, (kind, idx) in enumerate(st_):
                    allsteps.append((j, si_, len(st_), kind, idx))

            def emit_S(k):
                j, si_, ns_, kind, idx = allsteps[k]
                sA, sB = spairs[k % 3]
                for h in range(8):
                    b = sA if h % 2 == 0 else sB
                    r0 = (h % 2) * 64
                    ksrc = kT[r0:r0 + 64, h // 2, idx * 128:(idx + 1) * 128] if kind == 'l' else ckT[r0:r0 + 64, h // 2, idx * 128:(idx + 1) * 128]
                    tk.op('pe', lambda e, h=h, b=b, ksrc=ksrc, r0=r0: e.matmul(ps[b][:, (h // 2) * 128:(h // 2 + 1) * 128], lhsT=ksrc,
                                                                              rhs=qT[r0:r0 + 64, h // 2, j * 128:(j + 1) * 128], start=True, stop=True),
                          reads=['qT%d_%d' % (h // 2, j // 4), ('kTf%d_%d' % (h // 2, idx // 4)) if kind == 'l' else 'ckT'], writes=[psk[b]])

            def emit_rest(k):
                j, si_, ns_, kind, idx = allsteps[k]
                sA, sB = spairs[k % 3]
                sl = k % 3
                big = psbig[sA // 2]
                if kind == 'l':
                    jk = JKI[(j, idx)]
                    for hf in range(2):
                        tk.op('act', lambda e, hf=hf: e.activation(
                            out=Eb[sl][:, :, hf * 64:(hf + 1) * 64],
                            in_=big[:, :].rearrange("p (a c) -> p a c", a=8)[:, :, hf * 64:(hf + 1) * 64],
                            func=AF.Exp, scale=0.125, bias=rowb[:, jk * 2 + hf:jk * 2 + hf + 1]),
                            reads=[psk[sA], psk[sB], 'rowb'], writes=['Eb%d_%d' % (sl, hf)])
                    di = idx - j + 3
                    pl = k % 2
                    tk.op('dve', lambda e, di=di: e.tensor_tensor(out=Pb[pl][:], in0=Eb[sl][:], in1=expT[:, di, :, :], op=ALU.mult),
                          reads=['Eb%d_0' % sl, 'Eb%d_1' % sl, 'expT%d' % di], writes=['Pb%d' % pl])
                    lhs, lk = Pb[pl], ['Pb%d' % pl]
                    vsrc, vk = vaug, 'vaug%d' % idx
                else:
                    tk.op('act', lambda e: e.activation(out=Eb[sl][:].rearrange("p a c -> p (a c)"), in_=big[:, :], func=AF.Exp, scale=0.125),
                          reads=[psk[sA], psk[sB]], writes=['Eb%d_0' % sl, 'Eb%d_1' % sl])
                    lhs, lk = Eb[sl], ['Eb%d_0' % sl, 'Eb%d_1' % sl]
                    vsrc, vk = cvaug, 'cvaug'
                for e_ in range(8):
                    h = 2 * (e_ % 4) + e_ // 4
                    ob = OA if e_ < 4 else OB
                    tk.op('pe', lambda e, e_=e_, h=h, ob=ob, lhs=lhs, vsrc=vsrc: e.matmul(
                        ps[ob][:, (e_ % 4) * 66:(e_ % 4) * 66 + 66], lhsT=lhs[:, e_, :], rhs=vsrc[:, idx, h, :],
                        start=(si_ == 0 and e_ % 4 == 0), stop=(si_ == ns_ - 1), skip_group_check=True),
                        reads=lk + [vk] + (['vaug'] if kind == 'l' else []), writes=[psk[ob]])
                if si_ == ns_ - 1:
                    for bi, ob in enumerate((OA, OB)):
                        tk.op('dve', lambda e, bi=bi, ob=ob: e.reciprocal(out=rden[:, bi * 4:(bi + 1) * 4],
                                                                          in_=ps[ob][:, 0:264].rearrange("p (a c) -> p a c", a=4)[:, :, 64]),
                              reads=[psk[ob]], writes=['rden%d' % bi])
                    for bi, ob in enumerate((OA, OB)):
                        tk.op('dve', lambda e, bi=bi, ob=ob: e.tensor_tensor(
                            out=otmp[:, bi:8:2, :], in0=ps[ob][:, 0:264].rearrange("p (a c) -> p a c", a=4)[:, :, 0:64],
                            in1=rden[:, bi * 4:(bi + 1) * 4].unsqueeze(2).broadcast_to([128, 4, 64]), op=ALU.mult),
                            reads=[psk[ob], 'rden%d' % bi], writes=['otmp%d' % bi])
                    tk.op('dve', lambda e: e.tensor_tensor(out=mixed[:, j, 0:512], in0=otmp[:].rearrange("p h d -> p (h d)"), in1=sga[:, j, :], op=ALU.mult),
                          reads=['otmp0', 'otmp1', 'sga%d' % j], writes=['mixed%d' % j])

            emit_S(0)
            emit_S(1)
            for k in range(len(allsteps)):
                if k + 2 < len(allsteps):
                    emit_S(k + 2)
                emit_rest(k)

            if stop('A'):
                break
            tk.barrier()
            areset()
            qh = aget([128, NT, 256], BF16)
            sgb = aget([128, NT, 256])
            qE = sgb.rearrange("p a b -> p (a b)")[:, 0:1024].bitcast(BF16).rearrange("p (a b) -> p a b", a=NT)
            kE = sgb.rearrange("p a b -> p (a b)")[:, 1024:2048].bitcast(BF16).rearrange("p (a b) -> p a b", a=NT)
            vh = aget([128, NT, 256], BF16)
            vhmF = aget([128, 4096])
            vhm = vhmF.bitcast(BF16).rearrange("p (q t c) -> p q t c", q=4, t=NT)
            A_ = vhmF[:, 0:2048].rearrange("p (e c) -> p e c", e=64)
            B_ = vhmF[:, 2048:4096].rearrange("p (e c) -> p e c", e=64)
            sgr = aget([128, NT, 256], BF16)
            fS = aget([128, 4096])
            fbuf = fS[:, 0:2048].rearrange("p (a b) -> p a b", a=NT)
            SinPm = fS.bitcast(BF16).rearrange("p (h c e) -> p h c e", h=4, c=32)
            lfbuf = aget([128, NT, 256])
            kETm = lfbuf.rearrange("p a b -> p (a b)").bitcast(BF16).rearrange("p (h t) -> p h t", h=4)
            osq = lfbuf
            tmpE = [aget([128, 512]) for _ in range(2)]
            qET = aget([128, 2, T], BF16)
            ATm = [aget([128, 4, 128], BF16) for _ in range(2)]
            osum = aget([128, NT, 256])
            gs3 = aget([128, 2, 32, 3])
            es3 = aget([128, 2, 32, 3])
            S0b = aget([128, 2, 64])
            nsb = aget([128, 4, 2, 64])
            hss = aget([128, 32])
            tk.dma('sp', ghgB[:], bass.AP(ghg_d, l * 256, [[0, 128], [1, 256]]), writes=['ghgB'])
            wi = next_w()
            for t in range(NT):
                b = nb()
                proj_tm(wi, 0, 512, t, b)
                tk.op('act', lambda e, t=t: e.activation(out=qh[:, t, :], in_=ps[b][:, 0:256], func=AF.Silu), reads=[psk[b]], writes=['qh%d' % t])
                tk.op('act', lambda e, t=t: e.activation(out=sgb[:, t, :], in_=ps[b][:, 256:512], func=AF.Tanh, scale=0.5), reads=[psk[b]], writes=['sQ'])
            wi = next_w()
            for t in range(NT):
                b = nb()
                proj_tm(wi, 0, 512, t, b)
                tk.op('act', lambda e, t=t: e.activation(out=sgr[:, t, :], in_=ps[b][:, 256:512], func=AF.Silu), reads=[psk[b]], writes=['sgr%d' % t])
                tk.op('dve', lambda e, t=t: e.tensor_copy(out=vh[:, t, :], in_=ps[b][:, 0:256]), reads=[psk[b]], writes=['vh%d' % t])

            if stop('R0'):
                break
            rstop = False
            for dr in range(2):
                if l + 1 < nl:
                    mod_dma(l + 1, 3 * dr)
                lb_bc = lbl[:, dr, :].unsqueeze(1).broadcast_to([128, NT, 256])
                oml_bc = oml[:, dr, :].unsqueeze(1).broadcast_to([128, NT, 256])
                tk.op('dve', lambda e: e.tensor_tensor(out=fbuf[:], in0=sgb[:], in1=oml_bc, op=ALU.mult), reads=['sQ', 'oml'], writes=['fS'])
                tk.op('dve', lambda e: e.tensor_tensor(out=fbuf[:], in0=fbuf[:], in1=lb_bc, op=ALU.add), reads=['fS', 'lbl'], writes=['fS'])
                tk.op('act', lambda e: e.activation(out=lfbuf[:].rearrange("p a b -> p (a b)"), in_=fbuf[:].rearrange("p a b -> p (a b)"), func=AF.Ln),
                      reads=['fS'], writes=['lK'])
                tk.op('dve', lambda e: e.tensor_scalar(out=fbuf[:], in0=fbuf[:], scalar1=-1.0, scalar2=1.0, op0=ALU.mult, op1=ALU.add),
                      reads=['fS'], writes=['fS'])
                bS = nb()
                for t in range(NT):
                    for pr in range(2):
                        tt_ = t if dr == 0 else 7 - t
                        tk.op('pe', lambda e, t=t, pr=pr, tt_=tt_: e.matmul(ps[bS][:, (pr * 8 + tt_) * 8:(pr * 8 + tt_) * 8 + 8], lhsT=lfbuf[:, t, pr * 128:(pr + 1) * 128],
                                                                   rhs=SEL[dr], start=True, stop=True),
                              reads=['lK', 'cf32'], writes=[psk[bS]])
                psS = ps[bS][:, 0:128].rearrange("p (a c r) -> p a c r", a=2, c=32)
                tk.op('dve', lambda e: e.tensor_copy(out=gs3[:, :, :, 0:2], in_=psS), reads=[psk[bS]], writes=['gs3'])
                tk.op('dve', lambda e: e.tensor_tensor(out=gs3[:, :, :, 2], in0=gs3[:, :, :, 1], in1=gs3[:, :, :, 0], op=ALU.subtract),
                      reads=['gs3'], writes=['gs3'])
                tk.op('act', lambda e: e.activation(out=es3[:].rearrange("p a c r -> p (a c r)"), in_=gs3[:].rearrange("p a c r -> p (a c r)"), func=AF.Exp),
                      reads=['gs3'], writes=['es3'])
                tk.op('dve', lambda e: e.tensor_scalar(out=es3[:, :, 8:32:8, 0:2], in0=es3[:, :, 8:32:8, 0:2], scalar1=flags[:, 1:2], scalar2=None, op0=ALU.mult),
                      reads=['es3', 'flags'], writes=['es3'])
                for tp in range(4):
                    b = nb()
                    for i in range(2):
                        t = 2 * tp + i
                        tk.op('pe', lambda e, t=t, i=i: e.matmul(ps[b][:, i * 256:(i + 1) * 256], lhsT=TR[dr], rhs=lfbuf[:, t, :], start=True, stop=True),
                              reads=['cf32', 'lK'], writes=[psk[b]])
                    tk.op('act', lambda e: e.activation(out=tmpE[0][:], in_=ps[b][:, :], func=AF.Exp), reads=[psk[b]], writes=['tmpE0'])
                    tk.op('act', lambda e: e.activation(out=tmpE[1][:], in_=ps[b][:, :], func=AF.Exp, scale=-1.0), reads=[psk[b]], writes=['tmpE1'])
                    tk.op('dve', lambda e, tp=tp: e.tensor_tensor(out=qE[:, 2 * tp:2 * tp + 2, :].rearrange("p a c -> p (a c)"),
                                                                  in0=qh[:, 2 * tp:2 * tp + 2, :].rearrange("p a c -> p (a c)"), in1=tmpE[0][:], op=ALU.mult),
                          reads=['qh%d' % (2 * tp), 'qh%d' % (2 * tp + 1), 'tmpE0'], writes=['sQ', 'qk%d' % tp])
                    tk.op('dve', lambda e, tp=tp: e.tensor_tensor(out=kE[:, 2 * tp:2 * tp + 2, :].rearrange("p a c -> p (a c)"),
                                                                  in0=fbuf[:, 2 * tp:2 * tp + 2, :].rearrange("p a c -> p (a c)"), in1=tmpE[1][:], op=ALU.mult),
                          reads=['fS', 'tmpE1'], writes=['sQ', 'qk%d' % tp])
                for cp in range(4):
                    tk.op('dve', lambda e, cp=cp: e.tensor_scalar(out=vhm[:, cp, :, :], in0=vh[:], scalar1=QM[:, cp:cp + 1], scalar2=None, op0=ALU.mult),
                          reads=['vh%d' % t_ for t_ in range(NT)] + ['cf32'], writes=['vhm'])
                tk._deps('act', [], ['lK'])
                for t in range(NT):
                    b = nb()
                    for pr in range(2):
                        tk.op('pe', lambda e, t=t, pr=pr: e.transpose(out=psb16(b)[:, pr * 128:(pr + 1) * 128], in_=qE[:, t, pr * 128:(pr + 1) * 128], identity=IDB),
                              reads=['qk%d' % (t // 2), 'cb16'], writes=[psk[b]], war_only=['sQ'])
                        tk.op('pe', lambda e, t=t, pr=pr: e.transpose(out=psb16(b)[:, (2 + pr) * 128:(3 + pr) * 128], in_=kE[:, t, pr * 128:(pr + 1) * 128], identity=IDB),
                              reads=['qk%d' % (t // 2), 'cb16'], writes=[psk[b]], war_only=['sQ'])
                    tk.op('act', lambda e, t=t: e.copy(out=qET[:, :, t * 128:(t + 1) * 128], in_=psb16(b)[:, 0:256].rearrange("p (a c) -> p a c", a=2)),
                          reads=[psk[b]], writes=['qET%d' % t])
                    for h in range(4):
                        tk.op('act', lambda e, t=t, h=h: e.activation(out=kETm[:, h, t * 128:(t + 1) * 128], in_=psb16(b)[:, (2 + h // 2) * 128:(3 + h // 2) * 128],
                                                                      func=AF.Copy, scale=HM[:, h % 2:h % 2 + 1]),
                              reads=[psk[b], 'cf32'], writes=['kT%d_%d' % (t, h)])
                if stop('R1'):
                    rstop = True
                    break
                tk.dma('sp', S0b[:], s0_d.ap()[l, dr], writes=['S0b'])
                kvb_all = [[nb() for _ in range(4)] for _ in range(2)]
                for pr in range(2):
                    kvb = kvb_all[pr]
                    for c in range(32):
                        cq = c if dr == 0 else 31 - c
                        b = kvb[cq // 8]
                        for h in (2 * pr, 2 * pr + 1):
                            o = ps[b][(h % 2) * 64:(h % 2) * 64 + 64, (cq % 8) * 64:(cq % 8) * 64 + 64]
                            tk.op('pe', lambda e, c=c, h=h, o=o: e.matmul(o, lhsT=kE[:, c // 4, h * 64:(h + 1) * 64], rhs=vhm[:, c % 4, c // 4, h * 64:(h + 1) * 64],
                                                                          start=True, stop=True),
                                  reads=['sQ', 'vhm'], writes=[psk[b]])
                for pr in range(2):
                    kvb = kvb_all[pr]
                    for g in range(4):
                        tk.op('dve', lambda e, g=g: e.tensor_tensor(out=B_[:, :, g * 8:(g + 1) * 8].rearrange("p e c -> p c e"),
                                                                    in0=ps[kvb[g]][:, :].rearrange("p (c e) -> p c e", c=8),
                                                                    in1=es3[:, pr, g * 8:(g + 1) * 8, 2].unsqueeze(2).broadcast_to([128, 8, 64]), op=ALU.mult),
                              reads=[psk[kvb[g]], 'es3'], writes=['vhm'])
                    if dr == 0 and pr == 0:
                        wi = next_w()
                        for t in range(NT):
                            b = kvb[t % 4]
                            proj_tm(wi, 0, 256, t, b)
                            tk.op('act', lambda e, t=t: e.activation(out=sgb[:, t, :], in_=ps[b][:, 0:256], func=AF.Tanh, scale=0.5), reads=[psk[b]], writes=['sQ'])
                    if dr == 1 and pr == 0:
                        wi = next_w()
                        uT_h = qh[:].rearrange("p a b -> p (a b)").rearrange("p (c t) -> p c t", c=2)
                        sgf_h = qE
                        hb_ = 0
                        for ci in range(2):
                            for g in range(2):
                                b = kvb[hb_ % 4]
                                hb_ += 1
                                proj_fm(wi, ci, g, b)
                                tk.op('act', lambda e, ci=ci, g=g: e.copy(out=uT_h[:, ci, g * 512:(g + 1) * 512], in_=ps[b][:, :]), reads=[psk[b]], writes=['qh%d' % t_ for t_ in range(NT)])
                        for t in range(NT):
                            b = kvb[hb_ % 4]
                            hb_ += 1
                            proj_tm(wi, 256, 256, t, b)
                            tk.op('act', lambda e, t=t: e.activation(out=sgf_h[:, t, :], in_=ps[b][:, 0:256], func=AF.Silu), reads=[psk[b]], writes=['sQ'])
                    tk.op('dve', lambda e: e.scalar_tensor_tensor(out=B_[:, :, 0], in0=S0b[:, pr, :], scalar=es3[:, pr, 0, 1:2], in1=B_[:, :, 0], op0=ALU.mult, op1=ALU.add),
                          reads=['S0b', 'es3', 'vhm'], writes=['vhm'])
                    tk.op('dve', lambda e: e.tensor_copy(out=A_[:], in_=es3[:, pr, :, 1].unsqueeze(1).broadcast_to([128, 64, 32])), reads=['es3', 'vhm'], writes=['vhm'])
                    tk.op('dve', lambda e: e.memset(A_[:, :, 0:1], 0.0), reads=['vhm'], writes=['vhm'])
                    tk.op('dve', lambda e: e.tensor_tensor_scan(out=B_[:].rearrange("p e c -> p (e c)"), data0=A_[:].rearrange("p e c -> p (e c)"),
                                                                data1=B_[:].rearrange("p e c -> p (e c)"), initial=0.0, op0=ALU.mult, op1=ALU.add),
                          reads=['vhm'], writes=['vhm'])
                    tk.op('dve', lambda e: e.tensor_copy(out=nsb[:, :, pr, :], in_=B_[:, :, 7:32:8].rearrange("p e k -> p k e")), reads=['vhm'], writes=['nsb'])
                    for h2 in range(2):
                        tk.op('dve', lambda e, h2=h2: e.scalar_tensor_tensor(out=SinPm[:, 2 * pr + h2, 1:32, :], in0=B_[:, :, 0:31].rearrange("p e c -> p c e"),
                                                                             scalar=HM[:, h2:h2 + 1], in1=es3[:, pr, 1:32, 0].unsqueeze(2).broadcast_to([128, 31, 64]),
                                                                             op0=ALU.mult, op1=ALU.mult),
                              reads=['vhm', 'es3', 'cf32'], writes=['fS'])
                        tk.op('dve', lambda e, h2=h2: e.scalar_tensor_tensor(out=SinPm[:, 2 * pr + h2, 0, :], in0=S0b[:, pr, :], scalar=HM[:, h2:h2 + 1],
                                                                             in1=es3[:, pr, 0, 0:1].broadcast_to([128, 64]), op0=ALU.mult, op1=ALU.mult),
                              reads=['S0b', 'es3', 'cf32'], writes=['fS'])
                nsb3 = nsb[:].rearrange("p s a e -> p s (a e)")
                if dr == 0:
                    tk.dma('sp', ns_d.ap()[l, 0].rearrange("s p a e -> p s (a e)"), nsb3, reads=['nsb'])
                else:
                    for k in range(4):
                        tk.dma('sp', ns_d.ap()[l, 1, 3 - k].rearrange("p a e -> p (a e)"), nsb3[:, k, :], reads=['nsb'])
                if l + 1 < nl:
                    mod_mm(l + 1, 3 * dr)
                    mod_dma(l + 1, 3 * dr + 1)
                if stop('R2'):
                    rstop = True
                    break
                for t in range(NT):
                    bA_ = nb()
                    ai = t % 2
                    for h in range(4):
                        tk.op('pe', lambda e, t=t, h=h: e.matmul(ps[bA_][:, h * 128:(h + 1) * 128], lhsT=kETm[:, h, t * 128:(t + 1) * 128],
                                                                 rhs=qET[:, h // 2, t * 128:(t + 1) * 128], start=True, stop=True),
                              reads=['kT%d_%d' % (t, h), 'qET%d' % t], writes=[psk[bA_]], war_only=['lK'])
                    tk.op('dve', lambda e: e.tensor_tensor(out=ATm[ai][:], in0=ps[bA_][:, :].rearrange("p (a c) -> p a c", a=4),
                                                           in1=TRI[dr].unsqueeze(1).broadcast_to([128, 4, 128]), op=ALU.mult),
                          reads=[psk[bA_], 'cb16'], writes=['ATm%d' % ai])
                    bO = nb()
                    for h in range(4):
                        tk.op('pe', lambda e, t=t, h=h: e.matmul(ps[bO][:, h * 64:(h + 1) * 64], lhsT=ATm[ai][:, h, :], rhs=vh[:, t, h * 64:(h + 1) * 64],
                                                                 start=True, stop=False, skip_group_check=True),
                              reads=['ATm%d' % ai, 'vh%d' % t], writes=[psk[bO]])
                        for cp in range(4):
                            tk.op('pe', lambda e, t=t, h=h, cp=cp: e.matmul(ps[bO][cp * 32:(cp + 1) * 32, h * 64:(h + 1) * 64],
                                                                            lhsT=qET[:, h // 2, t * 128 + cp * 32:t * 128 + cp * 32 + 32],
                                                                            rhs=SinPm[:, h, (4 * t + cp) if dr == 0 else 31 - (4 * t + cp), :], start=False, stop=(cp == 3), skip_group_check=True,
                                                                            tile_position=(0, cp * 32)),
                                  reads=['qET%d' % t, 'fS'], writes=[psk[bO]])
                    if l + 1 < nl and t == 3:
                        mod_mm(l + 1, 3 * dr + 1)
                        mod_dma(l + 1, 3 * dr + 2)
                    if l + 1 < nl and t == 7:
                        mod_mm(l + 1, 3 * dr + 2)
                    if dr == 0:
                        tk.op('act', lambda e, t=t: e.copy(out=osum[:, t, :], in_=ps[bO][:, 0:256]), reads=[psk[bO]], writes=['osum%d' % t])
                    else:
                        tk.op('dve', lambda e, t=t: e.tensor_tensor(out=osum[:, t, :], in0=osum[:, t, :], in1=ps[bO][:, 0:256], op=ALU.add),
                              reads=[psk[bO], 'osum%d' % t], writes=['osum%d' % t])
                if stop('R3'):
                    rstop = True
                    break
            if rstop:
                break
            tk.op('dve', lambda e: e.tensor_tensor(out=osq[:], in0=osum[:], in1=osum[:], op=ALU.mult), reads=['osum%d' % t_ for t_ in range(NT)], writes=['lK'])
            tk.op('dve', lambda e: e.tensor_reduce(out=hss[:], in_=osq[:].rearrange("p t (h d) -> p (t h) d", h=4), axis=AX.X, op=ALU.add),
                  reads=['lK'], writes=['hss'])
            tk.op('dve', lambda e: e.tensor_scalar(out=hss[:], in0=hss[:], scalar1=1.0 / 64, scalar2=EPS, op0=ALU.mult, op1=ALU.add), reads=['hss'], writes=['hss'])
            tk.op('act', lambda e: e.activation(out=hss[:], in_=hss[:], func=AF.Ln), reads=['hss'], writes=['hss'])
            tk.op('act', lambda e: e.activation(out=hss[:], in_=hss[:], func=AF.Exp, scale=-0.5), reads=['hss'], writes=['hss'])
            tk.op('dve', lambda e: e.tensor_tensor(out=osum[:].rearrange("p t (h d) -> p (t h) d", h=4), in0=osum[:].rearrange("p t (h d) -> p (t h) d", h=4),
                                                   in1=hss[:].unsqueeze(2).broadcast_to([128, 32, 64]), op=ALU.mult),
                  reads=['osum%d' % t_ for t_ in range(NT)] + ['hss'], writes=['osum%d' % t_ for t_ in range(NT)])
            tk.op('dve', lambda e: e.tensor_tensor(out=osum[:], in0=osum[:], in1=ghgB[:].unsqueeze(1).broadcast_to([128, NT, 256]), op=ALU.mult),
                  reads=['osum%d' % t_ for t_ in range(NT)] + ['ghgB'], writes=['osum%d' % t_ for t_ in range(NT)])
            tk.op('dve', lambda e: e.tensor_tensor(out=mixed[:, :, 512:768], in0=osum[:], in1=sgr[:], op=ALU.mult),
                  reads=['osum%d' % t_ for t_ in range(NT)] + ['sgr%d' % t_ for t_ in range(NT)], writes=['mixed%d' % j for j in range(8)])

            if stop('R'):
                break
            tk.barrier()
            areset()
            uT = aget([128, 2, T], BF16)
            sgf = aget([128, NT, 256], BF16)
            ucs = aget([128, NT, 2, 256], BF16)
            yT = aget([128, 2, T], BF16)
            csnb = [aget([128, 2, 1024], BF16) for _ in range(4)]
            assert apos[0] + 2048 <= 11520
            wfs = aget([128, 2, 256])
            wfb = aget([128, 2, 256], BF16)
            junk = aget([128, 512], BF16)
            tmpf = aget([128, D])
            for kt_ in range(4):
                tk.dma('sp', csnb[kt_][:], csn_d.ap()[kt_], writes=['csnb%d' % kt_])
            assert True
            tk.dma('sp', wfs[:], wfn_d.ap()[l].rearrange("(c p) n -> p c n", p=128), writes=['wfs'])
            tk.op('pool', lambda e: e.tensor_copy(out=wfb[:], in_=wfs[:]), reads=['wfs'], writes=['wfb'])
            for t in range(NT):
                b = nb()
                for cs in range(2):
                    for ct in range(2):
                        tk.op('pe', lambda e, t=t, cs=cs, ct=ct: e.matmul(ps[b][:, cs * 256 + ct * 128:cs * 256 + ct * 128 + 128], lhsT=uT[:, ct, t * 128:(t + 1) * 128],
                                                                          rhs=C4S4[:, 3 + cs * 2 + ct, :], start=True, stop=True),
                              reads=['uT', 'cb16'], writes=[psk[b]])
                tk.op('act', lambda e, t=t: e.copy(out=ucs[:, t, :, :].rearrange("p a c -> p (a c)"), in_=ps[b][:, :]), reads=[psk[b]], writes=['ucs%d' % t])
            if stop('F0'):
                break
            yb = [nb() for _ in range(4)]
            for kt_ in range(8):
                ci = kt_ % 4
                if kt_ >= 4:
                    tk.dma('sp', csnb[ci][:], csn_d.ap()[kt_], writes=['csnb%d' % ci])
                for ct in range(2):
                    for g in range(2):
                        b = yb[ct * 2 + g]
                        for cs in range(2):
                            tk.op('pe', lambda e, kt_=kt_, ct=ct, g=g, cs=cs: e.matmul(ps[b][:, :], lhsT=ucs[:, kt_, cs, ct * 128:(ct + 1) * 128],
                                                                                       rhs=csnb[ci][:, cs, g * 512:(g + 1) * 512],
                                                                                       start=(kt_ == 0 and cs == 0), stop=(kt_ == 7 and cs == 1)),
                                  reads=['ucs%d' % kt_, 'csnb%d' % ci], writes=[psk[b]])
            for ct in range(2):
                for g in range(2):
                    b = yb[ct * 2 + g]
                    tk.op('act', lambda e, ct=ct, g=g, b=b: e.copy(out=yT[:, ct, g * 512:(g + 1) * 512], in_=ps[b][:, :]), reads=[psk[b]], writes=['yT%d_%d' % (ct, g)])
            for t in range(NT):
                b = nb()
                for ct in range(2):
                    tk.op('pe', lambda e, t=t, ct=ct: e.matmul(ps[b][:, 0:256], lhsT=yT[:, ct, t * 128:(t + 1) * 128], rhs=wfb[:, ct, :], start=(ct == 0), stop=(ct == 1)),
                          reads=['yT%d_%d' % (ct, t // 4), 'wfb'], writes=[psk[b]])
                tk.op('dve', lambda e, t=t: e.tensor_tensor(out=mixed[:, t, 768:1024], in0=ps[b][:, 0:256], in1=sgf[:, t, :], op=ALU.mult),
                      reads=[psk[b], 'sgf'], writes=['mixed%d' % t])

            if dbg and l == nl - 1:
                for t in range(NT):
                    tk.dma('sp', dbg_d.ap()[t * 128:(t + 1) * 128, :], mixed[:, t, :], reads=['mixed%d' % t])

            if stop('F1'):
                break
            w0 = next_w()
            w1 = next_w(prefetch=False)
            for t in range(NT):
                b = nb()
                for kc in range(8):
                    tk.op('pe', lambda e, t=t, kc=kc: e.transpose(out=psb16(b)[:, kc * 128:(kc + 1) * 128], in_=mixed[:, t, kc * 128:(kc + 1) * 128], identity=IDB),
                          reads=['mixed%d' % t, 'cb16'], writes=[psk[b]])
                tk.op('act', lambda e, t=t: e.copy(out=hT[:, :, t * 128:(t + 1) * 128], in_=psb16(b)[:, :].rearrange("p (k c) -> p k c", k=8)),
                      reads=[psk[b]], writes=['hT%d' % t])
            for t in range(NT):
                bb = [nb(), nb()]
                for hf, wi in enumerate((w0, w1)):
                    proj_tm(wi, 0, 512, t, bb[hf])
                    tk.op('act', lambda e, t=t, hf=hf: e.activation(out=junk[:], in_=ps[bb[hf]][:, :], func=AF.Square, accum_out=ssq[:, 8 + hf:9 + hf]),
                          reads=[psk[bb[hf]]], writes=['junk', 'ssq'])
                tk.op('dve', lambda e: e.tensor_tensor(out=rstd[:, 8:9], in0=ssq[:, 8:9], in1=ssq[:, 9:10], op=ALU.add), reads=['ssq'], writes=['rstd'])
                tk.op('dve', lambda e: e.tensor_scalar(out=rstd[:, 8:9], in0=rstd[:, 8:9], scalar1=1.0 / D, scalar2=EPS, op0=ALU.mult, op1=ALU.add),
                      reads=['rstd'], writes=['rstd'])
                tk.op('act', lambda e: e.activation(out=rstd[:, 8:9], in_=rstd[:, 8:9], func=AF.Ln), reads=['rstd'], writes=['rstd'])
                tk.op('act', lambda e: e.activation(out=rstd[:, 8:9], in_=rstd[:, 8:9], func=AF.Exp, scale=-0.5), reads=['rstd'], writes=['rstd'])
                for hf in range(2):
                    tk.op('dve', lambda e, hf=hf: e.scalar_tensor_tensor(out=tmpf[:, hf * 512:(hf + 1) * 512], in0=ps[bb[hf]][:, :], scalar=rstd[:, 8:9],
                                                                         in1=gg[:, hf * 512:(hf + 1) * 512], op0=ALU.mult, op1=ALU.mult),
                          reads=[psk[bb[hf]], 'rstd', 'gg'], writes=['tmpf'])
                tk.op('dve', lambda e, t=t: e.tensor_tensor(out=x_sb[:, t, :], in0=x_sb[:, t, :], in1=tmpf[:], op=ALU.add),
                      reads=['x%d' % t, 'tmpf'], writes=['x%d' % t])
            _issue(wstate['ptr'])

        for t in range(NT):
            tk.dma('sp', y_d.ap()[t * 128:(t + 1) * 128, :], x_sb[:, t, :], reads=['x%d' % t])
        tk.finish()
    return nc


def _consts(is_sample):
    cf32 = np.zeros((128, 7, 128), np.float32)
    p = np.arange(128)
    J2 = np.zeros((128, 128), np.float32)
    for a in range(2):
        for i in range(64):
            J2[a * 64 + i, a * 64 + 63 - i] = 1.0
    cf32[:, 0] = J2
    cf32[:, 1] = np.eye(128, dtype=np.float32)
    cm = np.zeros((128, 128), np.float32)
    if is_sample:
        qc = np.arange(64)
        c0 = np.clip(qc - 8, 0, 48)
        kc = np.arange(64)
        valid = (kc[:, None] >= c0[None, :]) & (kc[:, None] < c0[None, :] + 16)
        m = np.where(valid, 0.0, NEG).astype(np.float32)
        cm = np.tile(m, (2, 2))
    cf32[:, 2] = cm
    s = np.arange(32)[:, None]
    t = np.arange(32)[None, :]
    trf = (s <= t).astype(np.float32) - (s <= 15).astype(np.float32)
    trb = (s >= t).astype(np.float32) - (s >= 16).astype(np.float32)
    for a in range(4):
        cf32[a * 32:(a + 1) * 32, 3, a * 32:(a + 1) * 32] = trf
        cf32[a * 32:(a + 1) * 32, 4, a * 32:(a + 1) * 32] = trb
    sl = np.arange(128) % 32
    ch = np.arange(128) // 32
    selcols = np.zeros((128, 128), np.float32)
    for a in range(4):
        selcols[:, a * 2 + 0] = ((ch == a) & (sl <= 15))
        selcols[:, a * 2 + 1] = (ch == a)
        selcols[:, 8 + a * 2 + 0] = ((ch == 3 - a) & (sl >= 16))
        selcols[:, 8 + a * 2 + 1] = (ch == 3 - a)
        selcols[:, 18 + a] = (ch == a)
    selcols[:, 16] = (np.arange(128) < 64)
    selcols[:, 17] = (np.arange(128) >= 64)
    cf32[:, 5] = selcols
    cf32[0:64, 6, 0:64] = 1.0
    cf32[64:128, 6, 64:128] = 1.0
    rowb = np.zeros((128, 74), np.float32)
    for i, (j, kt) in enumerate(JK):
        for hf in range(2):
            for krl in range(2):
                if is_sample:
                    qr = 2 * j + hf
                    kr = 2 * kt + krl
                    r0 = int(np.clip(qr - 4, 0, 8))
                    ok = (r0 <= kr < r0 + 8)
                else:
                    ok = (kt // 2 == j // 2)
                rowb[krl * 64:(krl + 1) * 64, i * 2 + hf] = 0.0 if ok else NEG
    cb16 = np.zeros((128, 9, 128), np.float32)
    cb16[:, 7] = J2
    cb16[:, 8] = cm
    cb16[:, 0] = np.eye(128)
    mf = (s <= t).astype(np.float32)
    mb = (s >= t).astype(np.float32)
    z = np.zeros((64, 64), np.float32)
    for a in range(4):
        cb16[a * 32:(a + 1) * 32, 1, a * 32:(a + 1) * 32] = mf
        cb16[a * 32:(a + 1) * 32, 2, a * 32:(a + 1) * 32] = mb
    ang = 2 * np.pi * np.outer(np.arange(64), np.arange(64)) / 64
    c4 = np.cos(ang) / 8.0
    s4 = np.sin(ang) / 8.0
    for ct in range(2):
        cb16[:, 3 + ct] = np.block([[c4, z], [z, c4]])
        cb16[:, 5 + ct] = np.block([[s4, z], [z, s4]])
    n = 1024 if is_sample else 256
    idx = np.arange(n)
    a2 = 2 * np.pi * ((np.outer(idx, idx)) % n) / n
    cn = np.cos(a2) / np.sqrt(n)
    sn = -np.sin(a2) / np.sqrt(n)
    CN = np.zeros((1024, 1024), np.float64)
    SN = np.zeros((1024, 1024), np.float64)
    for i in range(1024 // n):
        CN[i * n:(i + 1) * n, i * n:(i + 1) * n] = cn
        SN[i * n:(i + 1) * n, i * n:(i + 1) * n] = sn
    csn = np.stack([CN.reshape(8, 128, 1024), SN.reshape(8, 128, 1024)], axis=2)
    return dict(cf32=cf32, rowbias=rowb, cb16=cb16.astype(ml_dtypes.bfloat16), csn=csn.astype(ml_dtypes.bfloat16))


def _in_maps(x_prompt, x_sample, cache_attn_k, cache_attn_v, state_hgrn, c, c_ctx,
             w_ada, b_ada, g_pre, w_in, rpb, lb_logits, g_hgrn, w_fnet, w_out, g_post):
    f = lambda a: np.ascontiguousarray(np.asarray(a, dtype=np.float32))
    shared = dict(w_ada=f(w_ada), b_ada=f(b_ada), g_pre=f(g_pre), w_in=f(w_in), lb_logits=f(lb_logits),
                  g_hgrn=f(g_hgrn), w_fnet=f(w_fnet), w_out=f(w_out), g_post=f(g_post))
    tp = np.zeros((NL, 8, 23, 127), np.float32)
    tp[:, :, 4:19, 48:79] = f(rpb)
    cs = _consts(True)
    cp = _consts(False)
    maps = []
    for i in range(8):
        m = dict(shared)
        if i < 4:
            m["x"] = f(x_sample[i])
            m["cvec"] = f(np.asarray(c[i]).reshape(8, 128).T)
            m["ctxk"] = f(np.asarray(cache_attn_k[i]).reshape(NL, 512, 512))
            m["ctxv"] = f(np.asarray(cache_attn_v[i]).reshape(NL, 512, 512))
            s = np.asarray(state_hgrn[i]).reshape(NL, 2, 2, 2, 64, 64)
            m["s0"] = f(s.transpose(0, 1, 3, 4, 2, 5).reshape(NL, 2, 128, 2, 64))
            m["flags"] = np.ones((128, 2), np.float32)
            m["tpad"] = tp
            m.update(cs)
        else:
            m["x"] = f(np.asarray(x_prompt[4 * (i - 4):4 * (i - 3)]).reshape(T, D))
            m["cvec"] = f(np.asarray(c_ctx).reshape(8, 128).T)
            m["ctxk"] = np.zeros((NL, 512, 512), np.float32)
            m["ctxv"] = np.zeros((NL, 512, 512), np.float32)
            m["s0"] = np.zeros((NL, 2, 128, 2, 64), np.float32)
            m["flags"] = np.zeros((128, 2), np.float32)
            m["tpad"] = np.zeros_like(tp)
            m.update(cp)
        maps.append(m)
    return maps


_NC_CACHE = {}


def kernel(**inputs):
    if 'nc' not in _NC_CACHE:
        _NC_CACHE['nc'] = build_nc()
    nc = _NC_CACHE['nc']
    maps = _in_maps(**inputs)
    res = run_bass_kernel_spmd(nc, maps, core_ids=list(range(8)))
    r = res.results
    y_sample = np.stack([r[i]["y"] for i in range(4)], axis=0).astype(np.float32)
    y_prompt = np.concatenate([r[i]["y"].reshape(4, 256, D) for i in range(4, 8)], axis=0).astype(np.float32)
    nk = np.concatenate([r[i]["newk"].reshape(NL, 4, 256, 8, 64).transpose(1, 0, 2, 3, 4) for i in range(4, 8)], axis=0)
    nv = np.concatenate([r[i]["newv"].reshape(NL, 4, 256, 8, 64).transpose(1, 0, 2, 3, 4) for i in range(4, 8)], axis=0)
    ns = np.concatenate([r[i]["news"].reshape(NL, 2, 4, 2, 64, 2, 64).transpose(2, 0, 1, 5, 3, 4, 6).reshape(4, NL, 2, 4, 64, 64)
                         for i in range(4, 8)], axis=0)
    return (y_prompt, y_sample, np.ascontiguousarray(nk, dtype=np.float32), np.ascontiguousarray(nv, dtype=np.float32),
            np.ascontiguousarray(ns, dtype=np.float32))
```
